# Optimizing a Trainium2 kernel written in Bass

```python
import math
import jax, jax.numpy as jnp
from jax import lax
import numpy as np

D_MODEL = 2048
BATCH = 32
SEQ = 256
DEPTH = 4
DEC_BATCH = 4
DEC_SEQ = 2048
PAST_LEN = 256

GRID_W = 64
WIN_R = 8
WIN_C = 16
N_HEADS_A = 16
HEAD_DIM_A = 64
W_A = N_HEADS_A * HEAD_DIM_A
Q_BLOCK = 128
N_HEADS_R = 16
HEAD_DIM_R = 64
W_R = N_HEADS_R * HEAD_DIM_R
LORA_W = 64
LORA_A = 64
LORA_G = 128
GN_EPS = 64e-5
W_C = 1024
POS_BANDS = 16
POS_EMB = 1 + 2 * POS_BANDS
FILTER_HIDDEN = 64
HY_TARGET = 1e-2
HY_FAST = 0.3
HY_SLOW = 1.5
D_FF = 4 * D_MODEL
N_MOD = 6
N_IN = 3 * W_A + 3 * W_R + 3 * W_C + 3 * D_MODEL
NORM_EPS = 1e-6
NEG_INF = -1e30

kernel_name = 'hybrid_natten_rwkv7_hyena_flow_step'


def rms_norm(x, g):
    xf = x.astype(jnp.float32)
    y = xf * lax.rsqrt(jnp.mean(xf * xf, -1, keepdims=True) + NORM_EPS)
    return (y * g.astype(jnp.float32)).astype(x.dtype)


def ada_modulation(cvec, w_mod, b_mod):
    m = jax.nn.silu(cvec) @ w_mod + b_mod
    return jnp.split(m[:, None, :], N_MOD, axis=-1)


def short_conv3(x, w, b):
    xp = jnp.pad(x, ((0, 0), (1, 1), (0, 0)))
    return xp[:, :-2] * w[0] + xp[:, 1:-1] * w[1] + xp[:, 2:] * w[2] + b


def attn_context(q, k, v):
    B, S, H, dh = q.shape
    nb = S // Q_BLOCK
    qb = jnp.moveaxis(q.reshape(B, nb, Q_BLOCK, H, dh), 1, 0)
    scale = HEAD_DIM_A ** -0.5

    def block(qi):
        s = jnp.einsum('bqhd,bkhd->bhqk', qi, k).astype(jnp.float32) * scale
        p = jax.nn.softmax(s, axis=-1).astype(v.dtype)
        return jnp.einsum('bhqk,bkhd->bqhd', p, v)

    o = lax.map(block, qb)
    return jnp.moveaxis(o, 0, 1).reshape(B, S, H * dh)


def na_indices(rows):
    wr = min(WIN_R, rows)
    r = np.arange(rows)
    r0 = np.clip(r - wr // 2, 0, rows - wr)
    key_rows = r0[:, None] + np.arange(wr)[None, :]
    dr = key_rows - r[:, None] + (WIN_R - 1)
    cq = np.arange(GRID_W)
    c0 = np.clip(cq - WIN_C // 2, 0, GRID_W - WIN_C)
    ck = np.arange(GRID_W)
    col_ok = (ck[None, :] >= c0[:, None]) & (ck[None, :] < c0[:, None] + WIN_C)
    dc = np.clip(ck[None, :] - cq[:, None], -(WIN_C - 1), WIN_C - 1) + (WIN_C - 1)
    mask = np.broadcast_to(col_ok[:, None, :], (GRID_W, wr, GRID_W)).reshape(GRID_W, wr * GRID_W)
    return wr, key_rows, dr, dc, mask


def attn_latent(q, k, v, k_ctx, v_ctx, rpb):
    B, L, H, dh = q.shape
    rows = L // GRID_W
    wr, key_rows, dr, dc, mask = na_indices(rows)
    nl = wr * GRID_W
    qg = q.reshape(B, rows, GRID_W, H, dh)
    kg = k.reshape(B, rows, GRID_W, H, dh)[:, key_rows].reshape(B, rows, nl, H, dh)
    vg = v.reshape(B, rows, GRID_W, H, dh)[:, key_rows].reshape(B, rows, nl, H, dh)
    scale = HEAD_DIM_A ** -0.5
    s_loc = jnp.einsum('brqhd,brkhd->bhrqk', qg, kg).astype(jnp.float32) * scale
    bias = rpb[:, dr[:, None, :, None], dc[None, :, None, :]]
    bias = bias.reshape(H, rows, GRID_W, nl).astype(jnp.float32)
    s_loc = jnp.where(mask, s_loc + bias[None], NEG_INF)
    s_ctx = jnp.einsum('brqhd,bkhd->bhrqk', qg, k_ctx).astype(jnp.float32) * scale
    p = jax.nn.softmax(jnp.concatenate([s_loc, s_ctx], -1), axis=-1).astype(v.dtype)
    o = (jnp.einsum('bhrqk,brkhd->brqhd', p[..., :nl], vg)
         + jnp.einsum('bhrqk,bkhd->brqhd', p[..., nl:], v_ctx))
    return o.reshape(B, L, H * dh)


def wkv_step(S, inp):
    r_t, w_t, k_t, v_t, kk_t, b_t = inp
    sa = jnp.einsum('behvk,behk->behv', S, -kk_t)
    S = S * w_t[..., None, :] + sa[..., None] * b_t[..., None, :] + v_t[..., None] * k_t[..., None, :]
    y = jnp.einsum('behvk,behk->behv', S, r_t)
    return S, y


def rwkv_branch(h, rkv, S0, lp):
    B, L, _ = h.shape
    f32 = jnp.float32
    heads = lambda t: t.reshape(t.shape[:-1] + (N_HEADS_R, HEAD_DIM_R))
    rkv = short_conv3(rkv, lp['wkv_conv_w'], lp['wkv_conv_b']).astype(f32)
    r, k, v = [heads(t) for t in jnp.split(rkv, 3, -1)]
    kk = k * heads(lp['wkv_k_k'].astype(f32))
    kk = kk * lax.rsqrt(jnp.sum(kk * kk, -1, keepdims=True) + 1e-12)
    lw = jnp.tanh(jnp.einsum('bld,edr->bler', h, lp['wkv_w1']).astype(f32))
    w_log = lp['wkv_w0'].astype(f32) + jnp.einsum('bler,erc->blec', lw, lp['wkv_w2'].astype(f32))
    decay = heads(jnp.exp(-jnp.exp(-jax.nn.softplus(-w_log) - 0.5)))
    la = jnp.einsum('bld,edr->bler', h, lp['wkv_a1']).astype(f32)
    a = heads(jax.nn.sigmoid(lp['wkv_a0'].astype(f32)
                             + jnp.einsum('bler,erc->blec', la, lp['wkv_a2'].astype(f32))))
    k_dir = k[:, :, None] * (1.0 + (a - 1.0) * heads(lp['wkv_k_a'].astype(f32)))
    kk_dir = jnp.broadcast_to(kk[:, :, None], a.shape)
    r_dir = jnp.broadcast_to(r[:, :, None], a.shape)
    v_dir = jnp.broadcast_to(v[:, :, None], a.shape)

    def orient(t):
        return jnp.stack([t[:, :, 0], jnp.flip(t[:, :, 1], 1)], 2)

    xs = tuple(jnp.moveaxis(orient(t), 1, 0)
               for t in (r_dir, decay, k_dir, v_dir, kk_dir, kk_dir * a))
    S_fin, ys = lax.scan(wkv_step, S0.astype(f32), xs)
    ys = jnp.moveaxis(ys, 0, 1)
    y = ys[:, :, 0] + jnp.flip(ys[:, :, 1], 1)
    mu = jnp.mean(y, -1, keepdims=True)
    var = jnp.mean(jnp.square(y - mu), -1, keepdims=True)
    yn = ((y - mu) * lax.rsqrt(var + GN_EPS)).reshape(B, L, W_R)
    yn = yn * lp['wkv_gn_g'].astype(f32) + lp['wkv_gn_b'].astype(f32)
    bonus = (jnp.sum(r * k * lp['wkv_r_k'].astype(f32), -1, keepdims=True) * v).reshape(B, L, W_R)
    g = (jax.nn.sigmoid(h @ lp['wkv_g1']) @ lp['wkv_g2']).astype(f32)
    return ((yn + bonus) * g).astype(h.dtype), S_fin


def hyena_pos_features(L):
    t = np.linspace(0.0, 1.0, L, dtype=np.float32)[:, None]
    w = 2.0 * np.pi * np.arange(L, dtype=np.float32)[:, None] / L
    f = np.linspace(1e-4, POS_BANDS - 1, POS_BANDS, dtype=np.float32)[None, :]
    z = np.concatenate([t, np.cos(f * w), -np.sin(f * w)], -1).astype(np.float32)
    dist = (np.abs(np.arange(L) - L // 2).astype(np.float32) / L)[:, None]
    deltas = np.abs(np.linspace(math.log(HY_TARGET) / HY_SLOW, math.log(HY_TARGET) / HY_FAST,
                                W_C, dtype=np.float32))[None, :]
    window = np.exp(-dist * deltas).astype(np.float32)
    return z, window


def hyena_branch(u, lp):
    B, L, _ = u.shape
    f32 = jnp.float32
    u = short_conv3(u, lp['hy_conv_w'], lp['hy_conv_b'])
    x0, x1, vv = jnp.split(u, 3, -1)
    z_pos, window = hyena_pos_features(L)
    freq = lp['hy_freq'].astype(f32)
    t = jnp.sin(freq * (z_pos @ lp['hy_f1'].astype(f32) + lp['hy_fb1'].astype(f32)))
    t = jnp.sin(freq * (t @ lp['hy_f2'].astype(f32) + lp['hy_fb2'].astype(f32)))
    filt = (t @ lp['hy_f3'].astype(f32)) * window
    filt = filt / (jnp.sum(jnp.abs(filt), 0, keepdims=True) + 1e-6)
    z = (vv * x1).astype(f32)
    n = 2 * L
    y = jnp.fft.irfft(jnp.fft.rfft(z, n=n, axis=1) * jnp.fft.rfft(filt, n=n, axis=0)[None],
                      n=n, axis=1)[:, L // 2: L // 2 + L]
    y = y + z * lp['hy_d'].astype(f32)
    return (x0.astype(f32) * y).astype(u.dtype)


def trunk_layer(x, cvec, lp, ctx_k=None, ctx_v=None, S0=None):
    B, L, _ = x.shape
    sh1, sc1, ga1, sh2, sc2, ga2 = ada_modulation(cvec, lp['w_mod'], lp['b_mod'])
    h = rms_norm(x, lp['ln1']) * (1.0 + sc1) + sh1
    proj = h @ lp['w_in']
    s1 = 3 * W_A
    s2 = s1 + 3 * W_R
    s3 = s2 + 3 * W_C
    qkv, rkv, hyu, gl = jnp.split(proj, [s1, s2, s3], -1)
    q, k, v = [t.reshape(B, L, N_HEADS_A, HEAD_DIM_A) for t in jnp.split(qkv, 3, -1)]
    if ctx_k is None:
        o_a = attn_context(q, k, v)
        S0 = jnp.zeros((B, 2, N_HEADS_R, HEAD_DIM_R, HEAD_DIM_R), jnp.float32)
    else:
        o_a = attn_latent(q, k, v, ctx_k, ctx_v, lp['rpb'])
    o_r, S_fin = rwkv_branch(h, rkv, S0, lp)
    o_c = hyena_branch(hyu, lp)
    g_a, g_r, g_c = jnp.split(jax.nn.sigmoid(gl), 3, -1)
    merged = g_a * (o_a @ lp['w_pa']) + g_r * (o_r @ lp['w_pr']) + g_c * (o_c @ lp['w_pc'])
    x = x + ga1 * (merged @ lp['w_out'])
    h2 = rms_norm(x, lp['ln2']) * (1.0 + sc2) + sh2
    f = jnp.square(jax.nn.relu(h2 @ lp['w_ff1'] + lp['b_ff1'])) @ lp['w_ff2'] + lp['b_ff2']
    x = x + ga2 * f
    return x, k, v, S_fin.astype(x.dtype)


def setup_inputs(seed: int = 0) -> dict:
    key = jax.random.key(seed)
    keys = iter(jax.random.split(key, 96))

    def nrm(shape, scale):
        return scale * jax.random.normal(next(keys), shape, jnp.float32)

    D = D_MODEL
    inp = {}
    inp['x_prompt'] = nrm((BATCH, SEQ, D), 1.0)
    inp['x_sample'] = nrm((DEC_BATCH, DEC_SEQ, D), 1.0)
    inp['cache_k'] = nrm((DEC_BATCH, DEPTH, PAST_LEN, N_HEADS_A, HEAD_DIM_A), 1.0)
    inp['cache_v'] = nrm((DEC_BATCH, DEPTH, PAST_LEN, N_HEADS_A, HEAD_DIM_A), 1.0)
    inp['state_wkv'] = nrm((DEC_BATCH, DEPTH, 2, N_HEADS_R, HEAD_DIM_R, HEAD_DIM_R), 0.5)
    inp['c'] = nrm((DEC_BATCH, D), 1.0)
    inp['c_ctx'] = nrm((D,), 1.0)
    inp['ln1_g'] = 1.0 + nrm((DEPTH, D), 0.01)
    inp['ln2_g'] = 1.0 + nrm((DEPTH, D), 0.01)
    inp['w_mod'] = nrm((DEPTH, D, N_MOD * D), 0.5 * D ** -0.5)
    inp['b_mod'] = nrm((DEPTH, N_MOD * D), 0.01)
    inp['w_in'] = nrm((DEPTH, D, N_IN), D ** -0.5)
    inp['rpb'] = nrm((DEPTH, N_HEADS_A, 2 * WIN_R - 1, 2 * WIN_C - 1), 0.1)
    inp['wkv_conv_w'] = nrm((DEPTH, 3, 3 * W_R), 0.5)
    inp['wkv_conv_b'] = nrm((DEPTH, 3 * W_R), 0.01)
    inp['wkv_w0'] = -1.0 + nrm((DEPTH, 2, W_R), 0.5)
    inp['wkv_w1'] = nrm((DEPTH, 2, D, LORA_W), D ** -0.5)
    inp['wkv_w2'] = nrm((DEPTH, 2, LORA_W, W_R), 0.1 * LORA_W ** -0.5)
    inp['wkv_a0'] = nrm((DEPTH, 2, W_R), 0.1)
    inp['wkv_a1'] = nrm((DEPTH, 2, D, LORA_A), D ** -0.5)
    inp['wkv_a2'] = nrm((DEPTH, 2, LORA_A, W_R), 0.1 * LORA_A ** -0.5)
    inp['wkv_g1'] = nrm((DEPTH, D, LORA_G), D ** -0.5)
    inp['wkv_g2'] = nrm((DEPTH, LORA_G, W_R), LORA_G ** -0.5)
    inp['wkv_k_k'] = 1.0 + nrm((DEPTH, W_R), 0.1)
    inp['wkv_k_a'] = 1.0 + nrm((DEPTH, W_R), 0.1)
    inp['wkv_r_k'] = nrm((DEPTH, N_HEADS_R, HEAD_DIM_R), 0.1)
    inp['wkv_gn_g'] = 1.0 + nrm((DEPTH, W_R), 0.01)
    inp['wkv_gn_b'] = nrm((DEPTH, W_R), 0.01)
    inp['hy_conv_w'] = nrm((DEPTH, 3, 3 * W_C), 0.5)
    inp['hy_conv_b'] = nrm((DEPTH, 3 * W_C), 0.01)
    inp['hy_f1'] = nrm((DEPTH, POS_EMB, FILTER_HIDDEN), POS_EMB ** -0.5)
    inp['hy_fb1'] = nrm((DEPTH, FILTER_HIDDEN), 0.1)
    inp['hy_f2'] = nrm((DEPTH, FILTER_HIDDEN, FILTER_HIDDEN), FILTER_HIDDEN ** -0.5)
    inp['hy_fb2'] = nrm((DEPTH, FILTER_HIDDEN), 0.1)
    inp['hy_freq'] = 1.0 + nrm((DEPTH, FILTER_HIDDEN), 0.1)
    inp['hy_f3'] = nrm((DEPTH, FILTER_HIDDEN, W_C), FILTER_HIDDEN ** -0.5)
    inp['hy_d'] = nrm((DEPTH, W_C), 1.0)
    inp['w_pa'] = nrm((DEPTH, W_A, D), W_A ** -0.5)
    inp['w_pr'] = nrm((DEPTH, W_R, D), W_R ** -0.5)
    inp['w_pc'] = nrm((DEPTH, W_C, D), W_C ** -0.5)
    inp['w_out'] = nrm((DEPTH, D, D), D ** -0.5)
    inp['w_ff1'] = nrm((DEPTH, D, D_FF), D ** -0.5)
    inp['b_ff1'] = nrm((DEPTH, D_FF), 0.01)
    inp['w_ff2'] = nrm((DEPTH, D_FF, D), D_FF ** -0.5)
    inp['b_ff2'] = nrm((DEPTH, D), 0.01)
    inp['final_g'] = 1.0 + nrm((D,), 0.01)
    return inp


def reference(x_prompt, x_sample, cache_k, cache_v, state_wkv, c, c_ctx,
              ln1_g, ln2_g, w_mod, b_mod, w_in, rpb,
              wkv_conv_w, wkv_conv_b, wkv_w0, wkv_w1, wkv_w2, wkv_a0, wkv_a1, wkv_a2,
              wkv_g1, wkv_g2, wkv_k_k, wkv_k_a, wkv_r_k, wkv_gn_g, wkv_gn_b,
              hy_conv_w, hy_conv_b, hy_f1, hy_fb1, hy_f2, hy_fb2, hy_freq, hy_f3, hy_d,
              w_pa, w_pr, w_pc, w_out, w_ff1, b_ff1, w_ff2, b_ff2, final_g):
    stacked = {
        'ln1': ln1_g, 'ln2': ln2_g, 'w_mod': w_mod, 'b_mod': b_mod, 'w_in': w_in, 'rpb': rpb,
        'wkv_conv_w': wkv_conv_w, 'wkv_conv_b': wkv_conv_b, 'wkv_w0': wkv_w0, 'wkv_w1': wkv_w1,
        'wkv_w2': wkv_w2, 'wkv_a0': wkv_a0, 'wkv_a1': wkv_a1, 'wkv_a2': wkv_a2,
        'wkv_g1': wkv_g1, 'wkv_g2': wkv_g2, 'wkv_k_k': wkv_k_k, 'wkv_k_a': wkv_k_a,
        'wkv_r_k': wkv_r_k, 'wkv_gn_g': wkv_gn_g, 'wkv_gn_b': wkv_gn_b,
        'hy_conv_w': hy_conv_w, 'hy_conv_b': hy_conv_b, 'hy_f1': hy_f1, 'hy_fb1': hy_fb1,
        'hy_f2': hy_f2, 'hy_fb2': hy_fb2, 'hy_freq': hy_freq, 'hy_f3': hy_f3, 'hy_d': hy_d,
        'w_pa': w_pa, 'w_pr': w_pr, 'w_pc': w_pc, 'w_out': w_out,
        'w_ff1': w_ff1, 'b_ff1': b_ff1, 'w_ff2': w_ff2, 'b_ff2': b_ff2,
    }
    yp = x_prompt
    ys = x_sample
    c_context = c_ctx[None, :]
    ks, vs, ss = [], [], []
    for l in range(DEPTH):
        lp = {name: arr[l] for name, arr in stacked.items()}
        yp, k_l, v_l, s_l = trunk_layer(yp, c_context, lp)
        ks.append(k_l)
        vs.append(v_l)
        ss.append(s_l)
        ys = trunk_layer(ys, c, lp, cache_k[:, l], cache_v[:, l], state_wkv[:, l])[0]
    y_prompt = rms_norm(yp, final_g)
    y_sample = rms_norm(ys, final_g)
    new_cache_k = jnp.stack(ks, 1)
    new_cache_v = jnp.stack(vs, 1)
    new_state_wkv = jnp.stack(ss, 1)
    return (y_prompt, y_sample, new_cache_k, new_cache_v, new_state_wkv)
```

```python
import math
from contextlib import ExitStack

import numpy as np
import concourse.bass as bass
import concourse.mybir as mybir
from concourse.bass_utils import run_bass_kernel_spmd

F32 = mybir.dt.float32
BF16 = mybir.dt.bfloat16
I32 = mybir.dt.int32
AF = mybir.ActivationFunctionType
ALU = mybir.AluOpType
AX = mybir.AxisListType
AP = bass.AP

D = 2048
DEPTH = 4
NP_TOK = 1024
NS_TOK = 2048
N_IN = 15360
D_FF = 8192


class Res:
    __slots__ = ("name", "w", "a", "r")

    def __init__(self, name):
        self.name = name
        self.w = {}
        self.a = {}
        self.r = {}


class Tile:
    def __init__(self, t, name):
        self.t = t
        self.res = Res(name)

    def __getitem__(self, k):
        return self.t[k]


class Sched:
    NDS = 40
    NPOOL = 8

    def __init__(self, nc, es):
        self.nc = nc
        self.E = {"pe": nc.tensor, "act": nc.scalar, "dve": nc.vector, "pool": nc.gpsimd, "sp": nc.sync}
        self.sem = {k: es.enter_context(nc.semaphore("c_" + k)) for k in self.E}
        self.cnt = {k: 0 for k in self.E}
        self.seen = {k: {} for k in self.E}
        self.dsem = [es.enter_context(nc.semaphore("d%d" % i)) for i in range(self.NDS)]
        self.dcnt = [0] * self.NDS
        self.dnext = 0
        self.dnext_pool = 0
        self.ninst = 0
        self.nwait = 0

    def _semobj(self, key):
        return self.sem[key] if isinstance(key, str) else self.dsem[key]

    def _wait(self, eng, deps, defer=False):
        need = {}
        for (k, v) in deps:
            if need.get(k, 0) < v:
                need[k] = v
        sn = self.seen[eng]
        todo = [(k, v) for k, v in need.items() if sn.get(k, 0) < v]
        last = None
        if defer and todo:
            last = todo.pop()
        for k, v in todo:
            self.E[eng].wait_ge(self._semobj(k), v)
            sn[k] = v
            self.ninst += 1
            self.nwait += 1
        if last is not None:
            sn[last[0]] = last[1]
        return last

    def _attach(self, ins, last):
        if last is not None:
            ins._wait_ge(self._semobj(last[0]), last[1])

    def _deps(self, eng, reads, writes, is_dma=False, acc=False):
        deps = []
        for r in reads:
            deps.extend(r.w.items())
            deps.extend(r.a.items())
        for w in writes:
            srcs = [w.w, w.r] if acc else [w.w, w.a, w.r]
            for d in srcs:
                deps.extend(d.items())
        return deps

    def _commit(self, tok, reads, writes, acc=False):
        k, v = tok
        for r in reads:
            r.r[k] = v
        for w in writes:
            if acc:
                w.a[k] = v
            else:
                w.w = {k: v}
                w.a = {}
                w.r = {}

    def op(self, eng, fn, reads=(), writes=()):
        last = self._wait(eng, self._deps(eng, reads, writes), defer=True)
        ins = fn(self.E[eng])
        self._attach(ins, last)
        self.cnt[eng] += 1
        ins.then_inc(self.sem[eng], 1)
        self.ninst += 1
        self._commit((eng, self.cnt[eng]), reads, writes)

    def mm(self, fns, reads, writes):
        last = self._wait("pe", self._deps("pe", reads, writes), defer=True)
        pe = self.E["pe"]
        ins = None
        for j, f in enumerate(fns):
            ins = f(pe)
            if j == 0:
                self._attach(ins, last)
        self.ninst += len(fns)
        self.cnt["pe"] += 1
        ins.then_inc(self.sem["pe"], 1)
        self._commit(("pe", self.cnt["pe"]), reads, writes)

    def dma(self, q, out, in_, reads=(), writes=(), acc=False, slow=False):
        if q == "pool":
            i = self.NDS - self.NPOOL + self.dnext_pool
            self.dnext_pool = (self.dnext_pool + 1) % self.NPOOL
        else:
            i = self.dnext
            self.dnext = (i + 1) % (self.NDS - self.NPOOL)
        deps = self._deps(q, reads, writes, is_dma=True, acc=acc)
        if self.dcnt[i] > 0:
            deps.append((i, self.dcnt[i]))
        last = self._wait(q, deps, defer=True)
        if slow:
            ins = self.E[q].dma_start(out=out, in_=in_, allow_slow_non_contiguous=True)
        else:
            ins = self.E[q].dma_start(out=out, in_=in_)
        self._attach(ins, last)
        ins.then_inc(self.dsem[i], 16)
        self.ninst += 1
        self.dcnt[i] += 16
        self._commit((i, self.dcnt[i]), reads, writes, acc=acc)

    def barrier(self):
        deps = [(k, self.cnt[k]) for k in self.E if k != "sp" and self.cnt[k] > 0]
        deps += [(i, c) for i, c in enumerate(self.dcnt) if c > 0]
        self._wait("sp", deps)
        ins = self.E["sp"].nop()
        self.cnt["sp"] += 1
        ins.then_inc(self.sem["sp"], 1)
        for k in self.E:
            if k != "sp":
                self._wait(k, [("sp", self.cnt["sp"])])
        for k in self.E:
            for k2 in self.E:
                self.seen[k][k2] = self.cnt[k2]
            for i, c in enumerate(self.dcnt):
                self.seen[k][i] = c


class Ctx:
    pass


def _col_ap(dram_ap_1d, n):
    return dram_ap_1d.rearrange("(j p) -> p j", p=128)


class Phase:
    def __init__(self, C, name):
        self.C = C
        self.name = name
        self.es = ExitStack()
        self.n = 0

    def __enter__(self):
        self.es.__enter__()
        return self

    def sb(self, shape, dt=F32, name=None):
        self.n += 1
        nm = "%s_%s_%d" % (self.name, name or "t", self.C.uid())
        t = self.es.enter_context(self.C.nc.sbuf_tensor(nm, list(shape), dt))
        return Tile(t, nm)

    def __exit__(self, *a):
        self.C.S.barrier()
        return self.es.__exit__(*a)


def build(cfg):
    nlayers = cfg.get("nlayers", DEPTH)
    dbg = cfg.get("dbg", ())
    mixers = cfg.get("mixers", ("a", "r", "c"))
    nc = bass.Bass("TRN2", target_bir_lowering=False)
    C = Ctx()
    C.nc = nc
    C._uid = 0

    def uid():
        C._uid += 1
        return C._uid
    C.uid = uid
    top = ExitStack()
    S = Sched(nc, top)
    C.S = S

    def din(name, shape, dt=F32):
        return nc.dram_tensor(name, list(shape), dt, kind="ExternalInput").ap()

    def dout(name, shape, dt=F32):
        return nc.dram_tensor(name, list(shape), dt, kind="ExternalOutput").ap()

    def dscr(name, shape, dt=F32):
        a = nc.dram_tensor(name, list(shape), dt, kind="Internal").ap()
        return a, Res(name)

    I = {}
    I["xp"] = din("xp", [NP_TOK, D])
    I["xs"] = din("xs", [NS_TOK, D])
    I["ck"] = din("ck", [DEPTH, 256, 1024])
    I["cv"] = din("cv", [DEPTH, 256, 1024])
    I["s0"] = din("s0", [DEPTH, 128, 1024])
    I["cvec"] = din("cvec", [2, D])
    wshapes = {
        "ln1_g": [DEPTH, D], "ln2_g": [DEPTH, D], "w_mod": [DEPTH, D, 6 * D], "b_mod": [DEPTH, 6 * D],
        "w_in": [DEPTH, D, N_IN], "rpb": [DEPTH, 16, 15, 31],
        "wkv_conv_w": [DEPTH, 3, 3072], "wkv_conv_b": [DEPTH, 3072], "wkv_w0": [DEPTH, 2, 1024],
        "wkv_w1": [DEPTH, 2, D, 64], "wkv_w2": [DEPTH, 2, 64, 1024], "wkv_a0": [DEPTH, 2, 1024],
        "wkv_a1": [DEPTH, 2, D, 64], "wkv_a2": [DEPTH, 2, 64, 1024], "wkv_g1": [DEPTH, D, 128],
        "wkv_g2": [DEPTH, 128, 1024], "wkv_k_k": [DEPTH, 1024], "wkv_k_a": [DEPTH, 1024],
        "wkv_r_k": [DEPTH, 1024], "wkv_gn_g": [DEPTH, 1024], "wkv_gn_b": [DEPTH, 1024],
        "hy_conv_w": [DEPTH, 3, 3072], "hy_conv_b": [DEPTH, 3072], "hy_f1": [DEPTH, 33, 64],
        "hy_fb1": [DEPTH, 64], "hy_f2": [DEPTH, 64, 64], "hy_fb2": [DEPTH, 64], "hy_freq": [DEPTH, 64],
        "hy_f3": [DEPTH, 64, 1024], "hy_d": [DEPTH, 1024],
        "w_pa": [DEPTH, 1024, D], "w_pr": [DEPTH, 1024, D], "w_pc": [DEPTH, 1024, D], "w_out": [DEPTH, D, D],
        "w_ff1": [DEPTH, D, D_FF], "b_ff1": [DEPTH, D_FF], "w_ff2": [DEPTH, D_FF, D], "b_ff2": [DEPTH, D],
        "final_g": [D],
    }
    W = {k: din(k, s) for k, s in wshapes.items()}
    O = {}
    O["yp"] = dout("yp", [NP_TOK, D])
    O["ys"] = dout("ys", [NS_TOK, D])
    O["nk"] = dout("nk", [4 * DEPTH * 256, 1024])
    O["nv"] = dout("nv", [4 * DEPTH * 256, 1024])
    O["ns"] = dout("ns", [4 * DEPTH * 32, 4096])
    ORES = {k: Res("o_" + k) for k in O}
    DBG = {}

    def dbg_out(name, shape, dt=F32):
        DBG[name] = dout("dbg_" + name, shape, dt)
        ORES["dbg_" + name] = Res("dbg_" + name)
        return DBG[name], ORES["dbg_" + name]

    G = []
    for gi, ntok in enumerate((NP_TOK, NS_TOK)):
        g = Ctx()
        g.i = gi
        g.ntok = ntok
        g.tag = "ps"[gi]
        g.xT, g.xT_r = dscr("xT%d" % gi, [16, 128, ntok])
        g.qkT, g.qkT_r = dscr("qkT%d" % gi, [16, 128, ntok], BF16)
        g.vtm, g.vtm_r = dscr("vtm%d" % gi, [ntok, 1024], BF16)
        g.rh, g.rh_r = dscr("rh%d" % gi, [ntok, 6144])
        g.gT, g.gT_r = dscr("gT%d" % gi, [48, 128, ntok])
        g.oT = []
        for b in range(3):
            g.oT.append(dscr("oT%d_%d" % (gi, b), [8, 128, ntok], BF16))
        g.f1T, g.f1T_r = dscr("f1T%d" % gi, [64, 128, ntok], BF16)
        G.append(g)
    mrow, mrow_r = dscr("mrow", [2, 6 * D])
    zero_r = Res("zeros")

    cst = Phase(C, "cst")
    cst.es.__enter__()
    ident = cst.sb([128, 128], F32, "ident")
    identb = cst.sb([128, 128], BF16, "identb")
    ones = cst.sb([128, 128], F32, "ones")
    S.op("pool", lambda e: e.memset(ident[:], 1.0), writes=[ident.res])
    S.op("pool", lambda e: e.affine_select(out=ident[:], in_=ident[:], pattern=[[-1, 128]], compare_op=ALU.is_equal,
                                            fill=0.0, base=0, channel_multiplier=1), reads=[ident.res], writes=[ident.res])
    S.op("dve", lambda e: e.tensor_copy(out=identb[:], in_=ident[:]), reads=[ident.res], writes=[identb.res])
    S.op("dve", lambda e: e.memset(ones[:], 1.0), writes=[ones.res])
    psum = []
    for i in range(8):
        t = top.enter_context(nc.psum_tensor("ps%d" % i, [128, 512], F32))
        psum.append(Tile(t, "ps%d" % i))
    C.ps_rr = 0

    def next_ps(lo=0, hi=8):
        C.ps_rr = (C.ps_rr + 1) % (hi - lo)
        return psum[lo + C.ps_rr]

    C.eng_rr = 0

    def alt_eng():
        C.eng_rr ^= 1
        return "act" if C.eng_rr else "dve"

    def gemm(ph, Wd, KC, col0, ncols, slabw, xT, ntok, form, epi, wbufs, Mrows=128, tok0=0, ps_lo=0, ps_hi=8):
        nslab = (ncols + slabw - 1) // slabw
        Wv = Wd.rearrange("(k p) n -> p k n", p=128)

        def load(s):
            wb = wbufs[s % len(wbufs)]
            c0 = col0 + s * slabw
            cw = min(slabw, col0 + ncols - c0)
            S.dma("pool", wb[:, :, 0:cw], Wv[:, :, c0:c0 + cw], writes=[wb.res])
        load(0)
        for s in range(nslab):
            if s + 1 < nslab:
                load(s + 1)
            wb = wbufs[s % len(wbufs)]
            c0 = col0 + s * slabw
            cw = min(slabw, col0 + ncols - c0)
            if form == "fm":
                for nb in range((cw + 127) // 128):
                    mw = min(128, cw - nb * 128)
                    for tt in range(ntok // 512):
                        pt = next_ps(ps_lo, ps_hi)
                        t0 = tok0 + tt * 512
                        fns = []
                        for kc in range(KC):
                            fns.append(lambda pe, kc=kc, pt=pt, wb=wb, nb=nb, mw=mw, t0=t0: pe.matmul(
                                pt[0:mw, :], lhsT=wb[:, kc, nb * 128:nb * 128 + mw], rhs=xT[:, kc, t0:t0 + 512],
                                start=(kc == 0), stop=(kc == KC - 1)))
                        S.mm(fns, reads=[wb.res, xT.res], writes=[pt.res])
                        epi(c0 + nb * 128, t0, pt, mw, 512)
            else:
                for tt in range(ntok // Mrows if Mrows == 128 else 1):
                    t0 = tok0 + tt * 128
                    for nh in range((cw + 511) // 512):
                        nw = min(512, cw - nh * 512)
                        pt = next_ps(ps_lo, ps_hi)
                        fns = []
                        for kc in range(KC):
                            fns.append(lambda pe, kc=kc, pt=pt, wb=wb, nh=nh, nw=nw, t0=t0: pe.matmul(
                                pt[0:Mrows, 0:nw], lhsT=xT[:, kc, t0:t0 + Mrows], rhs=wb[:, kc, nh * 512:nh * 512 + nw],
                                start=(kc == 0), stop=(kc == KC - 1)))
                        S.mm(fns, reads=[wb.res, xT.res], writes=[pt.res])
                        epi(c0 + nh * 512, t0, pt, Mrows, nw)

    def epi_store(ph, obufs, dst_fn, dst_res, func=None, bias_fn=None, scale=1.0):
        st = {"i": 0}

        def epi(c0, t0, pt, nr, ncv):
            ob = obufs[st["i"] % len(obufs)]
            st["i"] += 1
            if func is not None or bias_fn is not None:
                b = bias_fn(c0) if bias_fn is not None else None
                rd = [pt.res] + ([b[1]] if b is not None else [])
                S.op("act", lambda e: e.activation(out=ob[0:nr, 0:ncv], in_=pt[0:nr, 0:ncv], func=func or AF.Identity,
                                                   bias=(b[0] if b is not None else 0.0), scale=scale),
                     reads=rd, writes=[ob.res])
            else:
                eng = alt_eng()
                if eng == "act":
                    S.op("act", lambda e: e.copy(out=ob[0:nr, 0:ncv], in_=pt[0:nr, 0:ncv]), reads=[pt.res], writes=[ob.res])
                else:
                    S.op("dve", lambda e: e.tensor_copy(out=ob[0:nr, 0:ncv], in_=pt[0:nr, 0:ncv]), reads=[pt.res], writes=[ob.res])
            S.dma("sp", dst_fn(c0, t0, nr, ncv), ob[0:nr, 0:ncv], reads=[ob.res], writes=[dst_res], acc=True)
        return epi

    def load_x_T(g, src):
        with Phase(C, "ldx") as ph:
            xin = [ph.sb([128, D], F32, "xin") for _ in range(2)]
            xo = [ph.sb([128, 16, 128], F32, "xo") for _ in range(2)]
            for tt in range(g.ntok // 128):
                xi = xin[tt % 2]
                xq = xo[tt % 2]
                S.dma("sp", xi[:], src[tt * 128:(tt + 1) * 128, :], writes=[xi.res])
                for kq in range(4):
                    pt = next_ps()
                    fns = [lambda pe, j=j, pt=pt, xi=xi, kq=kq: pe.transpose(pt[:, j * 128:(j + 1) * 128],
                                                                                xi[:, (kq * 4 + j) * 128:(kq * 4 + j + 1) * 128], ident[:])
                           for j in range(4)]
                    S.mm(fns, reads=[xi.res, ident.res], writes=[pt.res])
                    eng = alt_eng()
                    dst = xq[:, kq * 4:(kq + 1) * 4, :]
                    srcp = pt[:, :].rearrange("p (a b) -> p a b", b=128)
                    if eng == "act":
                        S.op("act", lambda e: e.copy(out=dst, in_=srcp), reads=[pt.res], writes=[xq.res])
                    else:
                        S.op("dve", lambda e: e.tensor_copy(out=dst, in_=srcp), reads=[pt.res], writes=[xq.res])
                S.dma("sp", g.xT.rearrange("k p t -> p k t")[:, :, tt * 128:(tt + 1) * 128], xq[:], reads=[xq.res],
                      writes=[g.xT_r], acc=True)

    def modulation(l):
        with Phase(C, "mod") as ph:
            cT = ph.sb([128, 16, 2], F32, "cT")
            cTb = ph.sb([128, 16, 2], BF16, "cTb")
            for gi in range(2):
                S.dma("sp", cT[:, :, gi], I["cvec"][gi].rearrange("(k p) -> p k", p=128), writes=[cT.res], slow=True, acc=(gi > 0))
            S.op("act", lambda e: e.activation(out=cTb[:], in_=cT[:], func=AF.Silu), reads=[cT.res], writes=[cTb.res])
            wb = [ph.sb([128, 16, 512], BF16, "wb") for _ in range(2)]
            bm = [ph.sb([2, 512], F32, "bm") for _ in range(2)]
            ob = [ph.sb([2, 512], F32, "ob") for _ in range(2)]
            st = {"i": 0}

            def epi(c0, t0, pt, nr, ncv):
                i = st["i"] % 2
                st["i"] += 1
                S.dma("sp", bm[i][:], AP(W["b_mod"].tensor, l * 6 * D + c0, [[0, 2], [1, 512]]), writes=[bm[i].res])
                S.op("dve", lambda e: e.tensor_tensor(out=ob[i][:], in0=pt[0:2, :], in1=bm[i][:], op=ALU.add),
                     reads=[pt.res, bm[i].res], writes=[ob[i].res])
                S.dma("sp", mrow[:, c0:c0 + 512], ob[i][:], reads=[ob[i].res], writes=[mrow_r], acc=True)
            gemm(ph, W["w_mod"][l], 16, 0, 6 * D, 512, cTb, 2, "tm", epi, wb, Mrows=2)

    def load_cols(ph, l, g):
        cols = Ctx()
        m = ph.sb([128, 96], F32, "mcol")
        S.dma("sp", m[:], mrow[g.i].rearrange("(j p) -> p j", p=128), reads=[mrow_r], writes=[m.res], slow=True)
        ln = ph.sb([128, 32], F32, "lncol")
        S.dma("sp", ln[:, 0:16], W["ln1_g"][l].rearrange("(j p) -> p j", p=128), writes=[ln.res], slow=True)
        S.dma("sp", ln[:, 16:32], W["ln2_g"][l].rearrange("(j p) -> p j", p=128), writes=[ln.res], slow=True, acc=True)
        bf = ph.sb([128, 80], F32, "bfcol")
        S.dma("sp", bf[:, 0:64], W["b_ff1"][l].rearrange("(j p) -> p j", p=128), writes=[bf.res], slow=True)
        S.dma("sp", bf[:, 64:80], W["b_ff2"][l].rearrange("(j p) -> p j", p=128), writes=[bf.res], slow=True, acc=True)
        d = ph.sb([128, 48], F32, "dcol")
        S.op("dve", lambda e: e.scalar_tensor_tensor(out=d[:, 0:16], in0=m[:, 16:32], scalar=1.0, in1=ln[:, 0:16],
                                                     op0=ALU.add, op1=ALU.mult), reads=[m.res, ln.res], writes=[d.res])
        S.op("dve", lambda e: e.scalar_tensor_tensor(out=d[:, 16:32], in0=m[:, 64:80], scalar=1.0, in1=ln[:, 16:32],
                                                     op0=ALU.add, op1=ALU.mult), reads=[m.res, ln.res, d.res], writes=[d.res])
        S.op("dve", lambda e: e.tensor_tensor(out=d[:, 32:48], in0=m[:, 80:96], in1=bf[:, 64:80], op=ALU.mult),
             reads=[m.res, bf.res, d.res], writes=[d.res])
        cols.m, cols.d, cols.bf = m, d, bf
        cols.sh1 = lambda kc: m[:, kc:kc + 1]
        cols.ga1 = lambda kc: m[:, 32 + kc:33 + kc]
        cols.sh2 = lambda kc: m[:, 48 + kc:49 + kc]
        cols.ga2 = lambda kc: m[:, 80 + kc:81 + kc]
        cols.a1 = lambda kc: d[:, kc:kc + 1]
        cols.a2 = lambda kc: d[:, 16 + kc:17 + kc]
        cols.gb2 = lambda kc: d[:, 32 + kc:33 + kc]
        cols.b1 = lambda j: bf[:, j:j + 1]
        cols.res = [m.res, d.res, bf.res]
        return cols

    def norm_mod(ph, g, a_fn, sh_fn, cres, hT):
        TW = 256
        xt = [ph.sb([128, 16, TW], F32, "nx") for _ in range(2)]
        tm = [ph.sb([128, 16, TW], F32, "nt") for _ in range(2)]
        rs = [ph.sb([128, TW], F32, "nr") for _ in range(2)]
        xv = g.xT.rearrange("k p t -> p k t")
        for tt in range(g.ntok // TW):
            x_, t_, r_ = xt[tt % 2], tm[tt % 2], rs[tt % 2]
            S.dma("sp", x_[:], xv[:, :, tt * TW:(tt + 1) * TW], reads=[g.xT_r], writes=[x_.res])
            S.op("act", lambda e: e.activation(out=t_[:], in_=x_[:], func=AF.Square), reads=[x_.res], writes=[t_.res])
            pt = next_ps()
            fns = [lambda pe, kc=kc: pe.matmul(pt[:, 0:TW], lhsT=ones[:], rhs=t_[:, kc, :], start=(kc == 0), stop=(kc == 15))
                   for kc in range(16)]
            S.mm(fns, reads=[ones.res, t_.res], writes=[pt.res])
            S.op("dve", lambda e: e.tensor_scalar(out=r_[:], in0=pt[:, 0:TW], scalar1=1.0 / D, scalar2=1e-6, op0=ALU.mult,
                                                  op1=ALU.add), reads=[pt.res], writes=[r_.res])
            S.op("act", lambda e: e.sqrt(out=r_[:], in_=r_[:]), reads=[r_.res], writes=[r_.res])
            S.op("dve", lambda e: e.reciprocal(out=r_[:], in_=r_[:]), reads=[r_.res], writes=[r_.res])
            S.op("dve", lambda e: e.tensor_tensor(out=t_[:], in0=x_[:], in1=r_[:].unsqueeze(1).broadcast_to([128, 16, TW]),
                                                  op=ALU.mult), reads=[x_.res, r_.res], writes=[t_.res])
            for kc in range(16):
                S.op("act", lambda e, kc=kc: e.activation(out=hT[:, kc, tt * TW:(tt + 1) * TW], in_=t_[:, kc, :],
                                                          func=AF.Identity, scale=a_fn(kc), bias=sh_fn(kc)),
                     reads=[t_.res] + cres, writes=[hT.res])

    def in_proj(l, g, cols_holder):
        with Phase(C, "inp") as ph:
            cols = load_cols(ph, l, g)
            hT = ph.sb([128, 16, g.ntok], BF16, "hT")
            with Phase(C, "nrm") as ph2:
                norm_mod(ph2, g, cols.a1, cols.sh1, cols.res, hT)
            if "h" in dbg:
                S.dma("sp", DBG["h" + g.tag].rearrange("k p t -> p k t"), hT[:], reads=[hT.res], writes=[ORES["dbg_h" + g.tag]])
            wb = [ph.sb([128, 16, 512], BF16, "wb") for _ in range(2)]
            Wl = W["w_in"][l]
            ob = [ph.sb([128, 512], F32, "ob") for _ in range(4)]
            gemm(ph, Wl, 16, 9216, 6144, 512, hT, g.ntok, "fm",
                 epi_store(ph, ob, lambda c0, t0, nr, ncv: g.gT[(c0 - 9216) // 128, :, t0:t0 + ncv], g.gT_r, func=AF.Sigmoid), wb)
            if "a" in mixers:
                obb = [ph.sb([128, 512], BF16, "obb") for _ in range(4)]
                gemm(ph, Wl, 16, 0, 2048, 512, hT, g.ntok, "fm",
                     epi_store(ph, obb, lambda c0, t0, nr, ncv: g.qkT[c0 // 128, :, t0:t0 + ncv], g.qkT_r), wb)
            obf = [ph.sb([128, 512], F32, "obf") for _ in range(3)]
            obv = [ph.sb([128, 512], BF16, "obv") for _ in range(3)]
            st = {"i": 0}

            def epi_kv(c0, t0, pt, nr, ncv):
                i = st["i"] % 3
                st["i"] += 1
                isv = c0 >= 2048
                cc = c0 - (2048 if isv else 1024)
                if g.i == 0:
                    S.op("dve", lambda e: e.tensor_copy(out=obf[i][:], in_=pt[:, :]), reads=[pt.res], writes=[obf[i].res])
                    key = "nv" if isv else "nk"
                    r0_ = ((t0 // 256) * DEPTH + l) * 256 + (t0 % 256)
                    S.dma("sp", O[key][r0_:r0_ + 128, cc:cc + 512], obf[i][:], reads=[obf[i].res],
                          writes=[ORES[key]], acc=True)
                    if isv:
                        S.op("pool", lambda e: e.tensor_copy(out=obv[i][:], in_=obf[i][:]), reads=[obf[i].res], writes=[obv[i].res])
                elif isv:
                    S.op("dve", lambda e: e.tensor_copy(out=obv[i][:], in_=pt[:, :]), reads=[pt.res], writes=[obv[i].res])
                if isv:
                    S.dma("sp", g.vtm[t0:t0 + 128, cc:cc + 512], obv[i][:], reads=[obv[i].res], writes=[g.vtm_r], acc=True)
            if ("a" in mixers or g.i == 0) and not cfg.get("nokv"):
                if g.i == 0:
                    gemm(ph, Wl, 16, 1024, 2048, 512, hT, g.ntok, "tm", epi_kv, wb)
                else:
                    gemm(ph, Wl, 16, 2048, 1024, 512, hT, g.ntok, "tm", epi_kv, wb)
            if "r" in mixers or "c" in mixers:
                gemm(ph, Wl, 16, 3072, 6144, 512, hT, g.ntok, "tm",
                     epi_store(ph, ob, lambda c0, t0, nr, ncv: g.rh[t0:t0 + nr, c0 - 3072:c0 - 3072 + ncv], g.rh_r), wb)
            if "r" in mixers:
                rwkv_lora(ph, l, g, hT)
            return None

    SCALE = 0.125

    def softmax_rows(nr, ncol, sc_ap, sc_res, scale, small, pn, tag_reads=()):
        mx, nmx, rsum, rinv = small[0:nr, 0:1], small[0:nr, 1:2], small[0:nr, 2:3], small[0:nr, 3:4]
        S.op("dve", lambda e: e.tensor_reduce(out=mx, in_=sc_ap, axis=AX.X, op=ALU.max), reads=[sc_res], writes=[small.res])
        S.op("dve", lambda e: e.tensor_scalar(out=nmx, in0=mx, scalar1=-scale, scalar2=None, op0=ALU.mult),
             reads=[small.res], writes=[small.res])
        S.op("dve", lambda e: e.memset(rsum, 0.0), reads=[small.res], writes=[small.res])
        return mx, nmx, rsum, rinv

    def attn_prompt(l, g):
        with Phase(C, "attp") as ph:
            qk = ph.sb([128, 16, NP_TOK], BF16, "qk")
            V = ph.sb([128, 8, 1024], BF16, "V")
            oa = ph.sb([128, 8, NP_TOK], BF16, "oa")
            S.dma("sp", qk[:], g.qkT.rearrange("k p t -> p k t"), reads=[g.qkT_r], writes=[qk.res])
            S.dma("sp", V[:], g.vtm.rearrange("(j p) c -> p j c", p=128), reads=[g.vtm_r], writes=[V.res])
            pb = [ph.sb([128, 256], F32, "pb") for _ in range(2)]
            pn = [ph.sb([128, 256], BF16, "pn") for _ in range(2)]
            PT = [ph.sb([128, 2, 128], BF16, "PT") for _ in range(2)]
            sm = [ph.sb([128, 8], F32, "sm") for _ in range(2)]
            u = 0
            for s_ in range(4):
                for h in range(16):
                    c, p0 = h // 2, (h % 2) * 64
                    for qt in range(2):
                        i = u % 2
                        u += 1
                        q0 = s_ * 256 + qt * 128
                        ps = next_ps()
                        S.mm([lambda pe: pe.matmul(ps[:, 0:256], lhsT=qk[p0:p0 + 64, c, q0:q0 + 128],
                                                   rhs=qk[p0:p0 + 64, 8 + c, s_ * 256:(s_ + 1) * 256], start=True, stop=True)],
                             reads=[qk.res], writes=[ps.res])
                        mx, nmx, rsum, rinv = softmax_rows(128, 256, ps[:, 0:256], ps.res, SCALE, sm[i], None)
                        S.op("act", lambda e: e.activation(out=pb[i][:], in_=ps[:, 0:256], func=AF.Exp, bias=nmx, scale=SCALE,
                                                           accum_out=rsum), reads=[ps.res, sm[i].res], writes=[pb[i].res, sm[i].res])
                        S.op("dve", lambda e: e.reciprocal(out=rinv, in_=rsum), reads=[sm[i].res], writes=[sm[i].res])
                        S.op("dve", lambda e: e.tensor_scalar(out=pn[i][:], in0=pb[i][:], scalar1=rinv, scalar2=None, op0=ALU.mult),
                             reads=[pb[i].res, sm[i].res], writes=[pn[i].res])
                        pt2 = next_ps()
                        ptb = pt2[:, :].bitcast(BF16)
                        S.mm([lambda pe, kt=kt: pe.transpose(ptb[:, kt * 128:(kt + 1) * 128], pn[i][:, kt * 128:(kt + 1) * 128], identb[:])
                              for kt in range(2)], reads=[pn[i].res, identb.res], writes=[pt2.res])
                        S.op("act", lambda e: e.copy(out=PT[i][:], in_=ptb[:, 0:256].rearrange("p (a b) -> p a b", b=128)),
                             reads=[pt2.res], writes=[PT[i].res])
                        po = next_ps()
                        S.mm([lambda pe, kt=kt: pe.matmul(po[:, 0:128], lhsT=V[:, s_ * 2 + kt, c * 128:(c + 1) * 128], rhs=PT[i][:, kt, :],
                                                          start=(kt == 0), stop=(kt == 1)) for kt in range(2)],
                             reads=[V.res, PT[i].res], writes=[po.res])
                        S.op("dve", lambda e: e.tensor_copy(out=oa[p0:p0 + 64, c, q0:q0 + 128], in_=po[p0:p0 + 64, 0:128]),
                             reads=[po.res], writes=[oa.res])
            S.dma("sp", g.oT[0][0].rearrange("k p t -> p k t"), oa[:], reads=[oa.res], writes=[g.oT[0][1]])

    rpbp, rpbp_r = dscr("rpbp", [240, 157])
    rrep, rrep_r = dscr("rrep", [240, 64, 157])
    I["natmask"] = din("natmask", [64, 64])

    def rcls(r):
        return 7 - r if r <= 3 else (3 if r <= 28 else 31 - r)

    def attn_sample(l, g):
        with Phase(C, "atts") as ph:
            z = ph.sb([128, 157], F32, "z")
            S.op("pool", lambda e: e.memset(z[:], 0.0), writes=[z.res])
            S.dma("sp", rpbp[0:128, :], z[:], reads=[z.res], writes=[rpbp_r])
            S.dma("sp", rpbp[128:240, :], z[0:112, :], reads=[z.res], writes=[rpbp_r], acc=True)
            S.dma("sp", rpbp[:, 63:94], W["rpb"][l].rearrange("h r c -> (h r) c"), writes=[rpbp_r], slow=True)
            for q4 in range(4):
                S.dma("sp", rrep[q4 * 60:(q4 + 1) * 60], AP(rpbp.tensor, q4 * 60 * 157, [[157, 60], [0, 64], [1, 157]]),
                      reads=[rpbp_r], writes=[rrep_r], acc=(q4 > 0))
            mk = ph.sb([64, 64], F32, "mk")
            S.dma("sp", mk[:], I["natmask"], writes=[mk.res])
            qk = [ph.sb([128, 2, NS_TOK], BF16, "qk") for _ in range(2)]
            Ve = [ph.sb([128, 16, 128], BF16, "Ve") for _ in range(2)]
            Vo = [ph.sb([128, 15, 128], BF16, "Vo") for _ in range(2)]
            Vc = [ph.sb([128, 2, 128], BF16, "Vc") for _ in range(2)]
            ckt = [ph.sb([128, 2, 128], F32, "ckt") for _ in range(2)]
            kcT = [ph.sb([128, 256], BF16, "kcT") for _ in range(2)]
            ob = [ph.sb([128, NS_TOK], BF16, "ob") for _ in range(2)]
            bm = [ph.sb([64, 8, 512], F32, "bm") for _ in range(2)]
            sc = [ph.sb([64, 768], F32, "sc") for _ in range(2)]
            pb = [ph.sb([64, 768], F32, "pb") for _ in range(2)]
            pn = [ph.sb([64, 768], BF16, "pn") for _ in range(2)]
            PT = [ph.sb([128, 6, 64], BF16, "PT") for _ in range(2)]
            sm = [ph.sb([128, 8], F32, "sm") for _ in range(2)]
            qv = g.qkT.rearrange("k p t -> p k t")
            u = 0
            for c in range(8):
                b_ = c % 2
                S.dma("sp", qk[b_][:, 0, :], qv[:, c, :], reads=[g.qkT_r], writes=[qk[b_].res])
                S.dma("sp", qk[b_][:, 1, :], qv[:, 8 + c, :], reads=[g.qkT_r], writes=[qk[b_].res], acc=True)
                S.dma("sp", Ve[b_][:], g.vtm[:, c * 128:(c + 1) * 128].rearrange("(j p) c -> p j c", p=128), reads=[g.vtm_r],
                      writes=[Ve[b_].res])
                S.dma("sp", Vo[b_][:], g.vtm[64:64 + 15 * 128, c * 128:(c + 1) * 128].rearrange("(j p) c -> p j c", p=128),
                      reads=[g.vtm_r], writes=[Vo[b_].res])
                S.dma("pool", Vc[b_][:], I["cv"][l][:, c * 128:(c + 1) * 128].rearrange("(j p) c -> p j c", p=128), writes=[Vc[b_].res])
                S.dma("sp", ckt[b_][:], I["ck"][l][:, c * 128:(c + 1) * 128].rearrange("(j p) c -> p j c", p=128), writes=[ckt[b_].res])
                pk = next_ps()
                S.mm([lambda pe, t=t: pe.transpose(pk[:, t * 128:(t + 1) * 128], ckt[b_][:, t, :], ident[:]) for t in range(2)],
                     reads=[ckt[b_].res, ident.res], writes=[pk.res])
                S.op("act", lambda e: e.copy(out=kcT[b_][:], in_=pk[:, 0:256]), reads=[pk.res], writes=[kcT[b_].res])
                for hh in range(2):
                    h = 2 * c + hh
                    p0 = hh * 64
                    bmh = bm[hh]
                    for o in range(8):
                        S.dma("sp", bmh[:, o, :].rearrange("p (j k) -> p j k", k=64),
                              AP(rrep.tensor, ((h * 15 + o) * 64) * 157 + 78, [[156, 64], [64 * 157, 8], [1, 64]]),
                              reads=[rrep_r], writes=[bmh.res], acc=(o > 0))
                    S.op("pool", lambda e: e.tensor_tensor(out=bmh[:].rearrange("p o (j k) -> p (o j) k", k=64),
                                                           in0=bmh[:].rearrange("p o (j k) -> p (o j) k", k=64),
                                                           in1=mk[:].unsqueeze(1).broadcast_to([64, 64, 64]), op=ALU.add),
                         reads=[bmh.res, mk.res], writes=[bmh.res])
                    for r in range(32):
                        i = u % 2
                        u += 1
                        r0 = min(max(r - 4, 0), 24)
                        o = rcls(r)
                        psA, psB = next_ps(), next_ps()
                        qa = qk[b_][p0:p0 + 64, 0, r * 64:(r + 1) * 64]
                        S.mm([lambda pe: pe.matmul(psA[0:64, 0:512], lhsT=qa, rhs=qk[b_][p0:p0 + 64, 1, r0 * 64:r0 * 64 + 512],
                                                   start=True, stop=True)], reads=[qk[b_].res], writes=[psA.res])
                        S.mm([lambda pe: pe.matmul(psB[0:64, 0:256], lhsT=qa, rhs=kcT[b_][p0:p0 + 64, :], start=True, stop=True)],
                             reads=[qk[b_].res, kcT[b_].res], writes=[psB.res])
                        S.op("dve", lambda e: e.scalar_tensor_tensor(out=sc[i][:, 0:512], in0=psA[0:64, 0:512], scalar=SCALE,
                                                                     in1=bmh[:, o, :], op0=ALU.mult, op1=ALU.add),
                             reads=[psA.res, bmh.res], writes=[sc[i].res])
                        S.op("act", lambda e: e.mul(out=sc[i][:, 512:768], in_=psB[0:64, 0:256], mul=SCALE), reads=[psB.res, sc[i].res],
                             writes=[sc[i].res])
                        mx, nmx, rsum, rinv = softmax_rows(64, 768, sc[i][:], sc[i].res, 1.0, sm[i], None)
                        S.op("act", lambda e: e.activation(out=pb[i][:], in_=sc[i][:], func=AF.Exp, bias=nmx, scale=1.0,
                                                           accum_out=rsum), reads=[sc[i].res, sm[i].res], writes=[pb[i].res, sm[i].res])
                        S.op("dve", lambda e: e.reciprocal(out=rinv, in_=rsum), reads=[sm[i].res], writes=[sm[i].res])
                        S.op("dve", lambda e: e.tensor_scalar(out=pn[i][:], in0=pb[i][:], scalar1=rinv, scalar2=None, op0=ALU.mult),
                             reads=[pb[i].res, sm[i].res], writes=[pn[i].res])
                        pt2 = next_ps()
                        S.mm([lambda pe, j=j: pe.matmul(pt2[:, j * 64:(j + 1) * 64], lhsT=pn[i][:, j * 128:(j + 1) * 128], rhs=identb[0:64, 0:64],
                                                        start=True, stop=True)
                              for j in range(6)], reads=[pn[i].res, identb.res], writes=[pt2.res])
                        S.op("act", lambda e: e.copy(out=PT[i][:], in_=pt2[:, 0:384].rearrange("p (a b) -> p a b", b=64)),
                             reads=[pt2.res], writes=[PT[i].res])
                        po = next_ps()
                        fns = []
                        for j in range(6):
                            if j < 4:
                                vt = Ve[b_][:, r0 // 2 + j, :] if r0 % 2 == 0 else Vo[b_][:, (r0 - 1) // 2 + j, :]
                            else:
                                vt = Vc[b_][:, j - 4, :]
                            fns.append(lambda pe, j=j, vt=vt: pe.matmul(po[:, 0:64], lhsT=vt, rhs=PT[i][:, j, :], start=(j == 0), stop=(j == 5)))
                        S.mm(fns, reads=[Ve[b_].res, Vo[b_].res, Vc[b_].res, PT[i].res], writes=[po.res])
                        S.op("dve", lambda e: e.tensor_copy(out=ob[b_][p0:p0 + 64, r * 64:(r + 1) * 64], in_=po[p0:p0 + 64, 0:64]),
                             reads=[po.res], writes=[ob[b_].res])
                S.dma("sp", g.oT[0][0][c], ob[b_][:], reads=[ob[b_].res], writes=[g.oT[0][1]], acc=True)

    HY = {}
    for L_ in (256, 2048):
        HY[L_] = dict(zposT=din("zposT%d" % L_, [33, L_]), win=din("win%d" % L_, [L_, 1024]),
                      Ff=din("Ff%d" % L_, [L_, 2 * L_]), Fi=din("Fi%d" % L_, [2 * L_, L_]))
        HY[L_]["Hs"], HY[L_]["Hs_r"] = dscr("Hs%d" % L_, [2 * L_ // 128, 128, 1024])
    for g in G:
        g.zbs, g.zbs_r = dscr("zbs%d" % g.i, [g.ntok, 1024], BF16)
        g.zd, g.zd_r = dscr("zd%d" % g.i, [g.ntok, 1024])
        g.x0s, g.x0s_r = dscr("x0s%d" % g.i, [g.ntok, 1024])
    TWO_PI = 2.0 * math.pi

    def bcast_row(dram_ap_tensor, offset, n):
        return AP(dram_ap_tensor, offset, [[0, 128], [1, n]])

    def hyena_filter(l, L):
        hy_ = HY[L]
        LT = L // 128
        with Phase(C, "hyf") as ph:
            f1 = ph.sb([33, 64], F32, "f1")
            f2 = ph.sb([64, 64], F32, "f2")
            f3 = ph.sb([64, 1024], F32, "f3")
            cc = ph.sb([64, 4], F32, "cc")
            zp = ph.sb([33, L], F32, "zp")
            S.dma("sp", f1[:], W["hy_f1"][l], writes=[f1.res])
            S.dma("sp", f2[:], W["hy_f2"][l], writes=[f2.res])
            S.dma("sp", f3[:], W["hy_f3"][l], writes=[f3.res])
            for j, nm in enumerate(("hy_fb1", "hy_fb2", "hy_freq")):
                S.dma("sp", cc[:, j:j + 1], W[nm][l].rearrange("(p o) -> p o", o=1), writes=[cc.res], slow=True, acc=(j > 0))
            S.dma("sp", zp[:], hy_["zposT"], writes=[zp.res])
            t1 = ph.sb([64, L], F32, "t1")
            t2 = ph.sb([64, L], F32, "t2")
            a = ph.sb([64, 512], F32, "a")
            ki = ph.sb([64, 512], I32, "ki")
            kf = ph.sb([64, 512], F32, "kf")
            cw = min(512, L)

            def sin_layer(lt, K, src, dst, bcol):
                for cb in range(L // cw):
                    pt = next_ps(0, 6)
                    S.mm([lambda pe: pe.matmul(pt[0:64, 0:cw], lhsT=lt[0:K, :], rhs=src[0:K, cb * cw:(cb + 1) * cw], start=True, stop=True)],
                         reads=[lt.res, src.res], writes=[pt.res])
                    S.op("dve", lambda e: e.tensor_scalar(out=a[:, 0:cw], in0=pt[0:64, 0:cw], scalar1=cc[:, bcol:bcol + 1],
                                                          scalar2=cc[:, 2:3], op0=ALU.add, op1=ALU.mult), reads=[pt.res, cc.res], writes=[a.res])
                    S.op("dve", lambda e: e.tensor_scalar(out=ki[:, 0:cw], in0=a[:, 0:cw], scalar1=1.0 / TWO_PI, scalar2=None, op0=ALU.mult),
                         reads=[a.res], writes=[ki.res])
                    S.op("dve", lambda e: e.tensor_copy(out=kf[:, 0:cw], in_=ki[:, 0:cw]), reads=[ki.res], writes=[kf.res])
                    S.op("dve", lambda e: e.scalar_tensor_tensor(out=a[:, 0:cw], in0=kf[:, 0:cw], scalar=-TWO_PI, in1=a[:, 0:cw],
                                                                 op0=ALU.mult, op1=ALU.add), reads=[kf.res, a.res], writes=[a.res])
                    S.op("dve", lambda e: e.tensor_scalar(out=a[:, 0:cw], in0=a[:, 0:cw], scalar1=-math.pi, scalar2=math.pi, op0=ALU.max,
                                                          op1=ALU.min), reads=[a.res], writes=[a.res])
                    S.op("act", lambda e: e.activation(out=dst[:, cb * cw:(cb + 1) * cw], in_=a[:, 0:cw], func=AF.Sin),
                         reads=[a.res], writes=[dst.res])
            sin_layer(f1, 33, zp, t1, 0)
            sin_layer(f2, 64, t1, t2, 1)
            filtb = ph.sb([128, LT, 1024], BF16, "filtb")
            winb = [ph.sb([128, 1024], F32, "winb") for _ in range(2)]
            ft = [ph.sb([128, 1024], F32, "ft") for _ in range(2)]
            fa = [ph.sb([128, 1024], F32, "fa") for _ in range(2)]
            for tt in range(LT):
                i = tt % 2
                S.dma("sp", winb[i][:], hy_["win"][tt * 128:(tt + 1) * 128, :], writes=[winb[i].res])
                for hf in range(2):
                    pt = next_ps(0, 6)
                    S.mm([lambda pe: pe.matmul(pt[:, :], lhsT=t2[0:64, tt * 128:(tt + 1) * 128], rhs=f3[0:64, hf * 512:(hf + 1) * 512],
                                               start=True, stop=True)], reads=[t2.res, f3.res], writes=[pt.res])
                    S.op("dve", lambda e: e.tensor_tensor(out=ft[i][:, hf * 512:(hf + 1) * 512], in0=pt[:, :],
                                                          in1=winb[i][:, hf * 512:(hf + 1) * 512], op=ALU.mult),
                         reads=[pt.res, winb[i].res], writes=[ft[i].res])
                S.op("act", lambda e: e.activation(out=fa[i][:], in_=ft[i][:], func=AF.Abs), reads=[ft[i].res], writes=[fa[i].res])
                S.op("pool", lambda e: e.tensor_copy(out=filtb[:, tt, :], in_=ft[i][:]), reads=[ft[i].res], writes=[filtb.res])
                for hf in range(2):
                    pacc = psum[6 + hf]
                    S.mm([lambda pe: pe.matmul(pacc[:, :], lhsT=ones[:], rhs=fa[i][:, hf * 512:(hf + 1) * 512], start=(tt == 0),
                                               stop=(tt == LT - 1))], reads=[ones.res, fa[i].res], writes=[pacc.res])
            inv = ph.sb([128, 1024], F32, "inv")
            for hf in range(2):
                S.op("dve", lambda e: e.tensor_scalar(out=inv[:, hf * 512:(hf + 1) * 512], in0=psum[6 + hf][:, :], scalar1=1e-6,
                                                      scalar2=None, op0=ALU.add), reads=[psum[6 + hf].res, inv.res], writes=[inv.res])
            S.op("dve", lambda e: e.reciprocal(out=inv[:], in_=inv[:]), reads=[inv.res], writes=[inv.res])
            wb = [ph.sb([128, LT, 512], BF16, "wbF") for _ in range(2)]
            ob = [ph.sb([128, 512], F32, "ob") for _ in range(3)]
            st = {"i": 0}

            def epi(c0, t0, pt, nr, ncv):
                i = st["i"] % 3
                st["i"] += 1
                S.op("dve", lambda e: e.tensor_tensor(out=ob[i][:], in0=pt[:, :], in1=inv[:, t0:t0 + 512], op=ALU.mult),
                     reads=[pt.res, inv.res], writes=[ob[i].res])
                S.dma("sp", hy_["Hs"][c0 // 128, :, t0:t0 + 512], ob[i][:], reads=[ob[i].res], writes=[hy_["Hs_r"]], acc=True)
            gemm(ph, hy_["Ff"], LT, 0, 2 * L, 512, filtb, 1024, "fm", epi, wb, ps_lo=0, ps_hi=6)

    def hyena_conv(l, g, L):
        with Phase(C, "hyc") as ph:
            cwt = ph.sb([128, 3, 3072], F32, "cw")
            cbt = ph.sb([128, 3072], F32, "cb")
            drt = ph.sb([128, 1024], F32, "dr")
            for j in range(3):
                S.dma("sp", cwt[:, j, :], bcast_row(W["hy_conv_w"].tensor, (l * 3 + j) * 3072, 3072), writes=[cwt.res], acc=(j > 0))
            S.dma("sp", cbt[:], bcast_row(W["hy_conv_b"].tensor, l * 3072, 3072), writes=[cbt.res])
            S.dma("sp", drt[:], bcast_row(W["hy_d"].tensor, l * 1024, 1024), writes=[drt.res])
            xm = [ph.sb([128, 1024], F32, "xm") for _ in range(2)]
            xc = [ph.sb([128, 1024], F32, "xc") for _ in range(2)]
            xp = [ph.sb([128, 1024], F32, "xp") for _ in range(2)]
            uu = [[ph.sb([128, 1024], F32, "u%d" % cg) for cg in range(3)] for _ in range(2)]
            tq = [ph.sb([128, 1024], F32, "tq") for _ in range(2)]
            zf = [ph.sb([128, 1024], F32, "zf") for _ in range(2)]
            zbt = [ph.sb([128, 1024], BF16, "zb") for _ in range(2)]
            zdt = [ph.sb([128, 1024], F32, "zd") for _ in range(2)]
            k = 0
            for tt in range(g.ntok // 128):
                t0 = tt * 128
                sp_ = t0 % L
                bi = tt % 2
                for cg in range(3):
                    ki_ = k % 2
                    k += 1
                    c0 = 3072 + cg * 1024
                    xm_, xc_, xp_, u, t = xm[ki_], xc[ki_], xp[ki_], uu[bi][cg], tq[ki_]
                    if sp_ == 0:
                        S.op("pool", lambda e: e.memset(xm_[0:1, :], 0.0), writes=[xm_.res])
                        S.dma("sp", xm_[1:128, :], g.rh[t0:t0 + 127, c0:c0 + 1024], reads=[g.rh_r], writes=[xm_.res], acc=True)
                    else:
                        S.dma("sp", xm_[:], g.rh[t0 - 1:t0 + 127, c0:c0 + 1024], reads=[g.rh_r], writes=[xm_.res])
                    S.dma("sp", xc_[:], g.rh[t0:t0 + 128, c0:c0 + 1024], reads=[g.rh_r], writes=[xc_.res])
                    if sp_ + 128 == L:
                        S.op("pool", lambda e: e.memset(xp_[:], 0.0), writes=[xp_.res])
                        S.dma("sp", xp_[0:127, :], g.rh[t0 + 1:t0 + 128, c0:c0 + 1024], reads=[g.rh_r], writes=[xp_.res], acc=True)
                    else:
                        S.dma("sp", xp_[:], g.rh[t0 + 1:t0 + 129, c0:c0 + 1024], reads=[g.rh_r], writes=[xp_.res])
                    wv = lambda j: cwt[:, j, cg * 1024:(cg + 1) * 1024]
                    S.op("dve", lambda e: e.tensor_tensor(out=u[:], in0=xm_[:], in1=wv(0), op=ALU.mult), reads=[xm_.res, cwt.res], writes=[u.res])
                    S.op("pool", lambda e: e.tensor_tensor(out=t[:], in0=xc_[:], in1=wv(1), op=ALU.mult), reads=[xc_.res, cwt.res], writes=[t.res])
                    S.op("dve", lambda e: e.tensor_tensor(out=u[:], in0=u[:], in1=t[:], op=ALU.add), reads=[u.res, t.res], writes=[u.res])
                    S.op("pool", lambda e: e.tensor_tensor(out=t[:], in0=xp_[:], in1=wv(2), op=ALU.mult), reads=[xp_.res, cwt.res], writes=[t.res])
                    S.op("dve", lambda e: e.tensor_tensor(out=u[:], in0=u[:], in1=t[:], op=ALU.add), reads=[u.res, t.res], writes=[u.res])
                    S.op("dve", lambda e: e.tensor_tensor(out=u[:], in0=u[:], in1=cbt[:, cg * 1024:(cg + 1) * 1024], op=ALU.add),
                         reads=[u.res, cbt.res], writes=[u.res])
                u0, u1, u2 = uu[bi]
                S.dma("sp", g.x0s[t0:t0 + 128, :], u0[:], reads=[u0.res], writes=[g.x0s_r], acc=True)
                S.op("dve", lambda e: e.tensor_tensor(out=zf[bi][:], in0=u2[:], in1=u1[:], op=ALU.mult), reads=[u1.res, u2.res], writes=[zf[bi].res])
                S.op("act", lambda e: e.copy(out=zbt[bi][:], in_=zf[bi][:]), reads=[zf[bi].res], writes=[zbt[bi].res])
                S.op("pool", lambda e: e.tensor_tensor(out=zdt[bi][:], in0=zf[bi][:], in1=drt[:], op=ALU.mult), reads=[zf[bi].res, drt.res],
                     writes=[zdt[bi].res])
                S.dma("sp", g.zbs[t0:t0 + 128, :], zbt[bi][:], reads=[zbt[bi].res], writes=[g.zbs_r], acc=True)
                S.dma("sp", g.zd[t0:t0 + 128, :], zdt[bi][:], reads=[zdt[bi].res], writes=[g.zd_r], acc=True)

    def hyena_dft(l, g, L):
        hy_ = HY[L]
        LT = L // 128
        with Phase(C, "hyd") as ph:
            zT = ph.sb([128, LT, 512], BF16, "zT")
            YT = ph.sb([128, 2 * LT, 512], BF16, "YT")
            wbF = [ph.sb([128, LT, 512], BF16, "wbF") for _ in range(2)]
            wbI = [ph.sb([128, 2 * LT, 256], BF16, "wbI") for _ in range(2)]
            hre = [ph.sb([128, 512], F32, "hre") for _ in range(2)]
            him = [ph.sb([128, 512], F32, "him") for _ in range(2)]
            zre = [ph.sb([128, 512], F32, "zre") for _ in range(2)]
            zim = [ph.sb([128, 512], F32, "zim") for _ in range(2)]
            ta = [ph.sb([128, 512], F32, "ta") for _ in range(2)]
            tb = [ph.sb([128, 512], F32, "tb") for _ in range(2)]
            tc_ = [ph.sb([128, 512], F32, "tc") for _ in range(2)]
            td = [ph.sb([128, 512], F32, "td") for _ in range(2)]
            zdt = [ph.sb([128, 512], F32, "zdt") for _ in range(2)]
            x0t = [ph.sb([128, 512], F32, "x0t") for _ in range(2)]
            ot = [ph.sb([128, 512], F32, "ot") for _ in range(2)]
            otb = [ph.sb([128, 4, 128], BF16, "otb") for _ in range(2)]
            for sq in range(g.ntok // L):
                s0 = sq * L
                for half in range(2):
                    h0 = half * 512
                    S.dma("sp", zT[:], g.zbs[s0:s0 + L, h0:h0 + 512].rearrange("(j p) c -> p j c", p=128), reads=[g.zbs_r], writes=[zT.res])
                    st = {"i": 0, "re": None}

                    def epi_f(c0, t0, pt, nr, ncv):
                        blk = c0 // 128
                        if blk % 2 == 0:
                            i = st["i"] % 2
                            S.op("act", lambda e: e.copy(out=zre[i][:], in_=pt[:, :]), reads=[pt.res], writes=[zre[i].res])
                            S.dma("sp", hre[i][:], hy_["Hs"][blk, :, h0:h0 + 512], reads=[hy_["Hs_r"]], writes=[hre[i].res])
                            S.dma("sp", him[i][:], hy_["Hs"][blk + 1, :, h0:h0 + 512], reads=[hy_["Hs_r"]], writes=[him[i].res])
                            return
                        i = st["i"] % 2
                        st["i"] += 1
                        S.op("act", lambda e: e.copy(out=zim[i][:], in_=pt[:, :]), reads=[pt.res], writes=[zim[i].res])
                        S.op("dve", lambda e: e.tensor_tensor(out=ta[i][:], in0=zre[i][:], in1=hre[i][:], op=ALU.mult),
                             reads=[zre[i].res, hre[i].res], writes=[ta[i].res])
                        S.op("pool", lambda e: e.tensor_tensor(out=tb[i][:], in0=zim[i][:], in1=him[i][:], op=ALU.mult),
                             reads=[zim[i].res, him[i].res], writes=[tb[i].res])
                        S.op("dve", lambda e: e.tensor_tensor(out=YT[:, blk - 1, :], in0=ta[i][:], in1=tb[i][:], op=ALU.subtract),
                             reads=[ta[i].res, tb[i].res], writes=[YT.res])
                        S.op("pool", lambda e: e.tensor_tensor(out=tc_[i][:], in0=zre[i][:], in1=him[i][:], op=ALU.mult),
                             reads=[zre[i].res, him[i].res], writes=[tc_[i].res])
                        S.op("dve", lambda e: e.tensor_tensor(out=td[i][:], in0=zim[i][:], in1=hre[i][:], op=ALU.mult),
                             reads=[zim[i].res, hre[i].res], writes=[td[i].res])
                        S.op("dve", lambda e: e.tensor_tensor(out=YT[:, blk, :], in0=tc_[i][:], in1=td[i][:], op=ALU.add),
                             reads=[tc_[i].res, td[i].res], writes=[YT.res])
                    gemm(ph, hy_["Ff"], LT, 0, 2 * L, 512, zT, 512, "fm", epi_f, wbF)
                    st2 = {"i": 0}

                    def epi_i(c0, t0, pt, nr, ncv):
                        i = st2["i"] % 2
                        st2["i"] += 1
                        r0 = s0 + c0
                        S.dma("sp", zdt[i][:], g.zd[r0:r0 + 128, h0:h0 + 512], reads=[g.zd_r], writes=[zdt[i].res])
                        S.dma("sp", x0t[i][:], g.x0s[r0:r0 + 128, h0:h0 + 512], reads=[g.x0s_r], writes=[x0t[i].res])
                        S.op("dve", lambda e: e.tensor_tensor(out=ot[i][:], in0=pt[:, :], in1=zdt[i][:], op=ALU.add),
                             reads=[pt.res, zdt[i].res], writes=[ot[i].res])
                        S.op("pool", lambda e: e.tensor_tensor(out=ot[i][:], in0=ot[i][:], in1=x0t[i][:], op=ALU.mult),
                             reads=[ot[i].res, x0t[i].res], writes=[ot[i].res])
                        pt2 = next_ps()
                        S.mm([lambda pe, q=q: pe.transpose(pt2[:, q * 128:(q + 1) * 128], ot[i][:, q * 128:(q + 1) * 128], ident[:])
                              for q in range(4)], reads=[ot[i].res, ident.res], writes=[pt2.res])
                        S.op("act", lambda e: e.copy(out=otb[i][:], in_=pt2[:, :].rearrange("p (a b) -> p a b", b=128)),
                             reads=[pt2.res], writes=[otb[i].res])
                        S.dma("sp", g.oT[2][0][half * 4:(half + 1) * 4, :, r0:r0 + 128].rearrange("k p t -> p k t"), otb[i][:],
                              reads=[otb[i].res], writes=[g.oT[2][1]], acc=True)
                    gemm(ph, hy_["Fi"], 2 * LT, 0, L, 256, YT, 512, "fm", epi_i, wbI)

    for g in G:
        g.lora, g.lora_r = dscr("lora%d" % g.i, [4, 64, g.ntok], BF16)
        g.sg, g.sg_r = dscr("sg%d" % g.i, [128, g.ntok], BF16)
        g.SH, g.SH_r = dscr("SH%d" % g.i, [g.ntok, 3, 1024])
        g.DE = [dscr("DE%d_%d" % (g.i, e), [g.ntok, 3, 1024]) for e in range(2)]
        g.ysc = [dscr("ysc%d_%d" % (g.i, e), [g.ntok, 1024]) for e in range(2)]
        g.gsc, g.gsc_r = dscr("gsc%d" % g.i, [g.ntok, 1024])
        g.bon, g.bon_r = dscr("bon%d" % g.i, [g.ntok, 1024])

    def rwkv_lora(ph, l, g, hT):
        wbs = [ph.sb([128, 16, 128], BF16, "wbl") for _ in range(2)]
        obl = [ph.sb([128, 512], BF16, "obl") for _ in range(3)]
        for e in range(2):
            gemm(ph, W["wkv_w1"][l][e], 16, 0, 64, 64, hT, g.ntok, "fm",
                 epi_store(ph, obl, lambda c0, t0, nr, ncv, e=e: g.lora[e, :, t0:t0 + ncv], g.lora_r, func=AF.Tanh), wbs)
            gemm(ph, W["wkv_a1"][l][e], 16, 0, 64, 64, hT, g.ntok, "fm",
                 epi_store(ph, obl, lambda c0, t0, nr, ncv, e=e: g.lora[2 + e, :, t0:t0 + ncv], g.lora_r), wbs)
        gemm(ph, W["wkv_g1"][l], 16, 0, 128, 128, hT, g.ntok, "fm",
             epi_store(ph, obl, lambda c0, t0, nr, ncv: g.sg[:, t0:t0 + ncv], g.sg_r, func=AF.Sigmoid), wbs)

    def rwkv_pre(l, g, L):
        with Phase(C, "rwp") as ph:
            cwt = ph.sb([128, 3, 3072], F32, "cw")
            cbt = ph.sb([128, 3072], F32, "cb")
            for j in range(3):
                S.dma("sp", cwt[:, j, :], bcast_row(W["wkv_conv_w"].tensor, (l * 3 + j) * 3072, 3072), writes=[cwt.res], acc=(j > 0))
            S.dma("sp", cbt[:], bcast_row(W["wkv_conv_b"].tensor, l * 3072, 3072), writes=[cbt.res])
            rows = ph.sb([128, 8, 1024], F32, "rows")
            for e in range(2):
                S.dma("sp", rows[:, e, :], bcast_row(W["wkv_w0"].tensor, (l * 2 + e) * 1024, 1024), writes=[rows.res], acc=True)
                S.dma("sp", rows[:, 2 + e, :], bcast_row(W["wkv_a0"].tensor, (l * 2 + e) * 1024, 1024), writes=[rows.res], acc=True)
            S.dma("sp", rows[:, 4, :], bcast_row(W["wkv_k_k"].tensor, l * 1024, 1024), writes=[rows.res], acc=True)
            S.dma("sp", rows[:, 5, :], bcast_row(W["wkv_k_a"].tensor, l * 1024, 1024), writes=[rows.res], acc=True)
            S.dma("sp", rows[:, 7, :], bcast_row(W["wkv_r_k"].tensor, l * 1024, 1024), writes=[rows.res], acc=True)
            S.op("dve", lambda e_: e_.tensor_scalar(out=rows[:, 6, :], in0=rows[:, 5, :], scalar1=-1.0, scalar2=1.0, op0=ALU.mult, op1=ALU.add),
                 reads=[rows.res], writes=[rows.res])
            w2b = ph.sb([64, 4, 1024], BF16, "w2b")
            for e in range(2):
                S.dma("pool", w2b[:, e, :], W["wkv_w2"][l][e], writes=[w2b.res], acc=True)
                S.dma("pool", w2b[:, 2 + e, :], W["wkv_a2"][l][e], writes=[w2b.res], acc=True)
            g2b = ph.sb([128, 1024], BF16, "g2b")
            S.dma("pool", g2b[:], W["wkv_g2"][l], writes=[g2b.res])
            xm = ph.sb([128, 1024], F32, "xm")
            xc = ph.sb([128, 1024], F32, "xc")
            xp = ph.sb([128, 1024], F32, "xp")
            rkv = [ph.sb([128, 1024], F32, "rkv%d" % i) for i in range(3)]
            t1 = ph.sb([128, 1024], F32, "t1")
            t2 = ph.sb([128, 1024], F32, "t2")
            kk = ph.sb([128, 1024], F32, "kk")
            at = ph.sb([128, 1024], F32, "at")
            wt = ph.sb([128, 1024], F32, "wt")
            o1 = ph.sb([128, 1024], F32, "o1")
            o2 = ph.sb([128, 1024], F32, "o2")
            sm = ph.sb([128, 64], F32, "sm")
            lt = ph.sb([64, 4, 128], BF16, "lt")
            sgt = ph.sb([128, 128], BF16, "sgt")
            v3 = lambda t_: t_[:].rearrange("p (h k) -> p h k", k=64)
            for tt in range(g.ntok // 128):
                t0 = tt * 128
                sp_ = t0 % L
                for cg in range(3):
                    c0 = cg * 1024
                    u = rkv[cg]
                    if sp_ == 0:
                        S.op("pool", lambda e: e.memset(xm[0:1, :], 0.0), writes=[xm.res])
                        S.dma("sp", xm[1:128, :], g.rh[t0:t0 + 127, c0:c0 + 1024], reads=[g.rh_r], writes=[xm.res], acc=True)
                    else:
                        S.dma("sp", xm[:], g.rh[t0 - 1:t0 + 127, c0:c0 + 1024], reads=[g.rh_r], writes=[xm.res])
                    S.dma("sp", xc[:], g.rh[t0:t0 + 128, c0:c0 + 1024], reads=[g.rh_r], writes=[xc.res])
                    if sp_ + 128 == L:
                        S.op("pool", lambda e: e.memset(xp[:], 0.0), writes=[xp.res])
                        S.dma("sp", xp[0:127, :], g.rh[t0 + 1:t0 + 128, c0:c0 + 1024], reads=[g.rh_r], writes=[xp.res], acc=True)
                    else:
                        S.dma("sp", xp[:], g.rh[t0 + 1:t0 + 129, c0:c0 + 1024], reads=[g.rh_r], writes=[xp.res])
                    wv = lambda j: cwt[:, j, cg * 1024:(cg + 1) * 1024]
                    S.op("dve", lambda e: e.tensor_tensor(out=u[:], in0=xm[:], in1=wv(0), op=ALU.mult), reads=[xm.res, cwt.res], writes=[u.res])
                    S.op("pool", lambda e: e.tensor_tensor(out=t1[:], in0=xc[:], in1=wv(1), op=ALU.mult), reads=[xc.res, cwt.res], writes=[t1.res])
                    S.op("dve", lambda e: e.tensor_tensor(out=u[:], in0=u[:], in1=t1[:], op=ALU.add), reads=[u.res, t1.res], writes=[u.res])
                    S.op("pool", lambda e: e.tensor_tensor(out=t1[:], in0=xp[:], in1=wv(2), op=ALU.mult), reads=[xp.res, cwt.res], writes=[t1.res])
                    S.op("dve", lambda e: e.tensor_tensor(out=u[:], in0=u[:], in1=t1[:], op=ALU.add), reads=[u.res, t1.res], writes=[u.res])
                    S.op("dve", lambda e: e.tensor_tensor(out=u[:], in0=u[:], in1=cbt[:, cg * 1024:(cg + 1) * 1024], op=ALU.add),
                         reads=[u.res, cbt.res], writes=[u.res])
                r_, k_, v_ = rkv
                S.dma("sp", g.SH[t0:t0 + 128, 1, :], r_[:], reads=[r_.res], writes=[g.SH_r], acc=True)
                S.dma("sp", g.SH[t0:t0 + 128, 2, :], v_[:], reads=[v_.res], writes=[g.SH_r], acc=True)
                S.op("dve", lambda e: e.tensor_tensor(out=kk[:], in0=k_[:], in1=rows[:, 4, :], op=ALU.mult), reads=[k_.res, rows.res], writes=[kk.res])
                S.op("pool", lambda e: e.tensor_tensor(out=t1[:], in0=kk[:], in1=kk[:], op=ALU.mult), reads=[kk.res], writes=[t1.res])
                S.op("dve", lambda e: e.tensor_reduce(out=sm[:, 0:16], in_=v3(t1), axis=AX.X, op=ALU.add), reads=[t1.res], writes=[sm.res])
                S.op("dve", lambda e: e.tensor_scalar(out=sm[:, 0:16], in0=sm[:, 0:16], scalar1=1e-12, scalar2=None, op0=ALU.add),
                     reads=[sm.res], writes=[sm.res])
                S.op("act", lambda e: e.sqrt(out=sm[:, 0:16], in_=sm[:, 0:16]), reads=[sm.res], writes=[sm.res])
                S.op("dve", lambda e: e.reciprocal(out=sm[:, 0:16], in_=sm[:, 0:16]), reads=[sm.res], writes=[sm.res])
                S.op("dve", lambda e: e.tensor_tensor(out=v3(kk), in0=v3(kk), in1=sm[:, 0:16].unsqueeze(2).broadcast_to([128, 16, 64]), op=ALU.mult),
                     reads=[kk.res, sm.res], writes=[kk.res])
                S.dma("sp", g.SH[t0:t0 + 128, 0, :], kk[:], reads=[kk.res], writes=[g.SH_r], acc=True)
                S.op("pool", lambda e: e.tensor_tensor(out=t1[:], in0=r_[:], in1=k_[:], op=ALU.mult), reads=[r_.res, k_.res], writes=[t1.res])
                S.op("pool", lambda e: e.tensor_tensor(out=t1[:], in0=t1[:], in1=rows[:, 7, :], op=ALU.mult), reads=[t1.res, rows.res], writes=[t1.res])
                S.op("dve", lambda e: e.tensor_reduce(out=sm[:, 16:32], in_=v3(t1), axis=AX.X, op=ALU.add), reads=[t1.res, sm.res], writes=[sm.res])
                S.op("dve", lambda e: e.tensor_tensor(out=v3(o1), in0=v3(v_), in1=sm[:, 16:32].unsqueeze(2).broadcast_to([128, 16, 64]), op=ALU.mult),
                     reads=[v_.res, sm.res], writes=[o1.res])
                S.dma("sp", g.bon[t0:t0 + 128, :], o1[:], reads=[o1.res], writes=[g.bon_r], acc=True)
                S.dma("sp", lt[:], g.lora[:, :, t0:t0 + 128].rearrange("a p t -> p a t"), reads=[g.lora_r], writes=[lt.res])
                S.dma("sp", sgt[:], g.sg[:, t0:t0 + 128], reads=[g.sg_r], writes=[sgt.res])
                for hf in range(2):
                    pt = next_ps()
                    S.mm([lambda pe: pe.matmul(pt[:, :], lhsT=sgt[:], rhs=g2b[:, hf * 512:(hf + 1) * 512], start=True, stop=True)],
                         reads=[sgt.res, g2b.res], writes=[pt.res])
                    S.op("act", lambda e: e.copy(out=o2[:, hf * 512:(hf + 1) * 512], in_=pt[:, :]), reads=[pt.res, o2.res], writes=[o2.res])
                S.dma("sp", g.gsc[t0:t0 + 128, :], o2[:], reads=[o2.res], writes=[g.gsc_r], acc=True)
                for e in range(2):
                    for hf in range(2):
                        pt = next_ps()
                        S.mm([lambda pe: pe.matmul(pt[:, :], lhsT=lt[:, e, :], rhs=w2b[:, e, hf * 512:(hf + 1) * 512], start=True, stop=True)],
                             reads=[lt.res, w2b.res], writes=[pt.res])
                        S.op("dve", lambda e_: e_.tensor_tensor(out=wt[:, hf * 512:(hf + 1) * 512], in0=pt[:, :],
                                                               in1=rows[:, e, hf * 512:(hf + 1) * 512], op=ALU.add),
                             reads=[pt.res, rows.res, wt.res], writes=[wt.res])
                    S.op("act", lambda e_: e_.activation(out=wt[:], in_=wt[:], func=AF.Sigmoid), reads=[wt.res], writes=[wt.res])
                    S.op("act", lambda e_: e_.activation(out=wt[:], in_=wt[:], func=AF.Exp, scale=-math.exp(-0.5)), reads=[wt.res], writes=[wt.res])
                    S.dma("sp", g.DE[e][0][t0:t0 + 128, 0, :], wt[:], reads=[wt.res], writes=[g.DE[e][1]], acc=True)
                    for hf in range(2):
                        pt = next_ps()
                        S.mm([lambda pe: pe.matmul(pt[:, :], lhsT=lt[:, 2 + e, :], rhs=w2b[:, 2 + e, hf * 512:(hf + 1) * 512], start=True, stop=True)],
                             reads=[lt.res, w2b.res], writes=[pt.res])
                        S.op("dve", lambda e_: e_.tensor_tensor(out=at[:, hf * 512:(hf + 1) * 512], in0=pt[:, :],
                                                               in1=rows[:, 2 + e, hf * 512:(hf + 1) * 512], op=ALU.add),
                             reads=[pt.res, rows.res, at.res], writes=[at.res])
                    S.op("act", lambda e_: e_.activation(out=at[:], in_=at[:], func=AF.Sigmoid), reads=[at.res], writes=[at.res])
                    S.op("dve", lambda e_: e_.tensor_tensor(out=o1[:], in0=kk[:], in1=at[:], op=ALU.mult), reads=[kk.res, at.res], writes=[o1.res])
                    S.dma("sp", g.DE[e][0][t0:t0 + 128, 1, :], o1[:], reads=[o1.res], writes=[g.DE[e][1]], acc=True)
                    S.op("pool", lambda e_: e_.tensor_tensor(out=t2[:], in0=at[:], in1=rows[:, 5, :], op=ALU.mult), reads=[at.res, rows.res], writes=[t2.res])
                    S.op("pool", lambda e_: e_.tensor_tensor(out=t2[:], in0=t2[:], in1=rows[:, 6, :], op=ALU.add), reads=[t2.res, rows.res], writes=[t2.res])
                    S.op("dve", lambda e_: e_.tensor_tensor(out=o2[:], in0=k_[:], in1=t2[:], op=ALU.mult), reads=[k_.res, t2.res], writes=[o2.res])
                    S.dma("sp", g.DE[e][0][t0:t0 + 128, 2, :], o2[:], reads=[o2.res], writes=[g.DE[e][1]], acc=True)

    def rwkv_scan(l, g, L):
        sample = (g.i == 1)
        NV = 16 if sample else 64
        TC = 32 if sample else 16
        with Phase(C, "rws") as ph:
            St = ph.sb([128, NV, 64], F32, "S")
            tmp = ph.sb([128, NV, 64], F32, "tmp")
            At = ph.sb([128, NV, 64], F32, "A")
            Bt = ph.sb([128, NV, 64], F32, "B")
            sa = ph.sb([128, NV], F32, "sa")
            Dq = [[ph.sb([128, TC, 64], F32, "D%d" % q) for q in range(5)] for _ in range(2)]
            Vt = [ph.sb([128, TC, NV], F32, "V") for _ in range(2)]
            Yt = [ph.sb([128, TC, NV], F32, "Y") for _ in range(2)]
            if sample:
                S.dma("sp", St[:].rearrange("p a b -> p (a b)"), I["s0"][l], writes=[St.res])
            else:
                S.op("pool", lambda e: e.memset(St[:], 0.0), writes=[St.res])
            def srcs(e):
                return [(g.SH, g.SH_r, 0), (g.DE[e][0], g.DE[e][1], 0), (g.DE[e][0], g.DE[e][1], 1), (g.DE[e][0], g.DE[e][1], 2), (g.SH, g.SH_r, 1)]
            nchunk = L // TC
            for c in range(nchunk):
                bi = c % 2
                i0 = c * TC
                for e in range(2):
                    tstart = i0 if e == 0 else (L - 1 - i0)
                    sgn = 1 if e == 0 else -1
                    if sample:
                        for vq in range(4):
                            dstp = slice(e * 64 + vq, (e + 1) * 64, 4)
                            first = (e == 0 and vq == 0)
                            for q, (arr, arr_r, slot) in enumerate(srcs(e)):
                                S.dma("sp", Dq[bi][q][dstp, :, :],
                                      AP(arr.tensor, tstart * 3072 + slot * 1024, [[64, 16], [sgn * 3072, TC], [1, 64]]),
                                      reads=[arr_r], writes=[Dq[bi][q].res], acc=(not first))
                            S.dma("sp", Vt[bi][dstp, :, :],
                                  AP(g.SH.tensor, tstart * 3072 + 2 * 1024 + vq * 16, [[64, 16], [sgn * 3072, TC], [1, 16]]),
                                  reads=[g.SH_r], writes=[Vt[bi].res], acc=(not first))
                    else:
                        for sq in range(4):
                            p0 = sq * 32 + e * 16
                            first = (e == 0 and sq == 0)
                            for q, (arr, arr_r, slot) in enumerate(srcs(e)):
                                S.dma("sp", Dq[bi][q][p0:p0 + 16, :, :],
                                      AP(arr.tensor, (sq * L + tstart) * 3072 + slot * 1024, [[64, 16], [sgn * 3072, TC], [1, 64]]),
                                      reads=[arr_r], writes=[Dq[bi][q].res], acc=(not first))
                            S.dma("sp", Vt[bi][p0:p0 + 16, :, :],
                                  AP(g.SH.tensor, (sq * L + tstart) * 3072 + 2 * 1024, [[64, 16], [sgn * 3072, TC], [1, 64]]),
                                  reads=[g.SH_r], writes=[Vt[bi].res], acc=(not first))
                D = Dq[bi]
                for i in range(TC):
                    bc = lambda q: D[q][:, i, :].unsqueeze(1).broadcast_to([128, NV, 64])
                    S.op("dve", lambda e_: e_.tensor_tensor(out=tmp[:], in0=St[:], in1=bc(0), op=ALU.mult), reads=[St.res, D[0].res], writes=[tmp.res])
                    S.op("dve", lambda e_: e_.tensor_reduce(out=sa[:], in_=tmp[:], axis=AX.X, op=ALU.add), reads=[tmp.res], writes=[sa.res])
                    S.op("pool", lambda e_: e_.tensor_tensor(out=At[:], in0=Vt[bi][:, i, :].unsqueeze(2).broadcast_to([128, NV, 64]), in1=bc(3), op=ALU.mult),
                         reads=[Vt[bi].res, D[3].res], writes=[At.res])
                    S.op("pool", lambda e_: e_.tensor_tensor(out=Bt[:], in0=St[:], in1=bc(1), op=ALU.mult), reads=[St.res, D[1].res], writes=[Bt.res])
                    S.op("pool", lambda e_: e_.tensor_tensor(out=Bt[:], in0=Bt[:], in1=At[:], op=ALU.add), reads=[Bt.res, At.res], writes=[Bt.res])
                    S.op("dve", lambda e_: e_.tensor_tensor(out=tmp[:], in0=sa[:].unsqueeze(2).broadcast_to([128, NV, 64]), in1=bc(2), op=ALU.mult),
                         reads=[sa.res, D[2].res], writes=[tmp.res])
                    S.op("dve", lambda e_: e_.tensor_tensor(out=St[:], in0=Bt[:], in1=tmp[:], op=ALU.subtract), reads=[Bt.res, tmp.res], writes=[St.res])
                    S.op("dve", lambda e_: e_.tensor_tensor(out=tmp[:], in0=St[:], in1=bc(4), op=ALU.mult), reads=[St.res, D[4].res], writes=[tmp.res])
                    S.op("dve", lambda e_: e_.tensor_reduce(out=Yt[bi][:, i, :], in_=tmp[:], axis=AX.X, op=ALU.add), reads=[tmp.res], writes=[Yt[bi].res])
                for e in range(2):
                    tstart = i0 if e == 0 else (L - 1 - i0)
                    sgn = 1 if e == 0 else -1
                    ya, ya_r = g.ysc[e]
                    if sample:
                        for vq in range(4):
                            S.dma("sp", AP(ya.tensor, tstart * 1024 + vq * 16, [[64, 16], [sgn * 1024, TC], [1, 16]]),
                                  Yt[bi][slice(e * 64 + vq, (e + 1) * 64, 4), :, :], reads=[Yt[bi].res], writes=[ya_r], acc=True)
                    else:
                        for sq in range(4):
                            p0 = sq * 32 + e * 16
                            S.dma("sp", AP(ya.tensor, (sq * L + tstart) * 1024, [[64, 16], [sgn * 1024, TC], [1, 64]]), Yt[bi][p0:p0 + 16, :, :],
                                  reads=[Yt[bi].res], writes=[ya_r], acc=True)
            if not sample:
                for sq in range(4):
                    r0 = (sq * DEPTH + l) * 32
                    S.dma("sp", O["ns"][r0:r0 + 32, :], St[sq * 32:(sq + 1) * 32, :, :].rearrange("p a b -> p (a b)"), reads=[St.res],
                          writes=[ORES["ns"]], acc=True)

    def rwkv_post(l, g):
        with Phase(C, "rwo") as ph:
            rows = ph.sb([128, 2, 1024], F32, "rows")
            S.dma("sp", rows[:, 0, :], bcast_row(W["wkv_gn_g"].tensor, l * 1024, 1024), writes=[rows.res], acc=True)
            S.dma("sp", rows[:, 1, :], bcast_row(W["wkv_gn_b"].tensor, l * 1024, 1024), writes=[rows.res], acc=True)
            ya = [ph.sb([128, 1024], F32, "ya") for _ in range(2)]
            yb = [ph.sb([128, 1024], F32, "yb") for _ in range(2)]
            bo = [ph.sb([128, 1024], F32, "bo") for _ in range(2)]
            gg = [ph.sb([128, 1024], F32, "gg") for _ in range(2)]
            tq = [ph.sb([128, 1024], F32, "tq") for _ in range(2)]
            sm = [ph.sb([128, 64], F32, "sm") for _ in range(2)]
            otb = [ph.sb([128, 8, 128], BF16, "otb") for _ in range(2)]
            v3 = lambda t_: t_[:].rearrange("p (h k) -> p h k", k=64)
            for tt in range(g.ntok // 128):
                i = tt % 2
                t0 = tt * 128
                y, y2, b_, g_, t_, s_ = ya[i], yb[i], bo[i], gg[i], tq[i], sm[i]
                S.dma("sp", y[:], g.ysc[0][0][t0:t0 + 128, :], reads=[g.ysc[0][1]], writes=[y.res])
                S.dma("sp", y2[:], g.ysc[1][0][t0:t0 + 128, :], reads=[g.ysc[1][1]], writes=[y2.res])
                S.dma("sp", b_[:], g.bon[t0:t0 + 128, :], reads=[g.bon_r], writes=[b_.res])
                S.dma("sp", g_[:], g.gsc[t0:t0 + 128, :], reads=[g.gsc_r], writes=[g_.res])
                S.op("dve", lambda e: e.tensor_tensor(out=y[:], in0=y[:], in1=y2[:], op=ALU.add), reads=[y.res, y2.res], writes=[y.res])
                S.op("dve", lambda e: e.tensor_reduce(out=s_[:, 0:16], in_=v3(y), axis=AX.X, op=ALU.add), reads=[y.res], writes=[s_.res])
                S.op("dve", lambda e: e.tensor_scalar(out=s_[:, 0:16], in0=s_[:, 0:16], scalar1=-1.0 / 64, scalar2=None, op0=ALU.mult),
                     reads=[s_.res], writes=[s_.res])
                S.op("dve", lambda e: e.tensor_tensor(out=v3(y), in0=v3(y), in1=s_[:, 0:16].unsqueeze(2).broadcast_to([128, 16, 64]), op=ALU.add),
                     reads=[y.res, s_.res], writes=[y.res])
                S.op("pool", lambda e: e.tensor_tensor(out=t_[:], in0=y[:], in1=y[:], op=ALU.mult), reads=[y.res], writes=[t_.res])
                S.op("dve", lambda e: e.tensor_reduce(out=s_[:, 16:32], in_=v3(t_), axis=AX.X, op=ALU.add), reads=[t_.res, s_.res], writes=[s_.res])
                S.op("dve", lambda e: e.tensor_scalar(out=s_[:, 16:32], in0=s_[:, 16:32], scalar1=1.0 / 64, scalar2=64e-5, op0=ALU.mult, op1=ALU.add),
                     reads=[s_.res], writes=[s_.res])
                S.op("act", lambda e: e.sqrt(out=s_[:, 16:32], in_=s_[:, 16:32]), reads=[s_.res], writes=[s_.res])
                S.op("dve", lambda e: e.reciprocal(out=s_[:, 16:32], in_=s_[:, 16:32]), reads=[s_.res], writes=[s_.res])
                S.op("dve", lambda e: e.tensor_tensor(out=v3(y), in0=v3(y), in1=s_[:, 16:32].unsqueeze(2).broadcast_to([128, 16, 64]), op=ALU.mult),
                     reads=[y.res, s_.res], writes=[y.res])
                S.op("pool", lambda e: e.tensor_tensor(out=y[:], in0=y[:], in1=rows[:, 0, :], op=ALU.mult), reads=[y.res, rows.res], writes=[y.res])
                S.op("pool", lambda e: e.tensor_tensor(out=y[:], in0=y[:], in1=rows[:, 1, :], op=ALU.add), reads=[y.res, rows.res], writes=[y.res])
                S.op("dve", lambda e: e.tensor_tensor(out=y[:], in0=y[:], in1=b_[:], op=ALU.add), reads=[y.res, b_.res], writes=[y.res])
                S.op("dve", lambda e: e.tensor_tensor(out=y[:], in0=y[:], in1=g_[:], op=ALU.mult), reads=[y.res, g_.res], writes=[y.res])
                for hf in range(2):
                    pt2 = next_ps()
                    S.mm([lambda pe, q=q: pe.transpose(pt2[:, q * 128:(q + 1) * 128], y[:, (hf * 4 + q) * 128:(hf * 4 + q + 1) * 128], ident[:])
                          for q in range(4)], reads=[y.res, ident.res], writes=[pt2.res])
                    S.op("act", lambda e: e.copy(out=otb[i][:, hf * 4:(hf + 1) * 4, :], in_=pt2[:, :].rearrange("p (a b) -> p a b", b=128)),
                         reads=[pt2.res, otb[i].res], writes=[otb[i].res])
                S.dma("sp", g.oT[1][0][:, :, t0:t0 + 128].rearrange("k p t -> p k t"), otb[i][:], reads=[otb[i].res], writes=[g.oT[1][1]], acc=True)

    def merge_out(l, g):
        with Phase(C, "mrg") as ph:
            cols = load_cols(ph, l, g)
            mT = ph.sb([128, 16, g.ntok], BF16, "mT")
            if not mixers or cfg.get("nomerge"):
                S.op("pool", lambda e: e.memset(mT[:], 0.0), writes=[mT.res])
            else:
                with Phase(C, "mrg1") as ph1:
                    branches = [b for b in range(3) if "arc"[b] in mixers]
                    wnames = ["w_pa", "w_pr", "w_pc"]
                    HT = 1024
                    oTt = {b: ph1.sb([128, 8, HT], BF16, "oT%d" % b) for b in branches}
                    wbs = {b: [ph1.sb([128, 8, 512], BF16, "wp%d" % b) for _ in range(2)] for b in branches}
                    gts = {b: [ph1.sb([128, 512], F32, "g%d" % b) for _ in range(2)] for b in branches}
                    tmp = [ph1.sb([128, 512], F32, "mt") for _ in range(2)]
                    tmp2 = [ph1.sb([128, 512], F32, "mt2") for _ in range(2)]
                    cnt = 0
                    for half in range(g.ntok // HT):
                        for b in branches:
                            S.dma("sp", oTt[b][:], g.oT[b][0].rearrange("k p t -> p k t")[:, :, half * HT:(half + 1) * HT],
                                  reads=[g.oT[b][1]], writes=[oTt[b].res])

                        def loadw(s):
                            for b in branches:
                                wbuf = wbs[b][s % 2]
                                S.dma("pool", wbuf[:], W[wnames[b]][l].rearrange("(k p) n -> p k n", p=128)[:, :, s * 512:(s + 1) * 512],
                                      writes=[wbuf.res])
                        loadw(0)
                        for s in range(4):
                            if s + 1 < 4:
                                loadw(s + 1)
                            for nb in range(4):
                                nbg = s * 4 + nb
                                for tt in range(HT // 512):
                                    t0 = half * HT + tt * 512
                                    pts = {}
                                    for b in branches:
                                        pt = next_ps()
                                        pts[b] = pt
                                        wbuf = wbs[b][s % 2]
                                        fns = [lambda pe, kc=kc, pt=pt, wbuf=wbuf, b=b: pe.matmul(
                                            pt[:, :], lhsT=wbuf[:, kc, nb * 128:(nb + 1) * 128],
                                            rhs=oTt[b][:, kc, tt * 512:(tt + 1) * 512], start=(kc == 0), stop=(kc == 7))
                                            for kc in range(8)]
                                        S.mm(fns, reads=[wbuf.res, oTt[b].res], writes=[pt.res])
                                        gt = gts[b][cnt % 2]
                                        S.dma("sp", gt[:], g.gT[b * 16 + nbg, :, t0:t0 + 512], reads=[g.gT_r], writes=[gt.res])
                                    t1, t2 = tmp[cnt % 2], tmp2[cnt % 2]
                                    acc = None
                                    for bi, b in enumerate(branches):
                                        gt = gts[b][cnt % 2]
                                        last = (bi == len(branches) - 1)
                                        if acc is None:
                                            dst = mT[:, nbg, t0:t0 + 512] if last else t1[:]
                                            S.op("dve", lambda e, dst=dst, pt=pts[b], gt=gt: e.tensor_tensor(out=dst, in0=pt[:, :], in1=gt[:], op=ALU.mult),
                                                 reads=[pts[b].res, gt.res], writes=[mT.res if last else t1.res])
                                            acc = t1
                                        else:
                                            S.op("dve", lambda e, pt=pts[b], gt=gt: e.tensor_tensor(out=t2[:], in0=pt[:, :], in1=gt[:], op=ALU.mult),
                                                 reads=[pts[b].res, gt.res], writes=[t2.res])
                                            dst = mT[:, nbg, t0:t0 + 512] if last else t1[:]
                                            S.op("dve", lambda e, dst=dst: e.tensor_tensor(out=dst, in0=t1[:], in1=t2[:], op=ALU.add),
                                                 reads=[t1.res, t2.res], writes=[mT.res if last else t1.res])
                                    cnt += 1
            if "merged" in dbg:
                S.dma("sp", DBG["merged" + g.tag].rearrange("k p t -> p k t"), mT[:], reads=[mT.res], writes=[ORES["dbg_merged" + g.tag]])
            wb = [ph.sb([128, 16, 512], BF16, "wb") for _ in range(2)]
            resid_gemm(ph, l, g, W["w_out"][l], 16, 512, mT, g.ntok, 0, cols.ga1, None, cols.res, wb)

    def resid_gemm(ph, l, g, Wd, KC, slabw, xin, ntok, tok0_x, ga_fn, gb_fn, cres, wb, tok_base=0):
        xr = [ph.sb([128, 512], F32, "xr") for _ in range(3)]
        tb = [ph.sb([128, 512], F32, "tb") for _ in range(3)]
        st = {"i": 0}

        def epi(c0, t0, pt, nr, ncv):
            i = st["i"] % 3
            st["i"] += 1
            nb = c0 // 128
            tg = tok_base + (t0 - tok0_x)
            S.dma("sp", xr[i][:], g.xT[nb, :, tg:tg + 512], reads=[g.xT_r], writes=[xr[i].res])
            if gb_fn is None:
                S.op("dve", lambda e: e.scalar_tensor_tensor(out=xr[i][:], in0=pt[:, :], scalar=ga_fn(nb), in1=xr[i][:],
                                                             op0=ALU.mult, op1=ALU.add), reads=[pt.res, xr[i].res] + cres, writes=[xr[i].res])
            else:
                S.op("act", lambda e: e.activation(out=tb[i][:], in_=pt[:, :], func=AF.Identity, scale=ga_fn(nb), bias=gb_fn(nb)),
                     reads=[pt.res] + cres, writes=[tb[i].res])
                S.op("dve", lambda e: e.tensor_tensor(out=xr[i][:], in0=xr[i][:], in1=tb[i][:], op=ALU.add),
                     reads=[xr[i].res, tb[i].res], writes=[xr[i].res])
            S.dma("sp", g.xT[nb, :, tg:tg + 512], xr[i][:], reads=[xr[i].res], writes=[g.xT_r], acc=True)
        gemm(ph, Wd, KC, 0, D, slabw, xin, ntok, "fm", epi, wb, tok0=tok0_x)

    def mlp(l, g):
        with Phase(C, "ff1") as ph:
            cols = load_cols(ph, l, g)
            hT = ph.sb([128, 16, g.ntok], BF16, "h2T")
            with Phase(C, "nrm2") as ph2:
                norm_mod(ph2, g, cols.a2, cols.sh2, cols.res, hT)
            wb = [ph.sb([128, 16, 512], BF16, "wb") for _ in range(2)]
            rt = [ph.sb([128, 512], F32, "rt") for _ in range(3)]
            ob = [ph.sb([128, 512], BF16, "ob") for _ in range(3)]
            st = {"i": 0}

            def epi(c0, t0, pt, nr, ncv):
                i = st["i"] % 3
                st["i"] += 1
                nb = c0 // 128
                S.op("act", lambda e: e.activation(out=rt[i][:], in_=pt[:, :], func=AF.Relu, bias=cols.b1(nb), scale=1.0),
                     reads=[pt.res] + cols.res, writes=[rt[i].res])
                S.op("pool", lambda e: e.tensor_tensor(out=ob[i][:], in0=rt[i][:], in1=rt[i][:], op=ALU.mult),
                     reads=[rt[i].res], writes=[ob[i].res])
                S.dma("sp", g.f1T[nb, :, t0:t0 + 512], ob[i][:], reads=[ob[i].res], writes=[g.f1T_r], acc=True)
            gemm(ph, W["w_ff1"][l], 16, 0, D_FF, 512, hT, g.ntok, "fm", epi, wb)
        with Phase(C, "ff2") as ph:
            cols = load_cols(ph, l, g)
            f1 = [ph.sb([128, 64, 512], BF16, "f1") for _ in range(1)]
            wb = [ph.sb([128, 64, 128], BF16, "wb2") for _ in range(2)]
            fv = g.f1T.rearrange("k p t -> p k t")
            for tt in range(g.ntok // 512):
                ft = f1[tt % len(f1)]
                for q in range(4):
                    S.dma("sp", ft[:, q * 16:(q + 1) * 16, :], fv[:, q * 16:(q + 1) * 16, tt * 512:(tt + 1) * 512],
                          reads=[g.f1T_r], writes=[ft.res], acc=(q > 0))
                resid_gemm(ph, l, g, W["w_ff2"][l], 64, 128, ft, 512, 0, cols.ga2, cols.gb2, cols.res, wb, tok_base=tt * 512)

    def final_norm(g, dst, dst_res):
        with Phase(C, "fin") as ph:
            fg = ph.sb([128, 16], F32, "fg")
            S.dma("sp", fg[:], W["final_g"].rearrange("(j p) -> p j", p=128), writes=[fg.res], slow=True)
            TW = 256
            xt = [ph.sb([128, 16, TW], F32, "nx") for _ in range(2)]
            tm = [ph.sb([128, 16, TW], F32, "nt") for _ in range(2)]
            rs = [ph.sb([128, TW], F32, "nr") for _ in range(2)]
            yo = [ph.sb([128, D], F32, "yo") for _ in range(2)]
            xv = g.xT.rearrange("k p t -> p k t")
            for tt in range(g.ntok // TW):
                x_, t_, r_ = xt[tt % 2], tm[tt % 2], rs[tt % 2]
                S.dma("sp", x_[:], xv[:, :, tt * TW:(tt + 1) * TW], reads=[g.xT_r], writes=[x_.res])
                S.op("act", lambda e: e.activation(out=t_[:], in_=x_[:], func=AF.Square), reads=[x_.res], writes=[t_.res])
                pt = next_ps()
                fns = [lambda pe, kc=kc: pe.matmul(pt[:, 0:TW], lhsT=ones[:], rhs=t_[:, kc, :], start=(kc == 0), stop=(kc == 15))
                       for kc in range(16)]
                S.mm(fns, reads=[ones.res, t_.res], writes=[pt.res])
                S.op("dve", lambda e: e.tensor_scalar(out=r_[:], in0=pt[:, 0:TW], scalar1=1.0 / D, scalar2=1e-6, op0=ALU.mult,
                                                      op1=ALU.add), reads=[pt.res], writes=[r_.res])
                S.op("act", lambda e: e.sqrt(out=r_[:], in_=r_[:]), reads=[r_.res], writes=[r_.res])
                S.op("dve", lambda e: e.reciprocal(out=r_[:], in_=r_[:]), reads=[r_.res], writes=[r_.res])
                S.op("dve", lambda e: e.tensor_tensor(out=t_[:], in0=x_[:], in1=r_[:].unsqueeze(1).broadcast_to([128, 16, TW]),
                                                      op=ALU.mult), reads=[x_.res, r_.res], writes=[t_.res])
                for kc in range(16):
                    S.op("act", lambda e, kc=kc: e.activation(out=t_[:, kc, :], in_=t_[:, kc, :], func=AF.Identity,
                                                              scale=fg[:, kc:kc + 1]), reads=[t_.res, fg.res], writes=[t_.res])
                for sub in range(TW // 128):
                    y_ = yo[sub % 2]
                    for kq in range(4):
                        pt2 = next_ps()
                        fns = [lambda pe, j=j, pt2=pt2, kq=kq: pe.transpose(pt2[:, j * 128:(j + 1) * 128],
                                                                             t_[:, kq * 4 + j, sub * 128:(sub + 1) * 128], ident[:])
                               for j in range(4)]
                        S.mm(fns, reads=[t_.res, ident.res], writes=[pt2.res])
                        eng = alt_eng()
                        if eng == "act":
                            S.op("act", lambda e, pt2=pt2, kq=kq: e.copy(out=y_[:, kq * 512:(kq + 1) * 512], in_=pt2[:, :]),
                                 reads=[pt2.res], writes=[y_.res])
                        else:
                            S.op("dve", lambda e, pt2=pt2, kq=kq: e.tensor_copy(out=y_[:, kq * 512:(kq + 1) * 512], in_=pt2[:, :]),
                                 reads=[pt2.res], writes=[y_.res])
                    r0 = tt * TW + sub * 128
                    S.dma("sp", dst[r0:r0 + 128, :], y_[:], reads=[y_.res], writes=[dst_res], acc=True)

    for name in dbg:
        for g in G:
            if name in ("h", "merged"):
                dbg_out(name + g.tag, [16, 128, g.ntok], BF16)
            if name == "x":
                dbg_out(name + g.tag, [16, 128, g.ntok], F32)
            if name == "oT":
                for b in range(3):
                    dbg_out("oT%d%s" % (b, g.tag), [8, 128, g.ntok], BF16)
    load_x_T(G[0], I["xp"])
    load_x_T(G[1], I["xs"])
    for l in range(nlayers):
        modulation(l)
        for g in G:
            if g.i not in cfg.get("groups", (0, 1)):
                continue
            in_proj(l, g, None)
            if "a" in mixers and not cfg.get("noattn"):
                (attn_prompt if g.i == 0 else attn_sample)(l, g)
            if "r" in mixers:
                L_ = 256 if g.i == 0 else 2048
                rwkv_pre(l, g, L_)
                rwkv_scan(l, g, L_)
                rwkv_post(l, g)
            if "c" in mixers:
                L_ = 256 if g.i == 0 else 2048
                hyena_filter(l, L_)
                hyena_conv(l, g, L_)
                hyena_dft(l, g, L_)
            if "oT" in dbg:
                for b in range(3):
                    if "arc"[b] in mixers:
                        with Phase(C, "dbgo") as phd:
                            S.dma("sp", DBG["oT%d%s" % (b, g.tag)], g.oT[b][0], reads=[g.oT[b][1]], writes=[ORES["dbg_oT%d%s" % (b, g.tag)]])
            merge_out(l, g)
            mlp(l, g)
    if "x" in dbg:
        for g in G:
            with Phase(C, "dbgx") as ph:
                S.dma("sp", DBG["x" + g.tag], g.xT, reads=[g.xT_r], writes=[ORES["dbg_x" + g.tag]])
    final_norm(G[0], O["yp"], ORES["yp"])
    final_norm(G[1], O["ys"], ORES["ys"])
    S.barrier()
    cst.es.__exit__(None, None, None)
    top.close()
    C.ninst = S.ninst
    return nc, C


def make_in_maps(inputs):
    f = lambda a: np.ascontiguousarray(np.asarray(a, dtype=np.float32))
    maps = []
    wnames = ["ln1_g", "ln2_g", "w_mod", "b_mod", "w_in", "rpb", "wkv_conv_w", "wkv_conv_b", "wkv_w0", "wkv_w1", "wkv_w2",
              "wkv_a0", "wkv_a1", "wkv_a2", "wkv_g1", "wkv_g2", "wkv_k_k", "wkv_k_a", "wkv_r_k", "wkv_gn_g", "wkv_gn_b",
              "hy_conv_w", "hy_conv_b", "hy_f1", "hy_fb1", "hy_f2", "hy_fb2", "hy_freq", "hy_f3", "hy_d", "w_pa", "w_pr",
              "w_pc", "w_out", "w_ff1", "b_ff1", "w_ff2", "b_ff2", "final_g"]
    wd = {k: f(inputs[k]) for k in wnames}
    wd["wkv_r_k"] = wd["wkv_r_k"].reshape(DEPTH, 1024)
    for i in range(8):
        b = i // 2
        m = dict(wd)
        m["xp"] = f(inputs["x_prompt"][4 * i:4 * i + 4]).reshape(NP_TOK, D)
        m["xs"] = f(inputs["x_sample"][b])
        m["ck"] = f(inputs["cache_k"][b]).reshape(DEPTH, 256, 1024)
        m["cv"] = f(inputs["cache_v"][b]).reshape(DEPTH, 256, 1024)
        m["s0"] = f(inputs["state_wkv"][b]).reshape(DEPTH, 128, 1024)
        m["cvec"] = np.stack([f(inputs["c_ctx"]), f(inputs["c"][b])])
        m.update(CONSTS)
        maps.append(m)
    return maps


def _make_consts():
    cst = {}
    cq = np.arange(64)
    c0 = np.clip(cq - 8, 0, 48)
    ck = np.arange(64)
    ok = (ck[None, :] >= c0[:, None]) & (ck[None, :] < c0[:, None] + 16)
    cst["natmask"] = np.where(ok, 0.0, -1e30).astype(np.float32)
    for L in (256, 2048):
        t = np.linspace(0.0, 1.0, L, dtype=np.float32)[:, None]
        w = 2.0 * np.pi * np.arange(L, dtype=np.float32)[:, None] / L
        f = np.linspace(1e-4, 15, 16, dtype=np.float32)[None, :]
        z = np.concatenate([t, np.cos(f * w), -np.sin(f * w)], -1).astype(np.float32)
        cst["zposT%d" % L] = np.ascontiguousarray(z.T)
        dist = (np.abs(np.arange(L) - L // 2).astype(np.float32) / L)[:, None]
        deltas = np.abs(np.linspace(math.log(1e-2) / 1.5, math.log(1e-2) / 0.3, 1024, dtype=np.float32))[None, :]
        cst["win%d" % L] = np.exp(-dist * deltas).astype(np.float32)
        n = 2 * L
        k = np.arange(L, dtype=np.float64)
        om = 2.0 * np.pi * (k + 0.5) / n
        tt = np.arange(L, dtype=np.float64)
        ang = tt[:, None] * om[None, :]
        Ff = np.zeros((L, 2 * L), np.float32)
        Ffv = Ff.reshape(L, L // 128, 2, 128)
        Ffv[:, :, 0, :] = np.cos(ang).reshape(L, L // 128, 128)
        Ffv[:, :, 1, :] = (-np.sin(ang)).reshape(L, L // 128, 128)
        cst["Ff%d" % L] = Ff
        angi = om[:, None] * (tt[None, :] + L // 2)
        Fi = np.zeros((2 * L, L), np.float32)
        Fiv = Fi.reshape(L // 128, 2, 128, L)
        Fiv[:, 0] = ((2.0 / n) * np.cos(angi)).reshape(L // 128, 128, L)
        Fiv[:, 1] = (-(2.0 / n) * np.sin(angi)).reshape(L // 128, 128, L)
        cst["Fi%d" % L] = Fi
    return cst


CONSTS = _make_consts()
_CACHE = {}


def kernel(**inputs):
    if "nc" not in _CACHE:
        _CACHE["nc"] = build({})[0]
    nc = _CACHE["nc"]
    maps = make_in_maps(inputs)
    res = run_bass_kernel_spmd(nc, maps, core_ids=list(range(8)))
    R = res.results
    yp = np.concatenate([R[i]["yp"].reshape(4, 256, D) for i in range(8)], 0)
    ys = np.stack([R[2 * b]["ys"] for b in range(4)], 0)
    nk = np.concatenate([R[i]["nk"].reshape(4, DEPTH, 256, 16, 64) for i in range(8)], 0)
    nv = np.concatenate([R[i]["nv"].reshape(4, DEPTH, 256, 16, 64) for i in range(8)], 0)
    ns = np.concatenate([R[i]["ns"].reshape(4, DEPTH, 2, 16, 64, 64) for i in range(8)], 0)
    return (yp.astype(np.float32), ys.astype(np.float32), nk.astype(np.float32), nv.astype(np.float32), ns.astype(np.float32))
```

```python
import math
from contextlib import ExitStack

import numpy as np
import concourse.bass as bass
import concourse.mybir as mybir
from concourse.bass_utils import run_bass_kernel_spmd

F32 = mybir.dt.float32
BF16 = mybir.dt.bfloat16
I32 = mybir.dt.int32
AF = mybir.ActivationFunctionType
ALU = mybir.AluOpType
AX = mybir.AxisListType
AP = bass.AP

D = 2048
DEPTH = 4
NP_TOK = 1024
NS_TOK = 2048
N_IN = 15360
D_FF = 8192


class Res:
    __slots__ = ("name", "w", "a", "r")

    def __init__(self, name):
        self.name = name
        self.w = {}
        self.a = {}
        self.r = {}


class Tile:
    def __init__(self, t, name):
        self.t = t
        self.res = Res(name)

    def __getitem__(self, k):
        return self.t[k]


class Sched:
    NDS = 40
    NPOOL = 8

    def __init__(self, nc, es):
        self.nc = nc
        self.E = {"pe": nc.tensor, "act": nc.scalar, "dve": nc.vector, "pool": nc.gpsimd, "sp": nc.sync}
        self.sem = {k: es.enter_context(nc.semaphore("c_" + k)) for k in self.E}
        self.cnt = {k: 0 for k in self.E}
        self.seen = {k: {} for k in self.E}
        self.dsem = [es.enter_context(nc.semaphore("d%d" % i)) for i in range(self.NDS)]
        self.dcnt = [0] * self.NDS
        self.dnext = 0
        self.dnext_pool = 0
        self.ninst = 0
        self.nwait = 0

    def _semobj(self, key):
        return self.sem[key] if isinstance(key, str) else self.dsem[key]

    def _wait(self, eng, deps, defer=False):
        need = {}
        for (k, v) in deps:
            if need.get(k, 0) < v:
                need[k] = v
        sn = self.seen[eng]
        todo = [(k, v) for k, v in need.items() if sn.get(k, 0) < v]
        last = None
        if defer and todo:
            last = todo.pop()
        for k, v in todo:
            self.E[eng].wait_ge(self._semobj(k), v)
            sn[k] = v
            self.ninst += 1
            self.nwait += 1
        if last is not None:
            sn[last[0]] = last[1]
        return last

    def _attach(self, ins, last):
        if last is not None:
            ins._wait_ge(self._semobj(last[0]), last[1])

    def _deps(self, eng, reads, writes, is_dma=False, acc=False):
        deps = []
        for r in reads:
            deps.extend(r.w.items())
            deps.extend(r.a.items())
        for w in writes:
            srcs = [w.w, w.r] if acc else [w.w, w.a, w.r]
            for d in srcs:
                deps.extend(d.items())
        return deps

    def _commit(self, tok, reads, writes, acc=False):
        k, v = tok
        for r in reads:
            r.r[k] = v
        for w in writes:
            if acc:
                w.a[k] = v
            else:
                w.w = {k: v}
                w.a = {}
                w.r = {}

    def op(self, eng, fn, reads=(), writes=()):
        last = self._wait(eng, self._deps(eng, reads, writes), defer=True)
        ins = fn(self.E[eng])
        self._attach(ins, last)
        self.cnt[eng] += 1
        ins.then_inc(self.sem[eng], 1)
        self.ninst += 1
        self._commit((eng, self.cnt[eng]), reads, writes)

    def mm(self, fns, reads, writes):
        last = self._wait("pe", self._deps("pe", reads, writes), defer=True)
        pe = self.E["pe"]
        ins = None
        for j, f in enumerate(fns):
            ins = f(pe)
            if j == 0:
                self._attach(ins, last)
        self.ninst += len(fns)
        self.cnt["pe"] += 1
        ins.then_inc(self.sem["pe"], 1)
        self._commit(("pe", self.cnt["pe"]), reads, writes)

    def dma(self, q, out, in_, reads=(), writes=(), acc=False, slow=False):
        if q == "pool":
            i = self.NDS - self.NPOOL + self.dnext_pool
            self.dnext_pool = (self.dnext_pool + 1) % self.NPOOL
        else:
            i = self.dnext
            self.dnext = (i + 1) % (self.NDS - self.NPOOL)
        deps = self._deps(q, reads, writes, is_dma=True, acc=acc)
        if self.dcnt[i] > 0:
            deps.append((i, self.dcnt[i]))
        last = self._wait(q, deps, defer=True)
        if slow:
            ins = self.E[q].dma_start(out=out, in_=in_, allow_slow_non_contiguous=True)
        else:
            ins = self.E[q].dma_start(out=out, in_=in_)
        self._attach(ins, last)
        ins.then_inc(self.dsem[i], 16)
        self.ninst += 1
        self.dcnt[i] += 16
        self._commit((i, self.dcnt[i]), reads, writes, acc=acc)

    def barrier(self):
        deps = [(k, self.cnt[k]) for k in self.E if k != "sp" and self.cnt[k] > 0]
        deps += [(i, c) for i, c in enumerate(self.dcnt) if c > 0]
        self._wait("sp", deps)
        ins = self.E["sp"].nop()
        self.cnt["sp"] += 1
        ins.then_inc(self.sem["sp"], 1)
        for k in self.E:
            if k != "sp":
                self._wait(k, [("sp", self.cnt["sp"])])
        for k in self.E:
            for k2 in self.E:
                self.seen[k][k2] = self.cnt[k2]
            for i, c in enumerate(self.dcnt):
                self.seen[k][i] = c


class Ctx:
    pass


def _col_ap(dram_ap_1d, n):
    return dram_ap_1d.rearrange("(j p) -> p j", p=128)


class Phase:
    def __init__(self, C, name):
        self.C = C
        self.name = name
        self.es = ExitStack()
        self.n = 0

    def __enter__(self):
        self.es.__enter__()
        return self

    def sb(self, shape, dt=F32, name=None):
        self.n += 1
        nm = "%s_%s_%d" % (self.name, name or "t", self.C.uid())
        t = self.es.enter_context(self.C.nc.sbuf_tensor(nm, list(shape), dt))
        return Tile(t, nm)

    def __exit__(self, *a):
        self.C.S.barrier()
        return self.es.__exit__(*a)


def build(cfg):
    nlayers = cfg.get("nlayers", DEPTH)
    dbg = cfg.get("dbg", ())
    mixers = cfg.get("mixers", ("a", "r", "c"))
    nc = bass.Bass("TRN2", target_bir_lowering=False)
    C = Ctx()
    C.nc = nc
    C._uid = 0

    def uid():
        C._uid += 1
        return C._uid
    C.uid = uid
    top = ExitStack()
    S = Sched(nc, top)
    C.S = S

    def din(name, shape, dt=F32):
        return nc.dram_tensor(name, list(shape), dt, kind="ExternalInput").ap()

    def dout(name, shape, dt=F32):
        return nc.dram_tensor(name, list(shape), dt, kind="ExternalOutput").ap()

    def dscr(name, shape, dt=F32):
        a = nc.dram_tensor(name, list(shape), dt, kind="Internal").ap()
        return a, Res(name)

    I = {}
    I["xp"] = din("xp", [NP_TOK, D])
    I["xs"] = din("xs", [NS_TOK, D])
    I["ck"] = din("ck", [DEPTH, 256, 1024])
    I["cv"] = din("cv", [DEPTH, 256, 1024])
    I["s0"] = din("s0", [DEPTH, 128, 1024])
    I["cvec"] = din("cvec", [2, D])
    wshapes = {
        "ln1_g": [DEPTH, D], "ln2_g": [DEPTH, D], "w_mod": [DEPTH, D, 6 * D], "b_mod": [DEPTH, 6 * D],
        "w_in": [DEPTH, D, N_IN], "rpb": [DEPTH, 16, 15, 31],
        "wkv_conv_w": [DEPTH, 3, 3072], "wkv_conv_b": [DEPTH, 3072], "wkv_w0": [DEPTH, 2, 1024],
        "wkv_w1": [DEPTH, 2, D, 64], "wkv_w2": [DEPTH, 2, 64, 1024], "wkv_a0": [DEPTH, 2, 1024],
        "wkv_a1": [DEPTH, 2, D, 64], "wkv_a2": [DEPTH, 2, 64, 1024], "wkv_g1": [DEPTH, D, 128],
        "wkv_g2": [DEPTH, 128, 1024], "wkv_k_k": [DEPTH, 1024], "wkv_k_a": [DEPTH, 1024],
        "wkv_r_k": [DEPTH, 1024], "wkv_gn_g": [DEPTH, 1024], "wkv_gn_b": [DEPTH, 1024],
        "hy_conv_w": [DEPTH, 3, 3072], "hy_conv_b": [DEPTH, 3072], "hy_f1": [DEPTH, 33, 64],
        "hy_fb1": [DEPTH, 64], "hy_f2": [DEPTH, 64, 64], "hy_fb2": [DEPTH, 64], "hy_freq": [DEPTH, 64],
        "hy_f3": [DEPTH, 64, 1024], "hy_d": [DEPTH, 1024],
        "w_pa": [DEPTH, 1024, D], "w_pr": [DEPTH, 1024, D], "w_pc": [DEPTH, 1024, D], "w_out": [DEPTH, D, D],
        "w_ff1": [DEPTH, D, D_FF], "b_ff1": [DEPTH, D_FF], "w_ff2": [DEPTH, D_FF, D], "b_ff2": [DEPTH, D],
        "final_g": [D],
    }
    W = {k: din(k, s) for k, s in wshapes.items()}
    O = {}
    O["yp"] = dout("yp", [NP_TOK, D])
    O["ys"] = dout("ys", [NS_TOK, D])
    O["nk"] = dout("nk", [4 * DEPTH * 256, 1024])
    O["nv"] = dout("nv", [4 * DEPTH * 256, 1024])
    O["ns"] = dout("ns", [4 * DEPTH * 32, 4096])
    ORES = {k: Res("o_" + k) for k in O}
    DBG = {}

    def dbg_out(name, shape, dt=F32):
        DBG[name] = dout("dbg_" + name, shape, dt)
        ORES["dbg_" + name] = Res("dbg_" + name)
        return DBG[name], ORES["dbg_" + name]

    G = []
    for gi, ntok in enumerate((NP_TOK, NS_TOK)):
        g = Ctx()
        g.i = gi
        g.ntok = ntok
        g.tag = "ps"[gi]
        g.xT, g.xT_r = dscr("xT%d" % gi, [16, 128, ntok])
        g.qkT, g.qkT_r = dscr("qkT%d" % gi, [16, 128, ntok], BF16)
        g.vtm, g.vtm_r = dscr("vtm%d" % gi, [ntok, 1024], BF16)
        g.rh, g.rh_r = dscr("rh%d" % gi, [ntok, 6144])
        g.gT, g.gT_r = dscr("gT%d" % gi, [48, 128, ntok])
        g.oT = []
        for b in range(3):
            g.oT.append(dscr("oT%d_%d" % (gi, b), [8, 128, ntok], BF16))
        g.f1T, g.f1T_r = dscr("f1T%d" % gi, [64, 128, ntok], BF16)
        G.append(g)
    mrow, mrow_r = dscr("mrow", [2, 6 * D])
    zero_r = Res("zeros")

    cst = Phase(C, "cst")
    cst.es.__enter__()
    ident = cst.sb([128, 128], F32, "ident")
    identb = cst.sb([128, 128], BF16, "identb")
    ones = cst.sb([128, 128], F32, "ones")
    S.op("pool", lambda e: e.memset(ident[:], 1.0), writes=[ident.res])
    S.op("pool", lambda e: e.affine_select(out=ident[:], in_=ident[:], pattern=[[-1, 128]], compare_op=ALU.is_equal,
                                            fill=0.0, base=0, channel_multiplier=1), reads=[ident.res], writes=[ident.res])
    S.op("dve", lambda e: e.tensor_copy(out=identb[:], in_=ident[:]), reads=[ident.res], writes=[identb.res])
    S.op("dve", lambda e: e.memset(ones[:], 1.0), writes=[ones.res])
    psum = []
    for i in range(8):
        t = top.enter_context(nc.psum_tensor("ps%d" % i, [128, 512], F32))
        psum.append(Tile(t, "ps%d" % i))
    C.ps_rr = 0

    def next_ps(lo=0, hi=8):
        C.ps_rr = (C.ps_rr + 1) % (hi - lo)
        return psum[lo + C.ps_rr]

    C.eng_rr = 0

    def alt_eng():
        C.eng_rr ^= 1
        return "act" if C.eng_rr else "dve"

    def gemm(ph, Wd, KC, col0, ncols, slabw, xT, ntok, form, epi, wbufs, Mrows=128, tok0=0, ps_lo=0, ps_hi=8):
        nslab = (ncols + slabw - 1) // slabw
        Wv = Wd.rearrange("(k p) n -> p k n", p=128)

        def load(s):
            wb = wbufs[s % len(wbufs)]
            c0 = col0 + s * slabw
            cw = min(slabw, col0 + ncols - c0)
            S.dma("pool", wb[:, :, 0:cw], Wv[:, :, c0:c0 + cw], writes=[wb.res])
        load(0)
        for s in range(nslab):
            if s + 1 < nslab:
                load(s + 1)
            wb = wbufs[s % len(wbufs)]
            c0 = col0 + s * slabw
            cw = min(slabw, col0 + ncols - c0)
            if form == "fm":
                for nb in range((cw + 127) // 128):
                    mw = min(128, cw - nb * 128)
                    for tt in range(ntok // 512):
                        pt = next_ps(ps_lo, ps_hi)
                        t0 = tok0 + tt * 512
                        fns = []
                        for kc in range(KC):
                            fns.append(lambda pe, kc=kc, pt=pt, wb=wb, nb=nb, mw=mw, t0=t0: pe.matmul(
                                pt[0:mw, :], lhsT=wb[:, kc, nb * 128:nb * 128 + mw], rhs=xT[:, kc, t0:t0 + 512],
                                start=(kc == 0), stop=(kc == KC - 1)))
                        S.mm(fns, reads=[wb.res, xT.res], writes=[pt.res])
                        epi(c0 + nb * 128, t0, pt, mw, 512)
            else:
                for tt in range(ntok // Mrows if Mrows == 128 else 1):
                    t0 = tok0 + tt * 128
                    for nh in range((cw + 511) // 512):
                        nw = min(512, cw - nh * 512)
                        pt = next_ps(ps_lo, ps_hi)
                        fns = []
                        for kc in range(KC):
                            fns.append(lambda pe, kc=kc, pt=pt, wb=wb, nh=nh, nw=nw, t0=t0: pe.matmul(
                                pt[0:Mrows, 0:nw], lhsT=xT[:, kc, t0:t0 + Mrows], rhs=wb[:, kc, nh * 512:nh * 512 + nw],
                                start=(kc == 0), stop=(kc == KC - 1)))
                        S.mm(fns, reads=[wb.res, xT.res], writes=[pt.res])
                        epi(c0 + nh * 512, t0, pt, Mrows, nw)

    def epi_store(ph, obufs, dst_fn, dst_res, func=None, bias_fn=None, scale=1.0):
        st = {"i": 0}

        def epi(c0, t0, pt, nr, ncv):
            ob = obufs[st["i"] % len(obufs)]
            st["i"] += 1
            if func is not None or bias_fn is not None:
                b = bias_fn(c0) if bias_fn is not None else None
                rd = [pt.res] + ([b[1]] if b is not None else [])
                S.op("act", lambda e: e.activation(out=ob[0:nr, 0:ncv], in_=pt[0:nr, 0:ncv], func=func or AF.Identity,
                                                   bias=(b[0] if b is not None else 0.0), scale=scale),
                     reads=rd, writes=[ob.res])
            else:
                eng = alt_eng()
                if eng == "act":
                    S.op("act", lambda e: e.copy(out=ob[0:nr, 0:ncv], in_=pt[0:nr, 0:ncv]), reads=[pt.res], writes=[ob.res])
                else:
                    S.op("dve", lambda e: e.tensor_copy(out=ob[0:nr, 0:ncv], in_=pt[0:nr, 0:ncv]), reads=[pt.res], writes=[ob.res])
            S.dma("sp", dst_fn(c0, t0, nr, ncv), ob[0:nr, 0:ncv], reads=[ob.res], writes=[dst_res], acc=True)
        return epi

    def load_x_T(g, src):
        with Phase(C, "ldx") as ph:
            xin = [ph.sb([128, D], F32, "xin") for _ in range(2)]
            xo = [ph.sb([128, 16, 128], F32, "xo") for _ in range(2)]
            for tt in range(g.ntok // 128):
                xi = xin[tt % 2]
                xq = xo[tt % 2]
                S.dma("sp", xi[:], src[tt * 128:(tt + 1) * 128, :], writes=[xi.res])
                for kq in range(4):
                    pt = next_ps()
                    fns = [lambda pe, j=j, pt=pt, xi=xi, kq=kq: pe.transpose(pt[:, j * 128:(j + 1) * 128],
                                                                                xi[:, (kq * 4 + j) * 128:(kq * 4 + j + 1) * 128], ident[:])
                           for j in range(4)]
                    S.mm(fns, reads=[xi.res, ident.res], writes=[pt.res])
                    eng = alt_eng()
                    dst = xq[:, kq * 4:(kq + 1) * 4, :]
                    srcp = pt[:, :].rearrange("p (a b) -> p a b", b=128)
                    if eng == "act":
                        S.op("act", lambda e: e.copy(out=dst, in_=srcp), reads=[pt.res], writes=[xq.res])
                    else:
                        S.op("dve", lambda e: e.tensor_copy(out=dst, in_=srcp), reads=[pt.res], writes=[xq.res])
                S.dma("sp", g.xT.rearrange("k p t -> p k t")[:, :, tt * 128:(tt + 1) * 128], xq[:], reads=[xq.res],
                      writes=[g.xT_r], acc=True)

    def modulation(l):
        with Phase(C, "mod") as ph:
            cT = ph.sb([128, 16, 2], F32, "cT")
            cTb = ph.sb([128, 16, 2], BF16, "cTb")
            for gi in range(2):
                S.dma("sp", cT[:, :, gi], I["cvec"][gi].rearrange("(k p) -> p k", p=128), writes=[cT.res], slow=True, acc=(gi > 0))
            S.op("act", lambda e: e.activation(out=cTb[:], in_=cT[:], func=AF.Silu), reads=[cT.res], writes=[cTb.res])
            wb = [ph.sb([128, 16, 512], BF16, "wb") for _ in range(2)]
            bm = [ph.sb([2, 512], F32, "bm") for _ in range(2)]
            ob = [ph.sb([2, 512], F32, "ob") for _ in range(2)]
            st = {"i": 0}

            def epi(c0, t0, pt, nr, ncv):
                i = st["i"] % 2
                st["i"] += 1
                S.dma("sp", bm[i][:], AP(W["b_mod"].tensor, l * 6 * D + c0, [[0, 2], [1, 512]]), writes=[bm[i].res])
                S.op("dve", lambda e: e.tensor_tensor(out=ob[i][:], in0=pt[0:2, :], in1=bm[i][:], op=ALU.add),
                     reads=[pt.res, bm[i].res], writes=[ob[i].res])
                S.dma("sp", mrow[:, c0:c0 + 512], ob[i][:], reads=[ob[i].res], writes=[mrow_r], acc=True)
            gemm(ph, W["w_mod"][l], 16, 0, 6 * D, 512, cTb, 2, "tm", epi, wb, Mrows=2)

    def load_cols(ph, l, g):
        cols = Ctx()
        m = ph.sb([128, 96], F32, "mcol")
        S.dma("sp", m[:], mrow[g.i].rearrange("(j p) -> p j", p=128), reads=[mrow_r], writes=[m.res], slow=True)
        ln = ph.sb([128, 32], F32, "lncol")
        S.dma("sp", ln[:, 0:16], W["ln1_g"][l].rearrange("(j p) -> p j", p=128), writes=[ln.res], slow=True)
        S.dma("sp", ln[:, 16:32], W["ln2_g"][l].rearrange("(j p) -> p j", p=128), writes=[ln.res], slow=True, acc=True)
        bf = ph.sb([128, 80], F32, "bfcol")
        S.dma("sp", bf[:, 0:64], W["b_ff1"][l].rearrange("(j p) -> p j", p=128), writes=[bf.res], slow=True)
        S.dma("sp", bf[:, 64:80], W["b_ff2"][l].rearrange("(j p) -> p j", p=128), writes=[bf.res], slow=True, acc=True)
        d = ph.sb([128, 48], F32, "dcol")
        S.op("dve", lambda e: e.scalar_tensor_tensor(out=d[:, 0:16], in0=m[:, 16:32], scalar=1.0, in1=ln[:, 0:16],
                                                     op0=ALU.add, op1=ALU.mult), reads=[m.res, ln.res], writes=[d.res])
        S.op("dve", lambda e: e.scalar_tensor_tensor(out=d[:, 16:32], in0=m[:, 64:80], scalar=1.0, in1=ln[:, 16:32],
                                                     op0=ALU.add, op1=ALU.mult), reads=[m.res, ln.res, d.res], writes=[d.res])
        S.op("dve", lambda e: e.tensor_tensor(out=d[:, 32:48], in0=m[:, 80:96], in1=bf[:, 64:80], op=ALU.mult),
             reads=[m.res, bf.res, d.res], writes=[d.res])
        cols.m, cols.d, cols.bf = m, d, bf
        cols.sh1 = lambda kc: m[:, kc:kc + 1]
        cols.ga1 = lambda kc: m[:, 32 + kc:33 + kc]
        cols.sh2 = lambda kc: m[:, 48 + kc:49 + kc]
        cols.ga2 = lambda kc: m[:, 80 + kc:81 + kc]
        cols.a1 = lambda kc: d[:, kc:kc + 1]
        cols.a2 = lambda kc: d[:, 16 + kc:17 + kc]
        cols.gb2 = lambda kc: d[:, 32 + kc:33 + kc]
        cols.b1 = lambda j: bf[:, j:j + 1]
        cols.res = [m.res, d.res, bf.res]
        return cols

    def norm_mod(ph, g, a_fn, sh_fn, cres, hT):
        TW = 256
        xt = [ph.sb([128, 16, TW], F32, "nx") for _ in range(2)]
        tm = [ph.sb([128, 16, TW], F32, "nt") for _ in range(2)]
        rs = [ph.sb([128, TW], F32, "nr") for _ in range(2)]
        xv = g.xT.rearrange("k p t -> p k t")
        for tt in range(g.ntok // TW):
            x_, t_, r_ = xt[tt % 2], tm[tt % 2], rs[tt % 2]
            S.dma("sp", x_[:], xv[:, :, tt * TW:(tt + 1) * TW], reads=[g.xT_r], writes=[x_.res])
            S.op("act", lambda e: e.activation(out=t_[:], in_=x_[:], func=AF.Square), reads=[x_.res], writes=[t_.res])
            pt = next_ps()
            fns = [lambda pe, kc=kc: pe.matmul(pt[:, 0:TW], lhsT=ones[:], rhs=t_[:, kc, :], start=(kc == 0), stop=(kc == 15))
                   for kc in range(16)]
            S.mm(fns, reads=[ones.res, t_.res], writes=[pt.res])
            S.op("dve", lambda e: e.tensor_scalar(out=r_[:], in0=pt[:, 0:TW], scalar1=1.0 / D, scalar2=1e-6, op0=ALU.mult,
                                                  op1=ALU.add), reads=[pt.res], writes=[r_.res])
            S.op("act", lambda e: e.sqrt(out=r_[:], in_=r_[:]), reads=[r_.res], writes=[r_.res])
            S.op("dve", lambda e: e.reciprocal(out=r_[:], in_=r_[:]), reads=[r_.res], writes=[r_.res])
            S.op("dve", lambda e: e.tensor_tensor(out=t_[:], in0=x_[:], in1=r_[:].unsqueeze(1).broadcast_to([128, 16, TW]),
                                                  op=ALU.mult), reads=[x_.res, r_.res], writes=[t_.res])
            for kc in range(16):
                S.op("act", lambda e, kc=kc: e.activation(out=hT[:, kc, tt * TW:(tt + 1) * TW], in_=t_[:, kc, :],
                                                          func=AF.Identity, scale=a_fn(kc), bias=sh_fn(kc)),
                     reads=[t_.res] + cres, writes=[hT.res])

    def in_proj(l, g, cols_holder):
        with Phase(C, "inp") as ph:
            cols = load_cols(ph, l, g)
            hT = ph.sb([128, 16, g.ntok], BF16, "hT")
            with Phase(C, "nrm") as ph2:
                norm_mod(ph2, g, cols.a1, cols.sh1, cols.res, hT)
            if "h" in dbg:
                S.dma("sp", DBG["h" + g.tag].rearrange("k p t -> p k t"), hT[:], reads=[hT.res], writes=[ORES["dbg_h" + g.tag]])
            wb = [ph.sb([128, 16, 512], BF16, "wb") for _ in range(2)]
            Wl = W["w_in"][l]
            ob = [ph.sb([128, 512], F32, "ob") for _ in range(4)]
            gemm(ph, Wl, 16, 9216, 6144, 512, hT, g.ntok, "fm",
                 epi_store(ph, ob, lambda c0, t0, nr, ncv: g.gT[(c0 - 9216) // 128, :, t0:t0 + ncv], g.gT_r, func=AF.Sigmoid), wb)
            if "a" in mixers:
                obb = [ph.sb([128, 512], BF16, "obb") for _ in range(4)]
                gemm(ph, Wl, 16, 0, 2048, 512, hT, g.ntok, "fm",
                     epi_store(ph, obb, lambda c0, t0, nr, ncv: g.qkT[c0 // 128, :, t0:t0 + ncv], g.qkT_r), wb)
            obf = [ph.sb([128, 512], F32, "obf") for _ in range(3)]
            obv = [ph.sb([128, 512], BF16, "obv") for _ in range(3)]
            st = {"i": 0}

            def epi_kv(c0, t0, pt, nr, ncv):
                i = st["i"] % 3
                st["i"] += 1
                isv = c0 >= 2048
                cc = c0 - (2048 if isv else 1024)
                if g.i == 0:
                    S.op("dve", lambda e: e.tensor_copy(out=obf[i][:], in_=pt[:, :]), reads=[pt.res], writes=[obf[i].res])
                    key = "nv" if isv else "nk"
                    r0_ = ((t0 // 256) * DEPTH + l) * 256 + (t0 % 256)
                    S.dma("sp", O[key][r0_:r0_ + 128, cc:cc + 512], obf[i][:], reads=[obf[i].res],
                          writes=[ORES[key]], acc=True)
                    if isv:
                        S.op("pool", lambda e: e.tensor_copy(out=obv[i][:], in_=obf[i][:]), reads=[obf[i].res], writes=[obv[i].res])
                elif isv:
                    S.op("dve", lambda e: e.tensor_copy(out=obv[i][:], in_=pt[:, :]), reads=[pt.res], writes=[obv[i].res])
                if isv:
                    S.dma("sp", g.vtm[t0:t0 + 128, cc:cc + 512], obv[i][:], reads=[obv[i].res], writes=[g.vtm_r], acc=True)
            if ("a" in mixers or g.i == 0) and not cfg.get("nokv"):
                if g.i == 0:
                    gemm(ph, Wl, 16, 1024, 2048, 512, hT, g.ntok, "tm", epi_kv, wb)
                else:
                    gemm(ph, Wl, 16, 2048, 1024, 512, hT, g.ntok, "tm", epi_kv, wb)
            if "r" in mixers or "c" in mixers:
                gemm(ph, Wl, 16, 3072, 6144, 512, hT, g.ntok, "tm",
                     epi_store(ph, ob, lambda c0, t0, nr, ncv: g.rh[t0:t0 + nr, c0 - 3072:c0 - 3072 + ncv], g.rh_r), wb)
            if "r" in mixers:
                rwkv_lora(ph, l, g, hT)
            return None

    SCALE = 0.125

    def softmax_rows(nr, ncol, sc_ap, sc_res, scale, small, pn, tag_reads=()):
        mx, nmx, rsum, rinv = small[0:nr, 0:1], small[0:nr, 1:2], small[0:nr, 2:3], small[0:nr, 3:4]
        S.op("dve", lambda e: e.tensor_reduce(out=mx, in_=sc_ap, axis=AX.X, op=ALU.max), reads=[sc_res], writes=[small.res])
        S.op("dve", lambda e: e.tensor_scalar(out=nmx, in0=mx, scalar1=-scale, scalar2=None, op0=ALU.mult),
             reads=[small.res], writes=[small.res])
        S.op("dve", lambda e: e.memset(rsum, 0.0), reads=[small.res], writes=[small.res])
        return mx, nmx, rsum, rinv

    def attn_prompt(l, g):
        with Phase(C, "attp") as ph:
            qk = ph.sb([128, 16, NP_TOK], BF16, "qk")
            V = ph.sb([128, 8, 1024], BF16, "V")
            oa = ph.sb([128, 8, NP_TOK], BF16, "oa")
            S.dma("sp", qk[:], g.qkT.rearrange("k p t -> p k t"), reads=[g.qkT_r], writes=[qk.res])
            S.dma("sp", V[:], g.vtm.rearrange("(j p) c -> p j c", p=128), reads=[g.vtm_r], writes=[V.res])
            pb = [ph.sb([128, 256], F32, "pb") for _ in range(2)]
            pn = [ph.sb([128, 256], BF16, "pn") for _ in range(2)]
            PT = [ph.sb([128, 2, 128], BF16, "PT") for _ in range(2)]
            sm = [ph.sb([128, 8], F32, "sm") for _ in range(2)]
            u = 0
            for s_ in range(4):
                for h in range(16):
                    c, p0 = h // 2, (h % 2) * 64
                    for qt in range(2):
                        i = u % 2
                        u += 1
                        q0 = s_ * 256 + qt * 128
                        ps = next_ps()
                        S.mm([lambda pe: pe.matmul(ps[:, 0:256], lhsT=qk[p0:p0 + 64, c, q0:q0 + 128],
                                                   rhs=qk[p0:p0 + 64, 8 + c, s_ * 256:(s_ + 1) * 256], start=True, stop=True)],
                             reads=[qk.res], writes=[ps.res])
                        mx, nmx, rsum, rinv = softmax_rows(128, 256, ps[:, 0:256], ps.res, SCALE, sm[i], None)
                        S.op("act", lambda e: e.activation(out=pb[i][:], in_=ps[:, 0:256], func=AF.Exp, bias=nmx, scale=SCALE,
                                                           accum_out=rsum), reads=[ps.res, sm[i].res], writes=[pb[i].res, sm[i].res])
                        S.op("dve", lambda e: e.reciprocal(out=rinv, in_=rsum), reads=[sm[i].res], writes=[sm[i].res])
                        S.op("dve", lambda e: e.tensor_scalar(out=pn[i][:], in0=pb[i][:], scalar1=rinv, scalar2=None, op0=ALU.mult),
                             reads=[pb[i].res, sm[i].res], writes=[pn[i].res])
                        pt2 = next_ps()
                        ptb = pt2[:, :].bitcast(BF16)
                        S.mm([lambda pe, kt=kt: pe.transpose(ptb[:, kt * 128:(kt + 1) * 128], pn[i][:, kt * 128:(kt + 1) * 128], identb[:])
                              for kt in range(2)], reads=[pn[i].res, identb.res], writes=[pt2.res])
                        S.op("act", lambda e: e.copy(out=PT[i][:], in_=ptb[:, 0:256].rearrange("p (a b) -> p a b", b=128)),
                             reads=[pt2.res], writes=[PT[i].res])
                        po = next_ps()
                        S.mm([lambda pe, kt=kt: pe.matmul(po[:, 0:128], lhsT=V[:, s_ * 2 + kt, c * 128:(c + 1) * 128], rhs=PT[i][:, kt, :],
                                                          start=(kt == 0), stop=(kt == 1)) for kt in range(2)],
                             reads=[V.res, PT[i].res], writes=[po.res])
                        S.op("dve", lambda e: e.tensor_copy(out=oa[p0:p0 + 64, c, q0:q0 + 128], in_=po[p0:p0 + 64, 0:128]),
                             reads=[po.res], writes=[oa.res])
            S.dma("sp", g.oT[0][0].rearrange("k p t -> p k t"), oa[:], reads=[oa.res], writes=[g.oT[0][1]])

    rpbp, rpbp_r = dscr("rpbp", [240, 157])
    rrep, rrep_r = dscr("rrep", [240, 64, 157])
    I["natmask"] = din("natmask", [64, 64])

    def rcls(r):
        return 7 - r if r <= 3 else (3 if r <= 28 else 31 - r)

    def attn_sample(l, g):
        with Phase(C, "atts") as ph:
            z = ph.sb([128, 157], F32, "z")
            S.op("pool", lambda e: e.memset(z[:], 0.0), writes=[z.res])
            S.dma("sp", rpbp[0:128, :], z[:], reads=[z.res], writes=[rpbp_r])
            S.dma("sp", rpbp[128:240, :], z[0:112, :], reads=[z.res], writes=[rpbp_r], acc=True)
            S.dma("sp", rpbp[:, 63:94], W["rpb"][l].rearrange("h r c -> (h r) c"), writes=[rpbp_r], slow=True)
            for q4 in range(4):
                S.dma("sp", rrep[q4 * 60:(q4 + 1) * 60], AP(rpbp.tensor, q4 * 60 * 157, [[157, 60], [0, 64], [1, 157]]),
                      reads=[rpbp_r], writes=[rrep_r], acc=(q4 > 0))
            mk = ph.sb([64, 64], F32, "mk")
            S.dma("sp", mk[:], I["natmask"], writes=[mk.res])
            qk = [ph.sb([128, 2, NS_TOK], BF16, "qk") for _ in range(2)]
            Ve = [ph.sb([128, 16, 128], BF16, "Ve") for _ in range(2)]
            Vo = [ph.sb([128, 15, 128], BF16, "Vo") for _ in range(2)]
            Vc = [ph.sb([128, 2, 128], BF16, "Vc") for _ in range(2)]
            ckt = [ph.sb([128, 2, 128], F32, "ckt") for _ in range(2)]
            kcT = [ph.sb([128, 256], BF16, "kcT") for _ in range(2)]
            ob = [ph.sb([128, NS_TOK], BF16, "ob") for _ in range(2)]
            bm = [ph.sb([64, 8, 512], F32, "bm") for _ in range(2)]
            sc = [ph.sb([64, 768], F32, "sc") for _ in range(2)]
            pb = [ph.sb([64, 768], F32, "pb") for _ in range(2)]
            pn = [ph.sb([64, 768], BF16, "pn") for _ in range(2)]
            PT = [ph.sb([128, 6, 64], BF16, "PT") for _ in range(2)]
            sm = [ph.sb([128, 8], F32, "sm") for _ in range(2)]
            qv = g.qkT.rearrange("k p t -> p k t")
            u = 0
            for c in range(8):
                b_ = c % 2
                S.dma("sp", qk[b_][:, 0, :], qv[:, c, :], reads=[g.qkT_r], writes=[qk[b_].res])
                S.dma("sp", qk[b_][:, 1, :], qv[:, 8 + c, :], reads=[g.qkT_r], writes=[qk[b_].res], acc=True)
                S.dma("sp", Ve[b_][:], g.vtm[:, c * 128:(c + 1) * 128].rearrange("(j p) c -> p j c", p=128), reads=[g.vtm_r],
                      writes=[Ve[b_].res])
                S.dma("sp", Vo[b_][:], g.vtm[64:64 + 15 * 128, c * 128:(c + 1) * 128].rearrange("(j p) c -> p j c", p=128),
                      reads=[g.vtm_r], writes=[Vo[b_].res])
                S.dma("pool", Vc[b_][:], I["cv"][l][:, c * 128:(c + 1) * 128].rearrange("(j p) c -> p j c", p=128), writes=[Vc[b_].res])
                S.dma("sp", ckt[b_][:], I["ck"][l][:, c * 128:(c + 1) * 128].rearrange("(j p) c -> p j c", p=128), writes=[ckt[b_].res])
                pk = next_ps()
                S.mm([lambda pe, t=t: pe.transpose(pk[:, t * 128:(t + 1) * 128], ckt[b_][:, t, :], ident[:]) for t in range(2)],
                     reads=[ckt[b_].res, ident.res], writes=[pk.res])
                S.op("act", lambda e: e.copy(out=kcT[b_][:], in_=pk[:, 0:256]), reads=[pk.res], writes=[kcT[b_].res])
                for hh in range(2):
                    h = 2 * c + hh
                    p0 = hh * 64
                    bmh = bm[hh]
                    for o in range(8):
                        S.dma("sp", bmh[:, o, :].rearrange("p (j k) -> p j k", k=64),
                              AP(rrep.tensor, ((h * 15 + o) * 64) * 157 + 78, [[156, 64], [64 * 157, 8], [1, 64]]),
                              reads=[rrep_r], writes=[bmh.res], acc=(o > 0))
                    S.op("pool", lambda e: e.tensor_tensor(out=bmh[:].rearrange("p o (j k) -> p (o j) k", k=64),
                                                           in0=bmh[:].rearrange("p o (j k) -> p (o j) k", k=64),
                                                           in1=mk[:].unsqueeze(1).broadcast_to([64, 64, 64]), op=ALU.add),
                         reads=[bmh.res, mk.res], writes=[bmh.res])
                    def stage_a(r, i):
                        r0 = min(max(r - 4, 0), 24)
                        o = rcls(r)
                        psA, psB = next_ps(), next_ps()
                        qa = qk[b_][p0:p0 + 64, 0, r * 64:(r + 1) * 64]
                        S.mm([lambda pe: pe.matmul(psA[0:64, 0:512], lhsT=qa, rhs=qk[b_][p0:p0 + 64, 1, r0 * 64:r0 * 64 + 512],
                                                   start=True, stop=True)], reads=[qk[b_].res], writes=[psA.res])
                        S.mm([lambda pe: pe.matmul(psB[0:64, 0:256], lhsT=qa, rhs=kcT[b_][p0:p0 + 64, :], start=True, stop=True)],
                             reads=[qk[b_].res, kcT[b_].res], writes=[psB.res])
                        S.op("dve", lambda e: e.scalar_tensor_tensor(out=sc[i][:, 0:512], in0=psA[0:64, 0:512], scalar=SCALE,
                                                                     in1=bmh[:, o, :], op0=ALU.mult, op1=ALU.add),
                             reads=[psA.res, bmh.res], writes=[sc[i].res])
                        S.op("act", lambda e: e.mul(out=sc[i][:, 512:768], in_=psB[0:64, 0:256], mul=SCALE), reads=[psB.res, sc[i].res],
                             writes=[sc[i].res])
                        mx, nmx, rsum, rinv = softmax_rows(64, 768, sc[i][:], sc[i].res, 1.0, sm[i], None)
                        S.op("act", lambda e: e.activation(out=pb[i][:], in_=sc[i][:], func=AF.Exp, bias=nmx, scale=1.0,
                                                           accum_out=rsum), reads=[sc[i].res, sm[i].res], writes=[pb[i].res, sm[i].res])
                        S.op("dve", lambda e: e.reciprocal(out=rinv, in_=rsum), reads=[sm[i].res], writes=[sm[i].res])
                        S.op("dve", lambda e: e.tensor_scalar(out=pn[i][:], in0=pb[i][:], scalar1=rinv, scalar2=None, op0=ALU.mult),
                             reads=[pb[i].res, sm[i].res], writes=[pn[i].res])

                    def stage_b(r, i):
                        r0 = min(max(r - 4, 0), 24)
                        pt2 = next_ps()
                        S.mm([lambda pe, j=j: pe.matmul(pt2[:, j * 64:(j + 1) * 64], lhsT=pn[i][:, j * 128:(j + 1) * 128], rhs=identb[0:64, 0:64],
                                                        start=True, stop=True)
                              for j in range(6)], reads=[pn[i].res, identb.res], writes=[pt2.res])
                        S.op("act", lambda e: e.copy(out=PT[i][:], in_=pt2[:, 0:384].rearrange("p (a b) -> p a b", b=64)),
                             reads=[pt2.res], writes=[PT[i].res])
                        po = next_ps()
                        fns = []
                        for j in range(6):
                            if j < 4:
                                vt = Ve[b_][:, r0 // 2 + j, :] if r0 % 2 == 0 else Vo[b_][:, (r0 - 1) // 2 + j, :]
                            else:
                                vt = Vc[b_][:, j - 4, :]
                            fns.append(lambda pe, j=j, vt=vt: pe.matmul(po[:, 0:64], lhsT=vt, rhs=PT[i][:, j, :], start=(j == 0), stop=(j == 5)))
                        S.mm(fns, reads=[Ve[b_].res, Vo[b_].res, Vc[b_].res, PT[i].res], writes=[po.res])
                        S.op("dve", lambda e: e.tensor_copy(out=ob[b_][p0:p0 + 64, r * 64:(r + 1) * 64], in_=po[p0:p0 + 64, 0:64]),
                             reads=[po.res], writes=[ob[b_].res])
                    for r in range(33):
                        if r < 32:
                            stage_a(r, r % 2)
                        if r >= 1:
                            stage_b(r - 1, (r - 1) % 2)
                S.dma("sp", g.oT[0][0][c], ob[b_][:], reads=[ob[b_].res], writes=[g.oT[0][1]], acc=True)

    HY = {}
    for L_ in (256, 2048):
        HY[L_] = dict(zposT=din("zposT%d" % L_, [33, L_]), win=din("win%d" % L_, [L_, 1024]),
                      Ff=din("Ff%d" % L_, [L_, 2 * L_]), Fi=din("Fi%d" % L_, [2 * L_, L_]))
        HY[L_]["Hs"], HY[L_]["Hs_r"] = dscr("Hs%d" % L_, [2 * L_ // 128, 128, 1024])
    for g in G:
        g.zbs, g.zbs_r = dscr("zbs%d" % g.i, [g.ntok, 1024], BF16)
        g.zd, g.zd_r = dscr("zd%d" % g.i, [g.ntok, 1024])
        g.x0s, g.x0s_r = dscr("x0s%d" % g.i, [g.ntok, 1024])
    TWO_PI = 2.0 * math.pi

    def bcast_row(dram_ap_tensor, offset, n):
        return AP(dram_ap_tensor, offset, [[0, 128], [1, n]])

    def hyena_filter(l, L):
        hy_ = HY[L]
        LT = L // 128
        with Phase(C, "hyf") as ph:
            f1 = ph.sb([33, 64], F32, "f1")
            f2 = ph.sb([64, 64], F32, "f2")
            f3 = ph.sb([64, 1024], F32, "f3")
            cc = ph.sb([64, 4], F32, "cc")
            zp = ph.sb([33, L], F32, "zp")
            S.dma("sp", f1[:], W["hy_f1"][l], writes=[f1.res])
            S.dma("sp", f2[:], W["hy_f2"][l], writes=[f2.res])
            S.dma("sp", f3[:], W["hy_f3"][l], writes=[f3.res])
            for j, nm in enumerate(("hy_fb1", "hy_fb2", "hy_freq")):
                S.dma("sp", cc[:, j:j + 1], W[nm][l].rearrange("(p o) -> p o", o=1), writes=[cc.res], slow=True, acc=(j > 0))
            S.dma("sp", zp[:], hy_["zposT"], writes=[zp.res])
            t1 = ph.sb([64, L], F32, "t1")
            t2 = ph.sb([64, L], F32, "t2")
            a = ph.sb([64, 512], F32, "a")
            ki = ph.sb([64, 512], I32, "ki")
            kf = ph.sb([64, 512], F32, "kf")
            cw = min(512, L)

            def sin_layer(lt, K, src, dst, bcol):
                for cb in range(L // cw):
                    pt = next_ps(0, 6)
                    S.mm([lambda pe: pe.matmul(pt[0:64, 0:cw], lhsT=lt[0:K, :], rhs=src[0:K, cb * cw:(cb + 1) * cw], start=True, stop=True)],
                         reads=[lt.res, src.res], writes=[pt.res])
                    S.op("dve", lambda e: e.tensor_scalar(out=a[:, 0:cw], in0=pt[0:64, 0:cw], scalar1=cc[:, bcol:bcol + 1],
                                                          scalar2=cc[:, 2:3], op0=ALU.add, op1=ALU.mult), reads=[pt.res, cc.res], writes=[a.res])
                    S.op("dve", lambda e: e.tensor_scalar(out=ki[:, 0:cw], in0=a[:, 0:cw], scalar1=1.0 / TWO_PI, scalar2=None, op0=ALU.mult),
                         reads=[a.res], writes=[ki.res])
                    S.op("dve", lambda e: e.tensor_copy(out=kf[:, 0:cw], in_=ki[:, 0:cw]), reads=[ki.res], writes=[kf.res])
                    S.op("dve", lambda e: e.scalar_tensor_tensor(out=a[:, 0:cw], in0=kf[:, 0:cw], scalar=-TWO_PI, in1=a[:, 0:cw],
                                                                 op0=ALU.mult, op1=ALU.add), reads=[kf.res, a.res], writes=[a.res])
                    S.op("dve", lambda e: e.tensor_scalar(out=a[:, 0:cw], in0=a[:, 0:cw], scalar1=-math.pi, scalar2=math.pi, op0=ALU.max,
                                                          op1=ALU.min), reads=[a.res], writes=[a.res])
                    S.op("act", lambda e: e.activation(out=dst[:, cb * cw:(cb + 1) * cw], in_=a[:, 0:cw], func=AF.Sin),
                         reads=[a.res], writes=[dst.res])
            sin_layer(f1, 33, zp, t1, 0)
            sin_layer(f2, 64, t1, t2, 1)
            filtb = ph.sb([128, LT, 1024], BF16, "filtb")
            winb = [ph.sb([128, 1024], F32, "winb") for _ in range(2)]
            ft = [ph.sb([128, 1024], F32, "ft") for _ in range(2)]
            fa = [ph.sb([128, 1024], F32, "fa") for _ in range(2)]
            for tt in range(LT):
                i = tt % 2
                S.dma("sp", winb[i][:], hy_["win"][tt * 128:(tt + 1) * 128, :], writes=[winb[i].res])
                for hf in range(2):
                    pt = next_ps(0, 6)
                    S.mm([lambda pe: pe.matmul(pt[:, :], lhsT=t2[0:64, tt * 128:(tt + 1) * 128], rhs=f3[0:64, hf * 512:(hf + 1) * 512],
                                               start=True, stop=True)], reads=[t2.res, f3.res], writes=[pt.res])
                    S.op("dve", lambda e: e.tensor_tensor(out=ft[i][:, hf * 512:(hf + 1) * 512], in0=pt[:, :],
                                                          in1=winb[i][:, hf * 512:(hf + 1) * 512], op=ALU.mult),
                         reads=[pt.res, winb[i].res], writes=[ft[i].res])
                S.op("act", lambda e: e.activation(out=fa[i][:], in_=ft[i][:], func=AF.Abs), reads=[ft[i].res], writes=[fa[i].res])
                S.op("pool", lambda e: e.tensor_copy(out=filtb[:, tt, :], in_=ft[i][:]), reads=[ft[i].res], writes=[filtb.res])
                for hf in range(2):
                    pacc = psum[6 + hf]
                    S.mm([lambda pe: pe.matmul(pacc[:, :], lhsT=ones[:], rhs=fa[i][:, hf * 512:(hf + 1) * 512], start=(tt == 0),
                                               stop=(tt == LT - 1))], reads=[ones.res, fa[i].res], writes=[pacc.res])
            inv = ph.sb([128, 1024], F32, "inv")
            for hf in range(2):
                S.op("dve", lambda e: e.tensor_scalar(out=inv[:, hf * 512:(hf + 1) * 512], in0=psum[6 + hf][:, :], scalar1=1e-6,
                                                      scalar2=None, op0=ALU.add), reads=[psum[6 + hf].res, inv.res], writes=[inv.res])
            S.op("dve", lambda e: e.reciprocal(out=inv[:], in_=inv[:]), reads=[inv.res], writes=[inv.res])
            wb = [ph.sb([128, LT, 512], BF16, "wbF") for _ in range(2)]
            ob = [ph.sb([128, 512], F32, "ob") for _ in range(3)]
            st = {"i": 0}

            def epi(c0, t0, pt, nr, ncv):
                i = st["i"] % 3
                st["i"] += 1
                S.op("dve", lambda e: e.tensor_tensor(out=ob[i][:], in0=pt[:, :], in1=inv[:, t0:t0 + 512], op=ALU.mult),
                     reads=[pt.res, inv.res], writes=[ob[i].res])
                S.dma("sp", hy_["Hs"][c0 // 128, :, t0:t0 + 512], ob[i][:], reads=[ob[i].res], writes=[hy_["Hs_r"]], acc=True)
            gemm(ph, hy_["Ff"], LT, 0, 2 * L, 512, filtb, 1024, "fm", epi, wb, ps_lo=0, ps_hi=6)

    def hyena_conv(l, g, L):
        with Phase(C, "hyc") as ph:
            cwt = ph.sb([128, 3, 3072], F32, "cw")
            cbt = ph.sb([128, 3072], F32, "cb")
            drt = ph.sb([128, 1024], F32, "dr")
            for j in range(3):
                S.dma("sp", cwt[:, j, :], bcast_row(W["hy_conv_w"].tensor, (l * 3 + j) * 3072, 3072), writes=[cwt.res], acc=(j > 0))
            S.dma("sp", cbt[:], bcast_row(W["hy_conv_b"].tensor, l * 3072, 3072), writes=[cbt.res])
            S.dma("sp", drt[:], bcast_row(W["hy_d"].tensor, l * 1024, 1024), writes=[drt.res])
            xm = [ph.sb([128, 1024], F32, "xm") for _ in range(2)]
            xc = [ph.sb([128, 1024], F32, "xc") for _ in range(2)]
            xp = [ph.sb([128, 1024], F32, "xp") for _ in range(2)]
            uu = [[ph.sb([128, 1024], F32, "u%d" % cg) for cg in range(3)] for _ in range(2)]
            tq = [ph.sb([128, 1024], F32, "tq") for _ in range(2)]
            zf = [ph.sb([128, 1024], F32, "zf") for _ in range(2)]
            zbt = [ph.sb([128, 1024], BF16, "zb") for _ in range(2)]
            zdt = [ph.sb([128, 1024], F32, "zd") for _ in range(2)]
            k = 0
            for tt in range(g.ntok // 128):
                t0 = tt * 128
                sp_ = t0 % L
                bi = tt % 2
                for cg in range(3):
                    ki_ = k % 2
                    k += 1
                    c0 = 3072 + cg * 1024
                    xm_, xc_, xp_, u, t = xm[ki_], xc[ki_], xp[ki_], uu[bi][cg], tq[ki_]
                    if sp_ == 0:
                        S.op("pool", lambda e: e.memset(xm_[0:1, :], 0.0), writes=[xm_.res])
                        S.dma("sp", xm_[1:128, :], g.rh[t0:t0 + 127, c0:c0 + 1024], reads=[g.rh_r], writes=[xm_.res], acc=True)
                    else:
                        S.dma("sp", xm_[:], g.rh[t0 - 1:t0 + 127, c0:c0 + 1024], reads=[g.rh_r], writes=[xm_.res])
                    S.dma("sp", xc_[:], g.rh[t0:t0 + 128, c0:c0 + 1024], reads=[g.rh_r], writes=[xc_.res])
                    if sp_ + 128 == L:
                        S.op("pool", lambda e: e.memset(xp_[:], 0.0), writes=[xp_.res])
                        S.dma("sp", xp_[0:127, :], g.rh[t0 + 1:t0 + 128, c0:c0 + 1024], reads=[g.rh_r], writes=[xp_.res], acc=True)
                    else:
                        S.dma("sp", xp_[:], g.rh[t0 + 1:t0 + 129, c0:c0 + 1024], reads=[g.rh_r], writes=[xp_.res])
                    wv = lambda j: cwt[:, j, cg * 1024:(cg + 1) * 1024]
                    S.op("dve", lambda e: e.tensor_tensor(out=u[:], in0=xm_[:], in1=wv(0), op=ALU.mult), reads=[xm_.res, cwt.res], writes=[u.res])
                    S.op("pool", lambda e: e.tensor_tensor(out=t[:], in0=xc_[:], in1=wv(1), op=ALU.mult), reads=[xc_.res, cwt.res], writes=[t.res])
                    S.op("dve", lambda e: e.tensor_tensor(out=u[:], in0=u[:], in1=t[:], op=ALU.add), reads=[u.res, t.res], writes=[u.res])
                    S.op("pool", lambda e: e.tensor_tensor(out=t[:], in0=xp_[:], in1=wv(2), op=ALU.mult), reads=[xp_.res, cwt.res], writes=[t.res])
                    S.op("dve", lambda e: e.tensor_tensor(out=u[:], in0=u[:], in1=t[:], op=ALU.add), reads=[u.res, t.res], writes=[u.res])
                    S.op("dve", lambda e: e.tensor_tensor(out=u[:], in0=u[:], in1=cbt[:, cg * 1024:(cg + 1) * 1024], op=ALU.add),
                         reads=[u.res, cbt.res], writes=[u.res])
                u0, u1, u2 = uu[bi]
                S.dma("sp", g.x0s[t0:t0 + 128, :], u0[:], reads=[u0.res], writes=[g.x0s_r], acc=True)
                S.op("dve", lambda e: e.tensor_tensor(out=zf[bi][:], in0=u2[:], in1=u1[:], op=ALU.mult), reads=[u1.res, u2.res], writes=[zf[bi].res])
                S.op("act", lambda e: e.copy(out=zbt[bi][:], in_=zf[bi][:]), reads=[zf[bi].res], writes=[zbt[bi].res])
                S.op("pool", lambda e: e.tensor_tensor(out=zdt[bi][:], in0=zf[bi][:], in1=drt[:], op=ALU.mult), reads=[zf[bi].res, drt.res],
                     writes=[zdt[bi].res])
                S.dma("sp", g.zbs[t0:t0 + 128, :], zbt[bi][:], reads=[zbt[bi].res], writes=[g.zbs_r], acc=True)
                S.dma("sp", g.zd[t0:t0 + 128, :], zdt[bi][:], reads=[zdt[bi].res], writes=[g.zd_r], acc=True)

    def hyena_dft(l, g, L):
        hy_ = HY[L]
        LT = L // 128
        with Phase(C, "hyd") as ph:
            zT = ph.sb([128, LT, 512], BF16, "zT")
            YT = ph.sb([128, 2 * LT, 512], BF16, "YT")
            wbF = [ph.sb([128, LT, 512], BF16, "wbF") for _ in range(2)]
            wbI = [ph.sb([128, 2 * LT, 256], BF16, "wbI") for _ in range(2)]
            hre = [ph.sb([128, 512], F32, "hre") for _ in range(2)]
            him = [ph.sb([128, 512], F32, "him") for _ in range(2)]
            zre = [ph.sb([128, 512], F32, "zre") for _ in range(2)]
            zim = [ph.sb([128, 512], F32, "zim") for _ in range(2)]
            ta = [ph.sb([128, 512], F32, "ta") for _ in range(2)]
            tb = [ph.sb([128, 512], F32, "tb") for _ in range(2)]
            tc_ = [ph.sb([128, 512], F32, "tc") for _ in range(2)]
            td = [ph.sb([128, 512], F32, "td") for _ in range(2)]
            zdt = [ph.sb([128, 512], F32, "zdt") for _ in range(2)]
            x0t = [ph.sb([128, 512], F32, "x0t") for _ in range(2)]
            ot = [ph.sb([128, 512], F32, "ot") for _ in range(2)]
            otb = [ph.sb([128, 4, 128], BF16, "otb") for _ in range(2)]
            for sq in range(g.ntok // L):
                s0 = sq * L
                for half in range(2):
                    h0 = half * 512
                    S.dma("sp", zT[:], g.zbs[s0:s0 + L, h0:h0 + 512].rearrange("(j p) c -> p j c", p=128), reads=[g.zbs_r], writes=[zT.res])
                    st = {"i": 0, "re": None}

                    def epi_f(c0, t0, pt, nr, ncv):
                        blk = c0 // 128
                        if blk % 2 == 0:
                            i = st["i"] % 2
                            S.op("act", lambda e: e.copy(out=zre[i][:], in_=pt[:, :]), reads=[pt.res], writes=[zre[i].res])
                            S.dma("sp", hre[i][:], hy_["Hs"][blk, :, h0:h0 + 512], reads=[hy_["Hs_r"]], writes=[hre[i].res])
                            S.dma("sp", him[i][:], hy_["Hs"][blk + 1, :, h0:h0 + 512], reads=[hy_["Hs_r"]], writes=[him[i].res])
                            return
                        i = st["i"] % 2
                        st["i"] += 1
                        S.op("act", lambda e: e.copy(out=zim[i][:], in_=pt[:, :]), reads=[pt.res], writes=[zim[i].res])
                        S.op("dve", lambda e: e.tensor_tensor(out=ta[i][:], in0=zre[i][:], in1=hre[i][:], op=ALU.mult),
                             reads=[zre[i].res, hre[i].res], writes=[ta[i].res])
                        S.op("pool", lambda e: e.tensor_tensor(out=tb[i][:], in0=zim[i][:], in1=him[i][:], op=ALU.mult),
                             reads=[zim[i].res, him[i].res], writes=[tb[i].res])
                        S.op("dve", lambda e: e.tensor_tensor(out=YT[:, blk - 1, :], in0=ta[i][:], in1=tb[i][:], op=ALU.subtract),
                             reads=[ta[i].res, tb[i].res], writes=[YT.res])
                        S.op("pool", lambda e: e.tensor_tensor(out=tc_[i][:], in0=zre[i][:], in1=him[i][:], op=ALU.mult),
                             reads=[zre[i].res, him[i].res], writes=[tc_[i].res])
                        S.op("dve", lambda e: e.tensor_tensor(out=td[i][:], in0=zim[i][:], in1=hre[i][:], op=ALU.mult),
                             reads=[zim[i].res, hre[i].res], writes=[td[i].res])
                        S.op("dve", lambda e: e.tensor_tensor(out=YT[:, blk, :], in0=tc_[i][:], in1=td[i][:], op=ALU.add),
                             reads=[tc_[i].res, td[i].res], writes=[YT.res])
                    gemm(ph, hy_["Ff"], LT, 0, 2 * L, 512, zT, 512, "fm", epi_f, wbF)
                    st2 = {"i": 0}

                    def epi_i(c0, t0, pt, nr, ncv):
                        i = st2["i"] % 2
                        st2["i"] += 1
                        r0 = s0 + c0
                        S.dma("sp", zdt[i][:], g.zd[r0:r0 + 128, h0:h0 + 512], reads=[g.zd_r], writes=[zdt[i].res])
                        S.dma("sp", x0t[i][:], g.x0s[r0:r0 + 128, h0:h0 + 512], reads=[g.x0s_r], writes=[x0t[i].res])
                        S.op("dve", lambda e: e.tensor_tensor(out=ot[i][:], in0=pt[:, :], in1=zdt[i][:], op=ALU.add),
                             reads=[pt.res, zdt[i].res], writes=[ot[i].res])
                        S.op("pool", lambda e: e.tensor_tensor(out=ot[i][:], in0=ot[i][:], in1=x0t[i][:], op=ALU.mult),
                             reads=[ot[i].res, x0t[i].res], writes=[ot[i].res])
                        pt2 = next_ps()
                        S.mm([lambda pe, q=q: pe.transpose(pt2[:, q * 128:(q + 1) * 128], ot[i][:, q * 128:(q + 1) * 128], ident[:])
                              for q in range(4)], reads=[ot[i].res, ident.res], writes=[pt2.res])
                        S.op("act", lambda e: e.copy(out=otb[i][:], in_=pt2[:, :].rearrange("p (a b) -> p a b", b=128)),
                             reads=[pt2.res], writes=[otb[i].res])
                        S.dma("sp", g.oT[2][0][half * 4:(half + 1) * 4, :, r0:r0 + 128].rearrange("k p t -> p k t"), otb[i][:],
                              reads=[otb[i].res], writes=[g.oT[2][1]], acc=True)
                    gemm(ph, hy_["Fi"], 2 * LT, 0, L, 256, YT, 512, "fm", epi_i, wbI)

    for g in G:
        g.lora, g.lora_r = dscr("lora%d" % g.i, [4, 64, g.ntok], BF16)
        g.sg, g.sg_r = dscr("sg%d" % g.i, [128, g.ntok], BF16)
        g.SH, g.SH_r = dscr("SH%d" % g.i, [g.ntok, 3, 1024])
        g.DE = [dscr("DE%d_%d" % (g.i, e), [g.ntok, 3, 1024]) for e in range(2)]
        g.ysc = [dscr("ysc%d_%d" % (g.i, e), [g.ntok, 1024]) for e in range(2)]
        g.gsc, g.gsc_r = dscr("gsc%d" % g.i, [g.ntok, 1024])
        g.bon, g.bon_r = dscr("bon%d" % g.i, [g.ntok, 1024])

    def rwkv_lora(ph, l, g, hT):
        wbs = [ph.sb([128, 16, 128], BF16, "wbl") for _ in range(2)]
        obl = [ph.sb([128, 512], BF16, "obl") for _ in range(3)]
        for e in range(2):
            gemm(ph, W["wkv_w1"][l][e], 16, 0, 64, 64, hT, g.ntok, "fm",
                 epi_store(ph, obl, lambda c0, t0, nr, ncv, e=e: g.lora[e, :, t0:t0 + ncv], g.lora_r, func=AF.Tanh), wbs)
            gemm(ph, W["wkv_a1"][l][e], 16, 0, 64, 64, hT, g.ntok, "fm",
                 epi_store(ph, obl, lambda c0, t0, nr, ncv, e=e: g.lora[2 + e, :, t0:t0 + ncv], g.lora_r), wbs)
        gemm(ph, W["wkv_g1"][l], 16, 0, 128, 128, hT, g.ntok, "fm",
             epi_store(ph, obl, lambda c0, t0, nr, ncv: g.sg[:, t0:t0 + ncv], g.sg_r, func=AF.Sigmoid), wbs)

    def rwkv_pre(l, g, L):
        with Phase(C, "rwp") as ph:
            cwt = ph.sb([128, 3, 3072], F32, "cw")
            cbt = ph.sb([128, 3072], F32, "cb")
            for j in range(3):
                S.dma("sp", cwt[:, j, :], bcast_row(W["wkv_conv_w"].tensor, (l * 3 + j) * 3072, 3072), writes=[cwt.res], acc=(j > 0))
            S.dma("sp", cbt[:], bcast_row(W["wkv_conv_b"].tensor, l * 3072, 3072), writes=[cbt.res])
            rows = ph.sb([128, 8, 1024], F32, "rows")
            for e in range(2):
                S.dma("sp", rows[:, e, :], bcast_row(W["wkv_w0"].tensor, (l * 2 + e) * 1024, 1024), writes=[rows.res], acc=True)
                S.dma("sp", rows[:, 2 + e, :], bcast_row(W["wkv_a0"].tensor, (l * 2 + e) * 1024, 1024), writes=[rows.res], acc=True)
            S.dma("sp", rows[:, 4, :], bcast_row(W["wkv_k_k"].tensor, l * 1024, 1024), writes=[rows.res], acc=True)
            S.dma("sp", rows[:, 5, :], bcast_row(W["wkv_k_a"].tensor, l * 1024, 1024), writes=[rows.res], acc=True)
            S.dma("sp", rows[:, 7, :], bcast_row(W["wkv_r_k"].tensor, l * 1024, 1024), writes=[rows.res], acc=True)
            S.op("dve", lambda e_: e_.tensor_scalar(out=rows[:, 6, :], in0=rows[:, 5, :], scalar1=-1.0, scalar2=1.0, op0=ALU.mult, op1=ALU.add),
                 reads=[rows.res], writes=[rows.res])
            w2b = ph.sb([64, 4, 1024], BF16, "w2b")
            for e in range(2):
                S.dma("pool", w2b[:, e, :], W["wkv_w2"][l][e], writes=[w2b.res], acc=True)
                S.dma("pool", w2b[:, 2 + e, :], W["wkv_a2"][l][e], writes=[w2b.res], acc=True)
            g2b = ph.sb([128, 1024], BF16, "g2b")
            S.dma("pool", g2b[:], W["wkv_g2"][l], writes=[g2b.res])
            xm = ph.sb([128, 1024], F32, "xm")
            xc = ph.sb([128, 1024], F32, "xc")
            xp = ph.sb([128, 1024], F32, "xp")
            rkv = [ph.sb([128, 1024], F32, "rkv%d" % i) for i in range(3)]
            t1 = ph.sb([128, 1024], F32, "t1")
            t2 = ph.sb([128, 1024], F32, "t2")
            kk = ph.sb([128, 1024], F32, "kk")
            at = ph.sb([128, 1024], F32, "at")
            wt = ph.sb([128, 1024], F32, "wt")
            o1 = ph.sb([128, 1024], F32, "o1")
            o2 = ph.sb([128, 1024], F32, "o2")
            sm = ph.sb([128, 64], F32, "sm")
            lt = ph.sb([64, 4, 128], BF16, "lt")
            sgt = ph.sb([128, 128], BF16, "sgt")
            v3 = lambda t_: t_[:].rearrange("p (h k) -> p h k", k=64)
            for tt in range(g.ntok // 128):
                t0 = tt * 128
                sp_ = t0 % L
                for cg in range(3):
                    c0 = cg * 1024
                    u = rkv[cg]
                    if sp_ == 0:
                        S.op("pool", lambda e: e.memset(xm[0:1, :], 0.0), writes=[xm.res])
                        S.dma("sp", xm[1:128, :], g.rh[t0:t0 + 127, c0:c0 + 1024], reads=[g.rh_r], writes=[xm.res], acc=True)
                    else:
                        S.dma("sp", xm[:], g.rh[t0 - 1:t0 + 127, c0:c0 + 1024], reads=[g.rh_r], writes=[xm.res])
                    S.dma("sp", xc[:], g.rh[t0:t0 + 128, c0:c0 + 1024], reads=[g.rh_r], writes=[xc.res])
                    if sp_ + 128 == L:
                        S.op("pool", lambda e: e.memset(xp[:], 0.0), writes=[xp.res])
                        S.dma("sp", xp[0:127, :], g.rh[t0 + 1:t0 + 128, c0:c0 + 1024], reads=[g.rh_r], writes=[xp.res], acc=True)
                    else:
                        S.dma("sp", xp[:], g.rh[t0 + 1:t0 + 129, c0:c0 + 1024], reads=[g.rh_r], writes=[xp.res])
                    wv = lambda j: cwt[:, j, cg * 1024:(cg + 1) * 1024]
                    S.op("dve", lambda e: e.tensor_tensor(out=u[:], in0=xm[:], in1=wv(0), op=ALU.mult), reads=[xm.res, cwt.res], writes=[u.res])
                    S.op("pool", lambda e: e.tensor_tensor(out=t1[:], in0=xc[:], in1=wv(1), op=ALU.mult), reads=[xc.res, cwt.res], writes=[t1.res])
                    S.op("dve", lambda e: e.tensor_tensor(out=u[:], in0=u[:], in1=t1[:], op=ALU.add), reads=[u.res, t1.res], writes=[u.res])
                    S.op("pool", lambda e: e.tensor_tensor(out=t1[:], in0=xp[:], in1=wv(2), op=ALU.mult), reads=[xp.res, cwt.res], writes=[t1.res])
                    S.op("dve", lambda e: e.tensor_tensor(out=u[:], in0=u[:], in1=t1[:], op=ALU.add), reads=[u.res, t1.res], writes=[u.res])
                    S.op("dve", lambda e: e.tensor_tensor(out=u[:], in0=u[:], in1=cbt[:, cg * 1024:(cg + 1) * 1024], op=ALU.add),
                         reads=[u.res, cbt.res], writes=[u.res])
                r_, k_, v_ = rkv
                S.dma("sp", g.SH[t0:t0 + 128, 1, :], r_[:], reads=[r_.res], writes=[g.SH_r], acc=True)
                S.dma("sp", g.SH[t0:t0 + 128, 2, :], v_[:], reads=[v_.res], writes=[g.SH_r], acc=True)
                S.op("dve", lambda e: e.tensor_tensor(out=kk[:], in0=k_[:], in1=rows[:, 4, :], op=ALU.mult), reads=[k_.res, rows.res], writes=[kk.res])
                S.op("pool", lambda e: e.tensor_tensor(out=t1[:], in0=kk[:], in1=kk[:], op=ALU.mult), reads=[kk.res], writes=[t1.res])
                S.op("dve", lambda e: e.tensor_reduce(out=sm[:, 0:16], in_=v3(t1), axis=AX.X, op=ALU.add), reads=[t1.res], writes=[sm.res])
                S.op("dve", lambda e: e.tensor_scalar(out=sm[:, 0:16], in0=sm[:, 0:16], scalar1=1e-12, scalar2=None, op0=ALU.add),
                     reads=[sm.res], writes=[sm.res])
                S.op("act", lambda e: e.sqrt(out=sm[:, 0:16], in_=sm[:, 0:16]), reads=[sm.res], writes=[sm.res])
                S.op("dve", lambda e: e.reciprocal(out=sm[:, 0:16], in_=sm[:, 0:16]), reads=[sm.res], writes=[sm.res])
                S.op("dve", lambda e: e.tensor_tensor(out=v3(kk), in0=v3(kk), in1=sm[:, 0:16].unsqueeze(2).broadcast_to([128, 16, 64]), op=ALU.mult),
                     reads=[kk.res, sm.res], writes=[kk.res])
                S.dma("sp", g.SH[t0:t0 + 128, 0, :], kk[:], reads=[kk.res], writes=[g.SH_r], acc=True)
                S.op("pool", lambda e: e.tensor_tensor(out=t1[:], in0=r_[:], in1=k_[:], op=ALU.mult), reads=[r_.res, k_.res], writes=[t1.res])
                S.op("pool", lambda e: e.tensor_tensor(out=t1[:], in0=t1[:], in1=rows[:, 7, :], op=ALU.mult), reads=[t1.res, rows.res], writes=[t1.res])
                S.op("dve", lambda e: e.tensor_reduce(out=sm[:, 16:32], in_=v3(t1), axis=AX.X, op=ALU.add), reads=[t1.res, sm.res], writes=[sm.res])
                S.op("dve", lambda e: e.tensor_tensor(out=v3(o1), in0=v3(v_), in1=sm[:, 16:32].unsqueeze(2).broadcast_to([128, 16, 64]), op=ALU.mult),
                     reads=[v_.res, sm.res], writes=[o1.res])
                S.dma("sp", g.bon[t0:t0 + 128, :], o1[:], reads=[o1.res], writes=[g.bon_r], acc=True)
                S.dma("sp", lt[:], g.lora[:, :, t0:t0 + 128].rearrange("a p t -> p a t"), reads=[g.lora_r], writes=[lt.res])
                S.dma("sp", sgt[:], g.sg[:, t0:t0 + 128], reads=[g.sg_r], writes=[sgt.res])
                for hf in range(2):
                    pt = next_ps()
                    S.mm([lambda pe: pe.matmul(pt[:, :], lhsT=sgt[:], rhs=g2b[:, hf * 512:(hf + 1) * 512], start=True, stop=True)],
                         reads=[sgt.res, g2b.res], writes=[pt.res])
                    S.op("act", lambda e: e.copy(out=o2[:, hf * 512:(hf + 1) * 512], in_=pt[:, :]), reads=[pt.res, o2.res], writes=[o2.res])
                S.dma("sp", g.gsc[t0:t0 + 128, :], o2[:], reads=[o2.res], writes=[g.gsc_r], acc=True)
                for e in range(2):
                    for hf in range(2):
                        pt = next_ps()
                        S.mm([lambda pe: pe.matmul(pt[:, :], lhsT=lt[:, e, :], rhs=w2b[:, e, hf * 512:(hf + 1) * 512], start=True, stop=True)],
                             reads=[lt.res, w2b.res], writes=[pt.res])
                        S.op("dve", lambda e_: e_.tensor_tensor(out=wt[:, hf * 512:(hf + 1) * 512], in0=pt[:, :],
                                                               in1=rows[:, e, hf * 512:(hf + 1) * 512], op=ALU.add),
                             reads=[pt.res, rows.res, wt.res], writes=[wt.res])
                    S.op("act", lambda e_: e_.activation(out=wt[:], in_=wt[:], func=AF.Sigmoid), reads=[wt.res], writes=[wt.res])
                    S.op("act", lambda e_: e_.activation(out=wt[:], in_=wt[:], func=AF.Exp, scale=-math.exp(-0.5)), reads=[wt.res], writes=[wt.res])
                    S.dma("sp", g.DE[e][0][t0:t0 + 128, 0, :], wt[:], reads=[wt.res], writes=[g.DE[e][1]], acc=True)
                    for hf in range(2):
                        pt = next_ps()
                        S.mm([lambda pe: pe.matmul(pt[:, :], lhsT=lt[:, 2 + e, :], rhs=w2b[:, 2 + e, hf * 512:(hf + 1) * 512], start=True, stop=True)],
                             reads=[lt.res, w2b.res], writes=[pt.res])
                        S.op("dve", lambda e_: e_.tensor_tensor(out=at[:, hf * 512:(hf + 1) * 512], in0=pt[:, :],
                                                               in1=rows[:, 2 + e, hf * 512:(hf + 1) * 512], op=ALU.add),
                             reads=[pt.res, rows.res, at.res], writes=[at.res])
                    S.op("act", lambda e_: e_.activation(out=at[:], in_=at[:], func=AF.Sigmoid), reads=[at.res], writes=[at.res])
                    S.op("dve", lambda e_: e_.tensor_tensor(out=o1[:], in0=kk[:], in1=at[:], op=ALU.mult), reads=[kk.res, at.res], writes=[o1.res])
                    S.dma("sp", g.DE[e][0][t0:t0 + 128, 1, :], o1[:], reads=[o1.res], writes=[g.DE[e][1]], acc=True)
                    S.op("pool", lambda e_: e_.tensor_tensor(out=t2[:], in0=at[:], in1=rows[:, 5, :], op=ALU.mult), reads=[at.res, rows.res], writes=[t2.res])
                    S.op("pool", lambda e_: e_.tensor_tensor(out=t2[:], in0=t2[:], in1=rows[:, 6, :], op=ALU.add), reads=[t2.res, rows.res], writes=[t2.res])
                    S.op("dve", lambda e_: e_.tensor_tensor(out=o2[:], in0=k_[:], in1=t2[:], op=ALU.mult), reads=[k_.res, t2.res], writes=[o2.res])
                    S.dma("sp", g.DE[e][0][t0:t0 + 128, 2, :], o2[:], reads=[o2.res], writes=[g.DE[e][1]], acc=True)

    def rwkv_scan(l, g, L):
        sample = (g.i == 1)
        NV = 16 if sample else 64
        TC = 32 if sample else 16
        with Phase(C, "rws") as ph:
            St = ph.sb([128, NV, 64], F32, "S")
            tmp = ph.sb([128, NV, 64], F32, "tmp")
            At = ph.sb([128, NV, 64], F32, "A")
            Bt = ph.sb([128, NV, 64], F32, "B")
            sa = ph.sb([128, NV], F32, "sa")
            Dq = [[ph.sb([128, TC, 64], F32, "D%d" % q) for q in range(5)] for _ in range(2)]
            Vt = [ph.sb([128, TC, NV], F32, "V") for _ in range(2)]
            Yt = [ph.sb([128, TC, NV], F32, "Y") for _ in range(2)]
            if sample:
                S.dma("sp", St[:].rearrange("p a b -> p (a b)"), I["s0"][l], writes=[St.res])
            else:
                S.op("pool", lambda e: e.memset(St[:], 0.0), writes=[St.res])
            def srcs(e):
                return [(g.SH, g.SH_r, 0), (g.DE[e][0], g.DE[e][1], 0), (g.DE[e][0], g.DE[e][1], 1), (g.DE[e][0], g.DE[e][1], 2), (g.SH, g.SH_r, 1)]
            nchunk = L // TC
            for c in range(nchunk):
                bi = c % 2
                i0 = c * TC
                for e in range(2):
                    tstart = i0 if e == 0 else (L - 1 - i0)
                    sgn = 1 if e == 0 else -1
                    if sample:
                        for vq in range(4):
                            dstp = slice(e * 64 + vq, (e + 1) * 64, 4)
                            first = (e == 0 and vq == 0)
                            for q, (arr, arr_r, slot) in enumerate(srcs(e)):
                                S.dma("sp", Dq[bi][q][dstp, :, :],
                                      AP(arr.tensor, tstart * 3072 + slot * 1024, [[64, 16], [sgn * 3072, TC], [1, 64]]),
                                      reads=[arr_r], writes=[Dq[bi][q].res], acc=(not first))
                            S.dma("sp", Vt[bi][dstp, :, :],
                                  AP(g.SH.tensor, tstart * 3072 + 2 * 1024 + vq * 16, [[64, 16], [sgn * 3072, TC], [1, 16]]),
                                  reads=[g.SH_r], writes=[Vt[bi].res], acc=(not first))
                    else:
                        for sq in range(4):
                            p0 = sq * 32 + e * 16
                            first = (e == 0 and sq == 0)
                            for q, (arr, arr_r, slot) in enumerate(srcs(e)):
                                S.dma("sp", Dq[bi][q][p0:p0 + 16, :, :],
                                      AP(arr.tensor, (sq * L + tstart) * 3072 + slot * 1024, [[64, 16], [sgn * 3072, TC], [1, 64]]),
                                      reads=[arr_r], writes=[Dq[bi][q].res], acc=(not first))
                            S.dma("sp", Vt[bi][p0:p0 + 16, :, :],
                                  AP(g.SH.tensor, (sq * L + tstart) * 3072 + 2 * 1024, [[64, 16], [sgn * 3072, TC], [1, 64]]),
                                  reads=[g.SH_r], writes=[Vt[bi].res], acc=(not first))
                D = Dq[bi]
                for i in range(TC):
                    bc = lambda q: D[q][:, i, :].unsqueeze(1).broadcast_to([128, NV, 64])
                    S.op("dve", lambda e_: e_.tensor_tensor(out=tmp[:], in0=St[:], in1=bc(0), op=ALU.mult), reads=[St.res, D[0].res], writes=[tmp.res])
                    S.op("dve", lambda e_: e_.tensor_reduce(out=sa[:], in_=tmp[:], axis=AX.X, op=ALU.add), reads=[tmp.res], writes=[sa.res])
                    S.op("pool", lambda e_: e_.tensor_tensor(out=At[:], in0=Vt[bi][:, i, :].unsqueeze(2).broadcast_to([128, NV, 64]), in1=bc(3), op=ALU.mult),
                         reads=[Vt[bi].res, D[3].res], writes=[At.res])
                    S.op("pool", lambda e_: e_.tensor_tensor(out=Bt[:], in0=St[:], in1=bc(1), op=ALU.mult), reads=[St.res, D[1].res], writes=[Bt.res])
                    S.op("pool", lambda e_: e_.tensor_tensor(out=Bt[:], in0=Bt[:], in1=At[:], op=ALU.add), reads=[Bt.res, At.res], writes=[Bt.res])
                    S.op("dve", lambda e_: e_.tensor_tensor(out=tmp[:], in0=sa[:].unsqueeze(2).broadcast_to([128, NV, 64]), in1=bc(2), op=ALU.mult),
                         reads=[sa.res, D[2].res], writes=[tmp.res])
                    S.op("dve", lambda e_: e_.tensor_tensor(out=St[:], in0=Bt[:], in1=tmp[:], op=ALU.subtract), reads=[Bt.res, tmp.res], writes=[St.res])
                    S.op("dve", lambda e_: e_.tensor_tensor(out=tmp[:], in0=St[:], in1=bc(4), op=ALU.mult), reads=[St.res, D[4].res], writes=[tmp.res])
                    S.op("dve", lambda e_: e_.tensor_reduce(out=Yt[bi][:, i, :], in_=tmp[:], axis=AX.X, op=ALU.add), reads=[tmp.res], writes=[Yt[bi].res])
                for e in range(2):
                    tstart = i0 if e == 0 else (L - 1 - i0)
                    sgn = 1 if e == 0 else -1
                    ya, ya_r = g.ysc[e]
                    if sample:
                        for vq in range(4):
                            S.dma("sp", AP(ya.tensor, tstart * 1024 + vq * 16, [[64, 16], [sgn * 1024, TC], [1, 16]]),
                                  Yt[bi][slice(e * 64 + vq, (e + 1) * 64, 4), :, :], reads=[Yt[bi].res], writes=[ya_r], acc=True)
                    else:
                        for sq in range(4):
                            p0 = sq * 32 + e * 16
                            S.dma("sp", AP(ya.tensor, (sq * L + tstart) * 1024, [[64, 16], [sgn * 1024, TC], [1, 64]]), Yt[bi][p0:p0 + 16, :, :],
                                  reads=[Yt[bi].res], writes=[ya_r], acc=True)
            if not sample:
                for sq in range(4):
                    r0 = (sq * DEPTH + l) * 32
                    S.dma("sp", O["ns"][r0:r0 + 32, :], St[sq * 32:(sq + 1) * 32, :, :].rearrange("p a b -> p (a b)"), reads=[St.res],
                          writes=[ORES["ns"]], acc=True)

    def rwkv_post(l, g):
        with Phase(C, "rwo") as ph:
            rows = ph.sb([128, 2, 1024], F32, "rows")
            S.dma("sp", rows[:, 0, :], bcast_row(W["wkv_gn_g"].tensor, l * 1024, 1024), writes=[rows.res], acc=True)
            S.dma("sp", rows[:, 1, :], bcast_row(W["wkv_gn_b"].tensor, l * 1024, 1024), writes=[rows.res], acc=True)
            ya = [ph.sb([128, 1024], F32, "ya") for _ in range(2)]
            yb = [ph.sb([128, 1024], F32, "yb") for _ in range(2)]
            bo = [ph.sb([128, 1024], F32, "bo") for _ in range(2)]
            gg = [ph.sb([128, 1024], F32, "gg") for _ in range(2)]
            tq = [ph.sb([128, 1024], F32, "tq") for _ in range(2)]
            sm = [ph.sb([128, 64], F32, "sm") for _ in range(2)]
            otb = [ph.sb([128, 8, 128], BF16, "otb") for _ in range(2)]
            v3 = lambda t_: t_[:].rearrange("p (h k) -> p h k", k=64)
            for tt in range(g.ntok // 128):
                i = tt % 2
                t0 = tt * 128
                y, y2, b_, g_, t_, s_ = ya[i], yb[i], bo[i], gg[i], tq[i], sm[i]
                S.dma("sp", y[:], g.ysc[0][0][t0:t0 + 128, :], reads=[g.ysc[0][1]], writes=[y.res])
                S.dma("sp", y2[:], g.ysc[1][0][t0:t0 + 128, :], reads=[g.ysc[1][1]], writes=[y2.res])
                S.dma("sp", b_[:], g.bon[t0:t0 + 128, :], reads=[g.bon_r], writes=[b_.res])
                S.dma("sp", g_[:], g.gsc[t0:t0 + 128, :], reads=[g.gsc_r], writes=[g_.res])
                S.op("dve", lambda e: e.tensor_tensor(out=y[:], in0=y[:], in1=y2[:], op=ALU.add), reads=[y.res, y2.res], writes=[y.res])
                S.op("dve", lambda e: e.tensor_reduce(out=s_[:, 0:16], in_=v3(y), axis=AX.X, op=ALU.add), reads=[y.res], writes=[s_.res])
                S.op("dve", lambda e: e.tensor_scalar(out=s_[:, 0:16], in0=s_[:, 0:16], scalar1=-1.0 / 64, scalar2=None, op0=ALU.mult),
                     reads=[s_.res], writes=[s_.res])
                S.op("dve", lambda e: e.tensor_tensor(out=v3(y), in0=v3(y), in1=s_[:, 0:16].unsqueeze(2).broadcast_to([128, 16, 64]), op=ALU.add),
                     reads=[y.res, s_.res], writes=[y.res])
                S.op("pool", lambda e: e.tensor_tensor(out=t_[:], in0=y[:], in1=y[:], op=ALU.mult), reads=[y.res], writes=[t_.res])
                S.op("dve", lambda e: e.tensor_reduce(out=s_[:, 16:32], in_=v3(t_), axis=AX.X, op=ALU.add), reads=[t_.res, s_.res], writes=[s_.res])
                S.op("dve", lambda e: e.tensor_scalar(out=s_[:, 16:32], in0=s_[:, 16:32], scalar1=1.0 / 64, scalar2=64e-5, op0=ALU.mult, op1=ALU.add),
                     reads=[s_.res], writes=[s_.res])
                S.op("act", lambda e: e.sqrt(out=s_[:, 16:32], in_=s_[:, 16:32]), reads=[s_.res], writes=[s_.res])
                S.op("dve", lambda e: e.reciprocal(out=s_[:, 16:32], in_=s_[:, 16:32]), reads=[s_.res], writes=[s_.res])
                S.op("dve", lambda e: e.tensor_tensor(out=v3(y), in0=v3(y), in1=s_[:, 16:32].unsqueeze(2).broadcast_to([128, 16, 64]), op=ALU.mult),
                     reads=[y.res, s_.res], writes=[y.res])
                S.op("pool", lambda e: e.tensor_tensor(out=y[:], in0=y[:], in1=rows[:, 0, :], op=ALU.mult), reads=[y.res, rows.res], writes=[y.res])
                S.op("pool", lambda e: e.tensor_tensor(out=y[:], in0=y[:], in1=rows[:, 1, :], op=ALU.add), reads=[y.res, rows.res], writes=[y.res])
                S.op("dve", lambda e: e.tensor_tensor(out=y[:], in0=y[:], in1=b_[:], op=ALU.add), reads=[y.res, b_.res], writes=[y.res])
                S.op("dve", lambda e: e.tensor_tensor(out=y[:], in0=y[:], in1=g_[:], op=ALU.mult), reads=[y.res, g_.res], writes=[y.res])
                for hf in range(2):
                    pt2 = next_ps()
                    S.mm([lambda pe, q=q: pe.transpose(pt2[:, q * 128:(q + 1) * 128], y[:, (hf * 4 + q) * 128:(hf * 4 + q + 1) * 128], ident[:])
                          for q in range(4)], reads=[y.res, ident.res], writes=[pt2.res])
                    S.op("act", lambda e: e.copy(out=otb[i][:, hf * 4:(hf + 1) * 4, :], in_=pt2[:, :].rearrange("p (a b) -> p a b", b=128)),
                         reads=[pt2.res, otb[i].res], writes=[otb[i].res])
                S.dma("sp", g.oT[1][0][:, :, t0:t0 + 128].rearrange("k p t -> p k t"), otb[i][:], reads=[otb[i].res], writes=[g.oT[1][1]], acc=True)

    def merge_out(l, g):
        with Phase(C, "mrg") as ph:
            cols = load_cols(ph, l, g)
            mT = ph.sb([128, 16, g.ntok], BF16, "mT")
            if not mixers or cfg.get("nomerge"):
                S.op("pool", lambda e: e.memset(mT[:], 0.0), writes=[mT.res])
            else:
                with Phase(C, "mrg1") as ph1:
                    branches = [b for b in range(3) if "arc"[b] in mixers]
                    wnames = ["w_pa", "w_pr", "w_pc"]
                    HT = 1024
                    oTt = {b: ph1.sb([128, 8, HT], BF16, "oT%d" % b) for b in branches}
                    wbs = {b: [ph1.sb([128, 8, 512], BF16, "wp%d" % b) for _ in range(2)] for b in branches}
                    gts = {b: [ph1.sb([128, 512], F32, "g%d" % b) for _ in range(2)] for b in branches}
                    tmp = [ph1.sb([128, 512], F32, "mt") for _ in range(2)]
                    tmp2 = [ph1.sb([128, 512], F32, "mt2") for _ in range(2)]
                    cnt = 0
                    for half in range(g.ntok // HT):
                        for b in branches:
                            S.dma("sp", oTt[b][:], g.oT[b][0].rearrange("k p t -> p k t")[:, :, half * HT:(half + 1) * HT],
                                  reads=[g.oT[b][1]], writes=[oTt[b].res])

                        def loadw(s):
                            for b in branches:
                                wbuf = wbs[b][s % 2]
                                S.dma("pool", wbuf[:], W[wnames[b]][l].rearrange("(k p) n -> p k n", p=128)[:, :, s * 512:(s + 1) * 512],
                                      writes=[wbuf.res])
                        loadw(0)
                        for s in range(4):
                            if s + 1 < 4:
                                loadw(s + 1)
                            for nb in range(4):
                                nbg = s * 4 + nb
                                for tt in range(HT // 512):
                                    t0 = half * HT + tt * 512
                                    pts = {}
                                    for b in branches:
                                        pt = next_ps()
                                        pts[b] = pt
                                        wbuf = wbs[b][s % 2]
                                        fns = [lambda pe, kc=kc, pt=pt, wbuf=wbuf, b=b: pe.matmul(
                                            pt[:, :], lhsT=wbuf[:, kc, nb * 128:(nb + 1) * 128],
                                            rhs=oTt[b][:, kc, tt * 512:(tt + 1) * 512], start=(kc == 0), stop=(kc == 7))
                                            for kc in range(8)]
                                        S.mm(fns, reads=[wbuf.res, oTt[b].res], writes=[pt.res])
                                        gt = gts[b][cnt % 2]
                                        S.dma("sp", gt[:], g.gT[b * 16 + nbg, :, t0:t0 + 512], reads=[g.gT_r], writes=[gt.res])
                                    t1, t2 = tmp[cnt % 2], tmp2[cnt % 2]
                                    acc = None
                                    for bi, b in enumerate(branches):
                                        gt = gts[b][cnt % 2]
                                        last = (bi == len(branches) - 1)
                                        if acc is None:
                                            dst = mT[:, nbg, t0:t0 + 512] if last else t1[:]
                                            S.op("dve", lambda e, dst=dst, pt=pts[b], gt=gt: e.tensor_tensor(out=dst, in0=pt[:, :], in1=gt[:], op=ALU.mult),
                                                 reads=[pts[b].res, gt.res], writes=[mT.res if last else t1.res])
                                            acc = t1
                                        else:
                                            S.op("dve", lambda e, pt=pts[b], gt=gt: e.tensor_tensor(out=t2[:], in0=pt[:, :], in1=gt[:], op=ALU.mult),
                                                 reads=[pts[b].res, gt.res], writes=[t2.res])
                                            dst = mT[:, nbg, t0:t0 + 512] if last else t1[:]
                                            S.op("dve", lambda e, dst=dst: e.tensor_tensor(out=dst, in0=t1[:], in1=t2[:], op=ALU.add),
                                                 reads=[t1.res, t2.res], writes=[mT.res if last else t1.res])
                                    cnt += 1
            if "merged" in dbg:
                S.dma("sp", DBG["merged" + g.tag].rearrange("k p t -> p k t"), mT[:], reads=[mT.res], writes=[ORES["dbg_merged" + g.tag]])
            wb = [ph.sb([128, 16, 512], BF16, "wb") for _ in range(2)]
            resid_gemm(ph, l, g, W["w_out"][l], 16, 512, mT, g.ntok, 0, cols.ga1, None, cols.res, wb)

    def resid_gemm(ph, l, g, Wd, KC, slabw, xin, ntok, tok0_x, ga_fn, gb_fn, cres, wb, tok_base=0):
        xr = [ph.sb([128, 512], F32, "xr") for _ in range(3)]
        tb = [ph.sb([128, 512], F32, "tb") for _ in range(3)]
        st = {"i": 0}

        def epi(c0, t0, pt, nr, ncv):
            i = st["i"] % 3
            st["i"] += 1
            nb = c0 // 128
            tg = tok_base + (t0 - tok0_x)
            S.dma("sp", xr[i][:], g.xT[nb, :, tg:tg + 512], reads=[g.xT_r], writes=[xr[i].res])
            if gb_fn is None:
                S.op("dve", lambda e: e.scalar_tensor_tensor(out=xr[i][:], in0=pt[:, :], scalar=ga_fn(nb), in1=xr[i][:],
                                                             op0=ALU.mult, op1=ALU.add), reads=[pt.res, xr[i].res] + cres, writes=[xr[i].res])
            else:
                S.op("act", lambda e: e.activation(out=tb[i][:], in_=pt[:, :], func=AF.Identity, scale=ga_fn(nb), bias=gb_fn(nb)),
                     reads=[pt.res] + cres, writes=[tb[i].res])
                S.op("dve", lambda e: e.tensor_tensor(out=xr[i][:], in0=xr[i][:], in1=tb[i][:], op=ALU.add),
                     reads=[xr[i].res, tb[i].res], writes=[xr[i].res])
            S.dma("sp", g.xT[nb, :, tg:tg + 512], xr[i][:], reads=[xr[i].res], writes=[g.xT_r], acc=True)
        gemm(ph, Wd, KC, 0, D, slabw, xin, ntok, "fm", epi, wb, tok0=tok0_x)

    def mlp(l, g):
        with Phase(C, "ff1") as ph:
            cols = load_cols(ph, l, g)
            hT = ph.sb([128, 16, g.ntok], BF16, "h2T")
            with Phase(C, "nrm2") as ph2:
                norm_mod(ph2, g, cols.a2, cols.sh2, cols.res, hT)
            wb = [ph.sb([128, 16, 512], BF16, "wb") for _ in range(2)]
            rt = [ph.sb([128, 512], F32, "rt") for _ in range(3)]
            ob = [ph.sb([128, 512], BF16, "ob") for _ in range(3)]
            st = {"i": 0}

            def epi(c0, t0, pt, nr, ncv):
                i = st["i"] % 3
                st["i"] += 1
                nb = c0 // 128
                S.op("act", lambda e: e.activation(out=rt[i][:], in_=pt[:, :], func=AF.Relu, bias=cols.b1(nb), scale=1.0),
                     reads=[pt.res] + cols.res, writes=[rt[i].res])
                S.op("pool", lambda e: e.tensor_tensor(out=ob[i][:], in0=rt[i][:], in1=rt[i][:], op=ALU.mult),
                     reads=[rt[i].res], writes=[ob[i].res])
                S.dma("sp", g.f1T[nb, :, t0:t0 + 512], ob[i][:], reads=[ob[i].res], writes=[g.f1T_r], acc=True)
            gemm(ph, W["w_ff1"][l], 16, 0, D_FF, 512, hT, g.ntok, "fm", epi, wb)
        with Phase(C, "ff2") as ph:
            cols = load_cols(ph, l, g)
            f1 = [ph.sb([128, 64, 512], BF16, "f1") for _ in range(1)]
            wb = [ph.sb([128, 64, 128], BF16, "wb2") for _ in range(2)]
            fv = g.f1T.rearrange("k p t -> p k t")
            for tt in range(g.ntok // 512):
                ft = f1[tt % len(f1)]
                for q in range(4):
                    S.dma("sp", ft[:, q * 16:(q + 1) * 16, :], fv[:, q * 16:(q + 1) * 16, tt * 512:(tt + 1) * 512],
                          reads=[g.f1T_r], writes=[ft.res], acc=(q > 0))
                resid_gemm(ph, l, g, W["w_ff2"][l], 64, 128, ft, 512, 0, cols.ga2, cols.gb2, cols.res, wb, tok_base=tt * 512)

    def final_norm(g, dst, dst_res):
        with Phase(C, "fin") as ph:
            fg = ph.sb([128, 16], F32, "fg")
            S.dma("sp", fg[:], W["final_g"].rearrange("(j p) -> p j", p=128), writes=[fg.res], slow=True)
            TW = 256
            xt = [ph.sb([128, 16, TW], F32, "nx") for _ in range(2)]
            tm = [ph.sb([128, 16, TW], F32, "nt") for _ in range(2)]
            rs = [ph.sb([128, TW], F32, "nr") for _ in range(2)]
            yo = [ph.sb([128, D], F32, "yo") for _ in range(2)]
            xv = g.xT.rearrange("k p t -> p k t")
            for tt in range(g.ntok // TW):
                x_, t_, r_ = xt[tt % 2], tm[tt % 2], rs[tt % 2]
                S.dma("sp", x_[:], xv[:, :, tt * TW:(tt + 1) * TW], reads=[g.xT_r], writes=[x_.res])
                S.op("act", lambda e: e.activation(out=t_[:], in_=x_[:], func=AF.Square), reads=[x_.res], writes=[t_.res])
                pt = next_ps()
                fns = [lambda pe, kc=kc: pe.matmul(pt[:, 0:TW], lhsT=ones[:], rhs=t_[:, kc, :], start=(kc == 0), stop=(kc == 15))
                       for kc in range(16)]
                S.mm(fns, reads=[ones.res, t_.res], writes=[pt.res])
                S.op("dve", lambda e: e.tensor_scalar(out=r_[:], in0=pt[:, 0:TW], scalar1=1.0 / D, scalar2=1e-6, op0=ALU.mult,
                                                      op1=ALU.add), reads=[pt.res], writes=[r_.res])
                S.op("act", lambda e: e.sqrt(out=r_[:], in_=r_[:]), reads=[r_.res], writes=[r_.res])
                S.op("dve", lambda e: e.reciprocal(out=r_[:], in_=r_[:]), reads=[r_.res], writes=[r_.res])
                S.op("dve", lambda e: e.tensor_tensor(out=t_[:], in0=x_[:], in1=r_[:].unsqueeze(1).broadcast_to([128, 16, TW]),
                                                      op=ALU.mult), reads=[x_.res, r_.res], writes=[t_.res])
                for kc in range(16):
                    S.op("act", lambda e, kc=kc: e.activation(out=t_[:, kc, :], in_=t_[:, kc, :], func=AF.Identity,
                                                              scale=fg[:, kc:kc + 1]), reads=[t_.res, fg.res], writes=[t_.res])
                for sub in range(TW // 128):
                    y_ = yo[sub % 2]
                    for kq in range(4):
                        pt2 = next_ps()
                        fns = [lambda pe, j=j, pt2=pt2, kq=kq: pe.transpose(pt2[:, j * 128:(j + 1) * 128],
                                                                             t_[:, kq * 4 + j, sub * 128:(sub + 1) * 128], ident[:])
                               for j in range(4)]
                        S.mm(fns, reads=[t_.res, ident.res], writes=[pt2.res])
                        eng = alt_eng()
                        if eng == "act":
                            S.op("act", lambda e, pt2=pt2, kq=kq: e.copy(out=y_[:, kq * 512:(kq + 1) * 512], in_=pt2[:, :]),
                                 reads=[pt2.res], writes=[y_.res])
                        else:
                            S.op("dve", lambda e, pt2=pt2, kq=kq: e.tensor_copy(out=y_[:, kq * 512:(kq + 1) * 512], in_=pt2[:, :]),
                                 reads=[pt2.res], writes=[y_.res])
                    r0 = tt * TW + sub * 128
                    S.dma("sp", dst[r0:r0 + 128, :], y_[:], reads=[y_.res], writes=[dst_res], acc=True)

    for name in dbg:
        for g in G:
            if name in ("h", "merged"):
                dbg_out(name + g.tag, [16, 128, g.ntok], BF16)
            if name == "x":
                dbg_out(name + g.tag, [16, 128, g.ntok], F32)
            if name == "oT":
                for b in range(3):
                    dbg_out("oT%d%s" % (b, g.tag), [8, 128, g.ntok], BF16)
    load_x_T(G[0], I["xp"])
    load_x_T(G[1], I["xs"])
    for l in range(nlayers):
        modulation(l)
        for g in G:
            if g.i not in cfg.get("groups", (0, 1)):
                continue
            in_proj(l, g, None)
            if "a" in mixers and not cfg.get("noattn"):
                (attn_prompt if g.i == 0 else attn_sample)(l, g)
            if "r" in mixers:
                L_ = 256 if g.i == 0 else 2048
                rwkv_pre(l, g, L_)
                rwkv_scan(l, g, L_)
                rwkv_post(l, g)
            if "c" in mixers:
                L_ = 256 if g.i == 0 else 2048
                hyena_filter(l, L_)
                hyena_conv(l, g, L_)
                hyena_dft(l, g, L_)
            if "oT" in dbg:
                for b in range(3):
                    if "arc"[b] in mixers:
                        with Phase(C, "dbgo") as phd:
                            S.dma("sp", DBG["oT%d%s" % (b, g.tag)], g.oT[b][0], reads=[g.oT[b][1]], writes=[ORES["dbg_oT%d%s" % (b, g.tag)]])
            merge_out(l, g)
            mlp(l, g)
    if "x" in dbg:
        for g in G:
            with Phase(C, "dbgx") as ph:
                S.dma("sp", DBG["x" + g.tag], g.xT, reads=[g.xT_r], writes=[ORES["dbg_x" + g.tag]])
    final_norm(G[0], O["yp"], ORES["yp"])
    final_norm(G[1], O["ys"], ORES["ys"])
    S.barrier()
    cst.es.__exit__(None, None, None)
    top.close()
    C.ninst = S.ninst
    return nc, C


def make_in_maps(inputs):
    f = lambda a: np.ascontiguousarray(np.asarray(a, dtype=np.float32))
    maps = []
    wnames = ["ln1_g", "ln2_g", "w_mod", "b_mod", "w_in", "rpb", "wkv_conv_w", "wkv_conv_b", "wkv_w0", "wkv_w1", "wkv_w2",
              "wkv_a0", "wkv_a1", "wkv_a2", "wkv_g1", "wkv_g2", "wkv_k_k", "wkv_k_a", "wkv_r_k", "wkv_gn_g", "wkv_gn_b",
              "hy_conv_w", "hy_conv_b", "hy_f1", "hy_fb1", "hy_f2", "hy_fb2", "hy_freq", "hy_f3", "hy_d", "w_pa", "w_pr",
              "w_pc", "w_out", "w_ff1", "b_ff1", "w_ff2", "b_ff2", "final_g"]
    wd = {k: f(inputs[k]) for k in wnames}
    wd["wkv_r_k"] = wd["wkv_r_k"].reshape(DEPTH, 1024)
    for i in range(8):
        b = i // 2
        m = dict(wd)
        m["xp"] = f(inputs["x_prompt"][4 * i:4 * i + 4]).reshape(NP_TOK, D)
        m["xs"] = f(inputs["x_sample"][b])
        m["ck"] = f(inputs["cache_k"][b]).reshape(DEPTH, 256, 1024)
        m["cv"] = f(inputs["cache_v"][b]).reshape(DEPTH, 256, 1024)
        m["s0"] = f(inputs["state_wkv"][b]).reshape(DEPTH, 128, 1024)
        m["cvec"] = np.stack([f(inputs["c_ctx"]), f(inputs["c"][b])])
        m.update(CONSTS)
        maps.append(m)
    return maps


def _make_consts():
    cst = {}
    cq = np.arange(64)
    c0 = np.clip(cq - 8, 0, 48)
    ck = np.arange(64)
    ok = (ck[None, :] >= c0[:, None]) & (ck[None, :] < c0[:, None] + 16)
    cst["natmask"] = np.where(ok, 0.0, -1e30).astype(np.float32)
    for L in (256, 2048):
        t = np.linspace(0.0, 1.0, L, dtype=np.float32)[:, None]
        w = 2.0 * np.pi * np.arange(L, dtype=np.float32)[:, None] / L
        f = np.linspace(1e-4, 15, 16, dtype=np.float32)[None, :]
        z = np.concatenate([t, np.cos(f * w), -np.sin(f * w)], -1).astype(np.float32)
        cst["zposT%d" % L] = np.ascontiguousarray(z.T)
        dist = (np.abs(np.arange(L) - L // 2).astype(np.float32) / L)[:, None]
        deltas = np.abs(np.linspace(math.log(1e-2) / 1.5, math.log(1e-2) / 0.3, 1024, dtype=np.float32))[None, :]
        cst["win%d" % L] = np.exp(-dist * deltas).astype(np.float32)
        n = 2 * L
        k = np.arange(L, dtype=np.float64)
        om = 2.0 * np.pi * (k + 0.5) / n
        tt = np.arange(L, dtype=np.float64)
        ang = tt[:, None] * om[None, :]
        Ff = np.zeros((L, 2 * L), np.float32)
        Ffv = Ff.reshape(L, L // 128, 2, 128)
        Ffv[:, :, 0, :] = np.cos(ang).reshape(L, L // 128, 128)
        Ffv[:, :, 1, :] = (-np.sin(ang)).reshape(L, L // 128, 128)
        cst["Ff%d" % L] = Ff
        angi = om[:, None] * (tt[None, :] + L // 2)
        Fi = np.zeros((2 * L, L), np.float32)
        Fiv = Fi.reshape(L // 128, 2, 128, L)
        Fiv[:, 0] = ((2.0 / n) * np.cos(angi)).reshape(L // 128, 128, L)
        Fiv[:, 1] = (-(2.0 / n) * np.sin(angi)).reshape(L // 128, 128, L)
        cst["Fi%d" % L] = Fi
    return cst


CONSTS = _make_consts()
_CACHE = {}


def kernel(**inputs):
    if "nc" not in _CACHE:
        _CACHE["nc"] = build({})[0]
    nc = _CACHE["nc"]
    maps = make_in_maps(inputs)
    res = run_bass_kernel_spmd(nc, maps, core_ids=list(range(8)))
    R = res.results
    yp = np.concatenate([R[i]["yp"].reshape(4, 256, D) for i in range(8)], 0)
    ys = np.stack([R[2 * b]["ys"] for b in range(4)], 0)
    nk = np.concatenate([R[i]["nk"].reshape(4, DEPTH, 256, 16, 64) for i in range(8)], 0)
    nv = np.concatenate([R[i]["nv"].reshape(4, DEPTH, 256, 16, 64) for i in range(8)], 0)
    ns = np.concatenate([R[i]["ns"].reshape(4, DEPTH, 2, 16, 64, 64) for i in range(8)], 0)
    return (yp.astype(np.float32), ys.astype(np.float32), nk.astype(np.float32), nv.astype(np.float32), ns.astype(np.float32))
```

```python
import math
from contextlib import ExitStack

import numpy as np
import concourse.bass as bass
import concourse.mybir as mybir
from concourse.bass_utils import run_bass_kernel_spmd

F32 = mybir.dt.float32
BF16 = mybir.dt.bfloat16
I32 = mybir.dt.int32
AF = mybir.ActivationFunctionType
ALU = mybir.AluOpType
AX = mybir.AxisListType
AP = bass.AP

D = 2048
DEPTH = 4
NP_TOK = 1024
NS_TOK = 2048
N_IN = 15360
D_FF = 8192


class Res:
    __slots__ = ("name", "w", "a", "r")

    def __init__(self, name):
        self.name = name
        self.w = {}
        self.a = {}
        self.r = {}


class Tile:
    def __init__(self, t, name):
        self.t = t
        self.res = Res(name)

    def __getitem__(self, k):
        return self.t[k]


class Sched:
    NDS = 40
    NPOOL = 8

    def __init__(self, nc, es):
        self.nc = nc
        self.E = {"pe": nc.tensor, "act": nc.scalar, "dve": nc.vector, "pool": nc.gpsimd, "sp": nc.sync}
        self.sem = {k: es.enter_context(nc.semaphore("c_" + k)) for k in self.E}
        self.cnt = {k: 0 for k in self.E}
        self.seen = {k: {} for k in self.E}
        self.dsem = [es.enter_context(nc.semaphore("d%d" % i)) for i in range(self.NDS)]
        self.dcnt = [0] * self.NDS
        self.dnext = 0
        self.dnext_pool = 0
        self.ninst = 0
        self.nwait = 0

    def _semobj(self, key):
        return self.sem[key] if isinstance(key, str) else self.dsem[key]

    def _wait(self, eng, deps, defer=False):
        need = {}
        for (k, v) in deps:
            if need.get(k, 0) < v:
                need[k] = v
        sn = self.seen[eng]
        todo = [(k, v) for k, v in need.items() if sn.get(k, 0) < v]
        last = None
        if defer and todo:
            last = todo.pop()
        for k, v in todo:
            self.E[eng].wait_ge(self._semobj(k), v)
            sn[k] = v
            self.ninst += 1
            self.nwait += 1
        if last is not None:
            sn[last[0]] = last[1]
        return last

    def _attach(self, ins, last):
        if last is not None:
            ins._wait_ge(self._semobj(last[0]), last[1])

    def _deps(self, eng, reads, writes, is_dma=False, acc=False):
        deps = []
        for r in reads:
            deps.extend(r.w.items())
            deps.extend(r.a.items())
        for w in writes:
            srcs = [w.w, w.r] if acc else [w.w, w.a, w.r]
            for d in srcs:
                deps.extend(d.items())
        return deps

    def _commit(self, tok, reads, writes, acc=False):
        k, v = tok
        for r in reads:
            r.r[k] = v
        for w in writes:
            if acc:
                w.a[k] = v
            else:
                w.w = {k: v}
                w.a = {}
                w.r = {}

    def op(self, eng, fn, reads=(), writes=()):
        last = self._wait(eng, self._deps(eng, reads, writes), defer=True)
        ins = fn(self.E[eng])
        self._attach(ins, last)
        self.cnt[eng] += 1
        ins.then_inc(self.sem[eng], 1)
        self.ninst += 1
        self._commit((eng, self.cnt[eng]), reads, writes)

    def mm(self, fns, reads, writes):
        last = self._wait("pe", self._deps("pe", reads, writes), defer=True)
        pe = self.E["pe"]
        ins = None
        for j, f in enumerate(fns):
            ins = f(pe)
            if j == 0:
                self._attach(ins, last)
        self.ninst += len(fns)
        self.cnt["pe"] += 1
        ins.then_inc(self.sem["pe"], 1)
        self._commit(("pe", self.cnt["pe"]), reads, writes)

    def dma(self, q, out, in_, reads=(), writes=(), acc=False, slow=False):
        if q == "pool":
            i = self.NDS - self.NPOOL + self.dnext_pool
            self.dnext_pool = (self.dnext_pool + 1) % self.NPOOL
        else:
            i = self.dnext
            self.dnext = (i + 1) % (self.NDS - self.NPOOL)
        deps = self._deps(q, reads, writes, is_dma=True, acc=acc)
        if self.dcnt[i] > 0:
            deps.append((i, self.dcnt[i]))
        last = self._wait(q, deps, defer=True)
        if slow:
            ins = self.E[q].dma_start(out=out, in_=in_, allow_slow_non_contiguous=True)
        else:
            ins = self.E[q].dma_start(out=out, in_=in_)
        self._attach(ins, last)
        ins.then_inc(self.dsem[i], 16)
        self.ninst += 1
        self.dcnt[i] += 16
        self._commit((i, self.dcnt[i]), reads, writes, acc=acc)

    def barrier(self):
        deps = [(k, self.cnt[k]) for k in self.E if k != "sp" and self.cnt[k] > 0]
        deps += [(i, c) for i, c in enumerate(self.dcnt) if c > 0]
        self._wait("sp", deps)
        ins = self.E["sp"].nop()
        self.cnt["sp"] += 1
        ins.then_inc(self.sem["sp"], 1)
        for k in self.E:
            if k != "sp":
                self._wait(k, [("sp", self.cnt["sp"])])
        for k in self.E:
            for k2 in self.E:
                self.seen[k][k2] = self.cnt[k2]
            for i, c in enumerate(self.dcnt):
                self.seen[k][i] = c


class Ctx:
    pass


def _col_ap(dram_ap_1d, n):
    return dram_ap_1d.rearrange("(j p) -> p j", p=128)


class Phase:
    def __init__(self, C, name):
        self.C = C
        self.name = name
        self.es = ExitStack()
        self.n = 0

    def __enter__(self):
        self.es.__enter__()
        return self

    def sb(self, shape, dt=F32, name=None):
        self.n += 1
        nm = "%s_%s_%d" % (self.name, name or "t", self.C.uid())
        t = self.es.enter_context(self.C.nc.sbuf_tensor(nm, list(shape), dt))
        return Tile(t, nm)

    def __exit__(self, *a):
        self.C.S.barrier()
        return self.es.__exit__(*a)


def build(cfg):
    nlayers = cfg.get("nlayers", DEPTH)
    dbg = cfg.get("dbg", ())
    mixers = cfg.get("mixers", ("a", "r", "c"))
    nc = bass.Bass("TRN2", target_bir_lowering=False)
    C = Ctx()
    C.nc = nc
    C._uid = 0

    def uid():
        C._uid += 1
        return C._uid
    C.uid = uid
    top = ExitStack()
    S = Sched(nc, top)
    C.S = S

    def din(name, shape, dt=F32):
        return nc.dram_tensor(name, list(shape), dt, kind="ExternalInput").ap()

    def dout(name, shape, dt=F32):
        return nc.dram_tensor(name, list(shape), dt, kind="ExternalOutput").ap()

    def dscr(name, shape, dt=F32):
        a = nc.dram_tensor(name, list(shape), dt, kind="Internal").ap()
        return a, Res(name)

    I = {}
    I["xp"] = din("xp", [NP_TOK, D])
    I["xs"] = din("xs", [NS_TOK, D])
    I["ck"] = din("ck", [DEPTH, 256, 1024])
    I["cv"] = din("cv", [DEPTH, 256, 1024])
    I["s0"] = din("s0", [DEPTH, 128, 1024])
    I["cvec"] = din("cvec", [2, D])
    wshapes = {
        "ln1_g": [DEPTH, D], "ln2_g": [DEPTH, D], "w_mod": [DEPTH, D, 6 * D], "b_mod": [DEPTH, 6 * D],
        "w_in": [DEPTH, D, N_IN], "rpb": [DEPTH, 16, 15, 31],
        "wkv_conv_w": [DEPTH, 3, 3072], "wkv_conv_b": [DEPTH, 3072], "wkv_w0": [DEPTH, 2, 1024],
        "wkv_w1": [DEPTH, 2, D, 64], "wkv_w2": [DEPTH, 2, 64, 1024], "wkv_a0": [DEPTH, 2, 1024],
        "wkv_a1": [DEPTH, 2, D, 64], "wkv_a2": [DEPTH, 2, 64, 1024], "wkv_g1": [DEPTH, D, 128],
        "wkv_g2": [DEPTH, 128, 1024], "wkv_k_k": [DEPTH, 1024], "wkv_k_a": [DEPTH, 1024],
        "wkv_r_k": [DEPTH, 1024], "wkv_gn_g": [DEPTH, 1024], "wkv_gn_b": [DEPTH, 1024],
        "hy_conv_w": [DEPTH, 3, 3072], "hy_conv_b": [DEPTH, 3072], "hy_f1": [DEPTH, 33, 64],
        "hy_fb1": [DEPTH, 64], "hy_f2": [DEPTH, 64, 64], "hy_fb2": [DEPTH, 64], "hy_freq": [DEPTH, 64],
        "hy_f3": [DEPTH, 64, 1024], "hy_d": [DEPTH, 1024],
        "w_pa": [DEPTH, 1024, D], "w_pr": [DEPTH, 1024, D], "w_pc": [DEPTH, 1024, D], "w_out": [DEPTH, D, D],
        "w_ff1": [DEPTH, D, D_FF], "b_ff1": [DEPTH, D_FF], "w_ff2": [DEPTH, D_FF, D], "b_ff2": [DEPTH, D],
        "final_g": [D],
    }
    W = {k: din(k, s) for k, s in wshapes.items()}
    O = {}
    O["yp"] = dout("yp", [NP_TOK, D])
    O["ys"] = dout("ys", [NS_TOK, D])
    O["nk"] = dout("nk", [4 * DEPTH * 256, 1024])
    O["nv"] = dout("nv", [4 * DEPTH * 256, 1024])
    O["ns"] = dout("ns", [4 * DEPTH * 32, 4096])
    ORES = {k: Res("o_" + k) for k in O}
    DBG = {}

    def dbg_out(name, shape, dt=F32):
        DBG[name] = dout("dbg_" + name, shape, dt)
        ORES["dbg_" + name] = Res("dbg_" + name)
        return DBG[name], ORES["dbg_" + name]

    G = []
    for gi, ntok in enumerate((NP_TOK, NS_TOK)):
        g = Ctx()
        g.i = gi
        g.ntok = ntok
        g.tag = "ps"[gi]
        g.xT, g.xT_r = dscr("xT%d" % gi, [16, 128, ntok])
        g.qkT, g.qkT_r = dscr("qkT%d" % gi, [16, 128, ntok], BF16)
        g.vtm, g.vtm_r = dscr("vtm%d" % gi, [ntok, 1024], BF16)
        g.rh, g.rh_r = dscr("rh%d" % gi, [ntok, 6144])
        g.gT, g.gT_r = dscr("gT%d" % gi, [48, 128, ntok])
        g.oT = []
        for b in range(3):
            g.oT.append(dscr("oT%d_%d" % (gi, b), [8, 128, ntok], BF16))
        g.f1T, g.f1T_r = dscr("f1T%d" % gi, [64, 128, ntok], BF16)
        G.append(g)
    mrow, mrow_r = dscr("mrow", [2, 6 * D])
    zero_r = Res("zeros")

    cst = Phase(C, "cst")
    cst.es.__enter__()
    ident = cst.sb([128, 128], F32, "ident")
    identb = cst.sb([128, 128], BF16, "identb")
    ones = cst.sb([128, 128], F32, "ones")
    S.op("pool", lambda e: e.memset(ident[:], 1.0), writes=[ident.res])
    S.op("pool", lambda e: e.affine_select(out=ident[:], in_=ident[:], pattern=[[-1, 128]], compare_op=ALU.is_equal,
                                            fill=0.0, base=0, channel_multiplier=1), reads=[ident.res], writes=[ident.res])
    S.op("dve", lambda e: e.tensor_copy(out=identb[:], in_=ident[:]), reads=[ident.res], writes=[identb.res])
    S.op("dve", lambda e: e.memset(ones[:], 1.0), writes=[ones.res])
    psum = []
    for i in range(8):
        t = top.enter_context(nc.psum_tensor("ps%d" % i, [128, 512], F32))
        psum.append(Tile(t, "ps%d" % i))
    C.ps_rr = 0

    def next_ps(lo=0, hi=8):
        C.ps_rr = (C.ps_rr + 1) % (hi - lo)
        return psum[lo + C.ps_rr]

    C.eng_rr = 0

    def alt_eng():
        C.eng_rr ^= 1
        return "act" if C.eng_rr else "dve"

    def gemm(ph, Wd, KC, col0, ncols, slabw, xT, ntok, form, epi, wbufs, Mrows=128, tok0=0, ps_lo=0, ps_hi=8):
        nslab = (ncols + slabw - 1) // slabw
        Wv = Wd.rearrange("(k p) n -> p k n", p=128)

        def load(s):
            wb = wbufs[s % len(wbufs)]
            c0 = col0 + s * slabw
            cw = min(slabw, col0 + ncols - c0)
            S.dma("pool", wb[:, :, 0:cw], Wv[:, :, c0:c0 + cw], writes=[wb.res])
        load(0)
        for s in range(nslab):
            if s + 1 < nslab:
                load(s + 1)
            wb = wbufs[s % len(wbufs)]
            c0 = col0 + s * slabw
            cw = min(slabw, col0 + ncols - c0)
            if form == "fm":
                for nb in range((cw + 127) // 128):
                    mw = min(128, cw - nb * 128)
                    for tt in range(ntok // 512):
                        pt = next_ps(ps_lo, ps_hi)
                        t0 = tok0 + tt * 512
                        fns = []
                        for kc in range(KC):
                            fns.append(lambda pe, kc=kc, pt=pt, wb=wb, nb=nb, mw=mw, t0=t0: pe.matmul(
                                pt[0:mw, :], lhsT=wb[:, kc, nb * 128:nb * 128 + mw], rhs=xT[:, kc, t0:t0 + 512],
                                start=(kc == 0), stop=(kc == KC - 1)))
                        S.mm(fns, reads=[wb.res, xT.res], writes=[pt.res])
                        epi(c0 + nb * 128, t0, pt, mw, 512)
            else:
                for tt in range(ntok // Mrows if Mrows == 128 else 1):
                    t0 = tok0 + tt * 128
                    for nh in range((cw + 511) // 512):
                        nw = min(512, cw - nh * 512)
                        pt = next_ps(ps_lo, ps_hi)
                        fns = []
                        for kc in range(KC):
                            fns.append(lambda pe, kc=kc, pt=pt, wb=wb, nh=nh, nw=nw, t0=t0: pe.matmul(
                                pt[0:Mrows, 0:nw], lhsT=xT[:, kc, t0:t0 + Mrows], rhs=wb[:, kc, nh * 512:nh * 512 + nw],
                                start=(kc == 0), stop=(kc == KC - 1)))
                        S.mm(fns, reads=[wb.res, xT.res], writes=[pt.res])
                        epi(c0 + nh * 512, t0, pt, Mrows, nw)

    def epi_store(ph, obufs, dst_fn, dst_res, func=None, bias_fn=None, scale=1.0):
        st = {"i": 0}

        def epi(c0, t0, pt, nr, ncv):
            ob = obufs[st["i"] % len(obufs)]
            st["i"] += 1
            if func is not None or bias_fn is not None:
                b = bias_fn(c0) if bias_fn is not None else None
                rd = [pt.res] + ([b[1]] if b is not None else [])
                S.op("act", lambda e: e.activation(out=ob[0:nr, 0:ncv], in_=pt[0:nr, 0:ncv], func=func or AF.Identity,
                                                   bias=(b[0] if b is not None else 0.0), scale=scale),
                     reads=rd, writes=[ob.res])
            else:
                eng = alt_eng()
                if eng == "act":
                    S.op("act", lambda e: e.copy(out=ob[0:nr, 0:ncv], in_=pt[0:nr, 0:ncv]), reads=[pt.res], writes=[ob.res])
                else:
                    S.op("dve", lambda e: e.tensor_copy(out=ob[0:nr, 0:ncv], in_=pt[0:nr, 0:ncv]), reads=[pt.res], writes=[ob.res])
            S.dma("sp", dst_fn(c0, t0, nr, ncv), ob[0:nr, 0:ncv], reads=[ob.res], writes=[dst_res], acc=True)
        return epi

    def load_x_T(g, src):
        with Phase(C, "ldx") as ph:
            xin = [ph.sb([128, D], F32, "xin") for _ in range(2)]
            xo = [ph.sb([128, 16, 128], F32, "xo") for _ in range(2)]
            for tt in range(g.ntok // 128):
                xi = xin[tt % 2]
                xq = xo[tt % 2]
                S.dma("sp", xi[:], src[tt * 128:(tt + 1) * 128, :], writes=[xi.res])
                for kq in range(4):
                    pt = next_ps()
                    fns = [lambda pe, j=j, pt=pt, xi=xi, kq=kq: pe.transpose(pt[:, j * 128:(j + 1) * 128],
                                                                                xi[:, (kq * 4 + j) * 128:(kq * 4 + j + 1) * 128], ident[:])
                           for j in range(4)]
                    S.mm(fns, reads=[xi.res, ident.res], writes=[pt.res])
                    eng = alt_eng()
                    dst = xq[:, kq * 4:(kq + 1) * 4, :]
                    srcp = pt[:, :].rearrange("p (a b) -> p a b", b=128)
                    if eng == "act":
                        S.op("act", lambda e: e.copy(out=dst, in_=srcp), reads=[pt.res], writes=[xq.res])
                    else:
                        S.op("dve", lambda e: e.tensor_copy(out=dst, in_=srcp), reads=[pt.res], writes=[xq.res])
                S.dma("sp", g.xT.rearrange("k p t -> p k t")[:, :, tt * 128:(tt + 1) * 128], xq[:], reads=[xq.res],
                      writes=[g.xT_r], acc=True)

    def modulation(l):
        with Phase(C, "mod") as ph:
            cT = ph.sb([128, 16, 2], F32, "cT")
            cTb = ph.sb([128, 16, 2], BF16, "cTb")
            for gi in range(2):
                S.dma("sp", cT[:, :, gi], I["cvec"][gi].rearrange("(k p) -> p k", p=128), writes=[cT.res], slow=True, acc=(gi > 0))
            S.op("act", lambda e: e.activation(out=cTb[:], in_=cT[:], func=AF.Silu), reads=[cT.res], writes=[cTb.res])
            wb = [ph.sb([128, 16, 512], BF16, "wb") for _ in range(2)]
            bm = [ph.sb([2, 512], F32, "bm") for _ in range(2)]
            ob = [ph.sb([2, 512], F32, "ob") for _ in range(2)]
            st = {"i": 0}

            def epi(c0, t0, pt, nr, ncv):
                i = st["i"] % 2
                st["i"] += 1
                S.dma("sp", bm[i][:], AP(W["b_mod"].tensor, l * 6 * D + c0, [[0, 2], [1, 512]]), writes=[bm[i].res])
                S.op("dve", lambda e: e.tensor_tensor(out=ob[i][:], in0=pt[0:2, :], in1=bm[i][:], op=ALU.add),
                     reads=[pt.res, bm[i].res], writes=[ob[i].res])
                S.dma("sp", mrow[:, c0:c0 + 512], ob[i][:], reads=[ob[i].res], writes=[mrow_r], acc=True)
            gemm(ph, W["w_mod"][l], 16, 0, 6 * D, 512, cTb, 2, "tm", epi, wb, Mrows=2)

    def load_cols(ph, l, g):
        cols = Ctx()
        m = ph.sb([128, 96], F32, "mcol")
        S.dma("sp", m[:], mrow[g.i].rearrange("(j p) -> p j", p=128), reads=[mrow_r], writes=[m.res], slow=True)
        ln = ph.sb([128, 32], F32, "lncol")
        S.dma("sp", ln[:, 0:16], W["ln1_g"][l].rearrange("(j p) -> p j", p=128), writes=[ln.res], slow=True)
        S.dma("sp", ln[:, 16:32], W["ln2_g"][l].rearrange("(j p) -> p j", p=128), writes=[ln.res], slow=True, acc=True)
        bf = ph.sb([128, 80], F32, "bfcol")
        S.dma("sp", bf[:, 0:64], W["b_ff1"][l].rearrange("(j p) -> p j", p=128), writes=[bf.res], slow=True)
        S.dma("sp", bf[:, 64:80], W["b_ff2"][l].rearrange("(j p) -> p j", p=128), writes=[bf.res], slow=True, acc=True)
        d = ph.sb([128, 48], F32, "dcol")
        S.op("dve", lambda e: e.scalar_tensor_tensor(out=d[:, 0:16], in0=m[:, 16:32], scalar=1.0, in1=ln[:, 0:16],
                                                     op0=ALU.add, op1=ALU.mult), reads=[m.res, ln.res], writes=[d.res])
        S.op("dve", lambda e: e.scalar_tensor_tensor(out=d[:, 16:32], in0=m[:, 64:80], scalar=1.0, in1=ln[:, 16:32],
                                                     op0=ALU.add, op1=ALU.mult), reads=[m.res, ln.res, d.res], writes=[d.res])
        S.op("dve", lambda e: e.tensor_tensor(out=d[:, 32:48], in0=m[:, 80:96], in1=bf[:, 64:80], op=ALU.mult),
             reads=[m.res, bf.res, d.res], writes=[d.res])
        cols.m, cols.d, cols.bf = m, d, bf
        cols.sh1 = lambda kc: m[:, kc:kc + 1]
        cols.ga1 = lambda kc: m[:, 32 + kc:33 + kc]
        cols.sh2 = lambda kc: m[:, 48 + kc:49 + kc]
        cols.ga2 = lambda kc: m[:, 80 + kc:81 + kc]
        cols.a1 = lambda kc: d[:, kc:kc + 1]
        cols.a2 = lambda kc: d[:, 16 + kc:17 + kc]
        cols.gb2 = lambda kc: d[:, 32 + kc:33 + kc]
        cols.b1 = lambda j: bf[:, j:j + 1]
        cols.res = [m.res, d.res, bf.res]
        return cols

    def norm_mod(ph, g, a_fn, sh_fn, cres, hT):
        TW = 256
        xt = [ph.sb([128, 16, TW], F32, "nx") for _ in range(2)]
        tm = [ph.sb([128, 16, TW], F32, "nt") for _ in range(2)]
        rs = [ph.sb([128, TW], F32, "nr") for _ in range(2)]
        xv = g.xT.rearrange("k p t -> p k t")
        for tt in range(g.ntok // TW):
            x_, t_, r_ = xt[tt % 2], tm[tt % 2], rs[tt % 2]
            S.dma("sp", x_[:], xv[:, :, tt * TW:(tt + 1) * TW], reads=[g.xT_r], writes=[x_.res])
            S.op("act", lambda e: e.activation(out=t_[:], in_=x_[:], func=AF.Square), reads=[x_.res], writes=[t_.res])
            pt = next_ps()
            fns = [lambda pe, kc=kc: pe.matmul(pt[:, 0:TW], lhsT=ones[:], rhs=t_[:, kc, :], start=(kc == 0), stop=(kc == 15))
                   for kc in range(16)]
            S.mm(fns, reads=[ones.res, t_.res], writes=[pt.res])
            S.op("dve", lambda e: e.tensor_scalar(out=r_[:], in0=pt[:, 0:TW], scalar1=1.0 / D, scalar2=1e-6, op0=ALU.mult,
                                                  op1=ALU.add), reads=[pt.res], writes=[r_.res])
            S.op("act", lambda e: e.sqrt(out=r_[:], in_=r_[:]), reads=[r_.res], writes=[r_.res])
            S.op("dve", lambda e: e.reciprocal(out=r_[:], in_=r_[:]), reads=[r_.res], writes=[r_.res])
            S.op("dve", lambda e: e.tensor_tensor(out=t_[:], in0=x_[:], in1=r_[:].unsqueeze(1).broadcast_to([128, 16, TW]),
                                                  op=ALU.mult), reads=[x_.res, r_.res], writes=[t_.res])
            for kc in range(16):
                S.op("act", lambda e, kc=kc: e.activation(out=hT[:, kc, tt * TW:(tt + 1) * TW], in_=t_[:, kc, :],
                                                          func=AF.Identity, scale=a_fn(kc), bias=sh_fn(kc)),
                     reads=[t_.res] + cres, writes=[hT.res])

    def in_proj(l, g, cols_holder):
        with Phase(C, "inp") as ph:
            cols = load_cols(ph, l, g)
            hT = ph.sb([128, 16, g.ntok], BF16, "hT")
            with Phase(C, "nrm") as ph2:
                norm_mod(ph2, g, cols.a1, cols.sh1, cols.res, hT)
            if "h" in dbg:
                S.dma("sp", DBG["h" + g.tag].rearrange("k p t -> p k t"), hT[:], reads=[hT.res], writes=[ORES["dbg_h" + g.tag]])
            wb = [ph.sb([128, 16, 512], BF16, "wb") for _ in range(2)]
            Wl = W["w_in"][l]
            ob = [ph.sb([128, 512], F32, "ob") for _ in range(4)]
            gemm(ph, Wl, 16, 9216, 6144, 512, hT, g.ntok, "fm",
                 epi_store(ph, ob, lambda c0, t0, nr, ncv: g.gT[(c0 - 9216) // 128, :, t0:t0 + ncv], g.gT_r, func=AF.Sigmoid), wb)
            if "a" in mixers:
                obb = [ph.sb([128, 512], BF16, "obb") for _ in range(4)]
                gemm(ph, Wl, 16, 0, 2048, 512, hT, g.ntok, "fm",
                     epi_store(ph, obb, lambda c0, t0, nr, ncv: g.qkT[c0 // 128, :, t0:t0 + ncv], g.qkT_r), wb)
            obf = [ph.sb([128, 512], F32, "obf") for _ in range(3)]
            obv = [ph.sb([128, 512], BF16, "obv") for _ in range(3)]
            st = {"i": 0}

            def epi_kv(c0, t0, pt, nr, ncv):
                i = st["i"] % 3
                st["i"] += 1
                isv = c0 >= 2048
                cc = c0 - (2048 if isv else 1024)
                if g.i == 0:
                    S.op("dve", lambda e: e.tensor_copy(out=obf[i][:], in_=pt[:, :]), reads=[pt.res], writes=[obf[i].res])
                    key = "nv" if isv else "nk"
                    r0_ = ((t0 // 256) * DEPTH + l) * 256 + (t0 % 256)
                    S.dma("sp", O[key][r0_:r0_ + 128, cc:cc + 512], obf[i][:], reads=[obf[i].res],
                          writes=[ORES[key]], acc=True)
                    if isv:
                        S.op("pool", lambda e: e.tensor_copy(out=obv[i][:], in_=obf[i][:]), reads=[obf[i].res], writes=[obv[i].res])
                elif isv:
                    S.op("dve", lambda e: e.tensor_copy(out=obv[i][:], in_=pt[:, :]), reads=[pt.res], writes=[obv[i].res])
                if isv:
                    S.dma("sp", g.vtm[t0:t0 + 128, cc:cc + 512], obv[i][:], reads=[obv[i].res], writes=[g.vtm_r], acc=True)
            if ("a" in mixers or g.i == 0) and not cfg.get("nokv"):
                if g.i == 0:
                    gemm(ph, Wl, 16, 1024, 2048, 512, hT, g.ntok, "tm", epi_kv, wb)
                else:
                    gemm(ph, Wl, 16, 2048, 1024, 512, hT, g.ntok, "tm", epi_kv, wb)
            if "r" in mixers or "c" in mixers:
                gemm(ph, Wl, 16, 3072, 6144, 512, hT, g.ntok, "tm",
                     epi_store(ph, ob, lambda c0, t0, nr, ncv: g.rh[t0:t0 + nr, c0 - 3072:c0 - 3072 + ncv], g.rh_r), wb)
            if "r" in mixers:
                rwkv_lora(ph, l, g, hT)
            return None

    SCALE = 0.125

    def softmax_rows(nr, ncol, sc_ap, sc_res, scale, small, pn, tag_reads=()):
        mx, nmx, rsum, rinv = small[0:nr, 0:1], small[0:nr, 1:2], small[0:nr, 2:3], small[0:nr, 3:4]
        S.op("dve", lambda e: e.tensor_reduce(out=mx, in_=sc_ap, axis=AX.X, op=ALU.max), reads=[sc_res], writes=[small.res])
        S.op("dve", lambda e: e.tensor_scalar(out=nmx, in0=mx, scalar1=-scale, scalar2=None, op0=ALU.mult),
             reads=[small.res], writes=[small.res])
        S.op("dve", lambda e: e.memset(rsum, 0.0), reads=[small.res], writes=[small.res])
        return mx, nmx, rsum, rinv

    def attn_prompt(l, g):
        with Phase(C, "attp") as ph:
            qk = ph.sb([128, 16, NP_TOK], BF16, "qk")
            V = ph.sb([128, 8, 1024], BF16, "V")
            oa = ph.sb([128, 8, NP_TOK], BF16, "oa")
            S.dma("sp", qk[:], g.qkT.rearrange("k p t -> p k t"), reads=[g.qkT_r], writes=[qk.res])
            S.dma("sp", V[:], g.vtm.rearrange("(j p) c -> p j c", p=128), reads=[g.vtm_r], writes=[V.res])
            pb = [ph.sb([128, 256], F32, "pb") for _ in range(2)]
            pn = [ph.sb([128, 256], BF16, "pn") for _ in range(2)]
            PT = [ph.sb([128, 2, 128], BF16, "PT") for _ in range(2)]
            sm = [ph.sb([128, 8], F32, "sm") for _ in range(2)]
            def stage_a(s_, h, qt, i):
                c, p0 = h // 2, (h % 2) * 64
                q0 = s_ * 256 + qt * 128
                ps = next_ps()
                S.mm([lambda pe: pe.matmul(ps[:, 0:256], lhsT=qk[p0:p0 + 64, c, q0:q0 + 128],
                                           rhs=qk[p0:p0 + 64, 8 + c, s_ * 256:(s_ + 1) * 256], start=True, stop=True)],
                     reads=[qk.res], writes=[ps.res])
                mx, nmx, rsum, rinv = softmax_rows(128, 256, ps[:, 0:256], ps.res, SCALE, sm[i], None)
                S.op("act", lambda e: e.activation(out=pb[i][:], in_=ps[:, 0:256], func=AF.Exp, bias=nmx, scale=SCALE,
                                                   accum_out=rsum), reads=[ps.res, sm[i].res], writes=[pb[i].res, sm[i].res])
                S.op("dve", lambda e: e.reciprocal(out=rinv, in_=rsum), reads=[sm[i].res], writes=[sm[i].res])
                S.op("dve", lambda e: e.tensor_scalar(out=pn[i][:], in0=pb[i][:], scalar1=rinv, scalar2=None, op0=ALU.mult),
                     reads=[pb[i].res, sm[i].res], writes=[pn[i].res])

            def stage_b(s_, h, qt, i):
                c, p0 = h // 2, (h % 2) * 64
                q0 = s_ * 256 + qt * 128
                pt2 = next_ps()
                ptb = pt2[:, :].bitcast(BF16)
                S.mm([lambda pe, kt=kt: pe.transpose(ptb[:, kt * 128:(kt + 1) * 128], pn[i][:, kt * 128:(kt + 1) * 128], identb[:])
                      for kt in range(2)], reads=[pn[i].res, identb.res], writes=[pt2.res])
                S.op("act", lambda e: e.copy(out=PT[i][:], in_=ptb[:, 0:256].rearrange("p (a b) -> p a b", b=128)),
                     reads=[pt2.res], writes=[PT[i].res])
                po = next_ps()
                S.mm([lambda pe, kt=kt: pe.matmul(po[:, 0:128], lhsT=V[:, s_ * 2 + kt, c * 128:(c + 1) * 128], rhs=PT[i][:, kt, :],
                                                  start=(kt == 0), stop=(kt == 1)) for kt in range(2)],
                     reads=[V.res, PT[i].res], writes=[po.res])
                S.op("dve", lambda e: e.tensor_copy(out=oa[p0:p0 + 64, c, q0:q0 + 128], in_=po[p0:p0 + 64, 0:128]),
                     reads=[po.res], writes=[oa.res])
            units = [(s_, h, qt) for s_ in range(4) for h in range(16) for qt in range(2)]
            for u in range(len(units) + 1):
                if u < len(units):
                    stage_a(*units[u], u % 2)
                if u >= 1:
                    stage_b(*units[u - 1], (u - 1) % 2)
            S.dma("sp", g.oT[0][0].rearrange("k p t -> p k t"), oa[:], reads=[oa.res], writes=[g.oT[0][1]])

    rpbp, rpbp_r = dscr("rpbp", [240, 157])
    rrep, rrep_r = dscr("rrep", [240, 64, 157])
    I["natmask"] = din("natmask", [64, 64])

    def rcls(r):
        return 7 - r if r <= 3 else (3 if r <= 28 else 31 - r)

    def attn_sample(l, g):
        with Phase(C, "atts") as ph:
            z = ph.sb([128, 157], F32, "z")
            S.op("pool", lambda e: e.memset(z[:], 0.0), writes=[z.res])
            S.dma("sp", rpbp[0:128, :], z[:], reads=[z.res], writes=[rpbp_r])
            S.dma("sp", rpbp[128:240, :], z[0:112, :], reads=[z.res], writes=[rpbp_r], acc=True)
            S.dma("sp", rpbp[:, 63:94], W["rpb"][l].rearrange("h r c -> (h r) c"), writes=[rpbp_r], slow=True)
            for q4 in range(4):
                S.dma("sp", rrep[q4 * 60:(q4 + 1) * 60], AP(rpbp.tensor, q4 * 60 * 157, [[157, 60], [0, 64], [1, 157]]),
                      reads=[rpbp_r], writes=[rrep_r], acc=(q4 > 0))
            mk = ph.sb([64, 64], F32, "mk")
            S.dma("sp", mk[:], I["natmask"], writes=[mk.res])
            qk = [ph.sb([128, 2, NS_TOK], BF16, "qk") for _ in range(2)]
            Ve = [ph.sb([128, 16, 128], BF16, "Ve") for _ in range(2)]
            Vo = [ph.sb([128, 15, 128], BF16, "Vo") for _ in range(2)]
            Vc = [ph.sb([128, 2, 128], BF16, "Vc") for _ in range(2)]
            ckt = [ph.sb([128, 2, 128], F32, "ckt") for _ in range(2)]
            kcT = [ph.sb([128, 256], BF16, "kcT") for _ in range(2)]
            ob = [ph.sb([128, NS_TOK], BF16, "ob") for _ in range(2)]
            bm = [ph.sb([64, 8, 512], F32, "bm") for _ in range(2)]
            sc = [ph.sb([64, 768], F32, "sc") for _ in range(2)]
            pb = [ph.sb([64, 768], F32, "pb") for _ in range(2)]
            pn = [ph.sb([64, 768], BF16, "pn") for _ in range(2)]
            PT = [ph.sb([128, 6, 64], BF16, "PT") for _ in range(2)]
            sm = [ph.sb([128, 8], F32, "sm") for _ in range(2)]
            qv = g.qkT.rearrange("k p t -> p k t")
            u = 0
            for c in range(8):
                b_ = c % 2
                S.dma("sp", qk[b_][:, 0, :], qv[:, c, :], reads=[g.qkT_r], writes=[qk[b_].res])
                S.dma("sp", qk[b_][:, 1, :], qv[:, 8 + c, :], reads=[g.qkT_r], writes=[qk[b_].res], acc=True)
                S.dma("sp", Ve[b_][:], g.vtm[:, c * 128:(c + 1) * 128].rearrange("(j p) c -> p j c", p=128), reads=[g.vtm_r],
                      writes=[Ve[b_].res])
                S.dma("sp", Vo[b_][:], g.vtm[64:64 + 15 * 128, c * 128:(c + 1) * 128].rearrange("(j p) c -> p j c", p=128),
                      reads=[g.vtm_r], writes=[Vo[b_].res])
                S.dma("pool", Vc[b_][:], I["cv"][l][:, c * 128:(c + 1) * 128].rearrange("(j p) c -> p j c", p=128), writes=[Vc[b_].res])
                S.dma("sp", ckt[b_][:], I["ck"][l][:, c * 128:(c + 1) * 128].rearrange("(j p) c -> p j c", p=128), writes=[ckt[b_].res])
                pk = next_ps()
                S.mm([lambda pe, t=t: pe.transpose(pk[:, t * 128:(t + 1) * 128], ckt[b_][:, t, :], ident[:]) for t in range(2)],
                     reads=[ckt[b_].res, ident.res], writes=[pk.res])
                S.op("act", lambda e: e.copy(out=kcT[b_][:], in_=pk[:, 0:256]), reads=[pk.res], writes=[kcT[b_].res])
                for hh in range(2):
                    h = 2 * c + hh
                    p0 = hh * 64
                    bmh = bm[hh]
                    for o in range(8):
                        S.dma("sp", bmh[:, o, :].rearrange("p (j k) -> p j k", k=64),
                              AP(rrep.tensor, ((h * 15 + o) * 64) * 157 + 78, [[156, 64], [64 * 157, 8], [1, 64]]),
                              reads=[rrep_r], writes=[bmh.res], acc=(o > 0))
                    S.op("pool", lambda e: e.tensor_tensor(out=bmh[:].rearrange("p o (j k) -> p (o j) k", k=64),
                                                           in0=bmh[:].rearrange("p o (j k) -> p (o j) k", k=64),
                                                           in1=mk[:].unsqueeze(1).broadcast_to([64, 64, 64]), op=ALU.add),
                         reads=[bmh.res, mk.res], writes=[bmh.res])
                    def stage_a(r, i):
                        r0 = min(max(r - 4, 0), 24)
                        o = rcls(r)
                        psA, psB = next_ps(), next_ps()
                        qa = qk[b_][p0:p0 + 64, 0, r * 64:(r + 1) * 64]
                        S.mm([lambda pe: pe.matmul(psA[0:64, 0:512], lhsT=qa, rhs=qk[b_][p0:p0 + 64, 1, r0 * 64:r0 * 64 + 512],
                                                   start=True, stop=True)], reads=[qk[b_].res], writes=[psA.res])
                        S.mm([lambda pe: pe.matmul(psB[0:64, 0:256], lhsT=qa, rhs=kcT[b_][p0:p0 + 64, :], start=True, stop=True)],
                             reads=[qk[b_].res, kcT[b_].res], writes=[psB.res])
                        S.op("dve", lambda e: e.scalar_tensor_tensor(out=sc[i][:, 0:512], in0=psA[0:64, 0:512], scalar=SCALE,
                                                                     in1=bmh[:, o, :], op0=ALU.mult, op1=ALU.add),
                             reads=[psA.res, bmh.res], writes=[sc[i].res])
                        S.op("act", lambda e: e.mul(out=sc[i][:, 512:768], in_=psB[0:64, 0:256], mul=SCALE), reads=[psB.res, sc[i].res],
                             writes=[sc[i].res])
                        mx, nmx, rsum, rinv = softmax_rows(64, 768, sc[i][:], sc[i].res, 1.0, sm[i], None)
                        S.op("act", lambda e: e.activation(out=pb[i][:], in_=sc[i][:], func=AF.Exp, bias=nmx, scale=1.0,
                                                           accum_out=rsum), reads=[sc[i].res, sm[i].res], writes=[pb[i].res, sm[i].res])
                        S.op("dve", lambda e: e.reciprocal(out=rinv, in_=rsum), reads=[sm[i].res], writes=[sm[i].res])
                        S.op("dve", lambda e: e.tensor_scalar(out=pn[i][:], in0=pb[i][:], scalar1=rinv, scalar2=None, op0=ALU.mult),
                             reads=[pb[i].res, sm[i].res], writes=[pn[i].res])

                    def stage_b(r, i):
                        r0 = min(max(r - 4, 0), 24)
                        pt2 = next_ps()
                        S.mm([lambda pe, j=j: pe.matmul(pt2[:, j * 64:(j + 1) * 64], lhsT=pn[i][:, j * 128:(j + 1) * 128], rhs=identb[0:64, 0:64],
                                                        start=True, stop=True)
                              for j in range(6)], reads=[pn[i].res, identb.res], writes=[pt2.res])
                        S.op("act", lambda e: e.copy(out=PT[i][:], in_=pt2[:, 0:384].rearrange("p (a b) -> p a b", b=64)),
                             reads=[pt2.res], writes=[PT[i].res])
                        po = next_ps()
                        fns = []
                        for j in range(6):
                            if j < 4:
                                vt = Ve[b_][:, r0 // 2 + j, :] if r0 % 2 == 0 else Vo[b_][:, (r0 - 1) // 2 + j, :]
                            else:
                                vt = Vc[b_][:, j - 4, :]
                            fns.append(lambda pe, j=j, vt=vt: pe.matmul(po[:, 0:64], lhsT=vt, rhs=PT[i][:, j, :], start=(j == 0), stop=(j == 5)))
                        S.mm(fns, reads=[Ve[b_].res, Vo[b_].res, Vc[b_].res, PT[i].res], writes=[po.res])
                        S.op("dve", lambda e: e.tensor_copy(out=ob[b_][p0:p0 + 64, r * 64:(r + 1) * 64], in_=po[p0:p0 + 64, 0:64]),
                             reads=[po.res], writes=[ob[b_].res])
                    for r in range(33):
                        if r < 32:
                            stage_a(r, r % 2)
                        if r >= 1:
                            stage_b(r - 1, (r - 1) % 2)
                S.dma("sp", g.oT[0][0][c], ob[b_][:], reads=[ob[b_].res], writes=[g.oT[0][1]], acc=True)

    HY = {}
    for L_ in (256, 2048):
        HY[L_] = dict(zposT=din("zposT%d" % L_, [33, L_]), win=din("win%d" % L_, [L_, 1024]),
                      Ff=din("Ff%d" % L_, [L_, 2 * L_]), Fi=din("Fi%d" % L_, [2 * L_, L_]))
        HY[L_]["Hs"], HY[L_]["Hs_r"] = dscr("Hs%d" % L_, [2 * L_ // 128, 128, 1024])
    for g in G:
        g.zbs, g.zbs_r = dscr("zbs%d" % g.i, [g.ntok, 1024], BF16)
        g.zd, g.zd_r = dscr("zd%d" % g.i, [g.ntok, 1024])
        g.x0s, g.x0s_r = dscr("x0s%d" % g.i, [g.ntok, 1024])
    TWO_PI = 2.0 * math.pi

    def bcast_row(dram_ap_tensor, offset, n):
        return AP(dram_ap_tensor, offset, [[0, 128], [1, n]])

    def hyena_filter(l, L):
        hy_ = HY[L]
        LT = L // 128
        with Phase(C, "hyf") as ph:
            f1 = ph.sb([33, 64], F32, "f1")
            f2 = ph.sb([64, 64], F32, "f2")
            f3 = ph.sb([64, 1024], F32, "f3")
            cc = ph.sb([64, 4], F32, "cc")
            zp = ph.sb([33, L], F32, "zp")
            S.dma("sp", f1[:], W["hy_f1"][l], writes=[f1.res])
            S.dma("sp", f2[:], W["hy_f2"][l], writes=[f2.res])
            S.dma("sp", f3[:], W["hy_f3"][l], writes=[f3.res])
            for j, nm in enumerate(("hy_fb1", "hy_fb2", "hy_freq")):
                S.dma("sp", cc[:, j:j + 1], W[nm][l].rearrange("(p o) -> p o", o=1), writes=[cc.res], slow=True, acc=(j > 0))
            S.dma("sp", zp[:], hy_["zposT"], writes=[zp.res])
            t1 = ph.sb([64, L], F32, "t1")
            t2 = ph.sb([64, L], F32, "t2")
            a = ph.sb([64, 512], F32, "a")
            ki = ph.sb([64, 512], I32, "ki")
            kf = ph.sb([64, 512], F32, "kf")
            cw = min(512, L)

            def sin_layer(lt, K, src, dst, bcol):
                for cb in range(L // cw):
                    pt = next_ps(0, 6)
                    S.mm([lambda pe: pe.matmul(pt[0:64, 0:cw], lhsT=lt[0:K, :], rhs=src[0:K, cb * cw:(cb + 1) * cw], start=True, stop=True)],
                         reads=[lt.res, src.res], writes=[pt.res])
                    S.op("dve", lambda e: e.tensor_scalar(out=a[:, 0:cw], in0=pt[0:64, 0:cw], scalar1=cc[:, bcol:bcol + 1],
                                                          scalar2=cc[:, 2:3], op0=ALU.add, op1=ALU.mult), reads=[pt.res, cc.res], writes=[a.res])
                    S.op("dve", lambda e: e.tensor_scalar(out=ki[:, 0:cw], in0=a[:, 0:cw], scalar1=1.0 / TWO_PI, scalar2=None, op0=ALU.mult),
                         reads=[a.res], writes=[ki.res])
                    S.op("dve", lambda e: e.tensor_copy(out=kf[:, 0:cw], in_=ki[:, 0:cw]), reads=[ki.res], writes=[kf.res])
                    S.op("dve", lambda e: e.scalar_tensor_tensor(out=a[:, 0:cw], in0=kf[:, 0:cw], scalar=-TWO_PI, in1=a[:, 0:cw],
                                                                 op0=ALU.mult, op1=ALU.add), reads=[kf.res, a.res], writes=[a.res])
                    S.op("dve", lambda e: e.tensor_scalar(out=a[:, 0:cw], in0=a[:, 0:cw], scalar1=-math.pi, scalar2=math.pi, op0=ALU.max,
                                                          op1=ALU.min), reads=[a.res], writes=[a.res])
                    S.op("act", lambda e: e.activation(out=dst[:, cb * cw:(cb + 1) * cw], in_=a[:, 0:cw], func=AF.Sin),
                         reads=[a.res], writes=[dst.res])
            sin_layer(f1, 33, zp, t1, 0)
            sin_layer(f2, 64, t1, t2, 1)
            filtb = ph.sb([128, LT, 1024], BF16, "filtb")
            winb = [ph.sb([128, 1024], F32, "winb") for _ in range(2)]
            ft = [ph.sb([128, 1024], F32, "ft") for _ in range(2)]
            fa = [ph.sb([128, 1024], F32, "fa") for _ in range(2)]
            for tt in range(LT):
                i = tt % 2
                S.dma("sp", winb[i][:], hy_["win"][tt * 128:(tt + 1) * 128, :], writes=[winb[i].res])
                for hf in range(2):
                    pt = next_ps(0, 6)
                    S.mm([lambda pe: pe.matmul(pt[:, :], lhsT=t2[0:64, tt * 128:(tt + 1) * 128], rhs=f3[0:64, hf * 512:(hf + 1) * 512],
                                               start=True, stop=True)], reads=[t2.res, f3.res], writes=[pt.res])
                    S.op("dve", lambda e: e.tensor_tensor(out=ft[i][:, hf * 512:(hf + 1) * 512], in0=pt[:, :],
                                                          in1=winb[i][:, hf * 512:(hf + 1) * 512], op=ALU.mult),
                         reads=[pt.res, winb[i].res], writes=[ft[i].res])
                S.op("act", lambda e: e.activation(out=fa[i][:], in_=ft[i][:], func=AF.Abs), reads=[ft[i].res], writes=[fa[i].res])
                S.op("pool", lambda e: e.tensor_copy(out=filtb[:, tt, :], in_=ft[i][:]), reads=[ft[i].res], writes=[filtb.res])
                for hf in range(2):
                    pacc = psum[6 + hf]
                    S.mm([lambda pe: pe.matmul(pacc[:, :], lhsT=ones[:], rhs=fa[i][:, hf * 512:(hf + 1) * 512], start=(tt == 0),
                                               stop=(tt == LT - 1))], reads=[ones.res, fa[i].res], writes=[pacc.res])
            inv = ph.sb([128, 1024], F32, "inv")
            for hf in range(2):
                S.op("dve", lambda e: e.tensor_scalar(out=inv[:, hf * 512:(hf + 1) * 512], in0=psum[6 + hf][:, :], scalar1=1e-6,
                                                      scalar2=None, op0=ALU.add), reads=[psum[6 + hf].res, inv.res], writes=[inv.res])
            S.op("dve", lambda e: e.reciprocal(out=inv[:], in_=inv[:]), reads=[inv.res], writes=[inv.res])
            wb = [ph.sb([128, LT, 512], BF16, "wbF") for _ in range(2)]
            ob = [ph.sb([128, 512], F32, "ob") for _ in range(3)]
            st = {"i": 0}

            def epi(c0, t0, pt, nr, ncv):
                i = st["i"] % 3
                st["i"] += 1
                S.op("dve", lambda e: e.tensor_tensor(out=ob[i][:], in0=pt[:, :], in1=inv[:, t0:t0 + 512], op=ALU.mult),
                     reads=[pt.res, inv.res], writes=[ob[i].res])
                S.dma("sp", hy_["Hs"][c0 // 128, :, t0:t0 + 512], ob[i][:], reads=[ob[i].res], writes=[hy_["Hs_r"]], acc=True)
            gemm(ph, hy_["Ff"], LT, 0, 2 * L, 512, filtb, 1024, "fm", epi, wb, ps_lo=0, ps_hi=6)

    def hyena_conv(l, g, L):
        with Phase(C, "hyc") as ph:
            cwt = ph.sb([128, 3, 3072], F32, "cw")
            cbt = ph.sb([128, 3072], F32, "cb")
            drt = ph.sb([128, 1024], F32, "dr")
            for j in range(3):
                S.dma("sp", cwt[:, j, :], bcast_row(W["hy_conv_w"].tensor, (l * 3 + j) * 3072, 3072), writes=[cwt.res], acc=(j > 0))
            S.dma("sp", cbt[:], bcast_row(W["hy_conv_b"].tensor, l * 3072, 3072), writes=[cbt.res])
            S.dma("sp", drt[:], bcast_row(W["hy_d"].tensor, l * 1024, 1024), writes=[drt.res])
            xm = [ph.sb([128, 1024], F32, "xm") for _ in range(2)]
            xc = [ph.sb([128, 1024], F32, "xc") for _ in range(2)]
            xp = [ph.sb([128, 1024], F32, "xp") for _ in range(2)]
            uu = [[ph.sb([128, 1024], F32, "u%d" % cg) for cg in range(3)] for _ in range(2)]
            tq = [ph.sb([128, 1024], F32, "tq") for _ in range(2)]
            zf = [ph.sb([128, 1024], F32, "zf") for _ in range(2)]
            zbt = [ph.sb([128, 1024], BF16, "zb") for _ in range(2)]
            zdt = [ph.sb([128, 1024], F32, "zd") for _ in range(2)]
            k = 0
            for tt in range(g.ntok // 128):
                t0 = tt * 128
                sp_ = t0 % L
                bi = tt % 2
                for cg in range(3):
                    ki_ = k % 2
                    k += 1
                    c0 = 3072 + cg * 1024
                    xm_, xc_, xp_, u, t = xm[ki_], xc[ki_], xp[ki_], uu[bi][cg], tq[ki_]
                    if sp_ == 0:
                        S.op("pool", lambda e: e.memset(xm_[0:1, :], 0.0), writes=[xm_.res])
                        S.dma("sp", xm_[1:128, :], g.rh[t0:t0 + 127, c0:c0 + 1024], reads=[g.rh_r], writes=[xm_.res], acc=True)
                    else:
                        S.dma("sp", xm_[:], g.rh[t0 - 1:t0 + 127, c0:c0 + 1024], reads=[g.rh_r], writes=[xm_.res])
                    S.dma("sp", xc_[:], g.rh[t0:t0 + 128, c0:c0 + 1024], reads=[g.rh_r], writes=[xc_.res])
                    if sp_ + 128 == L:
                        S.op("pool", lambda e: e.memset(xp_[:], 0.0), writes=[xp_.res])
                        S.dma("sp", xp_[0:127, :], g.rh[t0 + 1:t0 + 128, c0:c0 + 1024], reads=[g.rh_r], writes=[xp_.res], acc=True)
                    else:
                        S.dma("sp", xp_[:], g.rh[t0 + 1:t0 + 129, c0:c0 + 1024], reads=[g.rh_r], writes=[xp_.res])
                    wv = lambda j: cwt[:, j, cg * 1024:(cg + 1) * 1024]
                    S.op("dve", lambda e: e.tensor_tensor(out=u[:], in0=xm_[:], in1=wv(0), op=ALU.mult), reads=[xm_.res, cwt.res], writes=[u.res])
                    S.op("pool", lambda e: e.tensor_tensor(out=t[:], in0=xc_[:], in1=wv(1), op=ALU.mult), reads=[xc_.res, cwt.res], writes=[t.res])
                    S.op("dve", lambda e: e.tensor_tensor(out=u[:], in0=u[:], in1=t[:], op=ALU.add), reads=[u.res, t.res], writes=[u.res])
                    S.op("pool", lambda e: e.tensor_tensor(out=t[:], in0=xp_[:], in1=wv(2), op=ALU.mult), reads=[xp_.res, cwt.res], writes=[t.res])
                    S.op("dve", lambda e: e.tensor_tensor(out=u[:], in0=u[:], in1=t[:], op=ALU.add), reads=[u.res, t.res], writes=[u.res])
                    S.op("dve", lambda e: e.tensor_tensor(out=u[:], in0=u[:], in1=cbt[:, cg * 1024:(cg + 1) * 1024], op=ALU.add),
                         reads=[u.res, cbt.res], writes=[u.res])
                u0, u1, u2 = uu[bi]
                S.dma("sp", g.x0s[t0:t0 + 128, :], u0[:], reads=[u0.res], writes=[g.x0s_r], acc=True)
                S.op("dve", lambda e: e.tensor_tensor(out=zf[bi][:], in0=u2[:], in1=u1[:], op=ALU.mult), reads=[u1.res, u2.res], writes=[zf[bi].res])
                S.op("act", lambda e: e.copy(out=zbt[bi][:], in_=zf[bi][:]), reads=[zf[bi].res], writes=[zbt[bi].res])
                S.op("pool", lambda e: e.tensor_tensor(out=zdt[bi][:], in0=zf[bi][:], in1=drt[:], op=ALU.mult), reads=[zf[bi].res, drt.res],
                     writes=[zdt[bi].res])
                S.dma("sp", g.zbs[t0:t0 + 128, :], zbt[bi][:], reads=[zbt[bi].res], writes=[g.zbs_r], acc=True)
                S.dma("sp", g.zd[t0:t0 + 128, :], zdt[bi][:], reads=[zdt[bi].res], writes=[g.zd_r], acc=True)

    def hyena_dft(l, g, L):
        hy_ = HY[L]
        LT = L // 128
        with Phase(C, "hyd") as ph:
            zT = ph.sb([128, LT, 512], BF16, "zT")
            YT = ph.sb([128, 2 * LT, 512], BF16, "YT")
            wbF = [ph.sb([128, LT, 512], BF16, "wbF") for _ in range(2)]
            wbI = [ph.sb([128, 2 * LT, 256], BF16, "wbI") for _ in range(2)]
            hre = [ph.sb([128, 512], F32, "hre") for _ in range(2)]
            him = [ph.sb([128, 512], F32, "him") for _ in range(2)]
            zre = [ph.sb([128, 512], F32, "zre") for _ in range(2)]
            zim = [ph.sb([128, 512], F32, "zim") for _ in range(2)]
            ta = [ph.sb([128, 512], F32, "ta") for _ in range(2)]
            tb = [ph.sb([128, 512], F32, "tb") for _ in range(2)]
            tc_ = [ph.sb([128, 512], F32, "tc") for _ in range(2)]
            td = [ph.sb([128, 512], F32, "td") for _ in range(2)]
            zdt = [ph.sb([128, 512], F32, "zdt") for _ in range(2)]
            x0t = [ph.sb([128, 512], F32, "x0t") for _ in range(2)]
            ot = [ph.sb([128, 512], F32, "ot") for _ in range(2)]
            otb = [ph.sb([128, 4, 128], BF16, "otb") for _ in range(2)]
            for sq in range(g.ntok // L):
                s0 = sq * L
                for half in range(2):
                    h0 = half * 512
                    S.dma("sp", zT[:], g.zbs[s0:s0 + L, h0:h0 + 512].rearrange("(j p) c -> p j c", p=128), reads=[g.zbs_r], writes=[zT.res])
                    st = {"i": 0, "re": None}

                    def epi_f(c0, t0, pt, nr, ncv):
                        blk = c0 // 128
                        if blk % 2 == 0:
                            i = st["i"] % 2
                            S.op("act", lambda e: e.copy(out=zre[i][:], in_=pt[:, :]), reads=[pt.res], writes=[zre[i].res])
                            S.dma("sp", hre[i][:], hy_["Hs"][blk, :, h0:h0 + 512], reads=[hy_["Hs_r"]], writes=[hre[i].res])
                            S.dma("sp", him[i][:], hy_["Hs"][blk + 1, :, h0:h0 + 512], reads=[hy_["Hs_r"]], writes=[him[i].res])
                            return
                        i = st["i"] % 2
                        st["i"] += 1
                        S.op("act", lambda e: e.copy(out=zim[i][:], in_=pt[:, :]), reads=[pt.res], writes=[zim[i].res])
                        S.op("dve", lambda e: e.tensor_tensor(out=ta[i][:], in0=zre[i][:], in1=hre[i][:], op=ALU.mult),
                             reads=[zre[i].res, hre[i].res], writes=[ta[i].res])
                        S.op("pool", lambda e: e.tensor_tensor(out=tb[i][:], in0=zim[i][:], in1=him[i][:], op=ALU.mult),
                             reads=[zim[i].res, him[i].res], writes=[tb[i].res])
                        S.op("dve", lambda e: e.tensor_tensor(out=YT[:, blk - 1, :], in0=ta[i][:], in1=tb[i][:], op=ALU.subtract),
                             reads=[ta[i].res, tb[i].res], writes=[YT.res])
                        S.op("pool", lambda e: e.tensor_tensor(out=tc_[i][:], in0=zre[i][:], in1=him[i][:], op=ALU.mult),
                             reads=[zre[i].res, him[i].res], writes=[tc_[i].res])
                        S.op("dve", lambda e: e.tensor_tensor(out=td[i][:], in0=zim[i][:], in1=hre[i][:], op=ALU.mult),
                             reads=[zim[i].res, hre[i].res], writes=[td[i].res])
                        S.op("dve", lambda e: e.tensor_tensor(out=YT[:, blk, :], in0=tc_[i][:], in1=td[i][:], op=ALU.add),
                             reads=[tc_[i].res, td[i].res], writes=[YT.res])
                    gemm(ph, hy_["Ff"], LT, 0, 2 * L, 512, zT, 512, "fm", epi_f, wbF)
                    st2 = {"i": 0}

                    def epi_i(c0, t0, pt, nr, ncv):
                        i = st2["i"] % 2
                        st2["i"] += 1
                        r0 = s0 + c0
                        S.dma("sp", zdt[i][:], g.zd[r0:r0 + 128, h0:h0 + 512], reads=[g.zd_r], writes=[zdt[i].res])
                        S.dma("sp", x0t[i][:], g.x0s[r0:r0 + 128, h0:h0 + 512], reads=[g.x0s_r], writes=[x0t[i].res])
                        S.op("dve", lambda e: e.tensor_tensor(out=ot[i][:], in0=pt[:, :], in1=zdt[i][:], op=ALU.add),
                             reads=[pt.res, zdt[i].res], writes=[ot[i].res])
                        S.op("pool", lambda e: e.tensor_tensor(out=ot[i][:], in0=ot[i][:], in1=x0t[i][:], op=ALU.mult),
                             reads=[ot[i].res, x0t[i].res], writes=[ot[i].res])
                        pt2 = next_ps()
                        S.mm([lambda pe, q=q: pe.transpose(pt2[:, q * 128:(q + 1) * 128], ot[i][:, q * 128:(q + 1) * 128], ident[:])
                              for q in range(4)], reads=[ot[i].res, ident.res], writes=[pt2.res])
                        S.op("act", lambda e: e.copy(out=otb[i][:], in_=pt2[:, :].rearrange("p (a b) -> p a b", b=128)),
                             reads=[pt2.res], writes=[otb[i].res])
                        S.dma("sp", g.oT[2][0][half * 4:(half + 1) * 4, :, r0:r0 + 128].rearrange("k p t -> p k t"), otb[i][:],
                              reads=[otb[i].res], writes=[g.oT[2][1]], acc=True)
                    gemm(ph, hy_["Fi"], 2 * LT, 0, L, 256, YT, 512, "fm", epi_i, wbI)

    for g in G:
        g.lora, g.lora_r = dscr("lora%d" % g.i, [4, 64, g.ntok], BF16)
        g.sg, g.sg_r = dscr("sg%d" % g.i, [128, g.ntok], BF16)
        g.SH, g.SH_r = dscr("SH%d" % g.i, [g.ntok, 3, 1024])
        g.DE = [dscr("DE%d_%d" % (g.i, e), [g.ntok, 3, 1024]) for e in range(2)]
        g.ysc = [dscr("ysc%d_%d" % (g.i, e), [g.ntok, 1024]) for e in range(2)]
        g.gsc, g.gsc_r = dscr("gsc%d" % g.i, [g.ntok, 1024])
        g.bon, g.bon_r = dscr("bon%d" % g.i, [g.ntok, 1024])

    def rwkv_lora(ph, l, g, hT):
        wbs = [ph.sb([128, 16, 128], BF16, "wbl") for _ in range(2)]
        obl = [ph.sb([128, 512], BF16, "obl") for _ in range(3)]
        for e in range(2):
            gemm(ph, W["wkv_w1"][l][e], 16, 0, 64, 64, hT, g.ntok, "fm",
                 epi_store(ph, obl, lambda c0, t0, nr, ncv, e=e: g.lora[e, :, t0:t0 + ncv], g.lora_r, func=AF.Tanh), wbs)
            gemm(ph, W["wkv_a1"][l][e], 16, 0, 64, 64, hT, g.ntok, "fm",
                 epi_store(ph, obl, lambda c0, t0, nr, ncv, e=e: g.lora[2 + e, :, t0:t0 + ncv], g.lora_r), wbs)
        gemm(ph, W["wkv_g1"][l], 16, 0, 128, 128, hT, g.ntok, "fm",
             epi_store(ph, obl, lambda c0, t0, nr, ncv: g.sg[:, t0:t0 + ncv], g.sg_r, func=AF.Sigmoid), wbs)

    def rwkv_pre(l, g, L):
        with Phase(C, "rwp") as ph:
            cwt = ph.sb([128, 3, 3072], F32, "cw")
            cbt = ph.sb([128, 3072], F32, "cb")
            for j in range(3):
                S.dma("sp", cwt[:, j, :], bcast_row(W["wkv_conv_w"].tensor, (l * 3 + j) * 3072, 3072), writes=[cwt.res], acc=(j > 0))
            S.dma("sp", cbt[:], bcast_row(W["wkv_conv_b"].tensor, l * 3072, 3072), writes=[cbt.res])
            rows = ph.sb([128, 8, 1024], F32, "rows")
            for e in range(2):
                S.dma("sp", rows[:, e, :], bcast_row(W["wkv_w0"].tensor, (l * 2 + e) * 1024, 1024), writes=[rows.res], acc=True)
                S.dma("sp", rows[:, 2 + e, :], bcast_row(W["wkv_a0"].tensor, (l * 2 + e) * 1024, 1024), writes=[rows.res], acc=True)
            S.dma("sp", rows[:, 4, :], bcast_row(W["wkv_k_k"].tensor, l * 1024, 1024), writes=[rows.res], acc=True)
            S.dma("sp", rows[:, 5, :], bcast_row(W["wkv_k_a"].tensor, l * 1024, 1024), writes=[rows.res], acc=True)
            S.dma("sp", rows[:, 7, :], bcast_row(W["wkv_r_k"].tensor, l * 1024, 1024), writes=[rows.res], acc=True)
            S.op("dve", lambda e_: e_.tensor_scalar(out=rows[:, 6, :], in0=rows[:, 5, :], scalar1=-1.0, scalar2=1.0, op0=ALU.mult, op1=ALU.add),
                 reads=[rows.res], writes=[rows.res])
            w2b = ph.sb([64, 4, 1024], BF16, "w2b")
            for e in range(2):
                S.dma("pool", w2b[:, e, :], W["wkv_w2"][l][e], writes=[w2b.res], acc=True)
                S.dma("pool", w2b[:, 2 + e, :], W["wkv_a2"][l][e], writes=[w2b.res], acc=True)
            g2b = ph.sb([128, 1024], BF16, "g2b")
            S.dma("pool", g2b[:], W["wkv_g2"][l], writes=[g2b.res])
            xms = [ph.sb([128, 1024], F32, "xm") for _ in range(2)]
            xcs = [ph.sb([128, 1024], F32, "xc") for _ in range(2)]
            xps = [ph.sb([128, 1024], F32, "xp") for _ in range(2)]
            kx = [0]
            rkv = [ph.sb([128, 1024], F32, "rkv%d" % i) for i in range(3)]
            t1 = ph.sb([128, 1024], F32, "t1")
            t2 = ph.sb([128, 1024], F32, "t2")
            kk = ph.sb([128, 1024], F32, "kk")
            at = ph.sb([128, 1024], F32, "at")
            wt = ph.sb([128, 1024], F32, "wt")
            o1 = ph.sb([128, 1024], F32, "o1")
            o2 = ph.sb([128, 1024], F32, "o2")
            sm = ph.sb([128, 64], F32, "sm")
            lt = ph.sb([64, 4, 128], BF16, "lt")
            sgt = ph.sb([128, 128], BF16, "sgt")
            v3 = lambda t_: t_[:].rearrange("p (h k) -> p h k", k=64)
            for tt in range(g.ntok // 128):
                t0 = tt * 128
                sp_ = t0 % L
                for cg in range(3):
                    c0 = cg * 1024
                    u = rkv[cg]
                    xm, xc, xp = xms[kx[0] % 2], xcs[kx[0] % 2], xps[kx[0] % 2]
                    kx[0] += 1
                    if sp_ == 0:
                        S.op("pool", lambda e: e.memset(xm[0:1, :], 0.0), writes=[xm.res])
                        S.dma("sp", xm[1:128, :], g.rh[t0:t0 + 127, c0:c0 + 1024], reads=[g.rh_r], writes=[xm.res], acc=True)
                    else:
                        S.dma("sp", xm[:], g.rh[t0 - 1:t0 + 127, c0:c0 + 1024], reads=[g.rh_r], writes=[xm.res])
                    S.dma("sp", xc[:], g.rh[t0:t0 + 128, c0:c0 + 1024], reads=[g.rh_r], writes=[xc.res])
                    if sp_ + 128 == L:
                        S.op("pool", lambda e: e.memset(xp[:], 0.0), writes=[xp.res])
                        S.dma("sp", xp[0:127, :], g.rh[t0 + 1:t0 + 128, c0:c0 + 1024], reads=[g.rh_r], writes=[xp.res], acc=True)
                    else:
                        S.dma("sp", xp[:], g.rh[t0 + 1:t0 + 129, c0:c0 + 1024], reads=[g.rh_r], writes=[xp.res])
                    wv = lambda j: cwt[:, j, cg * 1024:(cg + 1) * 1024]
                    S.op("dve", lambda e: e.tensor_tensor(out=u[:], in0=xm[:], in1=wv(0), op=ALU.mult), reads=[xm.res, cwt.res], writes=[u.res])
                    S.op("pool", lambda e: e.tensor_tensor(out=t1[:], in0=xc[:], in1=wv(1), op=ALU.mult), reads=[xc.res, cwt.res], writes=[t1.res])
                    S.op("dve", lambda e: e.tensor_tensor(out=u[:], in0=u[:], in1=t1[:], op=ALU.add), reads=[u.res, t1.res], writes=[u.res])
                    S.op("pool", lambda e: e.tensor_tensor(out=t1[:], in0=xp[:], in1=wv(2), op=ALU.mult), reads=[xp.res, cwt.res], writes=[t1.res])
                    S.op("dve", lambda e: e.tensor_tensor(out=u[:], in0=u[:], in1=t1[:], op=ALU.add), reads=[u.res, t1.res], writes=[u.res])
                    S.op("dve", lambda e: e.tensor_tensor(out=u[:], in0=u[:], in1=cbt[:, cg * 1024:(cg + 1) * 1024], op=ALU.add),
                         reads=[u.res, cbt.res], writes=[u.res])
                r_, k_, v_ = rkv
                S.dma("sp", g.SH[t0:t0 + 128, 1, :], r_[:], reads=[r_.res], writes=[g.SH_r], acc=True)
                S.dma("sp", g.SH[t0:t0 + 128, 2, :], v_[:], reads=[v_.res], writes=[g.SH_r], acc=True)
                S.op("dve", lambda e: e.tensor_tensor(out=kk[:], in0=k_[:], in1=rows[:, 4, :], op=ALU.mult), reads=[k_.res, rows.res], writes=[kk.res])
                S.op("pool", lambda e: e.tensor_tensor(out=t1[:], in0=kk[:], in1=kk[:], op=ALU.mult), reads=[kk.res], writes=[t1.res])
                S.op("dve", lambda e: e.tensor_reduce(out=sm[:, 0:16], in_=v3(t1), axis=AX.X, op=ALU.add), reads=[t1.res], writes=[sm.res])
                S.op("dve", lambda e: e.tensor_scalar(out=sm[:, 0:16], in0=sm[:, 0:16], scalar1=1e-12, scalar2=None, op0=ALU.add),
                     reads=[sm.res], writes=[sm.res])
                S.op("act", lambda e: e.sqrt(out=sm[:, 0:16], in_=sm[:, 0:16]), reads=[sm.res], writes=[sm.res])
                S.op("dve", lambda e: e.reciprocal(out=sm[:, 0:16], in_=sm[:, 0:16]), reads=[sm.res], writes=[sm.res])
                S.op("dve", lambda e: e.tensor_tensor(out=v3(kk), in0=v3(kk), in1=sm[:, 0:16].unsqueeze(2).broadcast_to([128, 16, 64]), op=ALU.mult),
                     reads=[kk.res, sm.res], writes=[kk.res])
                S.dma("sp", g.SH[t0:t0 + 128, 0, :], kk[:], reads=[kk.res], writes=[g.SH_r], acc=True)
                S.op("pool", lambda e: e.tensor_tensor(out=t1[:], in0=r_[:], in1=k_[:], op=ALU.mult), reads=[r_.res, k_.res], writes=[t1.res])
                S.op("pool", lambda e: e.tensor_tensor(out=t1[:], in0=t1[:], in1=rows[:, 7, :], op=ALU.mult), reads=[t1.res, rows.res], writes=[t1.res])
                S.op("dve", lambda e: e.tensor_reduce(out=sm[:, 16:32], in_=v3(t1), axis=AX.X, op=ALU.add), reads=[t1.res, sm.res], writes=[sm.res])
                S.op("dve", lambda e: e.tensor_tensor(out=v3(o1), in0=v3(v_), in1=sm[:, 16:32].unsqueeze(2).broadcast_to([128, 16, 64]), op=ALU.mult),
                     reads=[v_.res, sm.res], writes=[o1.res])
                S.dma("sp", g.bon[t0:t0 + 128, :], o1[:], reads=[o1.res], writes=[g.bon_r], acc=True)
                S.dma("sp", lt[:], g.lora[:, :, t0:t0 + 128].rearrange("a p t -> p a t"), reads=[g.lora_r], writes=[lt.res])
                S.dma("sp", sgt[:], g.sg[:, t0:t0 + 128], reads=[g.sg_r], writes=[sgt.res])
                for hf in range(2):
                    pt = next_ps()
                    S.mm([lambda pe: pe.matmul(pt[:, :], lhsT=sgt[:], rhs=g2b[:, hf * 512:(hf + 1) * 512], start=True, stop=True)],
                         reads=[sgt.res, g2b.res], writes=[pt.res])
                    S.op("act", lambda e: e.copy(out=o2[:, hf * 512:(hf + 1) * 512], in_=pt[:, :]), reads=[pt.res, o2.res], writes=[o2.res])
                S.dma("sp", g.gsc[t0:t0 + 128, :], o2[:], reads=[o2.res], writes=[g.gsc_r], acc=True)
                for e in range(2):
                    for hf in range(2):
                        pt = next_ps()
                        S.mm([lambda pe: pe.matmul(pt[:, :], lhsT=lt[:, e, :], rhs=w2b[:, e, hf * 512:(hf + 1) * 512], start=True, stop=True)],
                             reads=[lt.res, w2b.res], writes=[pt.res])
                        S.op("dve", lambda e_: e_.tensor_tensor(out=wt[:, hf * 512:(hf + 1) * 512], in0=pt[:, :],
                                                               in1=rows[:, e, hf * 512:(hf + 1) * 512], op=ALU.add),
                             reads=[pt.res, rows.res, wt.res], writes=[wt.res])
                    S.op("act", lambda e_: e_.activation(out=wt[:], in_=wt[:], func=AF.Sigmoid), reads=[wt.res], writes=[wt.res])
                    S.op("act", lambda e_: e_.activation(out=wt[:], in_=wt[:], func=AF.Exp, scale=-math.exp(-0.5)), reads=[wt.res], writes=[wt.res])
                    S.dma("sp", g.DE[e][0][t0:t0 + 128, 0, :], wt[:], reads=[wt.res], writes=[g.DE[e][1]], acc=True)
                    for hf in range(2):
                        pt = next_ps()
                        S.mm([lambda pe: pe.matmul(pt[:, :], lhsT=lt[:, 2 + e, :], rhs=w2b[:, 2 + e, hf * 512:(hf + 1) * 512], start=True, stop=True)],
                             reads=[lt.res, w2b.res], writes=[pt.res])
                        S.op("dve", lambda e_: e_.tensor_tensor(out=at[:, hf * 512:(hf + 1) * 512], in0=pt[:, :],
                                                               in1=rows[:, 2 + e, hf * 512:(hf + 1) * 512], op=ALU.add),
                             reads=[pt.res, rows.res, at.res], writes=[at.res])
                    S.op("act", lambda e_: e_.activation(out=at[:], in_=at[:], func=AF.Sigmoid), reads=[at.res], writes=[at.res])
                    S.op("dve", lambda e_: e_.tensor_tensor(out=o1[:], in0=kk[:], in1=at[:], op=ALU.mult), reads=[kk.res, at.res], writes=[o1.res])
                    S.dma("sp", g.DE[e][0][t0:t0 + 128, 1, :], o1[:], reads=[o1.res], writes=[g.DE[e][1]], acc=True)
                    S.op("pool", lambda e_: e_.tensor_tensor(out=t2[:], in0=at[:], in1=rows[:, 5, :], op=ALU.mult), reads=[at.res, rows.res], writes=[t2.res])
                    S.op("pool", lambda e_: e_.tensor_tensor(out=t2[:], in0=t2[:], in1=rows[:, 6, :], op=ALU.add), reads=[t2.res, rows.res], writes=[t2.res])
                    S.op("dve", lambda e_: e_.tensor_tensor(out=o2[:], in0=k_[:], in1=t2[:], op=ALU.mult), reads=[k_.res, t2.res], writes=[o2.res])
                    S.dma("sp", g.DE[e][0][t0:t0 + 128, 2, :], o2[:], reads=[o2.res], writes=[g.DE[e][1]], acc=True)

    def rwkv_scan(l, g, L):
        sample = (g.i == 1)
        NV = 16 if sample else 64
        TC = 32 if sample else 16
        with Phase(C, "rws") as ph:
            St = ph.sb([128, NV, 64], F32, "S")
            tmp = ph.sb([128, NV, 64], F32, "tmp")
            At = ph.sb([128, NV, 64], F32, "A")
            Bt = ph.sb([128, NV, 64], F32, "B")
            sa = ph.sb([128, NV], F32, "sa")
            Dq = [[ph.sb([128, TC, 64], F32, "D%d" % q) for q in range(5)] for _ in range(2)]
            Vt = [ph.sb([128, TC, NV], F32, "V") for _ in range(2)]
            Yt = [ph.sb([128, TC, NV], F32, "Y") for _ in range(2)]
            if sample:
                S.dma("sp", St[:].rearrange("p a b -> p (a b)"), I["s0"][l], writes=[St.res])
            else:
                S.op("pool", lambda e: e.memset(St[:], 0.0), writes=[St.res])
            def srcs(e):
                return [(g.SH, g.SH_r, 0), (g.DE[e][0], g.DE[e][1], 0), (g.DE[e][0], g.DE[e][1], 1), (g.DE[e][0], g.DE[e][1], 2), (g.SH, g.SH_r, 1)]
            nchunk = L // TC
            for c in range(nchunk):
                bi = c % 2
                i0 = c * TC
                for e in range(2):
                    tstart = i0 if e == 0 else (L - 1 - i0)
                    sgn = 1 if e == 0 else -1
                    if sample:
                        for vq in range(4):
                            dstp = slice(e * 64 + vq, (e + 1) * 64, 4)
                            first = (e == 0 and vq == 0)
                            for q, (arr, arr_r, slot) in enumerate(srcs(e)):
                                S.dma("sp", Dq[bi][q][dstp, :, :],
                                      AP(arr.tensor, tstart * 3072 + slot * 1024, [[64, 16], [sgn * 3072, TC], [1, 64]]),
                                      reads=[arr_r], writes=[Dq[bi][q].res], acc=(not first))
                            S.dma("sp", Vt[bi][dstp, :, :],
                                  AP(g.SH.tensor, tstart * 3072 + 2 * 1024 + vq * 16, [[64, 16], [sgn * 3072, TC], [1, 16]]),
                                  reads=[g.SH_r], writes=[Vt[bi].res], acc=(not first))
                    else:
                        for sq in range(4):
                            p0 = sq * 32 + e * 16
                            first = (e == 0 and sq == 0)
                            for q, (arr, arr_r, slot) in enumerate(srcs(e)):
                                S.dma("sp", Dq[bi][q][p0:p0 + 16, :, :],
                                      AP(arr.tensor, (sq * L + tstart) * 3072 + slot * 1024, [[64, 16], [sgn * 3072, TC], [1, 64]]),
                                      reads=[arr_r], writes=[Dq[bi][q].res], acc=(not first))
                            S.dma("sp", Vt[bi][p0:p0 + 16, :, :],
                                  AP(g.SH.tensor, (sq * L + tstart) * 3072 + 2 * 1024, [[64, 16], [sgn * 3072, TC], [1, 64]]),
                                  reads=[g.SH_r], writes=[Vt[bi].res], acc=(not first))
                D = Dq[bi]
                for i in range(TC):
                    bc = lambda q: D[q][:, i, :].unsqueeze(1).broadcast_to([128, NV, 64])
                    S.op("dve", lambda e_: e_.tensor_tensor(out=tmp[:], in0=St[:], in1=bc(0), op=ALU.mult), reads=[St.res, D[0].res], writes=[tmp.res])
                    S.op("dve", lambda e_: e_.tensor_reduce(out=sa[:], in_=tmp[:], axis=AX.X, op=ALU.add), reads=[tmp.res], writes=[sa.res])
                    S.op("pool", lambda e_: e_.tensor_tensor(out=At[:], in0=Vt[bi][:, i, :].unsqueeze(2).broadcast_to([128, NV, 64]), in1=bc(3), op=ALU.mult),
                         reads=[Vt[bi].res, D[3].res], writes=[At.res])
                    S.op("pool", lambda e_: e_.tensor_tensor(out=Bt[:], in0=St[:], in1=bc(1), op=ALU.mult), reads=[St.res, D[1].res], writes=[Bt.res])
                    S.op("pool", lambda e_: e_.tensor_tensor(out=Bt[:], in0=Bt[:], in1=At[:], op=ALU.add), reads=[Bt.res, At.res], writes=[Bt.res])
                    S.op("dve", lambda e_: e_.tensor_tensor(out=tmp[:], in0=sa[:].unsqueeze(2).broadcast_to([128, NV, 64]), in1=bc(2), op=ALU.mult),
                         reads=[sa.res, D[2].res], writes=[tmp.res])
                    S.op("dve", lambda e_: e_.tensor_tensor(out=St[:], in0=Bt[:], in1=tmp[:], op=ALU.subtract), reads=[Bt.res, tmp.res], writes=[St.res])
                    S.op("dve", lambda e_: e_.tensor_tensor(out=tmp[:], in0=St[:], in1=bc(4), op=ALU.mult), reads=[St.res, D[4].res], writes=[tmp.res])
                    S.op("dve", lambda e_: e_.tensor_reduce(out=Yt[bi][:, i, :], in_=tmp[:], axis=AX.X, op=ALU.add), reads=[tmp.res], writes=[Yt[bi].res])
                for e in range(2):
                    tstart = i0 if e == 0 else (L - 1 - i0)
                    sgn = 1 if e == 0 else -1
                    ya, ya_r = g.ysc[e]
                    if sample:
                        for vq in range(4):
                            S.dma("sp", AP(ya.tensor, tstart * 1024 + vq * 16, [[64, 16], [sgn * 1024, TC], [1, 16]]),
                                  Yt[bi][slice(e * 64 + vq, (e + 1) * 64, 4), :, :], reads=[Yt[bi].res], writes=[ya_r], acc=True)
                    else:
                        for sq in range(4):
                            p0 = sq * 32 + e * 16
                            S.dma("sp", AP(ya.tensor, (sq * L + tstart) * 1024, [[64, 16], [sgn * 1024, TC], [1, 64]]), Yt[bi][p0:p0 + 16, :, :],
                                  reads=[Yt[bi].res], writes=[ya_r], acc=True)
            if not sample:
                for sq in range(4):
                    r0 = (sq * DEPTH + l) * 32
                    S.dma("sp", O["ns"][r0:r0 + 32, :], St[sq * 32:(sq + 1) * 32, :, :].rearrange("p a b -> p (a b)"), reads=[St.res],
                          writes=[ORES["ns"]], acc=True)

    def rwkv_post(l, g):
        with Phase(C, "rwo") as ph:
            rows = ph.sb([128, 2, 1024], F32, "rows")
            S.dma("sp", rows[:, 0, :], bcast_row(W["wkv_gn_g"].tensor, l * 1024, 1024), writes=[rows.res], acc=True)
            S.dma("sp", rows[:, 1, :], bcast_row(W["wkv_gn_b"].tensor, l * 1024, 1024), writes=[rows.res], acc=True)
            ya = [ph.sb([128, 1024], F32, "ya") for _ in range(2)]
            yb = [ph.sb([128, 1024], F32, "yb") for _ in range(2)]
            bo = [ph.sb([128, 1024], F32, "bo") for _ in range(2)]
            gg = [ph.sb([128, 1024], F32, "gg") for _ in range(2)]
            tq = [ph.sb([128, 1024], F32, "tq") for _ in range(2)]
            sm = [ph.sb([128, 64], F32, "sm") for _ in range(2)]
            otb = [ph.sb([128, 8, 128], BF16, "otb") for _ in range(2)]
            v3 = lambda t_: t_[:].rearrange("p (h k) -> p h k", k=64)
            for tt in range(g.ntok // 128):
                i = tt % 2
                t0 = tt * 128
                y, y2, b_, g_, t_, s_ = ya[i], yb[i], bo[i], gg[i], tq[i], sm[i]
                S.dma("sp", y[:], g.ysc[0][0][t0:t0 + 128, :], reads=[g.ysc[0][1]], writes=[y.res])
                S.dma("sp", y2[:], g.ysc[1][0][t0:t0 + 128, :], reads=[g.ysc[1][1]], writes=[y2.res])
                S.dma("sp", b_[:], g.bon[t0:t0 + 128, :], reads=[g.bon_r], writes=[b_.res])
                S.dma("sp", g_[:], g.gsc[t0:t0 + 128, :], reads=[g.gsc_r], writes=[g_.res])
                S.op("dve", lambda e: e.tensor_tensor(out=y[:], in0=y[:], in1=y2[:], op=ALU.add), reads=[y.res, y2.res], writes=[y.res])
                S.op("dve", lambda e: e.tensor_reduce(out=s_[:, 0:16], in_=v3(y), axis=AX.X, op=ALU.add), reads=[y.res], writes=[s_.res])
                S.op("dve", lambda e: e.tensor_scalar(out=s_[:, 0:16], in0=s_[:, 0:16], scalar1=-1.0 / 64, scalar2=None, op0=ALU.mult),
                     reads=[s_.res], writes=[s_.res])
                S.op("dve", lambda e: e.tensor_tensor(out=v3(y), in0=v3(y), in1=s_[:, 0:16].unsqueeze(2).broadcast_to([128, 16, 64]), op=ALU.add),
                     reads=[y.res, s_.res], writes=[y.res])
                S.op("pool", lambda e: e.tensor_tensor(out=t_[:], in0=y[:], in1=y[:], op=ALU.mult), reads=[y.res], writes=[t_.res])
                S.op("dve", lambda e: e.tensor_reduce(out=s_[:, 16:32], in_=v3(t_), axis=AX.X, op=ALU.add), reads=[t_.res, s_.res], writes=[s_.res])
                S.op("dve", lambda e: e.tensor_scalar(out=s_[:, 16:32], in0=s_[:, 16:32], scalar1=1.0 / 64, scalar2=64e-5, op0=ALU.mult, op1=ALU.add),
                     reads=[s_.res], writes=[s_.res])
                S.op("act", lambda e: e.sqrt(out=s_[:, 16:32], in_=s_[:, 16:32]), reads=[s_.res], writes=[s_.res])
                S.op("dve", lambda e: e.reciprocal(out=s_[:, 16:32], in_=s_[:, 16:32]), reads=[s_.res], writes=[s_.res])
                S.op("dve", lambda e: e.tensor_tensor(out=v3(y), in0=v3(y), in1=s_[:, 16:32].unsqueeze(2).broadcast_to([128, 16, 64]), op=ALU.mult),
                     reads=[y.res, s_.res], writes=[y.res])
                S.op("pool", lambda e: e.tensor_tensor(out=y[:], in0=y[:], in1=rows[:, 0, :], op=ALU.mult), reads=[y.res, rows.res], writes=[y.res])
                S.op("pool", lambda e: e.tensor_tensor(out=y[:], in0=y[:], in1=rows[:, 1, :], op=ALU.add), reads=[y.res, rows.res], writes=[y.res])
                S.op("dve", lambda e: e.tensor_tensor(out=y[:], in0=y[:], in1=b_[:], op=ALU.add), reads=[y.res, b_.res], writes=[y.res])
                S.op("dve", lambda e: e.tensor_tensor(out=y[:], in0=y[:], in1=g_[:], op=ALU.mult), reads=[y.res, g_.res], writes=[y.res])
                for hf in range(2):
                    pt2 = next_ps()
                    S.mm([lambda pe, q=q: pe.transpose(pt2[:, q * 128:(q + 1) * 128], y[:, (hf * 4 + q) * 128:(hf * 4 + q + 1) * 128], ident[:])
                          for q in range(4)], reads=[y.res, ident.res], writes=[pt2.res])
                    S.op("act", lambda e: e.copy(out=otb[i][:, hf * 4:(hf + 1) * 4, :], in_=pt2[:, :].rearrange("p (a b) -> p a b", b=128)),
                         reads=[pt2.res, otb[i].res], writes=[otb[i].res])
                S.dma("sp", g.oT[1][0][:, :, t0:t0 + 128].rearrange("k p t -> p k t"), otb[i][:], reads=[otb[i].res], writes=[g.oT[1][1]], acc=True)

    def merge_out(l, g):
        with Phase(C, "mrg") as ph:
            cols = load_cols(ph, l, g)
            mT = ph.sb([128, 16, g.ntok], BF16, "mT")
            if not mixers or cfg.get("nomerge"):
                S.op("pool", lambda e: e.memset(mT[:], 0.0), writes=[mT.res])
            else:
                with Phase(C, "mrg1") as ph1:
                    branches = [b for b in range(3) if "arc"[b] in mixers]
                    wnames = ["w_pa", "w_pr", "w_pc"]
                    HT = 1024
                    oTt = {b: ph1.sb([128, 8, HT], BF16, "oT%d" % b) for b in branches}
                    wbs = {b: [ph1.sb([128, 8, 512], BF16, "wp%d" % b) for _ in range(2)] for b in branches}
                    gts = {b: [ph1.sb([128, 512], F32, "g%d" % b) for _ in range(2)] for b in branches}
                    tmp = [ph1.sb([128, 512], F32, "mt") for _ in range(2)]
                    tmp2 = [ph1.sb([128, 512], F32, "mt2") for _ in range(2)]
                    cnt = 0
                    for half in range(g.ntok // HT):
                        for b in branches:
                            S.dma("sp", oTt[b][:], g.oT[b][0].rearrange("k p t -> p k t")[:, :, half * HT:(half + 1) * HT],
                                  reads=[g.oT[b][1]], writes=[oTt[b].res])

                        def loadw(s):
                            for b in branches:
                                wbuf = wbs[b][s % 2]
                                S.dma("pool", wbuf[:], W[wnames[b]][l].rearrange("(k p) n -> p k n", p=128)[:, :, s * 512:(s + 1) * 512],
                                      writes=[wbuf.res])
                        loadw(0)
                        for s in range(4):
                            if s + 1 < 4:
                                loadw(s + 1)
                            for nb in range(4):
                                nbg = s * 4 + nb
                                for tt in range(HT // 512):
                                    t0 = half * HT + tt * 512
                                    pts = {}
                                    for b in branches:
                                        pt = next_ps()
                                        pts[b] = pt
                                        wbuf = wbs[b][s % 2]
                                        fns = [lambda pe, kc=kc, pt=pt, wbuf=wbuf, b=b: pe.matmul(
                                            pt[:, :], lhsT=wbuf[:, kc, nb * 128:(nb + 1) * 128],
                                            rhs=oTt[b][:, kc, tt * 512:(tt + 1) * 512], start=(kc == 0), stop=(kc == 7))
                                            for kc in range(8)]
                                        S.mm(fns, reads=[wbuf.res, oTt[b].res], writes=[pt.res])
                                        gt = gts[b][cnt % 2]
                                        S.dma("sp", gt[:], g.gT[b * 16 + nbg, :, t0:t0 + 512], reads=[g.gT_r], writes=[gt.res])
                                    t1, t2 = tmp[cnt % 2], tmp2[cnt % 2]
                                    acc = None
                                    for bi, b in enumerate(branches):
                                        gt = gts[b][cnt % 2]
                                        last = (bi == len(branches) - 1)
                                        if acc is None:
                                            dst = mT[:, nbg, t0:t0 + 512] if last else t1[:]
                                            S.op("dve", lambda e, dst=dst, pt=pts[b], gt=gt: e.tensor_tensor(out=dst, in0=pt[:, :], in1=gt[:], op=ALU.mult),
                                                 reads=[pts[b].res, gt.res], writes=[mT.res if last else t1.res])
                                            acc = t1
                                        else:
                                            S.op("dve", lambda e, pt=pts[b], gt=gt: e.tensor_tensor(out=t2[:], in0=pt[:, :], in1=gt[:], op=ALU.mult),
                                                 reads=[pts[b].res, gt.res], writes=[t2.res])
                                            dst = mT[:, nbg, t0:t0 + 512] if last else t1[:]
                                            S.op("dve", lambda e, dst=dst: e.tensor_tensor(out=dst, in0=t1[:], in1=t2[:], op=ALU.add),
                                                 reads=[t1.res, t2.res], writes=[mT.res if last else t1.res])
                                    cnt += 1
            if "merged" in dbg:
                S.dma("sp", DBG["merged" + g.tag].rearrange("k p t -> p k t"), mT[:], reads=[mT.res], writes=[ORES["dbg_merged" + g.tag]])
            wb = [ph.sb([128, 16, 512], BF16, "wb") for _ in range(2)]
            resid_gemm(ph, l, g, W["w_out"][l], 16, 512, mT, g.ntok, 0, cols.ga1, None, cols.res, wb)

    def resid_gemm(ph, l, g, Wd, KC, slabw, xin, ntok, tok0_x, ga_fn, gb_fn, cres, wb, tok_base=0):
        xr = [ph.sb([128, 512], F32, "xr") for _ in range(3)]
        tb = [ph.sb([128, 512], F32, "tb") for _ in range(3)]
        st = {"i": 0}

        def epi(c0, t0, pt, nr, ncv):
            i = st["i"] % 3
            st["i"] += 1
            nb = c0 // 128
            tg = tok_base + (t0 - tok0_x)
            S.dma("sp", xr[i][:], g.xT[nb, :, tg:tg + 512], reads=[g.xT_r], writes=[xr[i].res])
            if gb_fn is None:
                S.op("dve", lambda e: e.scalar_tensor_tensor(out=xr[i][:], in0=pt[:, :], scalar=ga_fn(nb), in1=xr[i][:],
                                                             op0=ALU.mult, op1=ALU.add), reads=[pt.res, xr[i].res] + cres, writes=[xr[i].res])
            else:
                S.op("act", lambda e: e.activation(out=tb[i][:], in_=pt[:, :], func=AF.Identity, scale=ga_fn(nb), bias=gb_fn(nb)),
                     reads=[pt.res] + cres, writes=[tb[i].res])
                S.op("dve", lambda e: e.tensor_tensor(out=xr[i][:], in0=xr[i][:], in1=tb[i][:], op=ALU.add),
                     reads=[xr[i].res, tb[i].res], writes=[xr[i].res])
            S.dma("sp", g.xT[nb, :, tg:tg + 512], xr[i][:], reads=[xr[i].res], writes=[g.xT_r], acc=True)
        gemm(ph, Wd, KC, 0, D, slabw, xin, ntok, "fm", epi, wb, tok0=tok0_x)

    def mlp(l, g):
        with Phase(C, "ff1") as ph:
            cols = load_cols(ph, l, g)
            hT = ph.sb([128, 16, g.ntok], BF16, "h2T")
            with Phase(C, "nrm2") as ph2:
                norm_mod(ph2, g, cols.a2, cols.sh2, cols.res, hT)
            wb = [ph.sb([128, 16, 512], BF16, "wb") for _ in range(2)]
            rt = [ph.sb([128, 512], F32, "rt") for _ in range(3)]
            ob = [ph.sb([128, 512], BF16, "ob") for _ in range(3)]
            st = {"i": 0}

            def epi(c0, t0, pt, nr, ncv):
                i = st["i"] % 3
                st["i"] += 1
                nb = c0 // 128
                S.op("act", lambda e: e.activation(out=rt[i][:], in_=pt[:, :], func=AF.Relu, bias=cols.b1(nb), scale=1.0),
                     reads=[pt.res] + cols.res, writes=[rt[i].res])
                S.op("pool", lambda e: e.tensor_tensor(out=ob[i][:], in0=rt[i][:], in1=rt[i][:], op=ALU.mult),
                     reads=[rt[i].res], writes=[ob[i].res])
                S.dma("sp", g.f1T[nb, :, t0:t0 + 512], ob[i][:], reads=[ob[i].res], writes=[g.f1T_r], acc=True)
            gemm(ph, W["w_ff1"][l], 16, 0, D_FF, 512, hT, g.ntok, "fm", epi, wb)
        with Phase(C, "ff2") as ph:
            cols = load_cols(ph, l, g)
            f1 = [ph.sb([128, 64, 512], BF16, "f1") for _ in range(1)]
            wb = [ph.sb([128, 64, 128], BF16, "wb2") for _ in range(2)]
            fv = g.f1T.rearrange("k p t -> p k t")
            for tt in range(g.ntok // 512):
                ft = f1[tt % len(f1)]
                for q in range(4):
                    S.dma("sp", ft[:, q * 16:(q + 1) * 16, :], fv[:, q * 16:(q + 1) * 16, tt * 512:(tt + 1) * 512],
                          reads=[g.f1T_r], writes=[ft.res], acc=(q > 0))
                resid_gemm(ph, l, g, W["w_ff2"][l], 64, 128, ft, 512, 0, cols.ga2, cols.gb2, cols.res, wb, tok_base=tt * 512)

    def final_norm(g, dst, dst_res):
        with Phase(C, "fin") as ph:
            fg = ph.sb([128, 16], F32, "fg")
            S.dma("sp", fg[:], W["final_g"].rearrange("(j p) -> p j", p=128), writes=[fg.res], slow=True)
            TW = 256
            xt = [ph.sb([128, 16, TW], F32, "nx") for _ in range(2)]
            tm = [ph.sb([128, 16, TW], F32, "nt") for _ in range(2)]
            rs = [ph.sb([128, TW], F32, "nr") for _ in range(2)]
            yo = [ph.sb([128, D], F32, "yo") for _ in range(2)]
            xv = g.xT.rearrange("k p t -> p k t")
            for tt in range(g.ntok // TW):
                x_, t_, r_ = xt[tt % 2], tm[tt % 2], rs[tt % 2]
                S.dma("sp", x_[:], xv[:, :, tt * TW:(tt + 1) * TW], reads=[g.xT_r], writes=[x_.res])
                S.op("act", lambda e: e.activation(out=t_[:], in_=x_[:], func=AF.Square), reads=[x_.res], writes=[t_.res])
                pt = next_ps()
                fns = [lambda pe, kc=kc: pe.matmul(pt[:, 0:TW], lhsT=ones[:], rhs=t_[:, kc, :], start=(kc == 0), stop=(kc == 15))
                       for kc in range(16)]
                S.mm(fns, reads=[ones.res, t_.res], writes=[pt.res])
                S.op("dve", lambda e: e.tensor_scalar(out=r_[:], in0=pt[:, 0:TW], scalar1=1.0 / D, scalar2=1e-6, op0=ALU.mult,
                                                      op1=ALU.add), reads=[pt.res], writes=[r_.res])
                S.op("act", lambda e: e.sqrt(out=r_[:], in_=r_[:]), reads=[r_.res], writes=[r_.res])
                S.op("dve", lambda e: e.reciprocal(out=r_[:], in_=r_[:]), reads=[r_.res], writes=[r_.res])
                S.op("dve", lambda e: e.tensor_tensor(out=t_[:], in0=x_[:], in1=r_[:].unsqueeze(1).broadcast_to([128, 16, TW]),
                                                      op=ALU.mult), reads=[x_.res, r_.res], writes=[t_.res])
                for kc in range(16):
                    S.op("act", lambda e, kc=kc: e.activation(out=t_[:, kc, :], in_=t_[:, kc, :], func=AF.Identity,
                                                              scale=fg[:, kc:kc + 1]), reads=[t_.res, fg.res], writes=[t_.res])
                for sub in range(TW // 128):
                    y_ = yo[sub % 2]
                    for kq in range(4):
                        pt2 = next_ps()
                        fns = [lambda pe, j=j, pt2=pt2, kq=kq: pe.transpose(pt2[:, j * 128:(j + 1) * 128],
                                                                             t_[:, kq * 4 + j, sub * 128:(sub + 1) * 128], ident[:])
                               for j in range(4)]
                        S.mm(fns, reads=[t_.res, ident.res], writes=[pt2.res])
                        eng = alt_eng()
                        if eng == "act":
                            S.op("act", lambda e, pt2=pt2, kq=kq: e.copy(out=y_[:, kq * 512:(kq + 1) * 512], in_=pt2[:, :]),
                                 reads=[pt2.res], writes=[y_.res])
                        else:
                            S.op("dve", lambda e, pt2=pt2, kq=kq: e.tensor_copy(out=y_[:, kq * 512:(kq + 1) * 512], in_=pt2[:, :]),
                                 reads=[pt2.res], writes=[y_.res])
                    r0 = tt * TW + sub * 128
                    S.dma("sp", dst[r0:r0 + 128, :], y_[:], reads=[y_.res], writes=[dst_res], acc=True)

    for name in dbg:
        for g in G:
            if name in ("h", "merged"):
                dbg_out(name + g.tag, [16, 128, g.ntok], BF16)
            if name == "x":
                dbg_out(name + g.tag, [16, 128, g.ntok], F32)
            if name == "oT":
                for b in range(3):
                    dbg_out("oT%d%s" % (b, g.tag), [8, 128, g.ntok], BF16)
    load_x_T(G[0], I["xp"])
    load_x_T(G[1], I["xs"])
    for l in range(nlayers):
        modulation(l)
        for g in G:
            if g.i not in cfg.get("groups", (0, 1)):
                continue
            in_proj(l, g, None)
            if "a" in mixers and not cfg.get("noattn"):
                (attn_prompt if g.i == 0 else attn_sample)(l, g)
            if "r" in mixers:
                L_ = 256 if g.i == 0 else 2048
                rwkv_pre(l, g, L_)
                rwkv_scan(l, g, L_)
                rwkv_post(l, g)
            if "c" in mixers:
                L_ = 256 if g.i == 0 else 2048
                hyena_filter(l, L_)
                hyena_conv(l, g, L_)
                hyena_dft(l, g, L_)
            if "oT" in dbg:
                for b in range(3):
                    if "arc"[b] in mixers:
                        with Phase(C, "dbgo") as phd:
                            S.dma("sp", DBG["oT%d%s" % (b, g.tag)], g.oT[b][0], reads=[g.oT[b][1]], writes=[ORES["dbg_oT%d%s" % (b, g.tag)]])
            merge_out(l, g)
            mlp(l, g)
    if "x" in dbg:
        for g in G:
            with Phase(C, "dbgx") as ph:
                S.dma("sp", DBG["x" + g.tag], g.xT, reads=[g.xT_r], writes=[ORES["dbg_x" + g.tag]])
    final_norm(G[0], O["yp"], ORES["yp"])
    final_norm(G[1], O["ys"], ORES["ys"])
    S.barrier()
    cst.es.__exit__(None, None, None)
    top.close()
    C.ninst = S.ninst
    return nc, C


def make_in_maps(inputs):
    f = lambda a: np.ascontiguousarray(np.asarray(a, dtype=np.float32))
    maps = []
    wnames = ["ln1_g", "ln2_g", "w_mod", "b_mod", "w_in", "rpb", "wkv_conv_w", "wkv_conv_b", "wkv_w0", "wkv_w1", "wkv_w2",
              "wkv_a0", "wkv_a1", "wkv_a2", "wkv_g1", "wkv_g2", "wkv_k_k", "wkv_k_a", "wkv_r_k", "wkv_gn_g", "wkv_gn_b",
              "hy_conv_w", "hy_conv_b", "hy_f1", "hy_fb1", "hy_f2", "hy_fb2", "hy_freq", "hy_f3", "hy_d", "w_pa", "w_pr",
              "w_pc", "w_out", "w_ff1", "b_ff1", "w_ff2", "b_ff2", "final_g"]
    wd = {k: f(inputs[k]) for k in wnames}
    wd["wkv_r_k"] = wd["wkv_r_k"].reshape(DEPTH, 1024)
    for i in range(8):
        b = i // 2
        m = dict(wd)
        m["xp"] = f(inputs["x_prompt"][4 * i:4 * i + 4]).reshape(NP_TOK, D)
        m["xs"] = f(inputs["x_sample"][b])
        m["ck"] = f(inputs["cache_k"][b]).reshape(DEPTH, 256, 1024)
        m["cv"] = f(inputs["cache_v"][b]).reshape(DEPTH, 256, 1024)
        m["s0"] = f(inputs["state_wkv"][b]).reshape(DEPTH, 128, 1024)
        m["cvec"] = np.stack([f(inputs["c_ctx"]), f(inputs["c"][b])])
        m.update(CONSTS)
        maps.append(m)
    return maps


def _make_consts():
    cst = {}
    cq = np.arange(64)
    c0 = np.clip(cq - 8, 0, 48)
    ck = np.arange(64)
    ok = (ck[None, :] >= c0[:, None]) & (ck[None, :] < c0[:, None] + 16)
    cst["natmask"] = np.where(ok, 0.0, -1e30).astype(np.float32)
    for L in (256, 2048):
        t = np.linspace(0.0, 1.0, L, dtype=np.float32)[:, None]
        w = 2.0 * np.pi * np.arange(L, dtype=np.float32)[:, None] / L
        f = np.linspace(1e-4, 15, 16, dtype=np.float32)[None, :]
        z = np.concatenate([t, np.cos(f * w), -np.sin(f * w)], -1).astype(np.float32)
        cst["zposT%d" % L] = np.ascontiguousarray(z.T)
        dist = (np.abs(np.arange(L) - L // 2).astype(np.float32) / L)[:, None]
        deltas = np.abs(np.linspace(math.log(1e-2) / 1.5, math.log(1e-2) / 0.3, 1024, dtype=np.float32))[None, :]
        cst["win%d" % L] = np.exp(-dist * deltas).astype(np.float32)
        n = 2 * L
        k = np.arange(L, dtype=np.float64)
        om = 2.0 * np.pi * (k + 0.5) / n
        tt = np.arange(L, dtype=np.float64)
        ang = tt[:, None] * om[None, :]
        Ff = np.zeros((L, 2 * L), np.float32)
        Ffv = Ff.reshape(L, L // 128, 2, 128)
        Ffv[:, :, 0, :] = np.cos(ang).reshape(L, L // 128, 128)
        Ffv[:, :, 1, :] = (-np.sin(ang)).reshape(L, L // 128, 128)
        cst["Ff%d" % L] = Ff
        angi = om[:, None] * (tt[None, :] + L // 2)
        Fi = np.zeros((2 * L, L), np.float32)
        Fiv = Fi.reshape(L // 128, 2, 128, L)
        Fiv[:, 0] = ((2.0 / n) * np.cos(angi)).reshape(L // 128, 128, L)
        Fiv[:, 1] = (-(2.0 / n) * np.sin(angi)).reshape(L // 128, 128, L)
        cst["Fi%d" % L] = Fi
    return cst


CONSTS = _make_consts()
_CACHE = {}


def kernel(**inputs):
    if "nc" not in _CACHE:
        _CACHE["nc"] = build({})[0]
    nc = _CACHE["nc"]
    maps = make_in_maps(inputs)
    res = run_bass_kernel_spmd(nc, maps, core_ids=list(range(8)))
    R = res.results
    yp = np.concatenate([R[i]["yp"].reshape(4, 256, D) for i in range(8)], 0)
    ys = np.stack([R[2 * b]["ys"] for b in range(4)], 0)
    nk = np.concatenate([R[i]["nk"].reshape(4, DEPTH, 256, 16, 64) for i in range(8)], 0)
    nv = np.concatenate([R[i]["nv"].reshape(4, DEPTH, 256, 16, 64) for i in range(8)], 0)
    ns = np.concatenate([R[i]["ns"].reshape(4, DEPTH, 2, 16, 64, 64) for i in range(8)], 0)
    return (yp.astype(np.float32), ys.astype(np.float32), nk.astype(np.float32), nv.astype(np.float32), ns.astype(np.float32))
```

```python
import math
from contextlib import ExitStack

import numpy as np
import concourse.bass as bass
import concourse.mybir as mybir
from concourse.bass_utils import run_bass_kernel_spmd

F32 = mybir.dt.float32
BF16 = mybir.dt.bfloat16
I32 = mybir.dt.int32
AF = mybir.ActivationFunctionType
ALU = mybir.AluOpType
AX = mybir.AxisListType
AP = bass.AP

D = 2048
DEPTH = 4
NP_TOK = 1024
NS_TOK = 2048
N_IN = 15360
D_FF = 8192


class Res:
    __slots__ = ("name", "w", "a", "r")

    def __init__(self, name):
        self.name = name
        self.w = {}
        self.a = {}
        self.r = {}


class Tile:
    def __init__(self, t, name):
        self.t = t
        self.res = Res(name)

    def __getitem__(self, k):
        return self.t[k]


class Sched:
    NDS = 40
    NPOOL = 8

    def __init__(self, nc, es):
        self.nc = nc
        self.E = {"pe": nc.tensor, "act": nc.scalar, "dve": nc.vector, "pool": nc.gpsimd, "sp": nc.sync}
        self.sem = {k: es.enter_context(nc.semaphore("c_" + k)) for k in self.E}
        self.cnt = {k: 0 for k in self.E}
        self.seen = {k: {} for k in self.E}
        self.dsem = [es.enter_context(nc.semaphore("d%d" % i)) for i in range(self.NDS)]
        self.dcnt = [0] * self.NDS
        self.dnext = 0
        self.dnext_pool = 0
        self.ninst = 0
        self.nwait = 0

    def _semobj(self, key):
        return self.sem[key] if isinstance(key, str) else self.dsem[key]

    def _wait(self, eng, deps, defer=False):
        need = {}
        for (k, v) in deps:
            if need.get(k, 0) < v:
                need[k] = v
        sn = self.seen[eng]
        todo = [(k, v) for k, v in need.items() if sn.get(k, 0) < v]
        last = None
        if defer and todo:
            last = todo.pop()
        for k, v in todo:
            self.E[eng].wait_ge(self._semobj(k), v)
            sn[k] = v
            self.ninst += 1
            self.nwait += 1
        if last is not None:
            sn[last[0]] = last[1]
        return last

    def _attach(self, ins, last):
        if last is not None:
            ins._wait_ge(self._semobj(last[0]), last[1])

    def _deps(self, eng, reads, writes, is_dma=False, acc=False):
        deps = []
        for r in reads:
            deps.extend(r.w.items())
            deps.extend(r.a.items())
        for w in writes:
            srcs = [w.w, w.r] if acc else [w.w, w.a, w.r]
            for d in srcs:
                deps.extend(d.items())
        return deps

    def _commit(self, tok, reads, writes, acc=False):
        k, v = tok
        for r in reads:
            r.r[k] = v
        for w in writes:
            if acc:
                w.a[k] = v
            else:
                w.w = {k: v}
                w.a = {}
                w.r = {}

    def op(self, eng, fn, reads=(), writes=()):
        last = self._wait(eng, self._deps(eng, reads, writes), defer=True)
        ins = fn(self.E[eng])
        self._attach(ins, last)
        self.cnt[eng] += 1
        ins.then_inc(self.sem[eng], 1)
        self.ninst += 1
        self._commit((eng, self.cnt[eng]), reads, writes)

    def mm(self, fns, reads, writes):
        last = self._wait("pe", self._deps("pe", reads, writes), defer=True)
        pe = self.E["pe"]
        ins = None
        for j, f in enumerate(fns):
            ins = f(pe)
            if j == 0:
                self._attach(ins, last)
        self.ninst += len(fns)
        self.cnt["pe"] += 1
        ins.then_inc(self.sem["pe"], 1)
        self._commit(("pe", self.cnt["pe"]), reads, writes)

    def dma(self, q, out, in_, reads=(), writes=(), acc=False, slow=False):
        if q == "pool":
            i = self.NDS - self.NPOOL + self.dnext_pool
            self.dnext_pool = (self.dnext_pool + 1) % self.NPOOL
        else:
            i = self.dnext
            self.dnext = (i + 1) % (self.NDS - self.NPOOL)
        deps = self._deps(q, reads, writes, is_dma=True, acc=acc)
        if self.dcnt[i] > 0:
            deps.append((i, self.dcnt[i]))
        last = self._wait(q, deps, defer=True)
        if slow:
            ins = self.E[q].dma_start(out=out, in_=in_, allow_slow_non_contiguous=True)
        else:
            ins = self.E[q].dma_start(out=out, in_=in_)
        self._attach(ins, last)
        ins.then_inc(self.dsem[i], 16)
        self.ninst += 1
        self.dcnt[i] += 16
        self._commit((i, self.dcnt[i]), reads, writes, acc=acc)

    def barrier(self):
        deps = [(k, self.cnt[k]) for k in self.E if k != "sp" and self.cnt[k] > 0]
        deps += [(i, c) for i, c in enumerate(self.dcnt) if c > 0]
        self._wait("sp", deps)
        ins = self.E["sp"].nop()
        self.cnt["sp"] += 1
        ins.then_inc(self.sem["sp"], 1)
        for k in self.E:
            if k != "sp":
                self._wait(k, [("sp", self.cnt["sp"])])
        for k in self.E:
            for k2 in self.E:
                self.seen[k][k2] = self.cnt[k2]
            for i, c in enumerate(self.dcnt):
                self.seen[k][i] = c


class Ctx:
    pass


def _col_ap(dram_ap_1d, n):
    return dram_ap_1d.rearrange("(j p) -> p j", p=128)


class Phase:
    def __init__(self, C, name):
        self.C = C
        self.name = name
        self.es = ExitStack()
        self.n = 0

    def __enter__(self):
        self.es.__enter__()
        return self

    def sb(self, shape, dt=F32, name=None):
        self.n += 1
        nm = "%s_%s_%d" % (self.name, name or "t", self.C.uid())
        t = self.es.enter_context(self.C.nc.sbuf_tensor(nm, list(shape), dt))
        return Tile(t, nm)

    def __exit__(self, *a):
        self.C.S.barrier()
        return self.es.__exit__(*a)


def build(cfg):
    nlayers = cfg.get("nlayers", DEPTH)
    dbg = cfg.get("dbg", ())
    mixers = cfg.get("mixers", ("a", "r", "c"))
    nc = bass.Bass("TRN2", target_bir_lowering=False)
    C = Ctx()
    C.nc = nc
    C._uid = 0

    def uid():
        C._uid += 1
        return C._uid
    C.uid = uid
    top = ExitStack()
    S = Sched(nc, top)
    C.S = S

    def din(name, shape, dt=F32):
        return nc.dram_tensor(name, list(shape), dt, kind="ExternalInput").ap()

    def dout(name, shape, dt=F32):
        return nc.dram_tensor(name, list(shape), dt, kind="ExternalOutput").ap()

    def dscr(name, shape, dt=F32):
        a = nc.dram_tensor(name, list(shape), dt, kind="Internal").ap()
        return a, Res(name)

    I = {}
    I["xp"] = din("xp", [NP_TOK, D])
    I["xs"] = din("xs", [NS_TOK, D])
    I["ck"] = din("ck", [DEPTH, 256, 1024])
    I["cv"] = din("cv", [DEPTH, 256, 1024])
    I["s0"] = din("s0", [DEPTH, 128, 1024])
    I["cvec"] = din("cvec", [2, D])
    wshapes = {
        "ln1_g": [DEPTH, D], "ln2_g": [DEPTH, D], "w_mod": [DEPTH, D, 6 * D], "b_mod": [DEPTH, 6 * D],
        "w_in": [DEPTH, D, N_IN], "rpb": [DEPTH, 16, 15, 31],
        "wkv_conv_w": [DEPTH, 3, 3072], "wkv_conv_b": [DEPTH, 3072], "wkv_w0": [DEPTH, 2, 1024],
        "wkv_w1": [DEPTH, 2, D, 64], "wkv_w2": [DEPTH, 2, 64, 1024], "wkv_a0": [DEPTH, 2, 1024],
        "wkv_a1": [DEPTH, 2, D, 64], "wkv_a2": [DEPTH, 2, 64, 1024], "wkv_g1": [DEPTH, D, 128],
        "wkv_g2": [DEPTH, 128, 1024], "wkv_k_k": [DEPTH, 1024], "wkv_k_a": [DEPTH, 1024],
        "wkv_r_k": [DEPTH, 1024], "wkv_gn_g": [DEPTH, 1024], "wkv_gn_b": [DEPTH, 1024],
        "hy_conv_w": [DEPTH, 3, 3072], "hy_conv_b": [DEPTH, 3072], "hy_f1": [DEPTH, 33, 64],
        "hy_fb1": [DEPTH, 64], "hy_f2": [DEPTH, 64, 64], "hy_fb2": [DEPTH, 64], "hy_freq": [DEPTH, 64],
        "hy_f3": [DEPTH, 64, 1024], "hy_d": [DEPTH, 1024],
        "w_pa": [DEPTH, 1024, D], "w_pr": [DEPTH, 1024, D], "w_pc": [DEPTH, 1024, D], "w_out": [DEPTH, D, D],
        "w_ff1": [DEPTH, D, D_FF], "b_ff1": [DEPTH, D_FF], "w_ff2": [DEPTH, D_FF, D], "b_ff2": [DEPTH, D],
        "final_g": [D],
    }
    W = {k: din(k, s) for k, s in wshapes.items()}
    O = {}
    O["yp"] = dout("yp", [NP_TOK, D])
    O["ys"] = dout("ys", [NS_TOK, D])
    O["nk"] = dout("nk", [4 * DEPTH * 256, 1024])
    O["nv"] = dout("nv", [4 * DEPTH * 256, 1024])
    O["ns"] = dout("ns", [4 * DEPTH * 32, 4096])
    ORES = {k: Res("o_" + k) for k in O}
    DBG = {}

    def dbg_out(name, shape, dt=F32):
        DBG[name] = dout("dbg_" + name, shape, dt)
        ORES["dbg_" + name] = Res("dbg_" + name)
        return DBG[name], ORES["dbg_" + name]

    G = []
    for gi, ntok in enumerate((NP_TOK, NS_TOK)):
        g = Ctx()
        g.i = gi
        g.ntok = ntok
        g.tag = "ps"[gi]
        g.xT, g.xT_r = dscr("xT%d" % gi, [16, 128, ntok])
        g.qkT, g.qkT_r = dscr("qkT%d" % gi, [16, 128, ntok], BF16)
        g.vtm, g.vtm_r = dscr("vtm%d" % gi, [ntok, 1024], BF16)
        g.rh, g.rh_r = dscr("rh%d" % gi, [ntok, 6144])
        g.gT, g.gT_r = dscr("gT%d" % gi, [48, 128, ntok])
        g.oT = []
        for b in range(3):
            g.oT.append(dscr("oT%d_%d" % (gi, b), [8, 128, ntok], BF16))
        g.f1T, g.f1T_r = dscr("f1T%d" % gi, [64, 128, ntok], BF16)
        G.append(g)
    mrow, mrow_r = dscr("mrow", [2, 6 * D])
    zero_r = Res("zeros")

    cst = Phase(C, "cst")
    cst.es.__enter__()
    ident = cst.sb([128, 128], F32, "ident")
    identb = cst.sb([128, 128], BF16, "identb")
    ones = cst.sb([128, 128], F32, "ones")
    S.op("pool", lambda e: e.memset(ident[:], 1.0), writes=[ident.res])
    S.op("pool", lambda e: e.affine_select(out=ident[:], in_=ident[:], pattern=[[-1, 128]], compare_op=ALU.is_equal,
                                            fill=0.0, base=0, channel_multiplier=1), reads=[ident.res], writes=[ident.res])
    S.op("dve", lambda e: e.tensor_copy(out=identb[:], in_=ident[:]), reads=[ident.res], writes=[identb.res])
    S.op("dve", lambda e: e.memset(ones[:], 1.0), writes=[ones.res])
    psum = []
    for i in range(8):
        t = top.enter_context(nc.psum_tensor("ps%d" % i, [128, 512], F32))
        psum.append(Tile(t, "ps%d" % i))
    C.ps_rr = 0

    def next_ps(lo=0, hi=8):
        C.ps_rr = (C.ps_rr + 1) % (hi - lo)
        return psum[lo + C.ps_rr]

    C.eng_rr = 0

    def alt_eng():
        C.eng_rr ^= 1
        return "act" if C.eng_rr else "dve"

    def gemm(ph, Wd, KC, col0, ncols, slabw, xT, ntok, form, epi, wbufs, Mrows=128, tok0=0, ps_lo=0, ps_hi=8):
        nslab = (ncols + slabw - 1) // slabw
        Wv = Wd.rearrange("(k p) n -> p k n", p=128)

        def load(s):
            wb = wbufs[s % len(wbufs)]
            c0 = col0 + s * slabw
            cw = min(slabw, col0 + ncols - c0)
            S.dma("pool", wb[:, :, 0:cw], Wv[:, :, c0:c0 + cw], writes=[wb.res])
        load(0)
        for s in range(nslab):
            if s + 1 < nslab:
                load(s + 1)
            wb = wbufs[s % len(wbufs)]
            c0 = col0 + s * slabw
            cw = min(slabw, col0 + ncols - c0)
            if form == "fm":
                for nb in range((cw + 127) // 128):
                    mw = min(128, cw - nb * 128)
                    for tt in range(ntok // 512):
                        pt = next_ps(ps_lo, ps_hi)
                        t0 = tok0 + tt * 512
                        fns = []
                        for kc in range(KC):
                            fns.append(lambda pe, kc=kc, pt=pt, wb=wb, nb=nb, mw=mw, t0=t0: pe.matmul(
                                pt[0:mw, :], lhsT=wb[:, kc, nb * 128:nb * 128 + mw], rhs=xT[:, kc, t0:t0 + 512],
                                start=(kc == 0), stop=(kc == KC - 1)))
                        S.mm(fns, reads=[wb.res, xT.res], writes=[pt.res])
                        epi(c0 + nb * 128, t0, pt, mw, 512)
            else:
                for tt in range(ntok // Mrows if Mrows == 128 else 1):
                    t0 = tok0 + tt * 128
                    for nh in range((cw + 511) // 512):
                        nw = min(512, cw - nh * 512)
                        pt = next_ps(ps_lo, ps_hi)
                        fns = []
                        for kc in range(KC):
                            fns.append(lambda pe, kc=kc, pt=pt, wb=wb, nh=nh, nw=nw, t0=t0: pe.matmul(
                                pt[0:Mrows, 0:nw], lhsT=xT[:, kc, t0:t0 + Mrows], rhs=wb[:, kc, nh * 512:nh * 512 + nw],
                                start=(kc == 0), stop=(kc == KC - 1)))
                        S.mm(fns, reads=[wb.res, xT.res], writes=[pt.res])
                        epi(c0 + nh * 512, t0, pt, Mrows, nw)

    def epi_store(ph, obufs, dst_fn, dst_res, func=None, bias_fn=None, scale=1.0):
        st = {"i": 0}

        def epi(c0, t0, pt, nr, ncv):
            ob = obufs[st["i"] % len(obufs)]
            st["i"] += 1
            if func is not None or bias_fn is not None:
                b = bias_fn(c0) if bias_fn is not None else None
                rd = [pt.res] + ([b[1]] if b is not None else [])
                S.op("act", lambda e: e.activation(out=ob[0:nr, 0:ncv], in_=pt[0:nr, 0:ncv], func=func or AF.Identity,
                                                   bias=(b[0] if b is not None else 0.0), scale=scale),
                     reads=rd, writes=[ob.res])
            else:
                eng = alt_eng()
                if eng == "act":
                    S.op("act", lambda e: e.copy(out=ob[0:nr, 0:ncv], in_=pt[0:nr, 0:ncv]), reads=[pt.res], writes=[ob.res])
                else:
                    S.op("dve", lambda e: e.tensor_copy(out=ob[0:nr, 0:ncv], in_=pt[0:nr, 0:ncv]), reads=[pt.res], writes=[ob.res])
            S.dma("sp", dst_fn(c0, t0, nr, ncv), ob[0:nr, 0:ncv], reads=[ob.res], writes=[dst_res], acc=True)
        return epi

    def load_x_T(g, src):
        with Phase(C, "ldx") as ph:
            xin = [ph.sb([128, D], F32, "xin") for _ in range(2)]
            xo = [ph.sb([128, 16, 128], F32, "xo") for _ in range(2)]
            for tt in range(g.ntok // 128):
                xi = xin[tt % 2]
                xq = xo[tt % 2]
                S.dma("sp", xi[:], src[tt * 128:(tt + 1) * 128, :], writes=[xi.res])
                for kq in range(4):
                    pt = next_ps()
                    fns = [lambda pe, j=j, pt=pt, xi=xi, kq=kq: pe.transpose(pt[:, j * 128:(j + 1) * 128],
                                                                                xi[:, (kq * 4 + j) * 128:(kq * 4 + j + 1) * 128], ident[:])
                           for j in range(4)]
                    S.mm(fns, reads=[xi.res, ident.res], writes=[pt.res])
                    eng = alt_eng()
                    dst = xq[:, kq * 4:(kq + 1) * 4, :]
                    srcp = pt[:, :].rearrange("p (a b) -> p a b", b=128)
                    if eng == "act":
                        S.op("act", lambda e: e.copy(out=dst, in_=srcp), reads=[pt.res], writes=[xq.res])
                    else:
                        S.op("dve", lambda e: e.tensor_copy(out=dst, in_=srcp), reads=[pt.res], writes=[xq.res])
                S.dma("sp", g.xT.rearrange("k p t -> p k t")[:, :, tt * 128:(tt + 1) * 128], xq[:], reads=[xq.res],
                      writes=[g.xT_r], acc=True)

    def modulation(l):
        with Phase(C, "mod") as ph:
            cT = ph.sb([128, 16, 2], F32, "cT")
            cTb = ph.sb([128, 16, 2], BF16, "cTb")
            for gi in range(2):
                S.dma("sp", cT[:, :, gi], I["cvec"][gi].rearrange("(k p) -> p k", p=128), writes=[cT.res], slow=True, acc=(gi > 0))
            S.op("act", lambda e: e.activation(out=cTb[:], in_=cT[:], func=AF.Silu), reads=[cT.res], writes=[cTb.res])
            wb = [ph.sb([128, 16, 512], BF16, "wb") for _ in range(2)]
            bm = [ph.sb([2, 512], F32, "bm") for _ in range(2)]
            ob = [ph.sb([2, 512], F32, "ob") for _ in range(2)]
            st = {"i": 0}

            def epi(c0, t0, pt, nr, ncv):
                i = st["i"] % 2
                st["i"] += 1
                S.dma("sp", bm[i][:], AP(W["b_mod"].tensor, l * 6 * D + c0, [[0, 2], [1, 512]]), writes=[bm[i].res])
                S.op("dve", lambda e: e.tensor_tensor(out=ob[i][:], in0=pt[0:2, :], in1=bm[i][:], op=ALU.add),
                     reads=[pt.res, bm[i].res], writes=[ob[i].res])
                S.dma("sp", mrow[:, c0:c0 + 512], ob[i][:], reads=[ob[i].res], writes=[mrow_r], acc=True)
            gemm(ph, W["w_mod"][l], 16, 0, 6 * D, 512, cTb, 2, "tm", epi, wb, Mrows=2)

    def load_cols(ph, l, g):
        cols = Ctx()
        m = ph.sb([128, 96], F32, "mcol")
        S.dma("sp", m[:], mrow[g.i].rearrange("(j p) -> p j", p=128), reads=[mrow_r], writes=[m.res], slow=True)
        ln = ph.sb([128, 32], F32, "lncol")
        S.dma("sp", ln[:, 0:16], W["ln1_g"][l].rearrange("(j p) -> p j", p=128), writes=[ln.res], slow=True)
        S.dma("sp", ln[:, 16:32], W["ln2_g"][l].rearrange("(j p) -> p j", p=128), writes=[ln.res], slow=True, acc=True)
        bf = ph.sb([128, 80], F32, "bfcol")
        S.dma("sp", bf[:, 0:64], W["b_ff1"][l].rearrange("(j p) -> p j", p=128), writes=[bf.res], slow=True)
        S.dma("sp", bf[:, 64:80], W["b_ff2"][l].rearrange("(j p) -> p j", p=128), writes=[bf.res], slow=True, acc=True)
        d = ph.sb([128, 48], F32, "dcol")
        S.op("dve", lambda e: e.scalar_tensor_tensor(out=d[:, 0:16], in0=m[:, 16:32], scalar=1.0, in1=ln[:, 0:16],
                                                     op0=ALU.add, op1=ALU.mult), reads=[m.res, ln.res], writes=[d.res])
        S.op("dve", lambda e: e.scalar_tensor_tensor(out=d[:, 16:32], in0=m[:, 64:80], scalar=1.0, in1=ln[:, 16:32],
                                                     op0=ALU.add, op1=ALU.mult), reads=[m.res, ln.res, d.res], writes=[d.res])
        S.op("dve", lambda e: e.tensor_tensor(out=d[:, 32:48], in0=m[:, 80:96], in1=bf[:, 64:80], op=ALU.mult),
             reads=[m.res, bf.res, d.res], writes=[d.res])
        cols.m, cols.d, cols.bf = m, d, bf
        cols.sh1 = lambda kc: m[:, kc:kc + 1]
        cols.ga1 = lambda kc: m[:, 32 + kc:33 + kc]
        cols.sh2 = lambda kc: m[:, 48 + kc:49 + kc]
        cols.ga2 = lambda kc: m[:, 80 + kc:81 + kc]
        cols.a1 = lambda kc: d[:, kc:kc + 1]
        cols.a2 = lambda kc: d[:, 16 + kc:17 + kc]
        cols.gb2 = lambda kc: d[:, 32 + kc:33 + kc]
        cols.b1 = lambda j: bf[:, j:j + 1]
        cols.res = [m.res, d.res, bf.res]
        return cols

    def norm_mod(ph, g, a_fn, sh_fn, cres, hT):
        TW = 256
        xt = [ph.sb([128, 16, TW], F32, "nx") for _ in range(2)]
        tm = [ph.sb([128, 16, TW], F32, "nt") for _ in range(2)]
        rs = [ph.sb([128, TW], F32, "nr") for _ in range(2)]
        xv = g.xT.rearrange("k p t -> p k t")
        for tt in range(g.ntok // TW):
            x_, t_, r_ = xt[tt % 2], tm[tt % 2], rs[tt % 2]
            S.dma("sp", x_[:], xv[:, :, tt * TW:(tt + 1) * TW], reads=[g.xT_r], writes=[x_.res])
            S.op("act", lambda e: e.activation(out=t_[:], in_=x_[:], func=AF.Square), reads=[x_.res], writes=[t_.res])
            pt = next_ps()
            fns = [lambda pe, kc=kc: pe.matmul(pt[:, 0:TW], lhsT=ones[:], rhs=t_[:, kc, :], start=(kc == 0), stop=(kc == 15))
                   for kc in range(16)]
            S.mm(fns, reads=[ones.res, t_.res], writes=[pt.res])
            S.op("dve", lambda e: e.tensor_scalar(out=r_[:], in0=pt[:, 0:TW], scalar1=1.0 / D, scalar2=1e-6, op0=ALU.mult,
                                                  op1=ALU.add), reads=[pt.res], writes=[r_.res])
            S.op("act", lambda e: e.sqrt(out=r_[:], in_=r_[:]), reads=[r_.res], writes=[r_.res])
            S.op("dve", lambda e: e.reciprocal(out=r_[:], in_=r_[:]), reads=[r_.res], writes=[r_.res])
            S.op("dve", lambda e: e.tensor_tensor(out=t_[:], in0=x_[:], in1=r_[:].unsqueeze(1).broadcast_to([128, 16, TW]),
                                                  op=ALU.mult), reads=[x_.res, r_.res], writes=[t_.res])
            for kc in range(16):
                S.op("act", lambda e, kc=kc: e.activation(out=hT[:, kc, tt * TW:(tt + 1) * TW], in_=t_[:, kc, :],
                                                          func=AF.Identity, scale=a_fn(kc), bias=sh_fn(kc)),
                     reads=[t_.res] + cres, writes=[hT.res])

    def in_proj(l, g, cols_holder):
        with Phase(C, "inp") as ph:
            cols = load_cols(ph, l, g)
            hT = ph.sb([128, 16, g.ntok], BF16, "hT")
            with Phase(C, "nrm") as ph2:
                norm_mod(ph2, g, cols.a1, cols.sh1, cols.res, hT)
            if "h" in dbg:
                S.dma("sp", DBG["h" + g.tag].rearrange("k p t -> p k t"), hT[:], reads=[hT.res], writes=[ORES["dbg_h" + g.tag]])
            wb = [ph.sb([128, 16, 512], BF16, "wb") for _ in range(2)]
            Wl = W["w_in"][l]
            ob = [ph.sb([128, 512], F32, "ob") for _ in range(4)]
            gemm(ph, Wl, 16, 9216, 6144, 512, hT, g.ntok, "fm",
                 epi_store(ph, ob, lambda c0, t0, nr, ncv: g.gT[(c0 - 9216) // 128, :, t0:t0 + ncv], g.gT_r, func=AF.Sigmoid), wb)
            if "a" in mixers:
                obb = [ph.sb([128, 512], BF16, "obb") for _ in range(4)]
                gemm(ph, Wl, 16, 0, 2048, 512, hT, g.ntok, "fm",
                     epi_store(ph, obb, lambda c0, t0, nr, ncv: g.qkT[c0 // 128, :, t0:t0 + ncv], g.qkT_r), wb)
            obf = [ph.sb([128, 512], F32, "obf") for _ in range(3)]
            obv = [ph.sb([128, 512], BF16, "obv") for _ in range(3)]
            st = {"i": 0}

            def epi_kv(c0, t0, pt, nr, ncv):
                i = st["i"] % 3
                st["i"] += 1
                isv = c0 >= 2048
                cc = c0 - (2048 if isv else 1024)
                if g.i == 0:
                    S.op("dve", lambda e: e.tensor_copy(out=obf[i][:], in_=pt[:, :]), reads=[pt.res], writes=[obf[i].res])
                    key = "nv" if isv else "nk"
                    r0_ = ((t0 // 256) * DEPTH + l) * 256 + (t0 % 256)
                    S.dma("sp", O[key][r0_:r0_ + 128, cc:cc + 512], obf[i][:], reads=[obf[i].res],
                          writes=[ORES[key]], acc=True)
                    if isv:
                        S.op("pool", lambda e: e.tensor_copy(out=obv[i][:], in_=obf[i][:]), reads=[obf[i].res], writes=[obv[i].res])
                elif isv:
                    S.op("dve", lambda e: e.tensor_copy(out=obv[i][:], in_=pt[:, :]), reads=[pt.res], writes=[obv[i].res])
                if isv:
                    S.dma("sp", g.vtm[t0:t0 + 128, cc:cc + 512], obv[i][:], reads=[obv[i].res], writes=[g.vtm_r], acc=True)
            if ("a" in mixers or g.i == 0) and not cfg.get("nokv"):
                if g.i == 0:
                    gemm(ph, Wl, 16, 1024, 2048, 512, hT, g.ntok, "tm", epi_kv, wb)
                else:
                    gemm(ph, Wl, 16, 2048, 1024, 512, hT, g.ntok, "tm", epi_kv, wb)
            if "r" in mixers or "c" in mixers:
                gemm(ph, Wl, 16, 3072, 6144, 512, hT, g.ntok, "tm",
                     epi_store(ph, ob, lambda c0, t0, nr, ncv: g.rh[t0:t0 + nr, c0 - 3072:c0 - 3072 + ncv], g.rh_r), wb)
            if "r" in mixers:
                rwkv_lora(ph, l, g, hT)
            return None

    SCALE = 0.125

    def softmax_rows(nr, ncol, sc_ap, sc_res, scale, small, pn, tag_reads=()):
        mx, nmx, rsum, rinv = small[0:nr, 0:1], small[0:nr, 1:2], small[0:nr, 2:3], small[0:nr, 3:4]
        S.op("dve", lambda e: e.tensor_reduce(out=mx, in_=sc_ap, axis=AX.X, op=ALU.max), reads=[sc_res], writes=[small.res])
        S.op("dve", lambda e: e.tensor_scalar(out=nmx, in0=mx, scalar1=-scale, scalar2=None, op0=ALU.mult),
             reads=[small.res], writes=[small.res])
        S.op("dve", lambda e: e.memset(rsum, 0.0), reads=[small.res], writes=[small.res])
        return mx, nmx, rsum, rinv

    def attn_prompt(l, g):
        with Phase(C, "attp") as ph:
            qk = ph.sb([128, 16, NP_TOK], BF16, "qk")
            V = ph.sb([128, 8, 1024], BF16, "V")
            oa = ph.sb([128, 8, NP_TOK], BF16, "oa")
            S.dma("sp", qk[:], g.qkT.rearrange("k p t -> p k t"), reads=[g.qkT_r], writes=[qk.res])
            S.dma("sp", V[:], g.vtm.rearrange("(j p) c -> p j c", p=128), reads=[g.vtm_r], writes=[V.res])
            pb = [ph.sb([128, 256], F32, "pb") for _ in range(2)]
            pn = [ph.sb([128, 256], BF16, "pn") for _ in range(2)]
            PT = [ph.sb([128, 2, 128], BF16, "PT") for _ in range(2)]
            sm = [ph.sb([128, 8], F32, "sm") for _ in range(2)]
            def stage_a(s_, h, qt, i):
                c, p0 = h // 2, (h % 2) * 64
                q0 = s_ * 256 + qt * 128
                ps = next_ps()
                S.mm([lambda pe: pe.matmul(ps[:, 0:256], lhsT=qk[p0:p0 + 64, c, q0:q0 + 128],
                                           rhs=qk[p0:p0 + 64, 8 + c, s_ * 256:(s_ + 1) * 256], start=True, stop=True)],
                     reads=[qk.res], writes=[ps.res])
                mx, nmx, rsum, rinv = softmax_rows(128, 256, ps[:, 0:256], ps.res, SCALE, sm[i], None)
                S.op("act", lambda e: e.activation(out=pb[i][:], in_=ps[:, 0:256], func=AF.Exp, bias=nmx, scale=SCALE,
                                                   accum_out=rsum), reads=[ps.res, sm[i].res], writes=[pb[i].res, sm[i].res])
                S.op("dve", lambda e: e.reciprocal(out=rinv, in_=rsum), reads=[sm[i].res], writes=[sm[i].res])
                S.op("dve", lambda e: e.tensor_scalar(out=pn[i][:], in0=pb[i][:], scalar1=rinv, scalar2=None, op0=ALU.mult),
                     reads=[pb[i].res, sm[i].res], writes=[pn[i].res])

            def stage_b(s_, h, qt, i):
                c, p0 = h // 2, (h % 2) * 64
                q0 = s_ * 256 + qt * 128
                pt2 = next_ps()
                ptb = pt2[:, :].bitcast(BF16)
                S.mm([lambda pe, kt=kt: pe.transpose(ptb[:, kt * 128:(kt + 1) * 128], pn[i][:, kt * 128:(kt + 1) * 128], identb[:])
                      for kt in range(2)], reads=[pn[i].res, identb.res], writes=[pt2.res])
                S.op("act", lambda e: e.copy(out=PT[i][:], in_=ptb[:, 0:256].rearrange("p (a b) -> p a b", b=128)),
                     reads=[pt2.res], writes=[PT[i].res])
                po = next_ps()
                S.mm([lambda pe, kt=kt: pe.matmul(po[:, 0:128], lhsT=V[:, s_ * 2 + kt, c * 128:(c + 1) * 128], rhs=PT[i][:, kt, :],
                                                  start=(kt == 0), stop=(kt == 1)) for kt in range(2)],
                     reads=[V.res, PT[i].res], writes=[po.res])
                S.op("dve", lambda e: e.tensor_copy(out=oa[p0:p0 + 64, c, q0:q0 + 128], in_=po[p0:p0 + 64, 0:128]),
                     reads=[po.res], writes=[oa.res])
            units = [(s_, h, qt) for s_ in range(4) for h in range(16) for qt in range(2)]
            for u in range(len(units) + 1):
                if u < len(units):
                    stage_a(*units[u], u % 2)
                if u >= 1:
                    stage_b(*units[u - 1], (u - 1) % 2)
            S.dma("sp", g.oT[0][0].rearrange("k p t -> p k t"), oa[:], reads=[oa.res], writes=[g.oT[0][1]])

    rpbp, rpbp_r = dscr("rpbp", [240, 157])
    rrep, rrep_r = dscr("rrep", [240, 64, 157])
    I["natmask"] = din("natmask", [64, 64])

    def rcls(r):
        return 7 - r if r <= 3 else (3 if r <= 28 else 31 - r)

    def attn_sample(l, g):
        with Phase(C, "atts") as ph:
            z = ph.sb([128, 157], F32, "z")
            S.op("pool", lambda e: e.memset(z[:], 0.0), writes=[z.res])
            S.dma("sp", rpbp[0:128, :], z[:], reads=[z.res], writes=[rpbp_r])
            S.dma("sp", rpbp[128:240, :], z[0:112, :], reads=[z.res], writes=[rpbp_r], acc=True)
            S.dma("sp", rpbp[:, 63:94], W["rpb"][l].rearrange("h r c -> (h r) c"), writes=[rpbp_r], slow=True)
            for q4 in range(4):
                S.dma("sp", rrep[q4 * 60:(q4 + 1) * 60], AP(rpbp.tensor, q4 * 60 * 157, [[157, 60], [0, 64], [1, 157]]),
                      reads=[rpbp_r], writes=[rrep_r], acc=(q4 > 0))
            mk = ph.sb([64, 64], F32, "mk")
            S.dma("sp", mk[:], I["natmask"], writes=[mk.res])
            qk = [ph.sb([128, 2, NS_TOK], BF16, "qk") for _ in range(2)]
            Ve = [ph.sb([128, 16, 128], BF16, "Ve") for _ in range(2)]
            Vo = [ph.sb([128, 15, 128], BF16, "Vo") for _ in range(2)]
            Vc = [ph.sb([128, 2, 128], BF16, "Vc") for _ in range(2)]
            ckt = [ph.sb([128, 2, 128], F32, "ckt") for _ in range(2)]
            kcT = [ph.sb([128, 256], BF16, "kcT") for _ in range(2)]
            ob = [ph.sb([128, NS_TOK], BF16, "ob") for _ in range(2)]
            bm = [ph.sb([64, 8, 512], F32, "bm") for _ in range(2)]
            sc = [ph.sb([64, 768], F32, "sc") for _ in range(2)]
            pb = [ph.sb([64, 768], F32, "pb") for _ in range(2)]
            pn = [ph.sb([64, 768], BF16, "pn") for _ in range(2)]
            PT = [ph.sb([128, 6, 64], BF16, "PT") for _ in range(2)]
            sm = [ph.sb([128, 8], F32, "sm") for _ in range(2)]
            qv = g.qkT.rearrange("k p t -> p k t")
            u = 0
            for c in range(8):
                b_ = c % 2
                S.dma("sp", qk[b_][:, 0, :], qv[:, c, :], reads=[g.qkT_r], writes=[qk[b_].res])
                S.dma("sp", qk[b_][:, 1, :], qv[:, 8 + c, :], reads=[g.qkT_r], writes=[qk[b_].res], acc=True)
                S.dma("sp", Ve[b_][:], g.vtm[:, c * 128:(c + 1) * 128].rearrange("(j p) c -> p j c", p=128), reads=[g.vtm_r],
                      writes=[Ve[b_].res])
                S.dma("sp", Vo[b_][:], g.vtm[64:64 + 15 * 128, c * 128:(c + 1) * 128].rearrange("(j p) c -> p j c", p=128),
                      reads=[g.vtm_r], writes=[Vo[b_].res])
                S.dma("pool", Vc[b_][:], I["cv"][l][:, c * 128:(c + 1) * 128].rearrange("(j p) c -> p j c", p=128), writes=[Vc[b_].res])
                S.dma("sp", ckt[b_][:], I["ck"][l][:, c * 128:(c + 1) * 128].rearrange("(j p) c -> p j c", p=128), writes=[ckt[b_].res])
                pk = next_ps()
                S.mm([lambda pe, t=t: pe.transpose(pk[:, t * 128:(t + 1) * 128], ckt[b_][:, t, :], ident[:]) for t in range(2)],
                     reads=[ckt[b_].res, ident.res], writes=[pk.res])
                S.op("act", lambda e: e.copy(out=kcT[b_][:], in_=pk[:, 0:256]), reads=[pk.res], writes=[kcT[b_].res])
                for hh in range(2):
                    h = 2 * c + hh
                    p0 = hh * 64
                    bmh = bm[hh]
                    for o in range(8):
                        S.dma("sp", bmh[:, o, :].rearrange("p (j k) -> p j k", k=64),
                              AP(rrep.tensor, ((h * 15 + o) * 64) * 157 + 78, [[156, 64], [64 * 157, 8], [1, 64]]),
                              reads=[rrep_r], writes=[bmh.res], acc=(o > 0))
                    S.op("pool", lambda e: e.tensor_tensor(out=bmh[:].rearrange("p o (j k) -> p (o j) k", k=64),
                                                           in0=bmh[:].rearrange("p o (j k) -> p (o j) k", k=64),
                                                           in1=mk[:].unsqueeze(1).broadcast_to([64, 64, 64]), op=ALU.add),
                         reads=[bmh.res, mk.res], writes=[bmh.res])
                    def stage_a(r, i):
                        r0 = min(max(r - 4, 0), 24)
                        o = rcls(r)
                        psA, psB = next_ps(), next_ps()
                        qa = qk[b_][p0:p0 + 64, 0, r * 64:(r + 1) * 64]
                        S.mm([lambda pe: pe.matmul(psA[0:64, 0:512], lhsT=qa, rhs=qk[b_][p0:p0 + 64, 1, r0 * 64:r0 * 64 + 512],
                                                   start=True, stop=True)], reads=[qk[b_].res], writes=[psA.res])
                        S.mm([lambda pe: pe.matmul(psB[0:64, 0:256], lhsT=qa, rhs=kcT[b_][p0:p0 + 64, :], start=True, stop=True)],
                             reads=[qk[b_].res, kcT[b_].res], writes=[psB.res])
                        S.op("dve", lambda e: e.scalar_tensor_tensor(out=sc[i][:, 0:512], in0=psA[0:64, 0:512], scalar=SCALE,
                                                                     in1=bmh[:, o, :], op0=ALU.mult, op1=ALU.add),
                             reads=[psA.res, bmh.res], writes=[sc[i].res])
                        S.op("act", lambda e: e.mul(out=sc[i][:, 512:768], in_=psB[0:64, 0:256], mul=SCALE), reads=[psB.res, sc[i].res],
                             writes=[sc[i].res])
                        mx, nmx, rsum, rinv = softmax_rows(64, 768, sc[i][:], sc[i].res, 1.0, sm[i], None)
                        S.op("act", lambda e: e.activation(out=pb[i][:], in_=sc[i][:], func=AF.Exp, bias=nmx, scale=1.0,
                                                           accum_out=rsum), reads=[sc[i].res, sm[i].res], writes=[pb[i].res, sm[i].res])
                        S.op("dve", lambda e: e.reciprocal(out=rinv, in_=rsum), reads=[sm[i].res], writes=[sm[i].res])
                        S.op("dve", lambda e: e.tensor_scalar(out=pn[i][:], in0=pb[i][:], scalar1=rinv, scalar2=None, op0=ALU.mult),
                             reads=[pb[i].res, sm[i].res], writes=[pn[i].res])

                    def stage_b(r, i):
                        r0 = min(max(r - 4, 0), 24)
                        pt2 = next_ps()
                        S.mm([lambda pe, j=j: pe.matmul(pt2[:, j * 64:(j + 1) * 64], lhsT=pn[i][:, j * 128:(j + 1) * 128], rhs=identb[0:64, 0:64],
                                                        start=True, stop=True)
                              for j in range(6)], reads=[pn[i].res, identb.res], writes=[pt2.res])
                        S.op("act", lambda e: e.copy(out=PT[i][:], in_=pt2[:, 0:384].rearrange("p (a b) -> p a b", b=64)),
                             reads=[pt2.res], writes=[PT[i].res])
                        po = next_ps()
                        fns = []
                        for j in range(6):
                            if j < 4:
                                vt = Ve[b_][:, r0 // 2 + j, :] if r0 % 2 == 0 else Vo[b_][:, (r0 - 1) // 2 + j, :]
                            else:
                                vt = Vc[b_][:, j - 4, :]
                            fns.append(lambda pe, j=j, vt=vt: pe.matmul(po[:, 0:64], lhsT=vt, rhs=PT[i][:, j, :], start=(j == 0), stop=(j == 5)))
                        S.mm(fns, reads=[Ve[b_].res, Vo[b_].res, Vc[b_].res, PT[i].res], writes=[po.res])
                        S.op("dve", lambda e: e.tensor_copy(out=ob[b_][p0:p0 + 64, r * 64:(r + 1) * 64], in_=po[p0:p0 + 64, 0:64]),
                             reads=[po.res], writes=[ob[b_].res])
                    for r in range(33):
                        if r < 32:
                            stage_a(r, r % 2)
                        if r >= 1:
                            stage_b(r - 1, (r - 1) % 2)
                S.dma("sp", g.oT[0][0][c], ob[b_][:], reads=[ob[b_].res], writes=[g.oT[0][1]], acc=True)

    HY = {}
    for L_ in (256, 2048):
        HY[L_] = dict(zposT=din("zposT%d" % L_, [33, L_]), win=din("win%d" % L_, [L_, 1024]),
                      Ff=din("Ff%d" % L_, [L_, 2 * L_]), Fi=din("Fi%d" % L_, [2 * L_, L_]))
        HY[L_]["Hs"], HY[L_]["Hs_r"] = dscr("Hs%d" % L_, [2 * L_ // 128, 128, 1024])
    for g in G:
        g.zbs, g.zbs_r = dscr("zbs%d" % g.i, [g.ntok, 1024], BF16)
        g.zd, g.zd_r = dscr("zd%d" % g.i, [g.ntok, 1024])
        g.x0s, g.x0s_r = dscr("x0s%d" % g.i, [g.ntok, 1024])
    TWO_PI = 2.0 * math.pi

    def bcast_row(dram_ap_tensor, offset, n):
        return AP(dram_ap_tensor, offset, [[0, 128], [1, n]])

    def hyena_filter(l, L):
        hy_ = HY[L]
        LT = L // 128
        with Phase(C, "hyf") as ph:
            f1 = ph.sb([33, 64], F32, "f1")
            f2 = ph.sb([64, 64], F32, "f2")
            f3 = ph.sb([64, 1024], F32, "f3")
            cc = ph.sb([64, 4], F32, "cc")
            zp = ph.sb([33, L], F32, "zp")
            S.dma("sp", f1[:], W["hy_f1"][l], writes=[f1.res])
            S.dma("sp", f2[:], W["hy_f2"][l], writes=[f2.res])
            S.dma("sp", f3[:], W["hy_f3"][l], writes=[f3.res])
            for j, nm in enumerate(("hy_fb1", "hy_fb2", "hy_freq")):
                S.dma("sp", cc[:, j:j + 1], W[nm][l].rearrange("(p o) -> p o", o=1), writes=[cc.res], slow=True, acc=(j > 0))
            S.dma("sp", zp[:], hy_["zposT"], writes=[zp.res])
            t1 = ph.sb([64, L], F32, "t1")
            t2 = ph.sb([64, L], F32, "t2")
            a = ph.sb([64, 512], F32, "a")
            ki = ph.sb([64, 512], I32, "ki")
            kf = ph.sb([64, 512], F32, "kf")
            cw = min(512, L)

            def sin_layer(lt, K, src, dst, bcol):
                for cb in range(L // cw):
                    pt = next_ps(0, 6)
                    S.mm([lambda pe: pe.matmul(pt[0:64, 0:cw], lhsT=lt[0:K, :], rhs=src[0:K, cb * cw:(cb + 1) * cw], start=True, stop=True)],
                         reads=[lt.res, src.res], writes=[pt.res])
                    S.op("dve", lambda e: e.tensor_scalar(out=a[:, 0:cw], in0=pt[0:64, 0:cw], scalar1=cc[:, bcol:bcol + 1],
                                                          scalar2=cc[:, 2:3], op0=ALU.add, op1=ALU.mult), reads=[pt.res, cc.res], writes=[a.res])
                    S.op("dve", lambda e: e.tensor_scalar(out=ki[:, 0:cw], in0=a[:, 0:cw], scalar1=1.0 / TWO_PI, scalar2=None, op0=ALU.mult),
                         reads=[a.res], writes=[ki.res])
                    S.op("dve", lambda e: e.tensor_copy(out=kf[:, 0:cw], in_=ki[:, 0:cw]), reads=[ki.res], writes=[kf.res])
                    S.op("dve", lambda e: e.scalar_tensor_tensor(out=a[:, 0:cw], in0=kf[:, 0:cw], scalar=-TWO_PI, in1=a[:, 0:cw],
                                                                 op0=ALU.mult, op1=ALU.add), reads=[kf.res, a.res], writes=[a.res])
                    S.op("dve", lambda e: e.tensor_scalar(out=a[:, 0:cw], in0=a[:, 0:cw], scalar1=-math.pi, scalar2=math.pi, op0=ALU.max,
                                                          op1=ALU.min), reads=[a.res], writes=[a.res])
                    S.op("act", lambda e: e.activation(out=dst[:, cb * cw:(cb + 1) * cw], in_=a[:, 0:cw], func=AF.Sin),
                         reads=[a.res], writes=[dst.res])
            sin_layer(f1, 33, zp, t1, 0)
            sin_layer(f2, 64, t1, t2, 1)
            filtb = ph.sb([128, LT, 1024], BF16, "filtb")
            winb = [ph.sb([128, 1024], F32, "winb") for _ in range(2)]
            ft = [ph.sb([128, 1024], F32, "ft") for _ in range(2)]
            fa = [ph.sb([128, 1024], F32, "fa") for _ in range(2)]
            for tt in range(LT):
                i = tt % 2
                S.dma("sp", winb[i][:], hy_["win"][tt * 128:(tt + 1) * 128, :], writes=[winb[i].res])
                for hf in range(2):
                    pt = next_ps(0, 6)
                    S.mm([lambda pe: pe.matmul(pt[:, :], lhsT=t2[0:64, tt * 128:(tt + 1) * 128], rhs=f3[0:64, hf * 512:(hf + 1) * 512],
                                               start=True, stop=True)], reads=[t2.res, f3.res], writes=[pt.res])
                    S.op("dve", lambda e: e.tensor_tensor(out=ft[i][:, hf * 512:(hf + 1) * 512], in0=pt[:, :],
                                                          in1=winb[i][:, hf * 512:(hf + 1) * 512], op=ALU.mult),
                         reads=[pt.res, winb[i].res], writes=[ft[i].res])
                S.op("act", lambda e: e.activation(out=fa[i][:], in_=ft[i][:], func=AF.Abs), reads=[ft[i].res], writes=[fa[i].res])
                S.op("pool", lambda e: e.tensor_copy(out=filtb[:, tt, :], in_=ft[i][:]), reads=[ft[i].res], writes=[filtb.res])
                for hf in range(2):
                    pacc = psum[6 + hf]
                    S.mm([lambda pe: pe.matmul(pacc[:, :], lhsT=ones[:], rhs=fa[i][:, hf * 512:(hf + 1) * 512], start=(tt == 0),
                                               stop=(tt == LT - 1))], reads=[ones.res, fa[i].res], writes=[pacc.res])
            inv = ph.sb([128, 1024], F32, "inv")
            for hf in range(2):
                S.op("dve", lambda e: e.tensor_scalar(out=inv[:, hf * 512:(hf + 1) * 512], in0=psum[6 + hf][:, :], scalar1=1e-6,
                                                      scalar2=None, op0=ALU.add), reads=[psum[6 + hf].res, inv.res], writes=[inv.res])
            S.op("dve", lambda e: e.reciprocal(out=inv[:], in_=inv[:]), reads=[inv.res], writes=[inv.res])
            wb = [ph.sb([128, LT, 512], BF16, "wbF") for _ in range(2)]
            ob = [ph.sb([128, 512], F32, "ob") for _ in range(3)]
            st = {"i": 0}

            def epi(c0, t0, pt, nr, ncv):
                i = st["i"] % 3
                st["i"] += 1
                S.op("dve", lambda e: e.tensor_tensor(out=ob[i][:], in0=pt[:, :], in1=inv[:, t0:t0 + 512], op=ALU.mult),
                     reads=[pt.res, inv.res], writes=[ob[i].res])
                S.dma("sp", hy_["Hs"][c0 // 128, :, t0:t0 + 512], ob[i][:], reads=[ob[i].res], writes=[hy_["Hs_r"]], acc=True)
            gemm(ph, hy_["Ff"], LT, 0, 2 * L, 512, filtb, 1024, "fm", epi, wb, ps_lo=0, ps_hi=6)

    def hyena_conv(l, g, L):
        with Phase(C, "hyc") as ph:
            cwt = ph.sb([128, 3, 3072], F32, "cw")
            cbt = ph.sb([128, 3072], F32, "cb")
            drt = ph.sb([128, 1024], F32, "dr")
            for j in range(3):
                S.dma("sp", cwt[:, j, :], bcast_row(W["hy_conv_w"].tensor, (l * 3 + j) * 3072, 3072), writes=[cwt.res], acc=(j > 0))
            S.dma("sp", cbt[:], bcast_row(W["hy_conv_b"].tensor, l * 3072, 3072), writes=[cbt.res])
            S.dma("sp", drt[:], bcast_row(W["hy_d"].tensor, l * 1024, 1024), writes=[drt.res])
            xm = [ph.sb([128, 1024], F32, "xm") for _ in range(2)]
            xc = [ph.sb([128, 1024], F32, "xc") for _ in range(2)]
            xp = [ph.sb([128, 1024], F32, "xp") for _ in range(2)]
            uu = [[ph.sb([128, 1024], F32, "u%d" % cg) for cg in range(3)] for _ in range(2)]
            tq = [ph.sb([128, 1024], F32, "tq") for _ in range(2)]
            zf = [ph.sb([128, 1024], F32, "zf") for _ in range(2)]
            zbt = [ph.sb([128, 1024], BF16, "zb") for _ in range(2)]
            zdt = [ph.sb([128, 1024], F32, "zd") for _ in range(2)]
            k = 0
            for tt in range(g.ntok // 128):
                t0 = tt * 128
                sp_ = t0 % L
                bi = tt % 2
                for cg in range(3):
                    ki_ = k % 2
                    k += 1
                    c0 = 3072 + cg * 1024
                    xm_, xc_, xp_, u, t = xm[ki_], xc[ki_], xp[ki_], uu[bi][cg], tq[ki_]
                    if sp_ == 0:
                        S.op("pool", lambda e: e.memset(xm_[0:1, :], 0.0), writes=[xm_.res])
                        S.dma("sp", xm_[1:128, :], g.rh[t0:t0 + 127, c0:c0 + 1024], reads=[g.rh_r], writes=[xm_.res], acc=True)
                    else:
                        S.dma("sp", xm_[:], g.rh[t0 - 1:t0 + 127, c0:c0 + 1024], reads=[g.rh_r], writes=[xm_.res])
                    S.dma("sp", xc_[:], g.rh[t0:t0 + 128, c0:c0 + 1024], reads=[g.rh_r], writes=[xc_.res])
                    if sp_ + 128 == L:
                        S.op("pool", lambda e: e.memset(xp_[:], 0.0), writes=[xp_.res])
                        S.dma("sp", xp_[0:127, :], g.rh[t0 + 1:t0 + 128, c0:c0 + 1024], reads=[g.rh_r], writes=[xp_.res], acc=True)
                    else:
                        S.dma("sp", xp_[:], g.rh[t0 + 1:t0 + 129, c0:c0 + 1024], reads=[g.rh_r], writes=[xp_.res])
                    wv = lambda j: cwt[:, j, cg * 1024:(cg + 1) * 1024]
                    S.op("dve", lambda e: e.tensor_tensor(out=u[:], in0=xm_[:], in1=wv(0), op=ALU.mult), reads=[xm_.res, cwt.res], writes=[u.res])
                    S.op("pool", lambda e: e.tensor_tensor(out=t[:], in0=xc_[:], in1=wv(1), op=ALU.mult), reads=[xc_.res, cwt.res], writes=[t.res])
                    S.op("dve", lambda e: e.tensor_tensor(out=u[:], in0=u[:], in1=t[:], op=ALU.add), reads=[u.res, t.res], writes=[u.res])
                    S.op("pool", lambda e: e.tensor_tensor(out=t[:], in0=xp_[:], in1=wv(2), op=ALU.mult), reads=[xp_.res, cwt.res], writes=[t.res])
                    S.op("dve", lambda e: e.tensor_tensor(out=u[:], in0=u[:], in1=t[:], op=ALU.add), reads=[u.res, t.res], writes=[u.res])
                    S.op("dve", lambda e: e.tensor_tensor(out=u[:], in0=u[:], in1=cbt[:, cg * 1024:(cg + 1) * 1024], op=ALU.add),
                         reads=[u.res, cbt.res], writes=[u.res])
                u0, u1, u2 = uu[bi]
                S.dma("sp", g.x0s[t0:t0 + 128, :], u0[:], reads=[u0.res], writes=[g.x0s_r], acc=True)
                S.op("dve", lambda e: e.tensor_tensor(out=zf[bi][:], in0=u2[:], in1=u1[:], op=ALU.mult), reads=[u1.res, u2.res], writes=[zf[bi].res])
                S.op("act", lambda e: e.copy(out=zbt[bi][:], in_=zf[bi][:]), reads=[zf[bi].res], writes=[zbt[bi].res])
                S.op("pool", lambda e: e.tensor_tensor(out=zdt[bi][:], in0=zf[bi][:], in1=drt[:], op=ALU.mult), reads=[zf[bi].res, drt.res],
                     writes=[zdt[bi].res])
                S.dma("sp", g.zbs[t0:t0 + 128, :], zbt[bi][:], reads=[zbt[bi].res], writes=[g.zbs_r], acc=True)
                S.dma("sp", g.zd[t0:t0 + 128, :], zdt[bi][:], reads=[zdt[bi].res], writes=[g.zd_r], acc=True)

    def hyena_dft(l, g, L):
        hy_ = HY[L]
        LT = L // 128
        with Phase(C, "hyd") as ph:
            zT = ph.sb([128, LT, 512], BF16, "zT")
            YT = ph.sb([128, 2 * LT, 512], BF16, "YT")
            wbF = [ph.sb([128, LT, 512], BF16, "wbF") for _ in range(2)]
            wbI = [ph.sb([128, 2 * LT, 256], BF16, "wbI") for _ in range(2)]
            hre = [ph.sb([128, 512], F32, "hre") for _ in range(2)]
            him = [ph.sb([128, 512], F32, "him") for _ in range(2)]
            zre = [ph.sb([128, 512], F32, "zre") for _ in range(2)]
            zim = [ph.sb([128, 512], F32, "zim") for _ in range(2)]
            ta = [ph.sb([128, 512], F32, "ta") for _ in range(2)]
            tb = [ph.sb([128, 512], F32, "tb") for _ in range(2)]
            tc_ = [ph.sb([128, 512], F32, "tc") for _ in range(2)]
            td = [ph.sb([128, 512], F32, "td") for _ in range(2)]
            zdt = [ph.sb([128, 512], F32, "zdt") for _ in range(2)]
            x0t = [ph.sb([128, 512], F32, "x0t") for _ in range(2)]
            ot = [ph.sb([128, 512], F32, "ot") for _ in range(2)]
            otb = [ph.sb([128, 4, 128], BF16, "otb") for _ in range(2)]
            for sq in range(g.ntok // L):
                s0 = sq * L
                for half in range(2):
                    h0 = half * 512
                    S.dma("sp", zT[:], g.zbs[s0:s0 + L, h0:h0 + 512].rearrange("(j p) c -> p j c", p=128), reads=[g.zbs_r], writes=[zT.res])
                    st = {"i": 0, "re": None}

                    def epi_f(c0, t0, pt, nr, ncv):
                        blk = c0 // 128
                        if blk % 2 == 0:
                            i = st["i"] % 2
                            S.op("act", lambda e: e.copy(out=zre[i][:], in_=pt[:, :]), reads=[pt.res], writes=[zre[i].res])
                            S.dma("sp", hre[i][:], hy_["Hs"][blk, :, h0:h0 + 512], reads=[hy_["Hs_r"]], writes=[hre[i].res])
                            S.dma("sp", him[i][:], hy_["Hs"][blk + 1, :, h0:h0 + 512], reads=[hy_["Hs_r"]], writes=[him[i].res])
                            return
                        i = st["i"] % 2
                        st["i"] += 1
                        S.op("act", lambda e: e.copy(out=zim[i][:], in_=pt[:, :]), reads=[pt.res], writes=[zim[i].res])
                        S.op("dve", lambda e: e.tensor_tensor(out=ta[i][:], in0=zre[i][:], in1=hre[i][:], op=ALU.mult),
                             reads=[zre[i].res, hre[i].res], writes=[ta[i].res])
                        S.op("pool", lambda e: e.tensor_tensor(out=tb[i][:], in0=zim[i][:], in1=him[i][:], op=ALU.mult),
                             reads=[zim[i].res, him[i].res], writes=[tb[i].res])
                        S.op("dve", lambda e: e.tensor_tensor(out=YT[:, blk - 1, :], in0=ta[i][:], in1=tb[i][:], op=ALU.subtract),
                             reads=[ta[i].res, tb[i].res], writes=[YT.res])
                        S.op("pool", lambda e: e.tensor_tensor(out=tc_[i][:], in0=zre[i][:], in1=him[i][:], op=ALU.mult),
                             reads=[zre[i].res, him[i].res], writes=[tc_[i].res])
                        S.op("dve", lambda e: e.tensor_tensor(out=td[i][:], in0=zim[i][:], in1=hre[i][:], op=ALU.mult),
                             reads=[zim[i].res, hre[i].res], writes=[td[i].res])
                        S.op("dve", lambda e: e.tensor_tensor(out=YT[:, blk, :], in0=tc_[i][:], in1=td[i][:], op=ALU.add),
                             reads=[tc_[i].res, td[i].res], writes=[YT.res])
                    gemm(ph, hy_["Ff"], LT, 0, 2 * L, 512, zT, 512, "fm", epi_f, wbF)
                    st2 = {"i": 0}

                    def epi_i(c0, t0, pt, nr, ncv):
                        i = st2["i"] % 2
                        st2["i"] += 1
                        r0 = s0 + c0
                        S.dma("sp", zdt[i][:], g.zd[r0:r0 + 128, h0:h0 + 512], reads=[g.zd_r], writes=[zdt[i].res])
                        S.dma("sp", x0t[i][:], g.x0s[r0:r0 + 128, h0:h0 + 512], reads=[g.x0s_r], writes=[x0t[i].res])
                        S.op("dve", lambda e: e.tensor_tensor(out=ot[i][:], in0=pt[:, :], in1=zdt[i][:], op=ALU.add),
                             reads=[pt.res, zdt[i].res], writes=[ot[i].res])
                        S.op("pool", lambda e: e.tensor_tensor(out=ot[i][:], in0=ot[i][:], in1=x0t[i][:], op=ALU.mult),
                             reads=[ot[i].res, x0t[i].res], writes=[ot[i].res])
                        pt2 = next_ps()
                        S.mm([lambda pe, q=q: pe.transpose(pt2[:, q * 128:(q + 1) * 128], ot[i][:, q * 128:(q + 1) * 128], ident[:])
                              for q in range(4)], reads=[ot[i].res, ident.res], writes=[pt2.res])
                        S.op("act", lambda e: e.copy(out=otb[i][:], in_=pt2[:, :].rearrange("p (a b) -> p a b", b=128)),
                             reads=[pt2.res], writes=[otb[i].res])
                        S.dma("sp", g.oT[2][0][half * 4:(half + 1) * 4, :, r0:r0 + 128].rearrange("k p t -> p k t"), otb[i][:],
                              reads=[otb[i].res], writes=[g.oT[2][1]], acc=True)
                    gemm(ph, hy_["Fi"], 2 * LT, 0, L, 256, YT, 512, "fm", epi_i, wbI)

    for g in G:
        g.lora, g.lora_r = dscr("lora%d" % g.i, [4, 64, g.ntok], BF16)
        g.sg, g.sg_r = dscr("sg%d" % g.i, [128, g.ntok], BF16)
        g.SH, g.SH_r = dscr("SH%d" % g.i, [g.ntok, 3, 1024])
        g.DE = [dscr("DE%d_%d" % (g.i, e), [g.ntok, 3, 1024]) for e in range(2)]
        g.ysc = [dscr("ysc%d_%d" % (g.i, e), [g.ntok, 1024]) for e in range(2)]
        g.gsc, g.gsc_r = dscr("gsc%d" % g.i, [g.ntok, 1024])
        g.bon, g.bon_r = dscr("bon%d" % g.i, [g.ntok, 1024])

    def rwkv_lora(ph, l, g, hT):
        wbs = [ph.sb([128, 16, 128], BF16, "wbl") for _ in range(2)]
        obl = [ph.sb([128, 512], BF16, "obl") for _ in range(3)]
        for e in range(2):
            gemm(ph, W["wkv_w1"][l][e], 16, 0, 64, 64, hT, g.ntok, "fm",
                 epi_store(ph, obl, lambda c0, t0, nr, ncv, e=e: g.lora[e, :, t0:t0 + ncv], g.lora_r, func=AF.Tanh), wbs)
            gemm(ph, W["wkv_a1"][l][e], 16, 0, 64, 64, hT, g.ntok, "fm",
                 epi_store(ph, obl, lambda c0, t0, nr, ncv, e=e: g.lora[2 + e, :, t0:t0 + ncv], g.lora_r), wbs)
        gemm(ph, W["wkv_g1"][l], 16, 0, 128, 128, hT, g.ntok, "fm",
             epi_store(ph, obl, lambda c0, t0, nr, ncv: g.sg[:, t0:t0 + ncv], g.sg_r, func=AF.Sigmoid), wbs)

    def rwkv_pre(l, g, L):
        with Phase(C, "rwp") as ph:
            cwt = ph.sb([128, 3, 3072], F32, "cw")
            cbt = ph.sb([128, 3072], F32, "cb")
            for j in range(3):
                S.dma("sp", cwt[:, j, :], bcast_row(W["wkv_conv_w"].tensor, (l * 3 + j) * 3072, 3072), writes=[cwt.res], acc=(j > 0))
            S.dma("sp", cbt[:], bcast_row(W["wkv_conv_b"].tensor, l * 3072, 3072), writes=[cbt.res])
            rows = ph.sb([128, 8, 1024], F32, "rows")
            for e in range(2):
                S.dma("sp", rows[:, e, :], bcast_row(W["wkv_w0"].tensor, (l * 2 + e) * 1024, 1024), writes=[rows.res], acc=True)
                S.dma("sp", rows[:, 2 + e, :], bcast_row(W["wkv_a0"].tensor, (l * 2 + e) * 1024, 1024), writes=[rows.res], acc=True)
            S.dma("sp", rows[:, 4, :], bcast_row(W["wkv_k_k"].tensor, l * 1024, 1024), writes=[rows.res], acc=True)
            S.dma("sp", rows[:, 5, :], bcast_row(W["wkv_k_a"].tensor, l * 1024, 1024), writes=[rows.res], acc=True)
            S.dma("sp", rows[:, 7, :], bcast_row(W["wkv_r_k"].tensor, l * 1024, 1024), writes=[rows.res], acc=True)
            S.op("dve", lambda e_: e_.tensor_scalar(out=rows[:, 6, :], in0=rows[:, 5, :], scalar1=-1.0, scalar2=1.0, op0=ALU.mult, op1=ALU.add),
                 reads=[rows.res], writes=[rows.res])
            w2b = ph.sb([64, 4, 1024], BF16, "w2b")
            for e in range(2):
                S.dma("pool", w2b[:, e, :], W["wkv_w2"][l][e], writes=[w2b.res], acc=True)
                S.dma("pool", w2b[:, 2 + e, :], W["wkv_a2"][l][e], writes=[w2b.res], acc=True)
            g2b = ph.sb([128, 1024], BF16, "g2b")
            S.dma("pool", g2b[:], W["wkv_g2"][l], writes=[g2b.res])
            xms = [ph.sb([128, 1024], F32, "xm") for _ in range(2)]
            xcs = [ph.sb([128, 1024], F32, "xc") for _ in range(2)]
            xps = [ph.sb([128, 1024], F32, "xp") for _ in range(2)]
            kx = [0]
            rkv = [ph.sb([128, 1024], F32, "rkv%d" % i) for i in range(3)]
            t1 = ph.sb([128, 1024], F32, "t1")
            t2 = ph.sb([128, 1024], F32, "t2")
            kk = ph.sb([128, 1024], F32, "kk")
            at = ph.sb([128, 1024], F32, "at")
            wt = ph.sb([128, 1024], F32, "wt")
            o1 = ph.sb([128, 1024], F32, "o1")
            o2 = ph.sb([128, 1024], F32, "o2")
            sm = ph.sb([128, 64], F32, "sm")
            lt = ph.sb([64, 4, 128], BF16, "lt")
            sgt = ph.sb([128, 128], BF16, "sgt")
            v3 = lambda t_: t_[:].rearrange("p (h k) -> p h k", k=64)
            for tt in range(g.ntok // 128):
                t0 = tt * 128
                sp_ = t0 % L
                for cg in range(3):
                    c0 = cg * 1024
                    u = rkv[cg]
                    xm, xc, xp = xms[kx[0] % 2], xcs[kx[0] % 2], xps[kx[0] % 2]
                    kx[0] += 1
                    if sp_ == 0:
                        S.op("pool", lambda e: e.memset(xm[0:1, :], 0.0), writes=[xm.res])
                        S.dma("sp", xm[1:128, :], g.rh[t0:t0 + 127, c0:c0 + 1024], reads=[g.rh_r], writes=[xm.res], acc=True)
                    else:
                        S.dma("sp", xm[:], g.rh[t0 - 1:t0 + 127, c0:c0 + 1024], reads=[g.rh_r], writes=[xm.res])
                    S.dma("sp", xc[:], g.rh[t0:t0 + 128, c0:c0 + 1024], reads=[g.rh_r], writes=[xc.res])
                    if sp_ + 128 == L:
                        S.op("pool", lambda e: e.memset(xp[:], 0.0), writes=[xp.res])
                        S.dma("sp", xp[0:127, :], g.rh[t0 + 1:t0 + 128, c0:c0 + 1024], reads=[g.rh_r], writes=[xp.res], acc=True)
                    else:
                        S.dma("sp", xp[:], g.rh[t0 + 1:t0 + 129, c0:c0 + 1024], reads=[g.rh_r], writes=[xp.res])
                    wv = lambda j: cwt[:, j, cg * 1024:(cg + 1) * 1024]
                    S.op("dve", lambda e: e.tensor_tensor(out=u[:], in0=xm[:], in1=wv(0), op=ALU.mult), reads=[xm.res, cwt.res], writes=[u.res])
                    S.op("pool", lambda e: e.tensor_tensor(out=t1[:], in0=xc[:], in1=wv(1), op=ALU.mult), reads=[xc.res, cwt.res], writes=[t1.res])
                    S.op("dve", lambda e: e.tensor_tensor(out=u[:], in0=u[:], in1=t1[:], op=ALU.add), reads=[u.res, t1.res], writes=[u.res])
                    S.op("pool", lambda e: e.tensor_tensor(out=t1[:], in0=xp[:], in1=wv(2), op=ALU.mult), reads=[xp.res, cwt.res], writes=[t1.res])
                    S.op("dve", lambda e: e.tensor_tensor(out=u[:], in0=u[:], in1=t1[:], op=ALU.add), reads=[u.res, t1.res], writes=[u.res])
                    S.op("dve", lambda e: e.tensor_tensor(out=u[:], in0=u[:], in1=cbt[:, cg * 1024:(cg + 1) * 1024], op=ALU.add),
                         reads=[u.res, cbt.res], writes=[u.res])
                r_, k_, v_ = rkv
                S.dma("sp", g.SH[t0:t0 + 128, 1, :], r_[:], reads=[r_.res], writes=[g.SH_r], acc=True)
                S.dma("sp", g.SH[t0:t0 + 128, 2, :], v_[:], reads=[v_.res], writes=[g.SH_r], acc=True)
                S.op("dve", lambda e: e.tensor_tensor(out=kk[:], in0=k_[:], in1=rows[:, 4, :], op=ALU.mult), reads=[k_.res, rows.res], writes=[kk.res])
                S.op("pool", lambda e: e.tensor_tensor(out=t1[:], in0=kk[:], in1=kk[:], op=ALU.mult), reads=[kk.res], writes=[t1.res])
                S.op("dve", lambda e: e.tensor_reduce(out=sm[:, 0:16], in_=v3(t1), axis=AX.X, op=ALU.add), reads=[t1.res], writes=[sm.res])
                S.op("dve", lambda e: e.tensor_scalar(out=sm[:, 0:16], in0=sm[:, 0:16], scalar1=1e-12, scalar2=None, op0=ALU.add),
                     reads=[sm.res], writes=[sm.res])
                S.op("act", lambda e: e.sqrt(out=sm[:, 0:16], in_=sm[:, 0:16]), reads=[sm.res], writes=[sm.res])
                S.op("dve", lambda e: e.reciprocal(out=sm[:, 0:16], in_=sm[:, 0:16]), reads=[sm.res], writes=[sm.res])
                S.op("dve", lambda e: e.tensor_tensor(out=v3(kk), in0=v3(kk), in1=sm[:, 0:16].unsqueeze(2).broadcast_to([128, 16, 64]), op=ALU.mult),
                     reads=[kk.res, sm.res], writes=[kk.res])
                S.dma("sp", g.SH[t0:t0 + 128, 0, :], kk[:], reads=[kk.res], writes=[g.SH_r], acc=True)
                S.op("pool", lambda e: e.tensor_tensor(out=t1[:], in0=r_[:], in1=k_[:], op=ALU.mult), reads=[r_.res, k_.res], writes=[t1.res])
                S.op("pool", lambda e: e.tensor_tensor(out=t1[:], in0=t1[:], in1=rows[:, 7, :], op=ALU.mult), reads=[t1.res, rows.res], writes=[t1.res])
                S.op("dve", lambda e: e.tensor_reduce(out=sm[:, 16:32], in_=v3(t1), axis=AX.X, op=ALU.add), reads=[t1.res, sm.res], writes=[sm.res])
                S.op("dve", lambda e: e.tensor_tensor(out=v3(o1), in0=v3(v_), in1=sm[:, 16:32].unsqueeze(2).broadcast_to([128, 16, 64]), op=ALU.mult),
                     reads=[v_.res, sm.res], writes=[o1.res])
                S.dma("sp", g.bon[t0:t0 + 128, :], o1[:], reads=[o1.res], writes=[g.bon_r], acc=True)
                S.dma("sp", lt[:], g.lora[:, :, t0:t0 + 128].rearrange("a p t -> p a t"), reads=[g.lora_r], writes=[lt.res])
                S.dma("sp", sgt[:], g.sg[:, t0:t0 + 128], reads=[g.sg_r], writes=[sgt.res])
                for hf in range(2):
                    pt = next_ps()
                    S.mm([lambda pe: pe.matmul(pt[:, :], lhsT=sgt[:], rhs=g2b[:, hf * 512:(hf + 1) * 512], start=True, stop=True)],
                         reads=[sgt.res, g2b.res], writes=[pt.res])
                    S.op("act", lambda e: e.copy(out=o2[:, hf * 512:(hf + 1) * 512], in_=pt[:, :]), reads=[pt.res, o2.res], writes=[o2.res])
                S.dma("sp", g.gsc[t0:t0 + 128, :], o2[:], reads=[o2.res], writes=[g.gsc_r], acc=True)
                for e in range(2):
                    for hf in range(2):
                        pt = next_ps()
                        S.mm([lambda pe: pe.matmul(pt[:, :], lhsT=lt[:, e, :], rhs=w2b[:, e, hf * 512:(hf + 1) * 512], start=True, stop=True)],
                             reads=[lt.res, w2b.res], writes=[pt.res])
                        S.op("dve", lambda e_: e_.tensor_tensor(out=wt[:, hf * 512:(hf + 1) * 512], in0=pt[:, :],
                                                               in1=rows[:, e, hf * 512:(hf + 1) * 512], op=ALU.add),
                             reads=[pt.res, rows.res, wt.res], writes=[wt.res])
                    S.op("act", lambda e_: e_.activation(out=wt[:], in_=wt[:], func=AF.Sigmoid), reads=[wt.res], writes=[wt.res])
                    S.op("act", lambda e_: e_.activation(out=wt[:], in_=wt[:], func=AF.Exp, scale=-math.exp(-0.5)), reads=[wt.res], writes=[wt.res])
                    S.dma("sp", g.DE[e][0][t0:t0 + 128, 0, :], wt[:], reads=[wt.res], writes=[g.DE[e][1]], acc=True)
                    for hf in range(2):
                        pt = next_ps()
                        S.mm([lambda pe: pe.matmul(pt[:, :], lhsT=lt[:, 2 + e, :], rhs=w2b[:, 2 + e, hf * 512:(hf + 1) * 512], start=True, stop=True)],
                             reads=[lt.res, w2b.res], writes=[pt.res])
                        S.op("dve", lambda e_: e_.tensor_tensor(out=at[:, hf * 512:(hf + 1) * 512], in0=pt[:, :],
                                                               in1=rows[:, 2 + e, hf * 512:(hf + 1) * 512], op=ALU.add),
                             reads=[pt.res, rows.res, at.res], writes=[at.res])
                    S.op("act", lambda e_: e_.activation(out=at[:], in_=at[:], func=AF.Sigmoid), reads=[at.res], writes=[at.res])
                    S.op("dve", lambda e_: e_.tensor_tensor(out=o1[:], in0=kk[:], in1=at[:], op=ALU.mult), reads=[kk.res, at.res], writes=[o1.res])
                    S.dma("sp", g.DE[e][0][t0:t0 + 128, 1, :], o1[:], reads=[o1.res], writes=[g.DE[e][1]], acc=True)
                    S.op("pool", lambda e_: e_.tensor_tensor(out=t2[:], in0=at[:], in1=rows[:, 5, :], op=ALU.mult), reads=[at.res, rows.res], writes=[t2.res])
                    S.op("pool", lambda e_: e_.tensor_tensor(out=t2[:], in0=t2[:], in1=rows[:, 6, :], op=ALU.add), reads=[t2.res, rows.res], writes=[t2.res])
                    S.op("dve", lambda e_: e_.tensor_tensor(out=o2[:], in0=k_[:], in1=t2[:], op=ALU.mult), reads=[k_.res, t2.res], writes=[o2.res])
                    S.dma("sp", g.DE[e][0][t0:t0 + 128, 2, :], o2[:], reads=[o2.res], writes=[g.DE[e][1]], acc=True)

    def rwkv_scan(l, g, L):
        sample = (g.i == 1)
        NV = 16 if sample else 64
        TC = 32 if sample else 16
        with Phase(C, "rws") as ph:
            St = ph.sb([128, NV, 64], F32, "S")
            tmp = ph.sb([128, NV, 64], F32, "tmp")
            At = ph.sb([128, NV, 64], F32, "A")
            Bt = ph.sb([128, NV, 64], F32, "B")
            sa = ph.sb([128, NV], F32, "sa")
            Dq = [[ph.sb([128, TC, 64], F32, "D%d" % q) for q in range(5)] for _ in range(2)]
            Vt = [ph.sb([128, TC, NV], F32, "V") for _ in range(2)]
            Yt = [ph.sb([128, TC, NV], F32, "Y") for _ in range(2)]
            if sample:
                S.dma("sp", St[:].rearrange("p a b -> p (a b)"), I["s0"][l], writes=[St.res])
            else:
                S.op("pool", lambda e: e.memset(St[:], 0.0), writes=[St.res])
            def srcs(e):
                return [(g.SH, g.SH_r, 0), (g.DE[e][0], g.DE[e][1], 0), (g.DE[e][0], g.DE[e][1], 1), (g.DE[e][0], g.DE[e][1], 2), (g.SH, g.SH_r, 1)]
            nchunk = L // TC
            for c in range(nchunk):
                bi = c % 2
                i0 = c * TC
                for e in range(2):
                    tstart = i0 if e == 0 else (L - 1 - i0)
                    sgn = 1 if e == 0 else -1
                    if sample:
                        for vq in range(4):
                            dstp = slice(e * 64 + vq, (e + 1) * 64, 4)
                            first = (e == 0 and vq == 0)
                            for q, (arr, arr_r, slot) in enumerate(srcs(e)):
                                S.dma("sp", Dq[bi][q][dstp, :, :],
                                      AP(arr.tensor, tstart * 3072 + slot * 1024, [[64, 16], [sgn * 3072, TC], [1, 64]]),
                                      reads=[arr_r], writes=[Dq[bi][q].res], acc=(not first))
                            S.dma("sp", Vt[bi][dstp, :, :],
                                  AP(g.SH.tensor, tstart * 3072 + 2 * 1024 + vq * 16, [[64, 16], [sgn * 3072, TC], [1, 16]]),
                                  reads=[g.SH_r], writes=[Vt[bi].res], acc=(not first))
                    else:
                        for sq in range(4):
                            p0 = sq * 32 + e * 16
                            first = (e == 0 and sq == 0)
                            for q, (arr, arr_r, slot) in enumerate(srcs(e)):
                                S.dma("sp", Dq[bi][q][p0:p0 + 16, :, :],
                                      AP(arr.tensor, (sq * L + tstart) * 3072 + slot * 1024, [[64, 16], [sgn * 3072, TC], [1, 64]]),
                                      reads=[arr_r], writes=[Dq[bi][q].res], acc=(not first))
                            S.dma("sp", Vt[bi][p0:p0 + 16, :, :],
                                  AP(g.SH.tensor, (sq * L + tstart) * 3072 + 2 * 1024, [[64, 16], [sgn * 3072, TC], [1, 64]]),
                                  reads=[g.SH_r], writes=[Vt[bi].res], acc=(not first))
                D = Dq[bi]
                for i in range(TC):
                    bc = lambda q: D[q][:, i, :].unsqueeze(1).broadcast_to([128, NV, 64])
                    S.op("dve", lambda e_: e_.tensor_tensor(out=tmp[:], in0=St[:], in1=bc(0), op=ALU.mult), reads=[St.res, D[0].res], writes=[tmp.res])
                    S.op("dve", lambda e_: e_.tensor_reduce(out=sa[:], in_=tmp[:], axis=AX.X, op=ALU.add), reads=[tmp.res], writes=[sa.res])
                    S.op("dve", lambda e_: e_.tensor_tensor(out=At[:], in0=Vt[bi][:, i, :].unsqueeze(2).broadcast_to([128, NV, 64]), in1=bc(3), op=ALU.mult),
                         reads=[Vt[bi].res, D[3].res], writes=[At.res])
                    S.op("dve", lambda e_: e_.tensor_tensor(out=Bt[:], in0=St[:], in1=bc(1), op=ALU.mult), reads=[St.res, D[1].res], writes=[Bt.res])
                    S.op("dve", lambda e_: e_.tensor_tensor(out=Bt[:], in0=Bt[:], in1=At[:], op=ALU.add), reads=[Bt.res, At.res], writes=[Bt.res])
                    S.op("dve", lambda e_: e_.tensor_tensor(out=tmp[:], in0=sa[:].unsqueeze(2).broadcast_to([128, NV, 64]), in1=bc(2), op=ALU.mult),
                         reads=[sa.res, D[2].res], writes=[tmp.res])
                    S.op("dve", lambda e_: e_.tensor_tensor(out=St[:], in0=Bt[:], in1=tmp[:], op=ALU.subtract), reads=[Bt.res, tmp.res], writes=[St.res])
                    S.op("dve", lambda e_: e_.tensor_tensor(out=tmp[:], in0=St[:], in1=bc(4), op=ALU.mult), reads=[St.res, D[4].res], writes=[tmp.res])
                    S.op("dve", lambda e_: e_.tensor_reduce(out=Yt[bi][:, i, :], in_=tmp[:], axis=AX.X, op=ALU.add), reads=[tmp.res], writes=[Yt[bi].res])
                for e in range(2):
                    tstart = i0 if e == 0 else (L - 1 - i0)
                    sgn = 1 if e == 0 else -1
                    ya, ya_r = g.ysc[e]
                    if sample:
                        for vq in range(4):
                            S.dma("sp", AP(ya.tensor, tstart * 1024 + vq * 16, [[64, 16], [sgn * 1024, TC], [1, 16]]),
                                  Yt[bi][slice(e * 64 + vq, (e + 1) * 64, 4), :, :], reads=[Yt[bi].res], writes=[ya_r], acc=True)
                    else:
                        for sq in range(4):
                            p0 = sq * 32 + e * 16
                            S.dma("sp", AP(ya.tensor, (sq * L + tstart) * 1024, [[64, 16], [sgn * 1024, TC], [1, 64]]), Yt[bi][p0:p0 + 16, :, :],
                                  reads=[Yt[bi].res], writes=[ya_r], acc=True)
            if not sample:
                for sq in range(4):
                    r0 = (sq * DEPTH + l) * 32
                    S.dma("sp", O["ns"][r0:r0 + 32, :], St[sq * 32:(sq + 1) * 32, :, :].rearrange("p a b -> p (a b)"), reads=[St.res],
                          writes=[ORES["ns"]], acc=True)

    def rwkv_post(l, g):
        with Phase(C, "rwo") as ph:
            rows = ph.sb([128, 2, 1024], F32, "rows")
            S.dma("sp", rows[:, 0, :], bcast_row(W["wkv_gn_g"].tensor, l * 1024, 1024), writes=[rows.res], acc=True)
            S.dma("sp", rows[:, 1, :], bcast_row(W["wkv_gn_b"].tensor, l * 1024, 1024), writes=[rows.res], acc=True)
            ya = [ph.sb([128, 1024], F32, "ya") for _ in range(2)]
            yb = [ph.sb([128, 1024], F32, "yb") for _ in range(2)]
            bo = [ph.sb([128, 1024], F32, "bo") for _ in range(2)]
            gg = [ph.sb([128, 1024], F32, "gg") for _ in range(2)]
            tq = [ph.sb([128, 1024], F32, "tq") for _ in range(2)]
            sm = [ph.sb([128, 64], F32, "sm") for _ in range(2)]
            otb = [ph.sb([128, 8, 128], BF16, "otb") for _ in range(2)]
            v3 = lambda t_: t_[:].rearrange("p (h k) -> p h k", k=64)
            for tt in range(g.ntok // 128):
                i = tt % 2
                t0 = tt * 128
                y, y2, b_, g_, t_, s_ = ya[i], yb[i], bo[i], gg[i], tq[i], sm[i]
                S.dma("sp", y[:], g.ysc[0][0][t0:t0 + 128, :], reads=[g.ysc[0][1]], writes=[y.res])
                S.dma("sp", y2[:], g.ysc[1][0][t0:t0 + 128, :], reads=[g.ysc[1][1]], writes=[y2.res])
                S.dma("sp", b_[:], g.bon[t0:t0 + 128, :], reads=[g.bon_r], writes=[b_.res])
                S.dma("sp", g_[:], g.gsc[t0:t0 + 128, :], reads=[g.gsc_r], writes=[g_.res])
                S.op("dve", lambda e: e.tensor_tensor(out=y[:], in0=y[:], in1=y2[:], op=ALU.add), reads=[y.res, y2.res], writes=[y.res])
                S.op("dve", lambda e: e.tensor_reduce(out=s_[:, 0:16], in_=v3(y), axis=AX.X, op=ALU.add), reads=[y.res], writes=[s_.res])
                S.op("dve", lambda e: e.tensor_scalar(out=s_[:, 0:16], in0=s_[:, 0:16], scalar1=-1.0 / 64, scalar2=None, op0=ALU.mult),
                     reads=[s_.res], writes=[s_.res])
                S.op("dve", lambda e: e.tensor_tensor(out=v3(y), in0=v3(y), in1=s_[:, 0:16].unsqueeze(2).broadcast_to([128, 16, 64]), op=ALU.add),
                     reads=[y.res, s_.res], writes=[y.res])
                S.op("pool", lambda e: e.tensor_tensor(out=t_[:], in0=y[:], in1=y[:], op=ALU.mult), reads=[y.res], writes=[t_.res])
                S.op("dve", lambda e: e.tensor_reduce(out=s_[:, 16:32], in_=v3(t_), axis=AX.X, op=ALU.add), reads=[t_.res, s_.res], writes=[s_.res])
                S.op("dve", lambda e: e.tensor_scalar(out=s_[:, 16:32], in0=s_[:, 16:32], scalar1=1.0 / 64, scalar2=64e-5, op0=ALU.mult, op1=ALU.add),
                     reads=[s_.res], writes=[s_.res])
                S.op("act", lambda e: e.sqrt(out=s_[:, 16:32], in_=s_[:, 16:32]), reads=[s_.res], writes=[s_.res])
                S.op("dve", lambda e: e.reciprocal(out=s_[:, 16:32], in_=s_[:, 16:32]), reads=[s_.res], writes=[s_.res])
                S.op("dve", lambda e: e.tensor_tensor(out=v3(y), in0=v3(y), in1=s_[:, 16:32].unsqueeze(2).broadcast_to([128, 16, 64]), op=ALU.mult),
                     reads=[y.res, s_.res], writes=[y.res])
                S.op("pool", lambda e: e.tensor_tensor(out=y[:], in0=y[:], in1=rows[:, 0, :], op=ALU.mult), reads=[y.res, rows.res], writes=[y.res])
                S.op("pool", lambda e: e.tensor_tensor(out=y[:], in0=y[:], in1=rows[:, 1, :], op=ALU.add), reads=[y.res, rows.res], writes=[y.res])
                S.op("dve", lambda e: e.tensor_tensor(out=y[:], in0=y[:], in1=b_[:], op=ALU.add), reads=[y.res, b_.res], writes=[y.res])
                S.op("dve", lambda e: e.tensor_tensor(out=y[:], in0=y[:], in1=g_[:], op=ALU.mult), reads=[y.res, g_.res], writes=[y.res])
                for hf in range(2):
                    pt2 = next_ps()
                    S.mm([lambda pe, q=q: pe.transpose(pt2[:, q * 128:(q + 1) * 128], y[:, (hf * 4 + q) * 128:(hf * 4 + q + 1) * 128], ident[:])
                          for q in range(4)], reads=[y.res, ident.res], writes=[pt2.res])
                    S.op("act", lambda e: e.copy(out=otb[i][:, hf * 4:(hf + 1) * 4, :], in_=pt2[:, :].rearrange("p (a b) -> p a b", b=128)),
                         reads=[pt2.res, otb[i].res], writes=[otb[i].res])
                S.dma("sp", g.oT[1][0][:, :, t0:t0 + 128].rearrange("k p t -> p k t"), otb[i][:], reads=[otb[i].res], writes=[g.oT[1][1]], acc=True)

    def merge_out(l, g):
        with Phase(C, "mrg") as ph:
            cols = load_cols(ph, l, g)
            mT = ph.sb([128, 16, g.ntok], BF16, "mT")
            if not mixers or cfg.get("nomerge"):
                S.op("pool", lambda e: e.memset(mT[:], 0.0), writes=[mT.res])
            else:
                with Phase(C, "mrg1") as ph1:
                    branches = [b for b in range(3) if "arc"[b] in mixers]
                    wnames = ["w_pa", "w_pr", "w_pc"]
                    HT = 1024
                    oTt = {b: ph1.sb([128, 8, HT], BF16, "oT%d" % b) for b in branches}
                    wbs = {b: [ph1.sb([128, 8, 512], BF16, "wp%d" % b) for _ in range(2)] for b in branches}
                    gts = {b: [ph1.sb([128, 512], F32, "g%d" % b) for _ in range(2)] for b in branches}
                    tmp = [ph1.sb([128, 512], F32, "mt") for _ in range(2)]
                    tmp2 = [ph1.sb([128, 512], F32, "mt2") for _ in range(2)]
                    cnt = 0
                    for half in range(g.ntok // HT):
                        for b in branches:
                            S.dma("sp", oTt[b][:], g.oT[b][0].rearrange("k p t -> p k t")[:, :, half * HT:(half + 1) * HT],
                                  reads=[g.oT[b][1]], writes=[oTt[b].res])

                        def loadw(s):
                            for b in branches:
                                wbuf = wbs[b][s % 2]
                                S.dma("pool", wbuf[:], W[wnames[b]][l].rearrange("(k p) n -> p k n", p=128)[:, :, s * 512:(s + 1) * 512],
                                      writes=[wbuf.res])
                        loadw(0)
                        for s in range(4):
                            if s + 1 < 4:
                                loadw(s + 1)
                            for nb in range(4):
                                nbg = s * 4 + nb
                                for tt in range(HT // 512):
                                    t0 = half * HT + tt * 512
                                    pts = {}
                                    for b in branches:
                                        pt = next_ps()
                                        pts[b] = pt
                                        wbuf = wbs[b][s % 2]
                                        fns = [lambda pe, kc=kc, pt=pt, wbuf=wbuf, b=b: pe.matmul(
                                            pt[:, :], lhsT=wbuf[:, kc, nb * 128:(nb + 1) * 128],
                                            rhs=oTt[b][:, kc, tt * 512:(tt + 1) * 512], start=(kc == 0), stop=(kc == 7))
                                            for kc in range(8)]
                                        S.mm(fns, reads=[wbuf.res, oTt[b].res], writes=[pt.res])
                                        gt = gts[b][cnt % 2]
                                        S.dma("sp", gt[:], g.gT[b * 16 + nbg, :, t0:t0 + 512], reads=[g.gT_r], writes=[gt.res])
                                    t1, t2 = tmp[cnt % 2], tmp2[cnt % 2]
                                    acc = None
                                    for bi, b in enumerate(branches):
                                        gt = gts[b][cnt % 2]
                                        last = (bi == len(branches) - 1)
                                        if acc is None:
                                            dst = mT[:, nbg, t0:t0 + 512] if last else t1[:]
                                            S.op("dve", lambda e, dst=dst, pt=pts[b], gt=gt: e.tensor_tensor(out=dst, in0=pt[:, :], in1=gt[:], op=ALU.mult),
                                                 reads=[pts[b].res, gt.res], writes=[mT.res if last else t1.res])
                                            acc = t1
                                        else:
                                            S.op("dve", lambda e, pt=pts[b], gt=gt: e.tensor_tensor(out=t2[:], in0=pt[:, :], in1=gt[:], op=ALU.mult),
                                                 reads=[pts[b].res, gt.res], writes=[t2.res])
                                            dst = mT[:, nbg, t0:t0 + 512] if last else t1[:]
                                            S.op("dve", lambda e, dst=dst: e.tensor_tensor(out=dst, in0=t1[:], in1=t2[:], op=ALU.add),
                                                 reads=[t1.res, t2.res], writes=[mT.res if last else t1.res])
                                    cnt += 1
            if "merged" in dbg:
                S.dma("sp", DBG["merged" + g.tag].rearrange("k p t -> p k t"), mT[:], reads=[mT.res], writes=[ORES["dbg_merged" + g.tag]])
            wb = [ph.sb([128, 16, 512], BF16, "wb") for _ in range(2)]
            resid_gemm(ph, l, g, W["w_out"][l], 16, 512, mT, g.ntok, 0, cols.ga1, None, cols.res, wb)

    def resid_gemm(ph, l, g, Wd, KC, slabw, xin, ntok, tok0_x, ga_fn, gb_fn, cres, wb, tok_base=0):
        xr = [ph.sb([128, 512], F32, "xr") for _ in range(3)]
        tb = [ph.sb([128, 512], F32, "tb") for _ in range(3)]
        st = {"i": 0}

        def epi(c0, t0, pt, nr, ncv):
            i = st["i"] % 3
            st["i"] += 1
            nb = c0 // 128
            tg = tok_base + (t0 - tok0_x)
            S.dma("sp", xr[i][:], g.xT[nb, :, tg:tg + 512], reads=[g.xT_r], writes=[xr[i].res])
            if gb_fn is None:
                S.op("dve", lambda e: e.scalar_tensor_tensor(out=xr[i][:], in0=pt[:, :], scalar=ga_fn(nb), in1=xr[i][:],
                                                             op0=ALU.mult, op1=ALU.add), reads=[pt.res, xr[i].res] + cres, writes=[xr[i].res])
            else:
                S.op("act", lambda e: e.activation(out=tb[i][:], in_=pt[:, :], func=AF.Identity, scale=ga_fn(nb), bias=gb_fn(nb)),
                     reads=[pt.res] + cres, writes=[tb[i].res])
                S.op("dve", lambda e: e.tensor_tensor(out=xr[i][:], in0=xr[i][:], in1=tb[i][:], op=ALU.add),
                     reads=[xr[i].res, tb[i].res], writes=[xr[i].res])
            S.dma("sp", g.xT[nb, :, tg:tg + 512], xr[i][:], reads=[xr[i].res], writes=[g.xT_r], acc=True)
        gemm(ph, Wd, KC, 0, D, slabw, xin, ntok, "fm", epi, wb, tok0=tok0_x)

    def mlp(l, g):
        with Phase(C, "ff1") as ph:
            cols = load_cols(ph, l, g)
            hT = ph.sb([128, 16, g.ntok], BF16, "h2T")
            with Phase(C, "nrm2") as ph2:
                norm_mod(ph2, g, cols.a2, cols.sh2, cols.res, hT)
            wb = [ph.sb([128, 16, 512], BF16, "wb") for _ in range(2)]
            rt = [ph.sb([128, 512], F32, "rt") for _ in range(3)]
            ob = [ph.sb([128, 512], BF16, "ob") for _ in range(3)]
            st = {"i": 0}

            def epi(c0, t0, pt, nr, ncv):
                i = st["i"] % 3
                st["i"] += 1
                nb = c0 // 128
                S.op("act", lambda e: e.activation(out=rt[i][:], in_=pt[:, :], func=AF.Relu, bias=cols.b1(nb), scale=1.0),
                     reads=[pt.res] + cols.res, writes=[rt[i].res])
                S.op("pool", lambda e: e.tensor_tensor(out=ob[i][:], in0=rt[i][:], in1=rt[i][:], op=ALU.mult),
                     reads=[rt[i].res], writes=[ob[i].res])
                S.dma("sp", g.f1T[nb, :, t0:t0 + 512], ob[i][:], reads=[ob[i].res], writes=[g.f1T_r], acc=True)
            gemm(ph, W["w_ff1"][l], 16, 0, D_FF, 512, hT, g.ntok, "fm", epi, wb)
        with Phase(C, "ff2") as ph:
            cols = load_cols(ph, l, g)
            f1 = [ph.sb([128, 64, 512], BF16, "f1") for _ in range(1)]
            wb = [ph.sb([128, 64, 128], BF16, "wb2") for _ in range(2)]
            fv = g.f1T.rearrange("k p t -> p k t")
            for tt in range(g.ntok // 512):
                ft = f1[tt % len(f1)]
                for q in range(4):
                    S.dma("sp", ft[:, q * 16:(q + 1) * 16, :], fv[:, q * 16:(q + 1) * 16, tt * 512:(tt + 1) * 512],
                          reads=[g.f1T_r], writes=[ft.res], acc=(q > 0))
                resid_gemm(ph, l, g, W["w_ff2"][l], 64, 128, ft, 512, 0, cols.ga2, cols.gb2, cols.res, wb, tok_base=tt * 512)

    def final_norm(g, dst, dst_res):
        with Phase(C, "fin") as ph:
            fg = ph.sb([128, 16], F32, "fg")
            S.dma("sp", fg[:], W["final_g"].rearrange("(j p) -> p j", p=128), writes=[fg.res], slow=True)
            TW = 256
            xt = [ph.sb([128, 16, TW], F32, "nx") for _ in range(2)]
            tm = [ph.sb([128, 16, TW], F32, "nt") for _ in range(2)]
            rs = [ph.sb([128, TW], F32, "nr") for _ in range(2)]
            yo = [ph.sb([128, D], F32, "yo") for _ in range(2)]
            xv = g.xT.rearrange("k p t -> p k t")
            for tt in range(g.ntok // TW):
                x_, t_, r_ = xt[tt % 2], tm[tt % 2], rs[tt % 2]
                S.dma("sp", x_[:], xv[:, :, tt * TW:(tt + 1) * TW], reads=[g.xT_r], writes=[x_.res])
                S.op("act", lambda e: e.activation(out=t_[:], in_=x_[:], func=AF.Square), reads=[x_.res], writes=[t_.res])
                pt = next_ps()
                fns = [lambda pe, kc=kc: pe.matmul(pt[:, 0:TW], lhsT=ones[:], rhs=t_[:, kc, :], start=(kc == 0), stop=(kc == 15))
                       for kc in range(16)]
                S.mm(fns, reads=[ones.res, t_.res], writes=[pt.res])
                S.op("dve", lambda e: e.tensor_scalar(out=r_[:], in0=pt[:, 0:TW], scalar1=1.0 / D, scalar2=1e-6, op0=ALU.mult,
                                                      op1=ALU.add), reads=[pt.res], writes=[r_.res])
                S.op("act", lambda e: e.sqrt(out=r_[:], in_=r_[:]), reads=[r_.res], writes=[r_.res])
                S.op("dve", lambda e: e.reciprocal(out=r_[:], in_=r_[:]), reads=[r_.res], writes=[r_.res])
                S.op("dve", lambda e: e.tensor_tensor(out=t_[:], in0=x_[:], in1=r_[:].unsqueeze(1).broadcast_to([128, 16, TW]),
                                                      op=ALU.mult), reads=[x_.res, r_.res], writes=[t_.res])
                for kc in range(16):
                    S.op("act", lambda e, kc=kc: e.activation(out=t_[:, kc, :], in_=t_[:, kc, :], func=AF.Identity,
                                                              scale=fg[:, kc:kc + 1]), reads=[t_.res, fg.res], writes=[t_.res])
                for sub in range(TW // 128):
                    y_ = yo[sub % 2]
                    for kq in range(4):
                        pt2 = next_ps()
                        fns = [lambda pe, j=j, pt2=pt2, kq=kq: pe.transpose(pt2[:, j * 128:(j + 1) * 128],
                                                                             t_[:, kq * 4 + j, sub * 128:(sub + 1) * 128], ident[:])
                               for j in range(4)]
                        S.mm(fns, reads=[t_.res, ident.res], writes=[pt2.res])
                        eng = alt_eng()
                        if eng == "act":
                            S.op("act", lambda e, pt2=pt2, kq=kq: e.copy(out=y_[:, kq * 512:(kq + 1) * 512], in_=pt2[:, :]),
                                 reads=[pt2.res], writes=[y_.res])
                        else:
                            S.op("dve", lambda e, pt2=pt2, kq=kq: e.tensor_copy(out=y_[:, kq * 512:(kq + 1) * 512], in_=pt2[:, :]),
                                 reads=[pt2.res], writes=[y_.res])
                    r0 = tt * TW + sub * 128
                    S.dma("sp", dst[r0:r0 + 128, :], y_[:], reads=[y_.res], writes=[dst_res], acc=True)

    for name in dbg:
        for g in G:
            if name in ("h", "merged"):
                dbg_out(name + g.tag, [16, 128, g.ntok], BF16)
            if name == "x":
                dbg_out(name + g.tag, [16, 128, g.ntok], F32)
            if name == "oT":
                for b in range(3):
                    dbg_out("oT%d%s" % (b, g.tag), [8, 128, g.ntok], BF16)
    load_x_T(G[0], I["xp"])
    load_x_T(G[1], I["xs"])
    for l in range(nlayers):
        modulation(l)
        for g in G:
            if g.i not in cfg.get("groups", (0, 1)):
                continue
            in_proj(l, g, None)
            if "a" in mixers and not cfg.get("noattn"):
                (attn_prompt if g.i == 0 else attn_sample)(l, g)
            if "r" in mixers:
                L_ = 256 if g.i == 0 else 2048
                rwkv_pre(l, g, L_)
                rwkv_scan(l, g, L_)
                rwkv_post(l, g)
            if "c" in mixers:
                L_ = 256 if g.i == 0 else 2048
                hyena_filter(l, L_)
                hyena_conv(l, g, L_)
                hyena_dft(l, g, L_)
            if "oT" in dbg:
                for b in range(3):
                    if "arc"[b] in mixers:
                        with Phase(C, "dbgo") as phd:
                            S.dma("sp", DBG["oT%d%s" % (b, g.tag)], g.oT[b][0], reads=[g.oT[b][1]], writes=[ORES["dbg_oT%d%s" % (b, g.tag)]])
            merge_out(l, g)
            mlp(l, g)
    if "x" in dbg:
        for g in G:
            with Phase(C, "dbgx") as ph:
                S.dma("sp", DBG["x" + g.tag], g.xT, reads=[g.xT_r], writes=[ORES["dbg_x" + g.tag]])
    final_norm(G[0], O["yp"], ORES["yp"])
    final_norm(G[1], O["ys"], ORES["ys"])
    S.barrier()
    cst.es.__exit__(None, None, None)
    top.close()
    C.ninst = S.ninst
    return nc, C


def make_in_maps(inputs):
    f = lambda a: np.ascontiguousarray(np.asarray(a, dtype=np.float32))
    maps = []
    wnames = ["ln1_g", "ln2_g", "w_mod", "b_mod", "w_in", "rpb", "wkv_conv_w", "wkv_conv_b", "wkv_w0", "wkv_w1", "wkv_w2",
              "wkv_a0", "wkv_a1", "wkv_a2", "wkv_g1", "wkv_g2", "wkv_k_k", "wkv_k_a", "wkv_r_k", "wkv_gn_g", "wkv_gn_b",
              "hy_conv_w", "hy_conv_b", "hy_f1", "hy_fb1", "hy_f2", "hy_fb2", "hy_freq", "hy_f3", "hy_d", "w_pa", "w_pr",
              "w_pc", "w_out", "w_ff1", "b_ff1", "w_ff2", "b_ff2", "final_g"]
    wd = {k: f(inputs[k]) for k in wnames}
    wd["wkv_r_k"] = wd["wkv_r_k"].reshape(DEPTH, 1024)
    for i in range(8):
        b = i // 2
        m = dict(wd)
        m["xp"] = f(inputs["x_prompt"][4 * i:4 * i + 4]).reshape(NP_TOK, D)
        m["xs"] = f(inputs["x_sample"][b])
        m["ck"] = f(inputs["cache_k"][b]).reshape(DEPTH, 256, 1024)
        m["cv"] = f(inputs["cache_v"][b]).reshape(DEPTH, 256, 1024)
        m["s0"] = f(inputs["state_wkv"][b]).reshape(DEPTH, 128, 1024)
        m["cvec"] = np.stack([f(inputs["c_ctx"]), f(inputs["c"][b])])
        m.update(CONSTS)
        maps.append(m)
    return maps


def _make_consts():
    cst = {}
    cq = np.arange(64)
    c0 = np.clip(cq - 8, 0, 48)
    ck = np.arange(64)
    ok = (ck[None, :] >= c0[:, None]) & (ck[None, :] < c0[:, None] + 16)
    cst["natmask"] = np.where(ok, 0.0, -1e30).astype(np.float32)
    for L in (256, 2048):
        t = np.linspace(0.0, 1.0, L, dtype=np.float32)[:, None]
        w = 2.0 * np.pi * np.arange(L, dtype=np.float32)[:, None] / L
        f = np.linspace(1e-4, 15, 16, dtype=np.float32)[None, :]
        z = np.concatenate([t, np.cos(f * w), -np.sin(f * w)], -1).astype(np.float32)
        cst["zposT%d" % L] = np.ascontiguousarray(z.T)
        dist = (np.abs(np.arange(L) - L // 2).astype(np.float32) / L)[:, None]
        deltas = np.abs(np.linspace(math.log(1e-2) / 1.5, math.log(1e-2) / 0.3, 1024, dtype=np.float32))[None, :]
        cst["win%d" % L] = np.exp(-dist * deltas).astype(np.float32)
        n = 2 * L
        k = np.arange(L, dtype=np.float64)
        om = 2.0 * np.pi * (k + 0.5) / n
        tt = np.arange(L, dtype=np.float64)
        ang = tt[:, None] * om[None, :]
        Ff = np.zeros((L, 2 * L), np.float32)
        Ffv = Ff.reshape(L, L // 128, 2, 128)
        Ffv[:, :, 0, :] = np.cos(ang).reshape(L, L // 128, 128)
        Ffv[:, :, 1, :] = (-np.sin(ang)).reshape(L, L // 128, 128)
        cst["Ff%d" % L] = Ff
        angi = om[:, None] * (tt[None, :] + L // 2)
        Fi = np.zeros((2 * L, L), np.float32)
        Fiv = Fi.reshape(L // 128, 2, 128, L)
        Fiv[:, 0] = ((2.0 / n) * np.cos(angi)).reshape(L // 128, 128, L)
        Fiv[:, 1] = (-(2.0 / n) * np.sin(angi)).reshape(L // 128, 128, L)
        cst["Fi%d" % L] = Fi
    return cst


CONSTS = _make_consts()
_CACHE = {}


def kernel(**inputs):
    if "nc" not in _CACHE:
        _CACHE["nc"] = build({})[0]
    nc = _CACHE["nc"]
    maps = make_in_maps(inputs)
    res = run_bass_kernel_spmd(nc, maps, core_ids=list(range(8)))
    R = res.results
    yp = np.concatenate([R[i]["yp"].reshape(4, 256, D) for i in range(8)], 0)
    ys = np.stack([R[2 * b]["ys"] for b in range(4)], 0)
    nk = np.concatenate([R[i]["nk"].reshape(4, DEPTH, 256, 16, 64) for i in range(8)], 0)
    nv = np.concatenate([R[i]["nv"].reshape(4, DEPTH, 256, 16, 64) for i in range(8)], 0)
    ns = np.concatenate([R[i]["ns"].reshape(4, DEPTH, 2, 16, 64, 64) for i in range(8)], 0)
    return (yp.astype(np.float32), ys.astype(np.float32), nk.astype(np.float32), nv.astype(np.float32), ns.astype(np.float32))
```

```python
import math
from contextlib import ExitStack

import numpy as np
import concourse.bass as bass
import concourse.mybir as mybir
from concourse.bass_utils import run_bass_kernel_spmd

F32 = mybir.dt.float32
BF16 = mybir.dt.bfloat16
I32 = mybir.dt.int32
AF = mybir.ActivationFunctionType
ALU = mybir.AluOpType
AX = mybir.AxisListType
AP = bass.AP

D = 2048
DEPTH = 4
NP_TOK = 1024
NS_TOK = 2048
N_IN = 15360
D_FF = 8192


class Res:
    __slots__ = ("name", "w", "a", "r")

    def __init__(self, name):
        self.name = name
        self.w = {}
        self.a = {}
        self.r = {}


class Tile:
    def __init__(self, t, name):
        self.t = t
        self.res = Res(name)

    def __getitem__(self, k):
        return self.t[k]


class Sched:
    NDS = 40
    NPOOL = 8

    def __init__(self, nc, es):
        self.nc = nc
        self.E = {"pe": nc.tensor, "act": nc.scalar, "dve": nc.vector, "pool": nc.gpsimd, "sp": nc.sync}
        self.sem = {k: es.enter_context(nc.semaphore("c_" + k)) for k in self.E}
        self.cnt = {k: 0 for k in self.E}
        self.seen = {k: {} for k in self.E}
        self.dsem = [es.enter_context(nc.semaphore("d%d" % i)) for i in range(self.NDS)]
        self.dcnt = [0] * self.NDS
        self.dnext = 0
        self.dnext_pool = 0
        self.ninst = 0
        self.nwait = 0

    def _semobj(self, key):
        return self.sem[key] if isinstance(key, str) else self.dsem[key]

    def _wait(self, eng, deps, defer=False):
        need = {}
        for (k, v) in deps:
            if need.get(k, 0) < v:
                need[k] = v
        sn = self.seen[eng]
        todo = [(k, v) for k, v in need.items() if sn.get(k, 0) < v]
        last = None
        if defer and todo:
            last = todo.pop()
        for k, v in todo:
            self.E[eng].wait_ge(self._semobj(k), v)
            sn[k] = v
            self.ninst += 1
            self.nwait += 1
        if last is not None:
            sn[last[0]] = last[1]
        return last

    def _attach(self, ins, last):
        if last is not None:
            ins._wait_ge(self._semobj(last[0]), last[1])

    def _deps(self, eng, reads, writes, is_dma=False, acc=False):
        deps = []
        for r in reads:
            deps.extend(r.w.items())
            deps.extend(r.a.items())
        for w in writes:
            srcs = [w.w, w.r] if acc else [w.w, w.a, w.r]
            for d in srcs:
                deps.extend(d.items())
        return deps

    def _commit(self, tok, reads, writes, acc=False):
        k, v = tok
        for r in reads:
            r.r[k] = v
        for w in writes:
            if acc:
                w.a[k] = v
            else:
                w.w = {k: v}
                w.a = {}
                w.r = {}

    def op(self, eng, fn, reads=(), writes=()):
        last = self._wait(eng, self._deps(eng, reads, writes), defer=True)
        ins = fn(self.E[eng])
        self._attach(ins, last)
        self.cnt[eng] += 1
        ins.then_inc(self.sem[eng], 1)
        self.ninst += 1
        self._commit((eng, self.cnt[eng]), reads, writes)

    def mm(self, fns, reads, writes):
        last = self._wait("pe", self._deps("pe", reads, writes), defer=True)
        pe = self.E["pe"]
        ins = None
        for j, f in enumerate(fns):
            ins = f(pe)
            if j == 0:
                self._attach(ins, last)
        self.ninst += len(fns)
        self.cnt["pe"] += 1
        ins.then_inc(self.sem["pe"], 1)
        self._commit(("pe", self.cnt["pe"]), reads, writes)

    def dma(self, q, out, in_, reads=(), writes=(), acc=False, slow=False):
        if q == "pool":
            i = self.NDS - self.NPOOL + self.dnext_pool
            self.dnext_pool = (self.dnext_pool + 1) % self.NPOOL
        else:
            i = self.dnext
            self.dnext = (i + 1) % (self.NDS - self.NPOOL)
        deps = self._deps(q, reads, writes, is_dma=True, acc=acc)
        if self.dcnt[i] > 0:
            deps.append((i, self.dcnt[i]))
        last = self._wait(q, deps, defer=True)
        if slow:
            ins = self.E[q].dma_start(out=out, in_=in_, allow_slow_non_contiguous=True)
        else:
            ins = self.E[q].dma_start(out=out, in_=in_)
        self._attach(ins, last)
        ins.then_inc(self.dsem[i], 16)
        self.ninst += 1
        self.dcnt[i] += 16
        self._commit((i, self.dcnt[i]), reads, writes, acc=acc)

    def barrier(self):
        deps = [(k, self.cnt[k]) for k in self.E if k != "sp" and self.cnt[k] > 0]
        deps += [(i, c) for i, c in enumerate(self.dcnt) if c > 0]
        self._wait("sp", deps)
        ins = self.E["sp"].nop()
        self.cnt["sp"] += 1
        ins.then_inc(self.sem["sp"], 1)
        for k in self.E:
            if k != "sp":
                self._wait(k, [("sp", self.cnt["sp"])])
        for k in self.E:
            for k2 in self.E:
                self.seen[k][k2] = self.cnt[k2]
            for i, c in enumerate(self.dcnt):
                self.seen[k][i] = c


class Ctx:
    pass


def _col_ap(dram_ap_1d, n):
    return dram_ap_1d.rearrange("(j p) -> p j", p=128)


class Phase:
    def __init__(self, C, name):
        self.C = C
        self.name = name
        self.es = ExitStack()
        self.n = 0

    def __enter__(self):
        self.es.__enter__()
        return self

    def sb(self, shape, dt=F32, name=None):
        self.n += 1
        nm = "%s_%s_%d" % (self.name, name or "t", self.C.uid())
        t = self.es.enter_context(self.C.nc.sbuf_tensor(nm, list(shape), dt))
        return Tile(t, nm)

    def __exit__(self, *a):
        self.C.S.barrier()
        return self.es.__exit__(*a)


def build(cfg):
    nlayers = cfg.get("nlayers", DEPTH)
    dbg = cfg.get("dbg", ())
    mixers = cfg.get("mixers", ("a", "r", "c"))
    nc = bass.Bass("TRN2", target_bir_lowering=False)
    C = Ctx()
    C.nc = nc
    C._uid = 0

    def uid():
        C._uid += 1
        return C._uid
    C.uid = uid
    top = ExitStack()
    S = Sched(nc, top)
    C.S = S

    def din(name, shape, dt=F32):
        return nc.dram_tensor(name, list(shape), dt, kind="ExternalInput").ap()

    def dout(name, shape, dt=F32):
        return nc.dram_tensor(name, list(shape), dt, kind="ExternalOutput").ap()

    def dscr(name, shape, dt=F32):
        a = nc.dram_tensor(name, list(shape), dt, kind="Internal").ap()
        return a, Res(name)

    I = {}
    I["xp"] = din("xp", [NP_TOK, D])
    I["xs"] = din("xs", [NS_TOK, D])
    I["ck"] = din("ck", [DEPTH, 256, 1024])
    I["cv"] = din("cv", [DEPTH, 256, 1024])
    I["s0"] = din("s0", [DEPTH, 128, 1024])
    I["cvec"] = din("cvec", [2, D])
    wshapes = {
        "ln1_g": [DEPTH, D], "ln2_g": [DEPTH, D], "w_mod": [DEPTH, D, 6 * D], "b_mod": [DEPTH, 6 * D],
        "w_in": [DEPTH, D, N_IN], "rpb": [DEPTH, 16, 15, 31],
        "wkv_conv_w": [DEPTH, 3, 3072], "wkv_conv_b": [DEPTH, 3072], "wkv_w0": [DEPTH, 2, 1024],
        "wkv_w1": [DEPTH, 2, D, 64], "wkv_w2": [DEPTH, 2, 64, 1024], "wkv_a0": [DEPTH, 2, 1024],
        "wkv_a1": [DEPTH, 2, D, 64], "wkv_a2": [DEPTH, 2, 64, 1024], "wkv_g1": [DEPTH, D, 128],
        "wkv_g2": [DEPTH, 128, 1024], "wkv_k_k": [DEPTH, 1024], "wkv_k_a": [DEPTH, 1024],
        "wkv_r_k": [DEPTH, 1024], "wkv_gn_g": [DEPTH, 1024], "wkv_gn_b": [DEPTH, 1024],
        "hy_conv_w": [DEPTH, 3, 3072], "hy_conv_b": [DEPTH, 3072], "hy_f1": [DEPTH, 33, 64],
        "hy_fb1": [DEPTH, 64], "hy_f2": [DEPTH, 64, 64], "hy_fb2": [DEPTH, 64], "hy_freq": [DEPTH, 64],
        "hy_f3": [DEPTH, 64, 1024], "hy_d": [DEPTH, 1024],
        "w_pa": [DEPTH, 1024, D], "w_pr": [DEPTH, 1024, D], "w_pc": [DEPTH, 1024, D], "w_out": [DEPTH, D, D],
        "w_ff1": [DEPTH, D, D_FF], "b_ff1": [DEPTH, D_FF], "w_ff2": [DEPTH, D_FF, D], "b_ff2": [DEPTH, D],
        "final_g": [D],
    }
    W = {k: din(k, s) for k, s in wshapes.items()}
    O = {}
    O["yp"] = dout("yp", [NP_TOK, D])
    O["ys"] = dout("ys", [NS_TOK, D])
    O["nk"] = dout("nk", [4 * DEPTH * 256, 1024])
    O["nv"] = dout("nv", [4 * DEPTH * 256, 1024])
    O["ns"] = dout("ns", [4 * DEPTH * 32, 4096])
    ORES = {k: Res("o_" + k) for k in O}
    DBG = {}

    def dbg_out(name, shape, dt=F32):
        DBG[name] = dout("dbg_" + name, shape, dt)
        ORES["dbg_" + name] = Res("dbg_" + name)
        return DBG[name], ORES["dbg_" + name]

    G = []
    for gi, ntok in enumerate((NP_TOK, NS_TOK)):
        g = Ctx()
        g.i = gi
        g.ntok = ntok
        g.tag = "ps"[gi]
        g.xT, g.xT_r = dscr("xT%d" % gi, [16, 128, ntok])
        g.qkT, g.qkT_r = dscr("qkT%d" % gi, [16, 128, ntok], BF16)
        g.vtm, g.vtm_r = dscr("vtm%d" % gi, [ntok, 1024], BF16)
        g.rh, g.rh_r = dscr("rh%d" % gi, [ntok, 6144])
        g.gT, g.gT_r = dscr("gT%d" % gi, [48, 128, ntok])
        g.oT = []
        for b in range(3):
            g.oT.append(dscr("oT%d_%d" % (gi, b), [8, 128, ntok], BF16))
        g.f1T, g.f1T_r = dscr("f1T%d" % gi, [64, 128, ntok], BF16)
        G.append(g)
    mrow, mrow_r = dscr("mrow", [2, 6 * D])
    zero_r = Res("zeros")

    cst = Phase(C, "cst")
    cst.es.__enter__()
    ident = cst.sb([128, 128], F32, "ident")
    identb = cst.sb([128, 128], BF16, "identb")
    ones = cst.sb([128, 128], F32, "ones")
    S.op("pool", lambda e: e.memset(ident[:], 1.0), writes=[ident.res])
    S.op("pool", lambda e: e.affine_select(out=ident[:], in_=ident[:], pattern=[[-1, 128]], compare_op=ALU.is_equal,
                                            fill=0.0, base=0, channel_multiplier=1), reads=[ident.res], writes=[ident.res])
    S.op("dve", lambda e: e.tensor_copy(out=identb[:], in_=ident[:]), reads=[ident.res], writes=[identb.res])
    S.op("dve", lambda e: e.memset(ones[:], 1.0), writes=[ones.res])
    psum = []
    for i in range(8):
        t = top.enter_context(nc.psum_tensor("ps%d" % i, [128, 512], F32))
        psum.append(Tile(t, "ps%d" % i))
    C.ps_rr = 0

    def next_ps(lo=0, hi=8):
        C.ps_rr = (C.ps_rr + 1) % (hi - lo)
        return psum[lo + C.ps_rr]

    C.eng_rr = 0

    def alt_eng():
        C.eng_rr ^= 1
        return "act" if C.eng_rr else "dve"

    def gemm(ph, Wd, KC, col0, ncols, slabw, xT, ntok, form, epi, wbufs, Mrows=128, tok0=0, ps_lo=0, ps_hi=8):
        nslab = (ncols + slabw - 1) // slabw
        Wv = Wd.rearrange("(k p) n -> p k n", p=128)

        def load(s):
            wb = wbufs[s % len(wbufs)]
            c0 = col0 + s * slabw
            cw = min(slabw, col0 + ncols - c0)
            S.dma("pool", wb[:, :, 0:cw], Wv[:, :, c0:c0 + cw], writes=[wb.res])
        load(0)
        for s in range(nslab):
            if s + 1 < nslab:
                load(s + 1)
            wb = wbufs[s % len(wbufs)]
            c0 = col0 + s * slabw
            cw = min(slabw, col0 + ncols - c0)
            if form == "fm":
                for nb in range((cw + 127) // 128):
                    mw = min(128, cw - nb * 128)
                    for tt in range(ntok // 512):
                        pt = next_ps(ps_lo, ps_hi)
                        t0 = tok0 + tt * 512
                        fns = []
                        for kc in range(KC):
                            fns.append(lambda pe, kc=kc, pt=pt, wb=wb, nb=nb, mw=mw, t0=t0: pe.matmul(
                                pt[0:mw, :], lhsT=wb[:, kc, nb * 128:nb * 128 + mw], rhs=xT[:, kc, t0:t0 + 512],
                                start=(kc == 0), stop=(kc == KC - 1)))
                        S.mm(fns, reads=[wb.res, xT.res], writes=[pt.res])
                        epi(c0 + nb * 128, t0, pt, mw, 512)
            else:
                for tt in range(ntok // Mrows if Mrows == 128 else 1):
                    t0 = tok0 + tt * 128
                    for nh in range((cw + 511) // 512):
                        nw = min(512, cw - nh * 512)
                        pt = next_ps(ps_lo, ps_hi)
                        fns = []
                        for kc in range(KC):
                            fns.append(lambda pe, kc=kc, pt=pt, wb=wb, nh=nh, nw=nw, t0=t0: pe.matmul(
                                pt[0:Mrows, 0:nw], lhsT=xT[:, kc, t0:t0 + Mrows], rhs=wb[:, kc, nh * 512:nh * 512 + nw],
                                start=(kc == 0), stop=(kc == KC - 1)))
                        S.mm(fns, reads=[wb.res, xT.res], writes=[pt.res])
                        epi(c0 + nh * 512, t0, pt, Mrows, nw)

    def epi_store(ph, obufs, dst_fn, dst_res, func=None, bias_fn=None, scale=1.0):
        st = {"i": 0}

        def epi(c0, t0, pt, nr, ncv):
            ob = obufs[st["i"] % len(obufs)]
            st["i"] += 1
            if func is not None or bias_fn is not None:
                b = bias_fn(c0) if bias_fn is not None else None
                rd = [pt.res] + ([b[1]] if b is not None else [])
                S.op("act", lambda e: e.activation(out=ob[0:nr, 0:ncv], in_=pt[0:nr, 0:ncv], func=func or AF.Identity,
                                                   bias=(b[0] if b is not None else 0.0), scale=scale),
                     reads=rd, writes=[ob.res])
            else:
                eng = alt_eng()
                if eng == "act":
                    S.op("act", lambda e: e.copy(out=ob[0:nr, 0:ncv], in_=pt[0:nr, 0:ncv]), reads=[pt.res], writes=[ob.res])
                else:
                    S.op("dve", lambda e: e.tensor_copy(out=ob[0:nr, 0:ncv], in_=pt[0:nr, 0:ncv]), reads=[pt.res], writes=[ob.res])
            S.dma("sp", dst_fn(c0, t0, nr, ncv), ob[0:nr, 0:ncv], reads=[ob.res], writes=[dst_res], acc=True)
        return epi

    def load_x_T(g, src):
        with Phase(C, "ldx") as ph:
            xin = [ph.sb([128, D], F32, "xin") for _ in range(2)]
            xo = [ph.sb([128, 16, 128], F32, "xo") for _ in range(2)]
            for tt in range(g.ntok // 128):
                xi = xin[tt % 2]
                xq = xo[tt % 2]
                S.dma("sp", xi[:], src[tt * 128:(tt + 1) * 128, :], writes=[xi.res])
                for kq in range(4):
                    pt = next_ps()
                    fns = [lambda pe, j=j, pt=pt, xi=xi, kq=kq: pe.transpose(pt[:, j * 128:(j + 1) * 128],
                                                                                xi[:, (kq * 4 + j) * 128:(kq * 4 + j + 1) * 128], ident[:])
                           for j in range(4)]
                    S.mm(fns, reads=[xi.res, ident.res], writes=[pt.res])
                    eng = alt_eng()
                    dst = xq[:, kq * 4:(kq + 1) * 4, :]
                    srcp = pt[:, :].rearrange("p (a b) -> p a b", b=128)
                    if eng == "act":
                        S.op("act", lambda e: e.copy(out=dst, in_=srcp), reads=[pt.res], writes=[xq.res])
                    else:
                        S.op("dve", lambda e: e.tensor_copy(out=dst, in_=srcp), reads=[pt.res], writes=[xq.res])
                S.dma("sp", g.xT.rearrange("k p t -> p k t")[:, :, tt * 128:(tt + 1) * 128], xq[:], reads=[xq.res],
                      writes=[g.xT_r], acc=True)

    def modulation(l):
        with Phase(C, "mod") as ph:
            cT = ph.sb([128, 16, 2], F32, "cT")
            cTb = ph.sb([128, 16, 2], BF16, "cTb")
            for gi in range(2):
                S.dma("sp", cT[:, :, gi], I["cvec"][gi].rearrange("(k p) -> p k", p=128), writes=[cT.res], slow=True, acc=(gi > 0))
            S.op("act", lambda e: e.activation(out=cTb[:], in_=cT[:], func=AF.Silu), reads=[cT.res], writes=[cTb.res])
            wb = [ph.sb([128, 16, 512], BF16, "wb") for _ in range(2)]
            bm = [ph.sb([2, 512], F32, "bm") for _ in range(2)]
            ob = [ph.sb([2, 512], F32, "ob") for _ in range(2)]
            st = {"i": 0}

            def epi(c0, t0, pt, nr, ncv):
                i = st["i"] % 2
                st["i"] += 1
                S.dma("sp", bm[i][:], AP(W["b_mod"].tensor, l * 6 * D + c0, [[0, 2], [1, 512]]), writes=[bm[i].res])
                S.op("dve", lambda e: e.tensor_tensor(out=ob[i][:], in0=pt[0:2, :], in1=bm[i][:], op=ALU.add),
                     reads=[pt.res, bm[i].res], writes=[ob[i].res])
                S.dma("sp", mrow[:, c0:c0 + 512], ob[i][:], reads=[ob[i].res], writes=[mrow_r], acc=True)
            gemm(ph, W["w_mod"][l], 16, 0, 6 * D, 512, cTb, 2, "tm", epi, wb, Mrows=2)

    def load_cols(ph, l, g):
        cols = Ctx()
        m = ph.sb([128, 96], F32, "mcol")
        S.dma("sp", m[:], mrow[g.i].rearrange("(j p) -> p j", p=128), reads=[mrow_r], writes=[m.res], slow=True)
        ln = ph.sb([128, 32], F32, "lncol")
        S.dma("sp", ln[:, 0:16], W["ln1_g"][l].rearrange("(j p) -> p j", p=128), writes=[ln.res], slow=True)
        S.dma("sp", ln[:, 16:32], W["ln2_g"][l].rearrange("(j p) -> p j", p=128), writes=[ln.res], slow=True, acc=True)
        bf = ph.sb([128, 80], F32, "bfcol")
        S.dma("sp", bf[:, 0:64], W["b_ff1"][l].rearrange("(j p) -> p j", p=128), writes=[bf.res], slow=True)
        S.dma("sp", bf[:, 64:80], W["b_ff2"][l].rearrange("(j p) -> p j", p=128), writes=[bf.res], slow=True, acc=True)
        d = ph.sb([128, 48], F32, "dcol")
        S.op("dve", lambda e: e.scalar_tensor_tensor(out=d[:, 0:16], in0=m[:, 16:32], scalar=1.0, in1=ln[:, 0:16],
                                                     op0=ALU.add, op1=ALU.mult), reads=[m.res, ln.res], writes=[d.res])
        S.op("dve", lambda e: e.scalar_tensor_tensor(out=d[:, 16:32], in0=m[:, 64:80], scalar=1.0, in1=ln[:, 16:32],
                                                     op0=ALU.add, op1=ALU.mult), reads=[m.res, ln.res, d.res], writes=[d.res])
        S.op("dve", lambda e: e.tensor_tensor(out=d[:, 32:48], in0=m[:, 80:96], in1=bf[:, 64:80], op=ALU.mult),
             reads=[m.res, bf.res, d.res], writes=[d.res])
        cols.m, cols.d, cols.bf = m, d, bf
        cols.sh1 = lambda kc: m[:, kc:kc + 1]
        cols.ga1 = lambda kc: m[:, 32 + kc:33 + kc]
        cols.sh2 = lambda kc: m[:, 48 + kc:49 + kc]
        cols.ga2 = lambda kc: m[:, 80 + kc:81 + kc]
        cols.a1 = lambda kc: d[:, kc:kc + 1]
        cols.a2 = lambda kc: d[:, 16 + kc:17 + kc]
        cols.gb2 = lambda kc: d[:, 32 + kc:33 + kc]
        cols.b1 = lambda j: bf[:, j:j + 1]
        cols.res = [m.res, d.res, bf.res]
        return cols

    def norm_mod(ph, g, a_fn, sh_fn, cres, hT):
        TW = 256
        xt = [ph.sb([128, 16, TW], F32, "nx") for _ in range(2)]
        tm = [ph.sb([128, 16, TW], F32, "nt") for _ in range(2)]
        rs = [ph.sb([128, TW], F32, "nr") for _ in range(2)]
        xv = g.xT.rearrange("k p t -> p k t")
        for tt in range(g.ntok // TW):
            x_, t_, r_ = xt[tt % 2], tm[tt % 2], rs[tt % 2]
            S.dma("sp", x_[:], xv[:, :, tt * TW:(tt + 1) * TW], reads=[g.xT_r], writes=[x_.res])
            S.op("act", lambda e: e.activation(out=t_[:], in_=x_[:], func=AF.Square), reads=[x_.res], writes=[t_.res])
            pt = next_ps()
            fns = [lambda pe, kc=kc: pe.matmul(pt[:, 0:TW], lhsT=ones[:], rhs=t_[:, kc, :], start=(kc == 0), stop=(kc == 15))
                   for kc in range(16)]
            S.mm(fns, reads=[ones.res, t_.res], writes=[pt.res])
            S.op("dve", lambda e: e.tensor_scalar(out=r_[:], in0=pt[:, 0:TW], scalar1=1.0 / D, scalar2=1e-6, op0=ALU.mult,
                                                  op1=ALU.add), reads=[pt.res], writes=[r_.res])
            S.op("act", lambda e: e.sqrt(out=r_[:], in_=r_[:]), reads=[r_.res], writes=[r_.res])
            S.op("dve", lambda e: e.reciprocal(out=r_[:], in_=r_[:]), reads=[r_.res], writes=[r_.res])
            S.op("dve", lambda e: e.tensor_tensor(out=t_[:], in0=x_[:], in1=r_[:].unsqueeze(1).broadcast_to([128, 16, TW]),
                                                  op=ALU.mult), reads=[x_.res, r_.res], writes=[t_.res])
            for kc in range(16):
                S.op("act", lambda e, kc=kc: e.activation(out=hT[:, kc, tt * TW:(tt + 1) * TW], in_=t_[:, kc, :],
                                                          func=AF.Identity, scale=a_fn(kc), bias=sh_fn(kc)),
                     reads=[t_.res] + cres, writes=[hT.res])

    def in_proj(l, g, cols_holder):
        with Phase(C, "inp") as ph:
            cols = load_cols(ph, l, g)
            hT = ph.sb([128, 16, g.ntok], BF16, "hT")
            with Phase(C, "nrm") as ph2:
                norm_mod(ph2, g, cols.a1, cols.sh1, cols.res, hT)
            if "h" in dbg:
                S.dma("sp", DBG["h" + g.tag].rearrange("k p t -> p k t"), hT[:], reads=[hT.res], writes=[ORES["dbg_h" + g.tag]])
            wb = [ph.sb([128, 16, 512], BF16, "wb") for _ in range(2)]
            Wl = W["w_in"][l]
            ob = [ph.sb([128, 512], F32, "ob") for _ in range(4)]
            gemm(ph, Wl, 16, 9216, 6144, 512, hT, g.ntok, "fm",
                 epi_store(ph, ob, lambda c0, t0, nr, ncv: g.gT[(c0 - 9216) // 128, :, t0:t0 + ncv], g.gT_r, func=AF.Sigmoid), wb)
            if "a" in mixers:
                obb = [ph.sb([128, 512], BF16, "obb") for _ in range(4)]
                gemm(ph, Wl, 16, 0, 2048, 512, hT, g.ntok, "fm",
                     epi_store(ph, obb, lambda c0, t0, nr, ncv: g.qkT[c0 // 128, :, t0:t0 + ncv], g.qkT_r), wb)
            obf = [ph.sb([128, 512], F32, "obf") for _ in range(3)]
            obv = [ph.sb([128, 512], BF16, "obv") for _ in range(3)]
            st = {"i": 0}

            def epi_kv(c0, t0, pt, nr, ncv):
                i = st["i"] % 3
                st["i"] += 1
                isv = c0 >= 2048
                cc = c0 - (2048 if isv else 1024)
                if g.i == 0:
                    S.op("dve", lambda e: e.tensor_copy(out=obf[i][:], in_=pt[:, :]), reads=[pt.res], writes=[obf[i].res])
                    key = "nv" if isv else "nk"
                    r0_ = ((t0 // 256) * DEPTH + l) * 256 + (t0 % 256)
                    S.dma("sp", O[key][r0_:r0_ + 128, cc:cc + 512], obf[i][:], reads=[obf[i].res],
                          writes=[ORES[key]], acc=True)
                    if isv:
                        S.op("pool", lambda e: e.tensor_copy(out=obv[i][:], in_=obf[i][:]), reads=[obf[i].res], writes=[obv[i].res])
                elif isv:
                    S.op("dve", lambda e: e.tensor_copy(out=obv[i][:], in_=pt[:, :]), reads=[pt.res], writes=[obv[i].res])
                if isv:
                    S.dma("sp", g.vtm[t0:t0 + 128, cc:cc + 512], obv[i][:], reads=[obv[i].res], writes=[g.vtm_r], acc=True)
            if ("a" in mixers or g.i == 0) and not cfg.get("nokv"):
                if g.i == 0:
                    gemm(ph, Wl, 16, 1024, 2048, 512, hT, g.ntok, "tm", epi_kv, wb)
                else:
                    gemm(ph, Wl, 16, 2048, 1024, 512, hT, g.ntok, "tm", epi_kv, wb)
            if "r" in mixers or "c" in mixers:
                gemm(ph, Wl, 16, 3072, 6144, 512, hT, g.ntok, "tm",
                     epi_store(ph, ob, lambda c0, t0, nr, ncv: g.rh[t0:t0 + nr, c0 - 3072:c0 - 3072 + ncv], g.rh_r), wb)
            if "r" in mixers:
                rwkv_lora(ph, l, g, hT)
            return None

    SCALE = 0.125

    def softmax_rows(nr, ncol, sc_ap, sc_res, scale, small, pn, tag_reads=()):
        mx, nmx, rsum, rinv = small[0:nr, 0:1], small[0:nr, 1:2], small[0:nr, 2:3], small[0:nr, 3:4]
        S.op("dve", lambda e: e.tensor_reduce(out=mx, in_=sc_ap, axis=AX.X, op=ALU.max), reads=[sc_res], writes=[small.res])
        S.op("dve", lambda e: e.tensor_scalar(out=nmx, in0=mx, scalar1=-scale, scalar2=None, op0=ALU.mult),
             reads=[small.res], writes=[small.res])
        S.op("dve", lambda e: e.memset(rsum, 0.0), reads=[small.res], writes=[small.res])
        return mx, nmx, rsum, rinv

    def attn_prompt(l, g):
        with Phase(C, "attp") as ph:
            qk = ph.sb([128, 16, NP_TOK], BF16, "qk")
            V = ph.sb([128, 8, 1024], BF16, "V")
            oa = ph.sb([128, 8, NP_TOK], BF16, "oa")
            S.dma("sp", qk[:], g.qkT.rearrange("k p t -> p k t"), reads=[g.qkT_r], writes=[qk.res])
            S.dma("sp", V[:], g.vtm.rearrange("(j p) c -> p j c", p=128), reads=[g.vtm_r], writes=[V.res])
            pb = [ph.sb([128, 256], F32, "pb") for _ in range(2)]
            pn = [ph.sb([128, 256], BF16, "pn") for _ in range(2)]
            PT = [ph.sb([128, 2, 128], BF16, "PT") for _ in range(2)]
            sm = [ph.sb([128, 8], F32, "sm") for _ in range(2)]
            def stage_a(s_, h, qt, i):
                c, p0 = h // 2, (h % 2) * 64
                q0 = s_ * 256 + qt * 128
                ps = next_ps()
                S.mm([lambda pe: pe.matmul(ps[:, 0:256], lhsT=qk[p0:p0 + 64, c, q0:q0 + 128],
                                           rhs=qk[p0:p0 + 64, 8 + c, s_ * 256:(s_ + 1) * 256], start=True, stop=True)],
                     reads=[qk.res], writes=[ps.res])
                mx, nmx, rsum, rinv = softmax_rows(128, 256, ps[:, 0:256], ps.res, SCALE, sm[i], None)
                S.op("act", lambda e: e.activation(out=pb[i][:], in_=ps[:, 0:256], func=AF.Exp, bias=nmx, scale=SCALE,
                                                   accum_out=rsum), reads=[ps.res, sm[i].res], writes=[pb[i].res, sm[i].res])
                S.op("dve", lambda e: e.reciprocal(out=rinv, in_=rsum), reads=[sm[i].res], writes=[sm[i].res])
                S.op("dve", lambda e: e.tensor_scalar(out=pn[i][:], in0=pb[i][:], scalar1=rinv, scalar2=None, op0=ALU.mult),
                     reads=[pb[i].res, sm[i].res], writes=[pn[i].res])

            def stage_b(s_, h, qt, i):
                c, p0 = h // 2, (h % 2) * 64
                q0 = s_ * 256 + qt * 128
                pt2 = next_ps()
                ptb = pt2[:, :].bitcast(BF16)
                S.mm([lambda pe, kt=kt: pe.transpose(ptb[:, kt * 128:(kt + 1) * 128], pn[i][:, kt * 128:(kt + 1) * 128], identb[:])
                      for kt in range(2)], reads=[pn[i].res, identb.res], writes=[pt2.res])
                S.op("act", lambda e: e.copy(out=PT[i][:], in_=ptb[:, 0:256].rearrange("p (a b) -> p a b", b=128)),
                     reads=[pt2.res], writes=[PT[i].res])
                po = next_ps()
                S.mm([lambda pe, kt=kt: pe.matmul(po[:, 0:128], lhsT=V[:, s_ * 2 + kt, c * 128:(c + 1) * 128], rhs=PT[i][:, kt, :],
                                                  start=(kt == 0), stop=(kt == 1)) for kt in range(2)],
                     reads=[V.res, PT[i].res], writes=[po.res])
                S.op("dve", lambda e: e.tensor_copy(out=oa[p0:p0 + 64, c, q0:q0 + 128], in_=po[p0:p0 + 64, 0:128]),
                     reads=[po.res], writes=[oa.res])
            units = [(s_, h, qt) for s_ in range(4) for h in range(16) for qt in range(2)]
            for u in range(len(units) + 1):
                if u < len(units):
                    stage_a(*units[u], u % 2)
                if u >= 1:
                    stage_b(*units[u - 1], (u - 1) % 2)
            S.dma("sp", g.oT[0][0].rearrange("k p t -> p k t"), oa[:], reads=[oa.res], writes=[g.oT[0][1]])

    rpbp, rpbp_r = dscr("rpbp", [240, 157])
    rrep, rrep_r = dscr("rrep", [240, 64, 157])
    I["natmask"] = din("natmask", [64, 64])

    def rcls(r):
        return 7 - r if r <= 3 else (3 if r <= 28 else 31 - r)

    def attn_sample(l, g):
        with Phase(C, "atts") as ph:
            z = ph.sb([128, 157], F32, "z")
            S.op("pool", lambda e: e.memset(z[:], 0.0), writes=[z.res])
            S.dma("sp", rpbp[0:128, :], z[:], reads=[z.res], writes=[rpbp_r])
            S.dma("sp", rpbp[128:240, :], z[0:112, :], reads=[z.res], writes=[rpbp_r], acc=True)
            S.dma("sp", rpbp[:, 63:94], W["rpb"][l].rearrange("h r c -> (h r) c"), writes=[rpbp_r], slow=True)
            for q4 in range(4):
                S.dma("sp", rrep[q4 * 60:(q4 + 1) * 60], AP(rpbp.tensor, q4 * 60 * 157, [[157, 60], [0, 64], [1, 157]]),
                      reads=[rpbp_r], writes=[rrep_r], acc=(q4 > 0))
            mk = ph.sb([64, 64], F32, "mk")
            S.dma("sp", mk[:], I["natmask"], writes=[mk.res])
            qk = [ph.sb([128, 2, NS_TOK], BF16, "qk") for _ in range(2)]
            Ve = [ph.sb([128, 16, 128], BF16, "Ve") for _ in range(2)]
            Vo = [ph.sb([128, 15, 128], BF16, "Vo") for _ in range(2)]
            Vc = [ph.sb([128, 2, 128], BF16, "Vc") for _ in range(2)]
            ckt = [ph.sb([128, 2, 128], F32, "ckt") for _ in range(2)]
            kcT = [ph.sb([128, 256], BF16, "kcT") for _ in range(2)]
            ob = [ph.sb([128, NS_TOK], BF16, "ob") for _ in range(2)]
            bm = [ph.sb([64, 8, 512], F32, "bm") for _ in range(2)]
            sc = [ph.sb([64, 768], F32, "sc") for _ in range(2)]
            pb = [ph.sb([64, 768], F32, "pb") for _ in range(2)]
            pn = [ph.sb([64, 768], BF16, "pn") for _ in range(2)]
            PT = [ph.sb([128, 6, 64], BF16, "PT") for _ in range(2)]
            sm = [ph.sb([128, 8], F32, "sm") for _ in range(2)]
            qv = g.qkT.rearrange("k p t -> p k t")
            u = 0
            for c in range(8):
                b_ = c % 2
                S.dma("sp", qk[b_][:, 0, :], qv[:, c, :], reads=[g.qkT_r], writes=[qk[b_].res])
                S.dma("sp", qk[b_][:, 1, :], qv[:, 8 + c, :], reads=[g.qkT_r], writes=[qk[b_].res], acc=True)
                S.dma("sp", Ve[b_][:], g.vtm[:, c * 128:(c + 1) * 128].rearrange("(j p) c -> p j c", p=128), reads=[g.vtm_r],
                      writes=[Ve[b_].res])
                S.dma("sp", Vo[b_][:], g.vtm[64:64 + 15 * 128, c * 128:(c + 1) * 128].rearrange("(j p) c -> p j c", p=128),
                      reads=[g.vtm_r], writes=[Vo[b_].res])
                S.dma("pool", Vc[b_][:], I["cv"][l][:, c * 128:(c + 1) * 128].rearrange("(j p) c -> p j c", p=128), writes=[Vc[b_].res])
                S.dma("sp", ckt[b_][:], I["ck"][l][:, c * 128:(c + 1) * 128].rearrange("(j p) c -> p j c", p=128), writes=[ckt[b_].res])
                pk = next_ps()
                S.mm([lambda pe, t=t: pe.transpose(pk[:, t * 128:(t + 1) * 128], ckt[b_][:, t, :], ident[:]) for t in range(2)],
                     reads=[ckt[b_].res, ident.res], writes=[pk.res])
                S.op("act", lambda e: e.copy(out=kcT[b_][:], in_=pk[:, 0:256]), reads=[pk.res], writes=[kcT[b_].res])
                for hh in range(2):
                    h = 2 * c + hh
                    p0 = hh * 64
                    bmh = bm[hh]
                    for o in range(8):
                        S.dma("sp", bmh[:, o, :].rearrange("p (j k) -> p j k", k=64),
                              AP(rrep.tensor, ((h * 15 + o) * 64) * 157 + 78, [[156, 64], [64 * 157, 8], [1, 64]]),
                              reads=[rrep_r], writes=[bmh.res], acc=(o > 0))
                    S.op("pool", lambda e: e.tensor_tensor(out=bmh[:].rearrange("p o (j k) -> p (o j) k", k=64),
                                                           in0=bmh[:].rearrange("p o (j k) -> p (o j) k", k=64),
                                                           in1=mk[:].unsqueeze(1).broadcast_to([64, 64, 64]), op=ALU.add),
                         reads=[bmh.res, mk.res], writes=[bmh.res])
                    def stage_a(r, i):
                        r0 = min(max(r - 4, 0), 24)
                        o = rcls(r)
                        psA, psB = next_ps(), next_ps()
                        qa = qk[b_][p0:p0 + 64, 0, r * 64:(r + 1) * 64]
                        S.mm([lambda pe: pe.matmul(psA[0:64, 0:512], lhsT=qa, rhs=qk[b_][p0:p0 + 64, 1, r0 * 64:r0 * 64 + 512],
                                                   start=True, stop=True)], reads=[qk[b_].res], writes=[psA.res])
                        S.mm([lambda pe: pe.matmul(psB[0:64, 0:256], lhsT=qa, rhs=kcT[b_][p0:p0 + 64, :], start=True, stop=True)],
                             reads=[qk[b_].res, kcT[b_].res], writes=[psB.res])
                        S.op("dve", lambda e: e.scalar_tensor_tensor(out=sc[i][:, 0:512], in0=psA[0:64, 0:512], scalar=SCALE,
                                                                     in1=bmh[:, o, :], op0=ALU.mult, op1=ALU.add),
                             reads=[psA.res, bmh.res], writes=[sc[i].res])
                        S.op("act", lambda e: e.mul(out=sc[i][:, 512:768], in_=psB[0:64, 0:256], mul=SCALE), reads=[psB.res, sc[i].res],
                             writes=[sc[i].res])
                        mx, nmx, rsum, rinv = softmax_rows(64, 768, sc[i][:], sc[i].res, 1.0, sm[i], None)
                        S.op("act", lambda e: e.activation(out=pb[i][:], in_=sc[i][:], func=AF.Exp, bias=nmx, scale=1.0,
                                                           accum_out=rsum), reads=[sc[i].res, sm[i].res], writes=[pb[i].res, sm[i].res])
                        S.op("dve", lambda e: e.reciprocal(out=rinv, in_=rsum), reads=[sm[i].res], writes=[sm[i].res])
                        S.op("dve", lambda e: e.tensor_scalar(out=pn[i][:], in0=pb[i][:], scalar1=rinv, scalar2=None, op0=ALU.mult),
                             reads=[pb[i].res, sm[i].res], writes=[pn[i].res])

                    def stage_b(r, i):
                        r0 = min(max(r - 4, 0), 24)
                        pt2 = next_ps()
                        S.mm([lambda pe, j=j: pe.matmul(pt2[:, j * 64:(j + 1) * 64], lhsT=pn[i][:, j * 128:(j + 1) * 128], rhs=identb[0:64, 0:64],
                                                        start=True, stop=True)
                              for j in range(6)], reads=[pn[i].res, identb.res], writes=[pt2.res])
                        S.op("act", lambda e: e.copy(out=PT[i][:], in_=pt2[:, 0:384].rearrange("p (a b) -> p a b", b=64)),
                             reads=[pt2.res], writes=[PT[i].res])
                        po = next_ps()
                        fns = []
                        for j in range(6):
                            if j < 4:
                                vt = Ve[b_][:, r0 // 2 + j, :] if r0 % 2 == 0 else Vo[b_][:, (r0 - 1) // 2 + j, :]
                            else:
                                vt = Vc[b_][:, j - 4, :]
                            fns.append(lambda pe, j=j, vt=vt: pe.matmul(po[:, 0:64], lhsT=vt, rhs=PT[i][:, j, :], start=(j == 0), stop=(j == 5)))
                        S.mm(fns, reads=[Ve[b_].res, Vo[b_].res, Vc[b_].res, PT[i].res], writes=[po.res])
                        S.op("dve", lambda e: e.tensor_copy(out=ob[b_][p0:p0 + 64, r * 64:(r + 1) * 64], in_=po[p0:p0 + 64, 0:64]),
                             reads=[po.res], writes=[ob[b_].res])
                    for r in range(33):
                        if r < 32:
                            stage_a(r, r % 2)
                        if r >= 1:
                            stage_b(r - 1, (r - 1) % 2)
                S.dma("sp", g.oT[0][0][c], ob[b_][:], reads=[ob[b_].res], writes=[g.oT[0][1]], acc=True)

    HY = {}
    for L_ in (256, 2048):
        HY[L_] = dict(zposT=din("zposT%d" % L_, [33, L_]), win=din("win%d" % L_, [L_, 1024]),
                      Ff=din("Ff%d" % L_, [L_, 2 * L_]), Fi=din("Fi%d" % L_, [2 * L_, L_]))
        HY[L_]["Hs"], HY[L_]["Hs_r"] = dscr("Hs%d" % L_, [2 * L_ // 128, 128, 1024])
    for g in G:
        g.zbs, g.zbs_r = dscr("zbs%d" % g.i, [g.ntok, 1024], BF16)
        g.zd, g.zd_r = dscr("zd%d" % g.i, [g.ntok, 1024])
        g.x0s, g.x0s_r = dscr("x0s%d" % g.i, [g.ntok, 1024])
    TWO_PI = 2.0 * math.pi

    def bcast_row(dram_ap_tensor, offset, n):
        return AP(dram_ap_tensor, offset, [[0, 128], [1, n]])

    def hyena_filter(l, L):
        hy_ = HY[L]
        LT = L // 128
        with Phase(C, "hyf") as ph:
            f1 = ph.sb([33, 64], F32, "f1")
            f2 = ph.sb([64, 64], F32, "f2")
            f3 = ph.sb([64, 1024], F32, "f3")
            cc = ph.sb([64, 4], F32, "cc")
            zp = ph.sb([33, L], F32, "zp")
            S.dma("sp", f1[:], W["hy_f1"][l], writes=[f1.res])
            S.dma("sp", f2[:], W["hy_f2"][l], writes=[f2.res])
            S.dma("sp", f3[:], W["hy_f3"][l], writes=[f3.res])
            for j, nm in enumerate(("hy_fb1", "hy_fb2", "hy_freq")):
                S.dma("sp", cc[:, j:j + 1], W[nm][l].rearrange("(p o) -> p o", o=1), writes=[cc.res], slow=True, acc=(j > 0))
            S.dma("sp", zp[:], hy_["zposT"], writes=[zp.res])
            t1 = ph.sb([64, L], F32, "t1")
            t2 = ph.sb([64, L], F32, "t2")
            a = ph.sb([64, 512], F32, "a")
            ki = ph.sb([64, 512], I32, "ki")
            kf = ph.sb([64, 512], F32, "kf")
            cw = min(512, L)

            def sin_layer(lt, K, src, dst, bcol):
                for cb in range(L // cw):
                    pt = next_ps(0, 6)
                    S.mm([lambda pe: pe.matmul(pt[0:64, 0:cw], lhsT=lt[0:K, :], rhs=src[0:K, cb * cw:(cb + 1) * cw], start=True, stop=True)],
                         reads=[lt.res, src.res], writes=[pt.res])
                    S.op("dve", lambda e: e.tensor_scalar(out=a[:, 0:cw], in0=pt[0:64, 0:cw], scalar1=cc[:, bcol:bcol + 1],
                                                          scalar2=cc[:, 2:3], op0=ALU.add, op1=ALU.mult), reads=[pt.res, cc.res], writes=[a.res])
                    S.op("dve", lambda e: e.tensor_scalar(out=ki[:, 0:cw], in0=a[:, 0:cw], scalar1=1.0 / TWO_PI, scalar2=None, op0=ALU.mult),
                         reads=[a.res], writes=[ki.res])
                    S.op("dve", lambda e: e.tensor_copy(out=kf[:, 0:cw], in_=ki[:, 0:cw]), reads=[ki.res], writes=[kf.res])
                    S.op("dve", lambda e: e.scalar_tensor_tensor(out=a[:, 0:cw], in0=kf[:, 0:cw], scalar=-TWO_PI, in1=a[:, 0:cw],
                                                                 op0=ALU.mult, op1=ALU.add), reads=[kf.res, a.res], writes=[a.res])
                    S.op("dve", lambda e: e.tensor_scalar(out=a[:, 0:cw], in0=a[:, 0:cw], scalar1=-math.pi, scalar2=math.pi, op0=ALU.max,
                                                          op1=ALU.min), reads=[a.res], writes=[a.res])
                    S.op("act", lambda e: e.activation(out=dst[:, cb * cw:(cb + 1) * cw], in_=a[:, 0:cw], func=AF.Sin),
                         reads=[a.res], writes=[dst.res])
            sin_layer(f1, 33, zp, t1, 0)
            sin_layer(f2, 64, t1, t2, 1)
            filtb = ph.sb([128, LT, 1024], BF16, "filtb")
            winb = [ph.sb([128, 1024], F32, "winb") for _ in range(2)]
            ft = [ph.sb([128, 1024], F32, "ft") for _ in range(2)]
            fa = [ph.sb([128, 1024], F32, "fa") for _ in range(2)]
            for tt in range(LT):
                i = tt % 2
                S.dma("sp", winb[i][:], hy_["win"][tt * 128:(tt + 1) * 128, :], writes=[winb[i].res])
                for hf in range(2):
                    pt = next_ps(0, 6)
                    S.mm([lambda pe: pe.matmul(pt[:, :], lhsT=t2[0:64, tt * 128:(tt + 1) * 128], rhs=f3[0:64, hf * 512:(hf + 1) * 512],
                                               start=True, stop=True)], reads=[t2.res, f3.res], writes=[pt.res])
                    S.op("dve", lambda e: e.tensor_tensor(out=ft[i][:, hf * 512:(hf + 1) * 512], in0=pt[:, :],
                                                          in1=winb[i][:, hf * 512:(hf + 1) * 512], op=ALU.mult),
                         reads=[pt.res, winb[i].res], writes=[ft[i].res])
                S.op("act", lambda e: e.activation(out=fa[i][:], in_=ft[i][:], func=AF.Abs), reads=[ft[i].res], writes=[fa[i].res])
                S.op("pool", lambda e: e.tensor_copy(out=filtb[:, tt, :], in_=ft[i][:]), reads=[ft[i].res], writes=[filtb.res])
                for hf in range(2):
                    pacc = psum[6 + hf]
                    S.mm([lambda pe: pe.matmul(pacc[:, :], lhsT=ones[:], rhs=fa[i][:, hf * 512:(hf + 1) * 512], start=(tt == 0),
                                               stop=(tt == LT - 1))], reads=[ones.res, fa[i].res], writes=[pacc.res])
            inv = ph.sb([128, 1024], F32, "inv")
            for hf in range(2):
                S.op("dve", lambda e: e.tensor_scalar(out=inv[:, hf * 512:(hf + 1) * 512], in0=psum[6 + hf][:, :], scalar1=1e-6,
                                                      scalar2=None, op0=ALU.add), reads=[psum[6 + hf].res, inv.res], writes=[inv.res])
            S.op("dve", lambda e: e.reciprocal(out=inv[:], in_=inv[:]), reads=[inv.res], writes=[inv.res])
            wb = [ph.sb([128, LT, 512], BF16, "wbF") for _ in range(2)]
            ob = [ph.sb([128, 512], F32, "ob") for _ in range(3)]
            st = {"i": 0}

            def epi(c0, t0, pt, nr, ncv):
                i = st["i"] % 3
                st["i"] += 1
                S.op("dve", lambda e: e.tensor_tensor(out=ob[i][:], in0=pt[:, :], in1=inv[:, t0:t0 + 512], op=ALU.mult),
                     reads=[pt.res, inv.res], writes=[ob[i].res])
                S.dma("sp", hy_["Hs"][c0 // 128, :, t0:t0 + 512], ob[i][:], reads=[ob[i].res], writes=[hy_["Hs_r"]], acc=True)
            gemm(ph, hy_["Ff"], LT, 0, 2 * L, 512, filtb, 1024, "fm", epi, wb, ps_lo=0, ps_hi=6)

    def hyena_conv(l, g, L):
        with Phase(C, "hyc") as ph:
            cwt = ph.sb([128, 3, 3072], F32, "cw")
            cbt = ph.sb([128, 3072], F32, "cb")
            drt = ph.sb([128, 1024], F32, "dr")
            for j in range(3):
                S.dma("sp", cwt[:, j, :], bcast_row(W["hy_conv_w"].tensor, (l * 3 + j) * 3072, 3072), writes=[cwt.res], acc=(j > 0))
            S.dma("sp", cbt[:], bcast_row(W["hy_conv_b"].tensor, l * 3072, 3072), writes=[cbt.res])
            S.dma("sp", drt[:], bcast_row(W["hy_d"].tensor, l * 1024, 1024), writes=[drt.res])
            xm = [ph.sb([128, 1024], F32, "xm") for _ in range(2)]
            xc = [ph.sb([128, 1024], F32, "xc") for _ in range(2)]
            xp = [ph.sb([128, 1024], F32, "xp") for _ in range(2)]
            uu = [[ph.sb([128, 1024], F32, "u%d" % cg) for cg in range(3)] for _ in range(2)]
            tq = [ph.sb([128, 1024], F32, "tq") for _ in range(2)]
            zf = [ph.sb([128, 1024], F32, "zf") for _ in range(2)]
            zbt = [ph.sb([128, 1024], BF16, "zb") for _ in range(2)]
            zdt = [ph.sb([128, 1024], F32, "zd") for _ in range(2)]
            k = 0
            for tt in range(g.ntok // 128):
                t0 = tt * 128
                sp_ = t0 % L
                bi = tt % 2
                for cg in range(3):
                    ki_ = k % 2
                    k += 1
                    c0 = 3072 + cg * 1024
                    xm_, xc_, xp_, u, t = xm[ki_], xc[ki_], xp[ki_], uu[bi][cg], tq[ki_]
                    if sp_ == 0:
                        S.op("pool", lambda e: e.memset(xm_[0:1, :], 0.0), writes=[xm_.res])
                        S.dma("sp", xm_[1:128, :], g.rh[t0:t0 + 127, c0:c0 + 1024], reads=[g.rh_r], writes=[xm_.res], acc=True)
                    else:
                        S.dma("sp", xm_[:], g.rh[t0 - 1:t0 + 127, c0:c0 + 1024], reads=[g.rh_r], writes=[xm_.res])
                    S.dma("sp", xc_[:], g.rh[t0:t0 + 128, c0:c0 + 1024], reads=[g.rh_r], writes=[xc_.res])
                    if sp_ + 128 == L:
                        S.op("pool", lambda e: e.memset(xp_[:], 0.0), writes=[xp_.res])
                        S.dma("sp", xp_[0:127, :], g.rh[t0 + 1:t0 + 128, c0:c0 + 1024], reads=[g.rh_r], writes=[xp_.res], acc=True)
                    else:
                        S.dma("sp", xp_[:], g.rh[t0 + 1:t0 + 129, c0:c0 + 1024], reads=[g.rh_r], writes=[xp_.res])
                    wv = lambda j: cwt[:, j, cg * 1024:(cg + 1) * 1024]
                    S.op("dve", lambda e: e.tensor_tensor(out=u[:], in0=xm_[:], in1=wv(0), op=ALU.mult), reads=[xm_.res, cwt.res], writes=[u.res])
                    S.op("pool", lambda e: e.tensor_tensor(out=t[:], in0=xc_[:], in1=wv(1), op=ALU.mult), reads=[xc_.res, cwt.res], writes=[t.res])
                    S.op("dve", lambda e: e.tensor_tensor(out=u[:], in0=u[:], in1=t[:], op=ALU.add), reads=[u.res, t.res], writes=[u.res])
                    S.op("pool", lambda e: e.tensor_tensor(out=t[:], in0=xp_[:], in1=wv(2), op=ALU.mult), reads=[xp_.res, cwt.res], writes=[t.res])
                    S.op("dve", lambda e: e.tensor_tensor(out=u[:], in0=u[:], in1=t[:], op=ALU.add), reads=[u.res, t.res], writes=[u.res])
                    S.op("dve", lambda e: e.tensor_tensor(out=u[:], in0=u[:], in1=cbt[:, cg * 1024:(cg + 1) * 1024], op=ALU.add),
                         reads=[u.res, cbt.res], writes=[u.res])
                u0, u1, u2 = uu[bi]
                S.dma("sp", g.x0s[t0:t0 + 128, :], u0[:], reads=[u0.res], writes=[g.x0s_r], acc=True)
                S.op("dve", lambda e: e.tensor_tensor(out=zf[bi][:], in0=u2[:], in1=u1[:], op=ALU.mult), reads=[u1.res, u2.res], writes=[zf[bi].res])
                S.op("act", lambda e: e.copy(out=zbt[bi][:], in_=zf[bi][:]), reads=[zf[bi].res], writes=[zbt[bi].res])
                S.op("pool", lambda e: e.tensor_tensor(out=zdt[bi][:], in0=zf[bi][:], in1=drt[:], op=ALU.mult), reads=[zf[bi].res, drt.res],
                     writes=[zdt[bi].res])
                S.dma("sp", g.zbs[t0:t0 + 128, :], zbt[bi][:], reads=[zbt[bi].res], writes=[g.zbs_r], acc=True)
                S.dma("sp", g.zd[t0:t0 + 128, :], zdt[bi][:], reads=[zdt[bi].res], writes=[g.zd_r], acc=True)

    def hyena_dft(l, g, L):
        hy_ = HY[L]
        LT = L // 128
        with Phase(C, "hyd") as ph:
            zT = ph.sb([128, LT, 512], BF16, "zT")
            YT = ph.sb([128, 2 * LT, 512], BF16, "YT")
            wbF = [ph.sb([128, LT, 512], BF16, "wbF") for _ in range(2)]
            wbI = [ph.sb([128, 2 * LT, 256], BF16, "wbI") for _ in range(2)]
            hre = [ph.sb([128, 512], F32, "hre") for _ in range(2)]
            him = [ph.sb([128, 512], F32, "him") for _ in range(2)]
            zre = [ph.sb([128, 512], F32, "zre") for _ in range(2)]
            zim = [ph.sb([128, 512], F32, "zim") for _ in range(2)]
            ta = [ph.sb([128, 512], F32, "ta") for _ in range(2)]
            tb = [ph.sb([128, 512], F32, "tb") for _ in range(2)]
            tc_ = [ph.sb([128, 512], F32, "tc") for _ in range(2)]
            td = [ph.sb([128, 512], F32, "td") for _ in range(2)]
            zdt = [ph.sb([128, 512], F32, "zdt") for _ in range(2)]
            x0t = [ph.sb([128, 512], F32, "x0t") for _ in range(2)]
            ot = [ph.sb([128, 512], F32, "ot") for _ in range(2)]
            otb = [ph.sb([128, 4, 128], BF16, "otb") for _ in range(2)]
            for sq in range(g.ntok // L):
                s0 = sq * L
                for half in range(2):
                    h0 = half * 512
                    S.dma("sp", zT[:], g.zbs[s0:s0 + L, h0:h0 + 512].rearrange("(j p) c -> p j c", p=128), reads=[g.zbs_r], writes=[zT.res])
                    st = {"i": 0, "re": None}

                    def epi_f(c0, t0, pt, nr, ncv):
                        blk = c0 // 128
                        if blk % 2 == 0:
                            i = st["i"] % 2
                            S.op("act", lambda e: e.copy(out=zre[i][:], in_=pt[:, :]), reads=[pt.res], writes=[zre[i].res])
                            S.dma("sp", hre[i][:], hy_["Hs"][blk, :, h0:h0 + 512], reads=[hy_["Hs_r"]], writes=[hre[i].res])
                            S.dma("sp", him[i][:], hy_["Hs"][blk + 1, :, h0:h0 + 512], reads=[hy_["Hs_r"]], writes=[him[i].res])
                            return
                        i = st["i"] % 2
                        st["i"] += 1
                        S.op("act", lambda e: e.copy(out=zim[i][:], in_=pt[:, :]), reads=[pt.res], writes=[zim[i].res])
                        S.op("dve", lambda e: e.tensor_tensor(out=ta[i][:], in0=zre[i][:], in1=hre[i][:], op=ALU.mult),
                             reads=[zre[i].res, hre[i].res], writes=[ta[i].res])
                        S.op("pool", lambda e: e.tensor_tensor(out=tb[i][:], in0=zim[i][:], in1=him[i][:], op=ALU.mult),
                             reads=[zim[i].res, him[i].res], writes=[tb[i].res])
                        S.op("dve", lambda e: e.tensor_tensor(out=YT[:, blk - 1, :], in0=ta[i][:], in1=tb[i][:], op=ALU.subtract),
                             reads=[ta[i].res, tb[i].res], writes=[YT.res])
                        S.op("pool", lambda e: e.tensor_tensor(out=tc_[i][:], in0=zre[i][:], in1=him[i][:], op=ALU.mult),
                             reads=[zre[i].res, him[i].res], writes=[tc_[i].res])
                        S.op("dve", lambda e: e.tensor_tensor(out=td[i][:], in0=zim[i][:], in1=hre[i][:], op=ALU.mult),
                             reads=[zim[i].res, hre[i].res], writes=[td[i].res])
                        S.op("dve", lambda e: e.tensor_tensor(out=YT[:, blk, :], in0=tc_[i][:], in1=td[i][:], op=ALU.add),
                             reads=[tc_[i].res, td[i].res], writes=[YT.res])
                    gemm(ph, hy_["Ff"], LT, 0, 2 * L, 512, zT, 512, "fm", epi_f, wbF)
                    st2 = {"i": 0}

                    def epi_i(c0, t0, pt, nr, ncv):
                        i = st2["i"] % 2
                        st2["i"] += 1
                        r0 = s0 + c0
                        S.dma("sp", zdt[i][:], g.zd[r0:r0 + 128, h0:h0 + 512], reads=[g.zd_r], writes=[zdt[i].res])
                        S.dma("sp", x0t[i][:], g.x0s[r0:r0 + 128, h0:h0 + 512], reads=[g.x0s_r], writes=[x0t[i].res])
                        S.op("dve", lambda e: e.tensor_tensor(out=ot[i][:], in0=pt[:, :], in1=zdt[i][:], op=ALU.add),
                             reads=[pt.res, zdt[i].res], writes=[ot[i].res])
                        S.op("pool", lambda e: e.tensor_tensor(out=ot[i][:], in0=ot[i][:], in1=x0t[i][:], op=ALU.mult),
                             reads=[ot[i].res, x0t[i].res], writes=[ot[i].res])
                        pt2 = next_ps()
                        S.mm([lambda pe, q=q: pe.transpose(pt2[:, q * 128:(q + 1) * 128], ot[i][:, q * 128:(q + 1) * 128], ident[:])
                              for q in range(4)], reads=[ot[i].res, ident.res], writes=[pt2.res])
                        S.op("act", lambda e: e.copy(out=otb[i][:], in_=pt2[:, :].rearrange("p (a b) -> p a b", b=128)),
                             reads=[pt2.res], writes=[otb[i].res])
                        S.dma("sp", g.oT[2][0][half * 4:(half + 1) * 4, :, r0:r0 + 128].rearrange("k p t -> p k t"), otb[i][:],
                              reads=[otb[i].res], writes=[g.oT[2][1]], acc=True)
                    gemm(ph, hy_["Fi"], 2 * LT, 0, L, 256, YT, 512, "fm", epi_i, wbI)

    for g in G:
        g.lora, g.lora_r = dscr("lora%d" % g.i, [4, 64, g.ntok], BF16)
        g.sg, g.sg_r = dscr("sg%d" % g.i, [128, g.ntok], BF16)
        g.SH, g.SH_r = dscr("SH%d" % g.i, [g.ntok, 3, 1024])
        g.DE = [dscr("DE%d_%d" % (g.i, e), [g.ntok, 3, 1024]) for e in range(2)]
        g.ysc = [dscr("ysc%d_%d" % (g.i, e), [g.ntok, 1024]) for e in range(2)]
        g.gsc, g.gsc_r = dscr("gsc%d" % g.i, [g.ntok, 1024])
        g.bon, g.bon_r = dscr("bon%d" % g.i, [g.ntok, 1024])

    def rwkv_lora(ph, l, g, hT):
        wbs = [ph.sb([128, 16, 128], BF16, "wbl") for _ in range(2)]
        obl = [ph.sb([128, 512], BF16, "obl") for _ in range(3)]
        for e in range(2):
            gemm(ph, W["wkv_w1"][l][e], 16, 0, 64, 64, hT, g.ntok, "fm",
                 epi_store(ph, obl, lambda c0, t0, nr, ncv, e=e: g.lora[e, :, t0:t0 + ncv], g.lora_r, func=AF.Tanh), wbs)
            gemm(ph, W["wkv_a1"][l][e], 16, 0, 64, 64, hT, g.ntok, "fm",
                 epi_store(ph, obl, lambda c0, t0, nr, ncv, e=e: g.lora[2 + e, :, t0:t0 + ncv], g.lora_r), wbs)
        gemm(ph, W["wkv_g1"][l], 16, 0, 128, 128, hT, g.ntok, "fm",
             epi_store(ph, obl, lambda c0, t0, nr, ncv: g.sg[:, t0:t0 + ncv], g.sg_r, func=AF.Sigmoid), wbs)

    def rwkv_pre(l, g, L):
        with Phase(C, "rwp") as ph:
            cwt = ph.sb([128, 3, 3072], F32, "cw")
            cbt = ph.sb([128, 3072], F32, "cb")
            for j in range(3):
                S.dma("sp", cwt[:, j, :], bcast_row(W["wkv_conv_w"].tensor, (l * 3 + j) * 3072, 3072), writes=[cwt.res], acc=(j > 0))
            S.dma("sp", cbt[:], bcast_row(W["wkv_conv_b"].tensor, l * 3072, 3072), writes=[cbt.res])
            rows = ph.sb([128, 8, 1024], F32, "rows")
            for e in range(2):
                S.dma("sp", rows[:, e, :], bcast_row(W["wkv_w0"].tensor, (l * 2 + e) * 1024, 1024), writes=[rows.res], acc=True)
                S.dma("sp", rows[:, 2 + e, :], bcast_row(W["wkv_a0"].tensor, (l * 2 + e) * 1024, 1024), writes=[rows.res], acc=True)
            S.dma("sp", rows[:, 4, :], bcast_row(W["wkv_k_k"].tensor, l * 1024, 1024), writes=[rows.res], acc=True)
            S.dma("sp", rows[:, 5, :], bcast_row(W["wkv_k_a"].tensor, l * 1024, 1024), writes=[rows.res], acc=True)
            S.dma("sp", rows[:, 7, :], bcast_row(W["wkv_r_k"].tensor, l * 1024, 1024), writes=[rows.res], acc=True)
            S.op("dve", lambda e_: e_.tensor_scalar(out=rows[:, 6, :], in0=rows[:, 5, :], scalar1=-1.0, scalar2=1.0, op0=ALU.mult, op1=ALU.add),
                 reads=[rows.res], writes=[rows.res])
            w2b = ph.sb([64, 4, 1024], BF16, "w2b")
            for e in range(2):
                S.dma("pool", w2b[:, e, :], W["wkv_w2"][l][e], writes=[w2b.res], acc=True)
                S.dma("pool", w2b[:, 2 + e, :], W["wkv_a2"][l][e], writes=[w2b.res], acc=True)
            g2b = ph.sb([128, 1024], BF16, "g2b")
            S.dma("pool", g2b[:], W["wkv_g2"][l], writes=[g2b.res])
            xms = [ph.sb([128, 1024], F32, "xm") for _ in range(2)]
            xcs = [ph.sb([128, 1024], F32, "xc") for _ in range(2)]
            xps = [ph.sb([128, 1024], F32, "xp") for _ in range(2)]
            kx = [0]
            rkv = [ph.sb([128, 1024], F32, "rkv%d" % i) for i in range(3)]
            t1 = ph.sb([128, 1024], F32, "t1")
            t2 = ph.sb([128, 1024], F32, "t2")
            kk = ph.sb([128, 1024], F32, "kk")
            at = ph.sb([128, 1024], F32, "at")
            wt = ph.sb([128, 1024], F32, "wt")
            o1 = ph.sb([128, 1024], F32, "o1")
            o2 = ph.sb([128, 1024], F32, "o2")
            sm = ph.sb([128, 64], F32, "sm")
            lt = ph.sb([64, 4, 128], BF16, "lt")
            sgt = ph.sb([128, 128], BF16, "sgt")
            v3 = lambda t_: t_[:].rearrange("p (h k) -> p h k", k=64)
            for tt in range(g.ntok // 128):
                t0 = tt * 128
                sp_ = t0 % L
                for cg in range(3):
                    c0 = cg * 1024
                    u = rkv[cg]
                    xm, xc, xp = xms[kx[0] % 2], xcs[kx[0] % 2], xps[kx[0] % 2]
                    kx[0] += 1
                    if sp_ == 0:
                        S.op("pool", lambda e: e.memset(xm[0:1, :], 0.0), writes=[xm.res])
                        S.dma("sp", xm[1:128, :], g.rh[t0:t0 + 127, c0:c0 + 1024], reads=[g.rh_r], writes=[xm.res], acc=True)
                    else:
                        S.dma("sp", xm[:], g.rh[t0 - 1:t0 + 127, c0:c0 + 1024], reads=[g.rh_r], writes=[xm.res])
                    S.dma("sp", xc[:], g.rh[t0:t0 + 128, c0:c0 + 1024], reads=[g.rh_r], writes=[xc.res])
                    if sp_ + 128 == L:
                        S.op("pool", lambda e: e.memset(xp[:], 0.0), writes=[xp.res])
                        S.dma("sp", xp[0:127, :], g.rh[t0 + 1:t0 + 128, c0:c0 + 1024], reads=[g.rh_r], writes=[xp.res], acc=True)
                    else:
                        S.dma("sp", xp[:], g.rh[t0 + 1:t0 + 129, c0:c0 + 1024], reads=[g.rh_r], writes=[xp.res])
                    wv = lambda j: cwt[:, j, cg * 1024:(cg + 1) * 1024]
                    S.op("dve", lambda e: e.tensor_tensor(out=u[:], in0=xm[:], in1=wv(0), op=ALU.mult), reads=[xm.res, cwt.res], writes=[u.res])
                    S.op("pool", lambda e: e.tensor_tensor(out=t1[:], in0=xc[:], in1=wv(1), op=ALU.mult), reads=[xc.res, cwt.res], writes=[t1.res])
                    S.op("dve", lambda e: e.tensor_tensor(out=u[:], in0=u[:], in1=t1[:], op=ALU.add), reads=[u.res, t1.res], writes=[u.res])
                    S.op("pool", lambda e: e.tensor_tensor(out=t1[:], in0=xp[:], in1=wv(2), op=ALU.mult), reads=[xp.res, cwt.res], writes=[t1.res])
                    S.op("dve", lambda e: e.tensor_tensor(out=u[:], in0=u[:], in1=t1[:], op=ALU.add), reads=[u.res, t1.res], writes=[u.res])
                    S.op("dve", lambda e: e.tensor_tensor(out=u[:], in0=u[:], in1=cbt[:, cg * 1024:(cg + 1) * 1024], op=ALU.add),
                         reads=[u.res, cbt.res], writes=[u.res])
                r_, k_, v_ = rkv
                S.dma("sp", g.SH[t0:t0 + 128, 1, :], r_[:], reads=[r_.res], writes=[g.SH_r], acc=True)
                S.dma("sp", g.SH[t0:t0 + 128, 2, :], v_[:], reads=[v_.res], writes=[g.SH_r], acc=True)
                S.op("dve", lambda e: e.tensor_tensor(out=kk[:], in0=k_[:], in1=rows[:, 4, :], op=ALU.mult), reads=[k_.res, rows.res], writes=[kk.res])
                S.op("pool", lambda e: e.tensor_tensor(out=t1[:], in0=kk[:], in1=kk[:], op=ALU.mult), reads=[kk.res], writes=[t1.res])
                S.op("dve", lambda e: e.tensor_reduce(out=sm[:, 0:16], in_=v3(t1), axis=AX.X, op=ALU.add), reads=[t1.res], writes=[sm.res])
                S.op("dve", lambda e: e.tensor_scalar(out=sm[:, 0:16], in0=sm[:, 0:16], scalar1=1e-12, scalar2=None, op0=ALU.add),
                     reads=[sm.res], writes=[sm.res])
                S.op("act", lambda e: e.sqrt(out=sm[:, 0:16], in_=sm[:, 0:16]), reads=[sm.res], writes=[sm.res])
                S.op("dve", lambda e: e.reciprocal(out=sm[:, 0:16], in_=sm[:, 0:16]), reads=[sm.res], writes=[sm.res])
                S.op("dve", lambda e: e.tensor_tensor(out=v3(kk), in0=v3(kk), in1=sm[:, 0:16].unsqueeze(2).broadcast_to([128, 16, 64]), op=ALU.mult),
                     reads=[kk.res, sm.res], writes=[kk.res])
                S.dma("sp", g.SH[t0:t0 + 128, 0, :], kk[:], reads=[kk.res], writes=[g.SH_r], acc=True)
                S.op("pool", lambda e: e.tensor_tensor(out=t1[:], in0=r_[:], in1=k_[:], op=ALU.mult), reads=[r_.res, k_.res], writes=[t1.res])
                S.op("pool", lambda e: e.tensor_tensor(out=t1[:], in0=t1[:], in1=rows[:, 7, :], op=ALU.mult), reads=[t1.res, rows.res], writes=[t1.res])
                S.op("dve", lambda e: e.tensor_reduce(out=sm[:, 16:32], in_=v3(t1), axis=AX.X, op=ALU.add), reads=[t1.res, sm.res], writes=[sm.res])
                S.op("dve", lambda e: e.tensor_tensor(out=v3(o1), in0=v3(v_), in1=sm[:, 16:32].unsqueeze(2).broadcast_to([128, 16, 64]), op=ALU.mult),
                     reads=[v_.res, sm.res], writes=[o1.res])
                S.dma("sp", g.bon[t0:t0 + 128, :], o1[:], reads=[o1.res], writes=[g.bon_r], acc=True)
                S.dma("sp", lt[:], g.lora[:, :, t0:t0 + 128].rearrange("a p t -> p a t"), reads=[g.lora_r], writes=[lt.res])
                S.dma("sp", sgt[:], g.sg[:, t0:t0 + 128], reads=[g.sg_r], writes=[sgt.res])
                for hf in range(2):
                    pt = next_ps()
                    S.mm([lambda pe: pe.matmul(pt[:, :], lhsT=sgt[:], rhs=g2b[:, hf * 512:(hf + 1) * 512], start=True, stop=True)],
                         reads=[sgt.res, g2b.res], writes=[pt.res])
                    S.op("act", lambda e: e.copy(out=o2[:, hf * 512:(hf + 1) * 512], in_=pt[:, :]), reads=[pt.res, o2.res], writes=[o2.res])
                S.dma("sp", g.gsc[t0:t0 + 128, :], o2[:], reads=[o2.res], writes=[g.gsc_r], acc=True)
                for e in range(2):
                    for hf in range(2):
                        pt = next_ps()
                        S.mm([lambda pe: pe.matmul(pt[:, :], lhsT=lt[:, e, :], rhs=w2b[:, e, hf * 512:(hf + 1) * 512], start=True, stop=True)],
                             reads=[lt.res, w2b.res], writes=[pt.res])
                        S.op("dve", lambda e_: e_.tensor_tensor(out=wt[:, hf * 512:(hf + 1) * 512], in0=pt[:, :],
                                                               in1=rows[:, e, hf * 512:(hf + 1) * 512], op=ALU.add),
                             reads=[pt.res, rows.res, wt.res], writes=[wt.res])
                    S.op("act", lambda e_: e_.activation(out=wt[:], in_=wt[:], func=AF.Sigmoid), reads=[wt.res], writes=[wt.res])
                    S.op("act", lambda e_: e_.activation(out=wt[:], in_=wt[:], func=AF.Exp, scale=-math.exp(-0.5)), reads=[wt.res], writes=[wt.res])
                    S.dma("sp", g.DE[e][0][t0:t0 + 128, 0, :], wt[:], reads=[wt.res], writes=[g.DE[e][1]], acc=True)
                    for hf in range(2):
                        pt = next_ps()
                        S.mm([lambda pe: pe.matmul(pt[:, :], lhsT=lt[:, 2 + e, :], rhs=w2b[:, 2 + e, hf * 512:(hf + 1) * 512], start=True, stop=True)],
                             reads=[lt.res, w2b.res], writes=[pt.res])
                        S.op("dve", lambda e_: e_.tensor_tensor(out=at[:, hf * 512:(hf + 1) * 512], in0=pt[:, :],
                                                               in1=rows[:, 2 + e, hf * 512:(hf + 1) * 512], op=ALU.add),
                             reads=[pt.res, rows.res, at.res], writes=[at.res])
                    S.op("act", lambda e_: e_.activation(out=at[:], in_=at[:], func=AF.Sigmoid), reads=[at.res], writes=[at.res])
                    S.op("dve", lambda e_: e_.tensor_tensor(out=o1[:], in0=kk[:], in1=at[:], op=ALU.mult), reads=[kk.res, at.res], writes=[o1.res])
                    S.dma("sp", g.DE[e][0][t0:t0 + 128, 1, :], o1[:], reads=[o1.res], writes=[g.DE[e][1]], acc=True)
                    S.op("pool", lambda e_: e_.tensor_tensor(out=t2[:], in0=at[:], in1=rows[:, 5, :], op=ALU.mult), reads=[at.res, rows.res], writes=[t2.res])
                    S.op("pool", lambda e_: e_.tensor_tensor(out=t2[:], in0=t2[:], in1=rows[:, 6, :], op=ALU.add), reads=[t2.res, rows.res], writes=[t2.res])
                    S.op("dve", lambda e_: e_.tensor_tensor(out=o2[:], in0=k_[:], in1=t2[:], op=ALU.mult), reads=[k_.res, t2.res], writes=[o2.res])
                    S.dma("sp", g.DE[e][0][t0:t0 + 128, 2, :], o2[:], reads=[o2.res], writes=[g.DE[e][1]], acc=True)

    def rwkv_scan(l, g, L):
        sample = (g.i == 1)
        NV = 16 if sample else 64
        TC = 32 if sample else 16
        with Phase(C, "rws") as ph:
            St = ph.sb([128, NV, 64], F32, "S")
            tmp = ph.sb([128, NV, 64], F32, "tmp")
            At = ph.sb([128, NV, 64], F32, "A")
            Bt = ph.sb([128, NV, 64], F32, "B")
            sa = ph.sb([128, NV], F32, "sa")
            tmpY = ph.sb([128, NV, 64], F32, "tmpY")
            Dq = [[ph.sb([128, TC, 64], F32, "D%d" % q) for q in range(5)] for _ in range(2)]
            Vt = [ph.sb([128, TC, NV], F32, "V") for _ in range(2)]
            Yt = [ph.sb([128, TC, NV], F32, "Y") for _ in range(2)]
            if sample:
                S.dma("sp", St[:].rearrange("p a b -> p (a b)"), I["s0"][l], writes=[St.res])
            else:
                S.op("pool", lambda e: e.memset(St[:], 0.0), writes=[St.res])
            def srcs(e):
                return [(g.SH, g.SH_r, 0), (g.DE[e][0], g.DE[e][1], 0), (g.DE[e][0], g.DE[e][1], 1), (g.DE[e][0], g.DE[e][1], 2), (g.SH, g.SH_r, 1)]
            nchunk = L // TC
            for c in range(nchunk):
                bi = c % 2
                i0 = c * TC
                for e in range(2):
                    tstart = i0 if e == 0 else (L - 1 - i0)
                    sgn = 1 if e == 0 else -1
                    if sample:
                        for vq in range(4):
                            dstp = slice(e * 64 + vq, (e + 1) * 64, 4)
                            first = (e == 0 and vq == 0)
                            for q, (arr, arr_r, slot) in enumerate(srcs(e)):
                                S.dma("sp", Dq[bi][q][dstp, :, :],
                                      AP(arr.tensor, tstart * 3072 + slot * 1024, [[64, 16], [sgn * 3072, TC], [1, 64]]),
                                      reads=[arr_r], writes=[Dq[bi][q].res], acc=(not first))
                            S.dma("sp", Vt[bi][dstp, :, :],
                                  AP(g.SH.tensor, tstart * 3072 + 2 * 1024 + vq * 16, [[64, 16], [sgn * 3072, TC], [1, 16]]),
                                  reads=[g.SH_r], writes=[Vt[bi].res], acc=(not first))
                    else:
                        for sq in range(4):
                            p0 = sq * 32 + e * 16
                            first = (e == 0 and sq == 0)
                            for q, (arr, arr_r, slot) in enumerate(srcs(e)):
                                S.dma("sp", Dq[bi][q][p0:p0 + 16, :, :],
                                      AP(arr.tensor, (sq * L + tstart) * 3072 + slot * 1024, [[64, 16], [sgn * 3072, TC], [1, 64]]),
                                      reads=[arr_r], writes=[Dq[bi][q].res], acc=(not first))
                            S.dma("sp", Vt[bi][p0:p0 + 16, :, :],
                                  AP(g.SH.tensor, (sq * L + tstart) * 3072 + 2 * 1024, [[64, 16], [sgn * 3072, TC], [1, 64]]),
                                  reads=[g.SH_r], writes=[Vt[bi].res], acc=(not first))
                D = Dq[bi]
                pend = None
                for i in range(TC):
                    bc = lambda q: D[q][:, i, :].unsqueeze(1).broadcast_to([128, NV, 64])
                    S.op("dve", lambda e_: e_.tensor_tensor(out=tmp[:], in0=St[:], in1=bc(0), op=ALU.mult), reads=[St.res, D[0].res], writes=[tmp.res])
                    if pend is not None:
                        S.op("dve", lambda e_: e_.tensor_reduce(out=Yt[bi][:, pend, :], in_=tmpY[:], axis=AX.X, op=ALU.add), reads=[tmpY.res], writes=[Yt[bi].res])
                    S.op("dve", lambda e_: e_.tensor_tensor(out=At[:], in0=Vt[bi][:, i, :].unsqueeze(2).broadcast_to([128, NV, 64]), in1=bc(3), op=ALU.mult),
                         reads=[Vt[bi].res, D[3].res], writes=[At.res])
                    S.op("dve", lambda e_: e_.tensor_reduce(out=sa[:], in_=tmp[:], axis=AX.X, op=ALU.add), reads=[tmp.res], writes=[sa.res])
                    S.op("dve", lambda e_: e_.tensor_tensor(out=Bt[:], in0=St[:], in1=bc(1), op=ALU.mult), reads=[St.res, D[1].res], writes=[Bt.res])
                    S.op("dve", lambda e_: e_.tensor_tensor(out=tmp[:], in0=sa[:].unsqueeze(2).broadcast_to([128, NV, 64]), in1=bc(2), op=ALU.mult),
                         reads=[sa.res, D[2].res], writes=[tmp.res])
                    S.op("dve", lambda e_: e_.tensor_tensor(out=Bt[:], in0=Bt[:], in1=At[:], op=ALU.add), reads=[Bt.res, At.res], writes=[Bt.res])
                    S.op("dve", lambda e_: e_.tensor_tensor(out=St[:], in0=Bt[:], in1=tmp[:], op=ALU.subtract), reads=[Bt.res, tmp.res], writes=[St.res])
                    S.op("dve", lambda e_: e_.tensor_tensor(out=tmpY[:], in0=St[:], in1=bc(4), op=ALU.mult), reads=[St.res, D[4].res], writes=[tmpY.res])
                    pend = i
                S.op("dve", lambda e_: e_.tensor_reduce(out=Yt[bi][:, pend, :], in_=tmpY[:], axis=AX.X, op=ALU.add), reads=[tmpY.res], writes=[Yt[bi].res])
                for e in range(2):
                    tstart = i0 if e == 0 else (L - 1 - i0)
                    sgn = 1 if e == 0 else -1
                    ya, ya_r = g.ysc[e]
                    if sample:
                        for vq in range(4):
                            S.dma("sp", AP(ya.tensor, tstart * 1024 + vq * 16, [[64, 16], [sgn * 1024, TC], [1, 16]]),
                                  Yt[bi][slice(e * 64 + vq, (e + 1) * 64, 4), :, :], reads=[Yt[bi].res], writes=[ya_r], acc=True)
                    else:
                        for sq in range(4):
                            p0 = sq * 32 + e * 16
                            S.dma("sp", AP(ya.tensor, (sq * L + tstart) * 1024, [[64, 16], [sgn * 1024, TC], [1, 64]]), Yt[bi][p0:p0 + 16, :, :],
                                  reads=[Yt[bi].res], writes=[ya_r], acc=True)
            if not sample:
                for sq in range(4):
                    r0 = (sq * DEPTH + l) * 32
                    S.dma("sp", O["ns"][r0:r0 + 32, :], St[sq * 32:(sq + 1) * 32, :, :].rearrange("p a b -> p (a b)"), reads=[St.res],
                          writes=[ORES["ns"]], acc=True)

    def rwkv_post(l, g):
        with Phase(C, "rwo") as ph:
            rows = ph.sb([128, 2, 1024], F32, "rows")
            S.dma("sp", rows[:, 0, :], bcast_row(W["wkv_gn_g"].tensor, l * 1024, 1024), writes=[rows.res], acc=True)
            S.dma("sp", rows[:, 1, :], bcast_row(W["wkv_gn_b"].tensor, l * 1024, 1024), writes=[rows.res], acc=True)
            ya = [ph.sb([128, 1024], F32, "ya") for _ in range(2)]
            yb = [ph.sb([128, 1024], F32, "yb") for _ in range(2)]
            bo = [ph.sb([128, 1024], F32, "bo") for _ in range(2)]
            gg = [ph.sb([128, 1024], F32, "gg") for _ in range(2)]
            tq = [ph.sb([128, 1024], F32, "tq") for _ in range(2)]
            sm = [ph.sb([128, 64], F32, "sm") for _ in range(2)]
            otb = [ph.sb([128, 8, 128], BF16, "otb") for _ in range(2)]
            v3 = lambda t_: t_[:].rearrange("p (h k) -> p h k", k=64)
            for tt in range(g.ntok // 128):
                i = tt % 2
                t0 = tt * 128
                y, y2, b_, g_, t_, s_ = ya[i], yb[i], bo[i], gg[i], tq[i], sm[i]
                S.dma("sp", y[:], g.ysc[0][0][t0:t0 + 128, :], reads=[g.ysc[0][1]], writes=[y.res])
                S.dma("sp", y2[:], g.ysc[1][0][t0:t0 + 128, :], reads=[g.ysc[1][1]], writes=[y2.res])
                S.dma("sp", b_[:], g.bon[t0:t0 + 128, :], reads=[g.bon_r], writes=[b_.res])
                S.dma("sp", g_[:], g.gsc[t0:t0 + 128, :], reads=[g.gsc_r], writes=[g_.res])
                S.op("dve", lambda e: e.tensor_tensor(out=y[:], in0=y[:], in1=y2[:], op=ALU.add), reads=[y.res, y2.res], writes=[y.res])
                S.op("dve", lambda e: e.tensor_reduce(out=s_[:, 0:16], in_=v3(y), axis=AX.X, op=ALU.add), reads=[y.res], writes=[s_.res])
                S.op("dve", lambda e: e.tensor_scalar(out=s_[:, 0:16], in0=s_[:, 0:16], scalar1=-1.0 / 64, scalar2=None, op0=ALU.mult),
                     reads=[s_.res], writes=[s_.res])
                S.op("dve", lambda e: e.tensor_tensor(out=v3(y), in0=v3(y), in1=s_[:, 0:16].unsqueeze(2).broadcast_to([128, 16, 64]), op=ALU.add),
                     reads=[y.res, s_.res], writes=[y.res])
                S.op("pool", lambda e: e.tensor_tensor(out=t_[:], in0=y[:], in1=y[:], op=ALU.mult), reads=[y.res], writes=[t_.res])
                S.op("dve", lambda e: e.tensor_reduce(out=s_[:, 16:32], in_=v3(t_), axis=AX.X, op=ALU.add), reads=[t_.res, s_.res], writes=[s_.res])
                S.op("dve", lambda e: e.tensor_scalar(out=s_[:, 16:32], in0=s_[:, 16:32], scalar1=1.0 / 64, scalar2=64e-5, op0=ALU.mult, op1=ALU.add),
                     reads=[s_.res], writes=[s_.res])
                S.op("act", lambda e: e.sqrt(out=s_[:, 16:32], in_=s_[:, 16:32]), reads=[s_.res], writes=[s_.res])
                S.op("dve", lambda e: e.reciprocal(out=s_[:, 16:32], in_=s_[:, 16:32]), reads=[s_.res], writes=[s_.res])
                S.op("dve", lambda e: e.tensor_tensor(out=v3(y), in0=v3(y), in1=s_[:, 16:32].unsqueeze(2).broadcast_to([128, 16, 64]), op=ALU.mult),
                     reads=[y.res, s_.res], writes=[y.res])
                S.op("pool", lambda e: e.tensor_tensor(out=y[:], in0=y[:], in1=rows[:, 0, :], op=ALU.mult), reads=[y.res, rows.res], writes=[y.res])
                S.op("pool", lambda e: e.tensor_tensor(out=y[:], in0=y[:], in1=rows[:, 1, :], op=ALU.add), reads=[y.res, rows.res], writes=[y.res])
                S.op("dve", lambda e: e.tensor_tensor(out=y[:], in0=y[:], in1=b_[:], op=ALU.add), reads=[y.res, b_.res], writes=[y.res])
                S.op("dve", lambda e: e.tensor_tensor(out=y[:], in0=y[:], in1=g_[:], op=ALU.mult), reads=[y.res, g_.res], writes=[y.res])
                for hf in range(2):
                    pt2 = next_ps()
                    S.mm([lambda pe, q=q: pe.transpose(pt2[:, q * 128:(q + 1) * 128], y[:, (hf * 4 + q) * 128:(hf * 4 + q + 1) * 128], ident[:])
                          for q in range(4)], reads=[y.res, ident.res], writes=[pt2.res])
                    S.op("act", lambda e: e.copy(out=otb[i][:, hf * 4:(hf + 1) * 4, :], in_=pt2[:, :].rearrange("p (a b) -> p a b", b=128)),
                         reads=[pt2.res, otb[i].res], writes=[otb[i].res])
                S.dma("sp", g.oT[1][0][:, :, t0:t0 + 128].rearrange("k p t -> p k t"), otb[i][:], reads=[otb[i].res], writes=[g.oT[1][1]], acc=True)

    def merge_out(l, g):
        with Phase(C, "mrg") as ph:
            cols = load_cols(ph, l, g)
            mT = ph.sb([128, 16, g.ntok], BF16, "mT")
            if not mixers or cfg.get("nomerge"):
                S.op("pool", lambda e: e.memset(mT[:], 0.0), writes=[mT.res])
            else:
                with Phase(C, "mrg1") as ph1:
                    branches = [b for b in range(3) if "arc"[b] in mixers]
                    wnames = ["w_pa", "w_pr", "w_pc"]
                    HT = 1024
                    oTt = {b: ph1.sb([128, 8, HT], BF16, "oT%d" % b) for b in branches}
                    wbs = {b: [ph1.sb([128, 8, 512], BF16, "wp%d" % b) for _ in range(2)] for b in branches}
                    gts = {b: [ph1.sb([128, 512], F32, "g%d" % b) for _ in range(2)] for b in branches}
                    tmp = [ph1.sb([128, 512], F32, "mt") for _ in range(2)]
                    tmp2 = [ph1.sb([128, 512], F32, "mt2") for _ in range(2)]
                    cnt = 0
                    for half in range(g.ntok // HT):
                        for b in branches:
                            S.dma("sp", oTt[b][:], g.oT[b][0].rearrange("k p t -> p k t")[:, :, half * HT:(half + 1) * HT],
                                  reads=[g.oT[b][1]], writes=[oTt[b].res])

                        def loadw(s):
                            for b in branches:
                                wbuf = wbs[b][s % 2]
                                S.dma("pool", wbuf[:], W[wnames[b]][l].rearrange("(k p) n -> p k n", p=128)[:, :, s * 512:(s + 1) * 512],
                                      writes=[wbuf.res])
                        loadw(0)
                        for s in range(4):
                            if s + 1 < 4:
                                loadw(s + 1)
                            for nb in range(4):
                                nbg = s * 4 + nb
                                for tt in range(HT // 512):
                                    t0 = half * HT + tt * 512
                                    pts = {}
                                    for b in branches:
                                        pt = next_ps()
                                        pts[b] = pt
                                        wbuf = wbs[b][s % 2]
                                        fns = [lambda pe, kc=kc, pt=pt, wbuf=wbuf, b=b: pe.matmul(
                                            pt[:, :], lhsT=wbuf[:, kc, nb * 128:(nb + 1) * 128],
                                            rhs=oTt[b][:, kc, tt * 512:(tt + 1) * 512], start=(kc == 0), stop=(kc == 7))
                                            for kc in range(8)]
                                        S.mm(fns, reads=[wbuf.res, oTt[b].res], writes=[pt.res])
                                        gt = gts[b][cnt % 2]
                                        S.dma("sp", gt[:], g.gT[b * 16 + nbg, :, t0:t0 + 512], reads=[g.gT_r], writes=[gt.res])
                                    t1, t2 = tmp[cnt % 2], tmp2[cnt % 2]
                                    acc = None
                                    for bi, b in enumerate(branches):
                                        gt = gts[b][cnt % 2]
                                        last = (bi == len(branches) - 1)
                                        if acc is None:
                                            dst = mT[:, nbg, t0:t0 + 512] if last else t1[:]
                                            S.op("dve", lambda e, dst=dst, pt=pts[b], gt=gt: e.tensor_tensor(out=dst, in0=pt[:, :], in1=gt[:], op=ALU.mult),
                                                 reads=[pts[b].res, gt.res], writes=[mT.res if last else t1.res])
                                            acc = t1
                                        else:
                                            S.op("dve", lambda e, pt=pts[b], gt=gt: e.tensor_tensor(out=t2[:], in0=pt[:, :], in1=gt[:], op=ALU.mult),
                                                 reads=[pts[b].res, gt.res], writes=[t2.res])
                                            dst = mT[:, nbg, t0:t0 + 512] if last else t1[:]
                                            S.op("dve", lambda e, dst=dst: e.tensor_tensor(out=dst, in0=t1[:], in1=t2[:], op=ALU.add),
                                                 reads=[t1.res, t2.res], writes=[mT.res if last else t1.res])
                                    cnt += 1
            if "merged" in dbg:
                S.dma("sp", DBG["merged" + g.tag].rearrange("k p t -> p k t"), mT[:], reads=[mT.res], writes=[ORES["dbg_merged" + g.tag]])
            wb = [ph.sb([128, 16, 512], BF16, "wb") for _ in range(2)]
            resid_gemm(ph, l, g, W["w_out"][l], 16, 512, mT, g.ntok, 0, cols.ga1, None, cols.res, wb)

    def resid_gemm(ph, l, g, Wd, KC, slabw, xin, ntok, tok0_x, ga_fn, gb_fn, cres, wb, tok_base=0):
        xr = [ph.sb([128, 512], F32, "xr") for _ in range(3)]
        tb = [ph.sb([128, 512], F32, "tb") for _ in range(3)]
        st = {"i": 0}

        def epi(c0, t0, pt, nr, ncv):
            i = st["i"] % 3
            st["i"] += 1
            nb = c0 // 128
            tg = tok_base + (t0 - tok0_x)
            S.dma("sp", xr[i][:], g.xT[nb, :, tg:tg + 512], reads=[g.xT_r], writes=[xr[i].res])
            if gb_fn is None:
                S.op("dve", lambda e: e.scalar_tensor_tensor(out=xr[i][:], in0=pt[:, :], scalar=ga_fn(nb), in1=xr[i][:],
                                                             op0=ALU.mult, op1=ALU.add), reads=[pt.res, xr[i].res] + cres, writes=[xr[i].res])
            else:
                S.op("act", lambda e: e.activation(out=tb[i][:], in_=pt[:, :], func=AF.Identity, scale=ga_fn(nb), bias=gb_fn(nb)),
                     reads=[pt.res] + cres, writes=[tb[i].res])
                S.op("dve", lambda e: e.tensor_tensor(out=xr[i][:], in0=xr[i][:], in1=tb[i][:], op=ALU.add),
                     reads=[xr[i].res, tb[i].res], writes=[xr[i].res])
            S.dma("sp", g.xT[nb, :, tg:tg + 512], xr[i][:], reads=[xr[i].res], writes=[g.xT_r], acc=True)
        gemm(ph, Wd, KC, 0, D, slabw, xin, ntok, "fm", epi, wb, tok0=tok0_x)

    def mlp(l, g):
        with Phase(C, "ff1") as ph:
            cols = load_cols(ph, l, g)
            hT = ph.sb([128, 16, g.ntok], BF16, "h2T")
            with Phase(C, "nrm2") as ph2:
                norm_mod(ph2, g, cols.a2, cols.sh2, cols.res, hT)
            wb = [ph.sb([128, 16, 512], BF16, "wb") for _ in range(2)]
            rt = [ph.sb([128, 512], F32, "rt") for _ in range(3)]
            ob = [ph.sb([128, 512], BF16, "ob") for _ in range(3)]
            st = {"i": 0}

            def epi(c0, t0, pt, nr, ncv):
                i = st["i"] % 3
                st["i"] += 1
                nb = c0 // 128
                S.op("act", lambda e: e.activation(out=rt[i][:], in_=pt[:, :], func=AF.Relu, bias=cols.b1(nb), scale=1.0),
                     reads=[pt.res] + cols.res, writes=[rt[i].res])
                S.op("pool", lambda e: e.tensor_tensor(out=ob[i][:], in0=rt[i][:], in1=rt[i][:], op=ALU.mult),
                     reads=[rt[i].res], writes=[ob[i].res])
                S.dma("sp", g.f1T[nb, :, t0:t0 + 512], ob[i][:], reads=[ob[i].res], writes=[g.f1T_r], acc=True)
            gemm(ph, W["w_ff1"][l], 16, 0, D_FF, 512, hT, g.ntok, "fm", epi, wb)
        with Phase(C, "ff2") as ph:
            cols = load_cols(ph, l, g)
            f1 = [ph.sb([128, 64, 512], BF16, "f1") for _ in range(1)]
            wb = [ph.sb([128, 64, 128], BF16, "wb2") for _ in range(2)]
            fv = g.f1T.rearrange("k p t -> p k t")
            for tt in range(g.ntok // 512):
                ft = f1[tt % len(f1)]
                for q in range(4):
                    S.dma("sp", ft[:, q * 16:(q + 1) * 16, :], fv[:, q * 16:(q + 1) * 16, tt * 512:(tt + 1) * 512],
                          reads=[g.f1T_r], writes=[ft.res], acc=(q > 0))
                resid_gemm(ph, l, g, W["w_ff2"][l], 64, 128, ft, 512, 0, cols.ga2, cols.gb2, cols.res, wb, tok_base=tt * 512)

    def final_norm(g, dst, dst_res):
        with Phase(C, "fin") as ph:
            fg = ph.sb([128, 16], F32, "fg")
            S.dma("sp", fg[:], W["final_g"].rearrange("(j p) -> p j", p=128), writes=[fg.res], slow=True)
            TW = 256
            xt = [ph.sb([128, 16, TW], F32, "nx") for _ in range(2)]
            tm = [ph.sb([128, 16, TW], F32, "nt") for _ in range(2)]
            rs = [ph.sb([128, TW], F32, "nr") for _ in range(2)]
            yo = [ph.sb([128, D], F32, "yo") for _ in range(2)]
            xv = g.xT.rearrange("k p t -> p k t")
            for tt in range(g.ntok // TW):
                x_, t_, r_ = xt[tt % 2], tm[tt % 2], rs[tt % 2]
                S.dma("sp", x_[:], xv[:, :, tt * TW:(tt + 1) * TW], reads=[g.xT_r], writes=[x_.res])
                S.op("act", lambda e: e.activation(out=t_[:], in_=x_[:], func=AF.Square), reads=[x_.res], writes=[t_.res])
                pt = next_ps()
                fns = [lambda pe, kc=kc: pe.matmul(pt[:, 0:TW], lhsT=ones[:], rhs=t_[:, kc, :], start=(kc == 0), stop=(kc == 15))
                       for kc in range(16)]
                S.mm(fns, reads=[ones.res, t_.res], writes=[pt.res])
                S.op("dve", lambda e: e.tensor_scalar(out=r_[:], in0=pt[:, 0:TW], scalar1=1.0 / D, scalar2=1e-6, op0=ALU.mult,
                                                      op1=ALU.add), reads=[pt.res], writes=[r_.res])
                S.op("act", lambda e: e.sqrt(out=r_[:], in_=r_[:]), reads=[r_.res], writes=[r_.res])
                S.op("dve", lambda e: e.reciprocal(out=r_[:], in_=r_[:]), reads=[r_.res], writes=[r_.res])
                S.op("dve", lambda e: e.tensor_tensor(out=t_[:], in0=x_[:], in1=r_[:].unsqueeze(1).broadcast_to([128, 16, TW]),
                                                      op=ALU.mult), reads=[x_.res, r_.res], writes=[t_.res])
                for kc in range(16):
                    S.op("act", lambda e, kc=kc: e.activation(out=t_[:, kc, :], in_=t_[:, kc, :], func=AF.Identity,
                                                              scale=fg[:, kc:kc + 1]), reads=[t_.res, fg.res], writes=[t_.res])
                for sub in range(TW // 128):
                    y_ = yo[sub % 2]
                    for kq in range(4):
                        pt2 = next_ps()
                        fns = [lambda pe, j=j, pt2=pt2, kq=kq: pe.transpose(pt2[:, j * 128:(j + 1) * 128],
                                                                             t_[:, kq * 4 + j, sub * 128:(sub + 1) * 128], ident[:])
                               for j in range(4)]
                        S.mm(fns, reads=[t_.res, ident.res], writes=[pt2.res])
                        eng = alt_eng()
                        if eng == "act":
                            S.op("act", lambda e, pt2=pt2, kq=kq: e.copy(out=y_[:, kq * 512:(kq + 1) * 512], in_=pt2[:, :]),
                                 reads=[pt2.res], writes=[y_.res])
                        else:
                            S.op("dve", lambda e, pt2=pt2, kq=kq: e.tensor_copy(out=y_[:, kq * 512:(kq + 1) * 512], in_=pt2[:, :]),
                                 reads=[pt2.res], writes=[y_.res])
                    r0 = tt * TW + sub * 128
                    S.dma("sp", dst[r0:r0 + 128, :], y_[:], reads=[y_.res], writes=[dst_res], acc=True)

    for name in dbg:
        for g in G:
            if name in ("h", "merged"):
                dbg_out(name + g.tag, [16, 128, g.ntok], BF16)
            if name == "x":
                dbg_out(name + g.tag, [16, 128, g.ntok], F32)
            if name == "oT":
                for b in range(3):
                    dbg_out("oT%d%s" % (b, g.tag), [8, 128, g.ntok], BF16)
    load_x_T(G[0], I["xp"])
    load_x_T(G[1], I["xs"])
    for l in range(nlayers):
        modulation(l)
        for g in G:
            if g.i not in cfg.get("groups", (0, 1)):
                continue
            in_proj(l, g, None)
            if "a" in mixers and not cfg.get("noattn"):
                (attn_prompt if g.i == 0 else attn_sample)(l, g)
            if "r" in mixers:
                L_ = 256 if g.i == 0 else 2048
                rwkv_pre(l, g, L_)
                rwkv_scan(l, g, L_)
                rwkv_post(l, g)
            if "c" in mixers:
                L_ = 256 if g.i == 0 else 2048
                hyena_filter(l, L_)
                hyena_conv(l, g, L_)
                hyena_dft(l, g, L_)
            if "oT" in dbg:
                for b in range(3):
                    if "arc"[b] in mixers:
                        with Phase(C, "dbgo") as phd:
                            S.dma("sp", DBG["oT%d%s" % (b, g.tag)], g.oT[b][0], reads=[g.oT[b][1]], writes=[ORES["dbg_oT%d%s" % (b, g.tag)]])
            merge_out(l, g)
            mlp(l, g)
    if "x" in dbg:
        for g in G:
            with Phase(C, "dbgx") as ph:
                S.dma("sp", DBG["x" + g.tag], g.xT, reads=[g.xT_r], writes=[ORES["dbg_x" + g.tag]])
    final_norm(G[0], O["yp"], ORES["yp"])
    final_norm(G[1], O["ys"], ORES["ys"])
    S.barrier()
    cst.es.__exit__(None, None, None)
    top.close()
    C.ninst = S.ninst
    return nc, C


def make_in_maps(inputs):
    f = lambda a: np.ascontiguousarray(np.asarray(a, dtype=np.float32))
    maps = []
    wnames = ["ln1_g", "ln2_g", "w_mod", "b_mod", "w_in", "rpb", "wkv_conv_w", "wkv_conv_b", "wkv_w0", "wkv_w1", "wkv_w2",
              "wkv_a0", "wkv_a1", "wkv_a2", "wkv_g1", "wkv_g2", "wkv_k_k", "wkv_k_a", "wkv_r_k", "wkv_gn_g", "wkv_gn_b",
              "hy_conv_w", "hy_conv_b", "hy_f1", "hy_fb1", "hy_f2", "hy_fb2", "hy_freq", "hy_f3", "hy_d", "w_pa", "w_pr",
              "w_pc", "w_out", "w_ff1", "b_ff1", "w_ff2", "b_ff2", "final_g"]
    wd = {k: f(inputs[k]) for k in wnames}
    wd["wkv_r_k"] = wd["wkv_r_k"].reshape(DEPTH, 1024)
    for i in range(8):
        b = i // 2
        m = dict(wd)
        m["xp"] = f(inputs["x_prompt"][4 * i:4 * i + 4]).reshape(NP_TOK, D)
        m["xs"] = f(inputs["x_sample"][b])
        m["ck"] = f(inputs["cache_k"][b]).reshape(DEPTH, 256, 1024)
        m["cv"] = f(inputs["cache_v"][b]).reshape(DEPTH, 256, 1024)
        m["s0"] = f(inputs["state_wkv"][b]).reshape(DEPTH, 128, 1024)
        m["cvec"] = np.stack([f(inputs["c_ctx"]), f(inputs["c"][b])])
        m.update(CONSTS)
        maps.append(m)
    return maps


def _make_consts():
    cst = {}
    cq = np.arange(64)
    c0 = np.clip(cq - 8, 0, 48)
    ck = np.arange(64)
    ok = (ck[None, :] >= c0[:, None]) & (ck[None, :] < c0[:, None] + 16)
    cst["natmask"] = np.where(ok, 0.0, -1e30).astype(np.float32)
    for L in (256, 2048):
        t = np.linspace(0.0, 1.0, L, dtype=np.float32)[:, None]
        w = 2.0 * np.pi * np.arange(L, dtype=np.float32)[:, None] / L
        f = np.linspace(1e-4, 15, 16, dtype=np.float32)[None, :]
        z = np.concatenate([t, np.cos(f * w), -np.sin(f * w)], -1).astype(np.float32)
        cst["zposT%d" % L] = np.ascontiguousarray(z.T)
        dist = (np.abs(np.arange(L) - L // 2).astype(np.float32) / L)[:, None]
        deltas = np.abs(np.linspace(math.log(1e-2) / 1.5, math.log(1e-2) / 0.3, 1024, dtype=np.float32))[None, :]
        cst["win%d" % L] = np.exp(-dist * deltas).astype(np.float32)
        n = 2 * L
        k = np.arange(L, dtype=np.float64)
        om = 2.0 * np.pi * (k + 0.5) / n
        tt = np.arange(L, dtype=np.float64)
        ang = tt[:, None] * om[None, :]
        Ff = np.zeros((L, 2 * L), np.float32)
        Ffv = Ff.reshape(L, L // 128, 2, 128)
        Ffv[:, :, 0, :] = np.cos(ang).reshape(L, L // 128, 128)
        Ffv[:, :, 1, :] = (-np.sin(ang)).reshape(L, L // 128, 128)
        cst["Ff%d" % L] = Ff
        angi = om[:, None] * (tt[None, :] + L // 2)
        Fi = np.zeros((2 * L, L), np.float32)
        Fiv = Fi.reshape(L // 128, 2, 128, L)
        Fiv[:, 0] = ((2.0 / n) * np.cos(angi)).reshape(L // 128, 128, L)
        Fiv[:, 1] = (-(2.0 / n) * np.sin(angi)).reshape(L // 128, 128, L)
        cst["Fi%d" % L] = Fi
    return cst


CONSTS = _make_consts()
_CACHE = {}


def kernel(**inputs):
    if "nc" not in _CACHE:
        _CACHE["nc"] = build({})[0]
    nc = _CACHE["nc"]
    maps = make_in_maps(inputs)
    res = run_bass_kernel_spmd(nc, maps, core_ids=list(range(8)))
    R = res.results
    yp = np.concatenate([R[i]["yp"].reshape(4, 256, D) for i in range(8)], 0)
    ys = np.stack([R[2 * b]["ys"] for b in range(4)], 0)
    nk = np.concatenate([R[i]["nk"].reshape(4, DEPTH, 256, 16, 64) for i in range(8)], 0)
    nv = np.concatenate([R[i]["nv"].reshape(4, DEPTH, 256, 16, 64) for i in range(8)], 0)
    ns = np.concatenate([R[i]["ns"].reshape(4, DEPTH, 2, 16, 64, 64) for i in range(8)], 0)
    return (yp.astype(np.float32), ys.astype(np.float32), nk.astype(np.float32), nv.astype(np.float32), ns.astype(np.float32))
```

```python
import math
from contextlib import ExitStack

import numpy as np
import concourse.bass as bass
import concourse.mybir as mybir
from concourse.bass_utils import run_bass_kernel_spmd

F32 = mybir.dt.float32
BF16 = mybir.dt.bfloat16
I32 = mybir.dt.int32
AF = mybir.ActivationFunctionType
ALU = mybir.AluOpType
AX = mybir.AxisListType
AP = bass.AP

D = 2048
DEPTH = 4
NP_TOK = 1024
NS_TOK = 2048
N_IN = 15360
D_FF = 8192


class Res:
    __slots__ = ("name", "w", "a", "r")

    def __init__(self, name):
        self.name = name
        self.w = {}
        self.a = {}
        self.r = {}


class Tile:
    def __init__(self, t, name):
        self.t = t
        self.res = Res(name)

    def __getitem__(self, k):
        return self.t[k]


class Sched:
    NDS = 40
    NPOOL = 8

    def __init__(self, nc, es):
        self.nc = nc
        self.E = {"pe": nc.tensor, "act": nc.scalar, "dve": nc.vector, "pool": nc.gpsimd, "sp": nc.sync}
        self.sem = {k: es.enter_context(nc.semaphore("c_" + k)) for k in self.E}
        self.cnt = {k: 0 for k in self.E}
        self.seen = {k: {} for k in self.E}
        self.dsem = [es.enter_context(nc.semaphore("d%d" % i)) for i in range(self.NDS)]
        self.dcnt = [0] * self.NDS
        self.dnext = 0
        self.dnext_pool = 0
        self.ninst = 0
        self.nwait = 0

    def _semobj(self, key):
        return self.sem[key] if isinstance(key, str) else self.dsem[key]

    def _wait(self, eng, deps, defer=False):
        need = {}
        for (k, v) in deps:
            if need.get(k, 0) < v:
                need[k] = v
        sn = self.seen[eng]
        todo = [(k, v) for k, v in need.items() if sn.get(k, 0) < v]
        last = None
        if defer and todo:
            last = todo.pop()
        for k, v in todo:
            self.E[eng].wait_ge(self._semobj(k), v)
            sn[k] = v
            self.ninst += 1
            self.nwait += 1
        if last is not None:
            sn[last[0]] = last[1]
        return last

    def _attach(self, ins, last):
        if last is not None:
            ins._wait_ge(self._semobj(last[0]), last[1])

    def _deps(self, eng, reads, writes, is_dma=False, acc=False):
        deps = []
        for r in reads:
            deps.extend(r.w.items())
            deps.extend(r.a.items())
        for w in writes:
            srcs = [w.w, w.r] if acc else [w.w, w.a, w.r]
            for d in srcs:
                deps.extend(d.items())
        return deps

    def _commit(self, tok, reads, writes, acc=False):
        k, v = tok
        for r in reads:
            r.r[k] = v
        for w in writes:
            if acc:
                w.a[k] = v
            else:
                w.w = {k: v}
                w.a = {}
                w.r = {}

    def op(self, eng, fn, reads=(), writes=()):
        last = self._wait(eng, self._deps(eng, reads, writes), defer=True)
        ins = fn(self.E[eng])
        self._attach(ins, last)
        self.cnt[eng] += 1
        ins.then_inc(self.sem[eng], 1)
        self.ninst += 1
        self._commit((eng, self.cnt[eng]), reads, writes)

    def mm(self, fns, reads, writes):
        last = self._wait("pe", self._deps("pe", reads, writes), defer=True)
        pe = self.E["pe"]
        ins = None
        for j, f in enumerate(fns):
            ins = f(pe)
            if j == 0:
                self._attach(ins, last)
        self.ninst += len(fns)
        self.cnt["pe"] += 1
        ins.then_inc(self.sem["pe"], 1)
        self._commit(("pe", self.cnt["pe"]), reads, writes)

    def dma(self, q, out, in_, reads=(), writes=(), acc=False, slow=False):
        if q == "pool":
            i = self.NDS - self.NPOOL + self.dnext_pool
            self.dnext_pool = (self.dnext_pool + 1) % self.NPOOL
        else:
            i = self.dnext
            self.dnext = (i + 1) % (self.NDS - self.NPOOL)
        deps = self._deps(q, reads, writes, is_dma=True, acc=acc)
        if self.dcnt[i] > 0:
            deps.append((i, self.dcnt[i]))
        last = self._wait(q, deps, defer=True)
        if slow:
            ins = self.E[q].dma_start(out=out, in_=in_, allow_slow_non_contiguous=True)
        else:
            ins = self.E[q].dma_start(out=out, in_=in_)
        self._attach(ins, last)
        ins.then_inc(self.dsem[i], 16)
        self.ninst += 1
        self.dcnt[i] += 16
        self._commit((i, self.dcnt[i]), reads, writes, acc=acc)

    def barrier(self):
        deps = [(k, self.cnt[k]) for k in self.E if k != "sp" and self.cnt[k] > 0]
        deps += [(i, c) for i, c in enumerate(self.dcnt) if c > 0]
        self._wait("sp", deps)
        ins = self.E["sp"].nop()
        self.cnt["sp"] += 1
        ins.then_inc(self.sem["sp"], 1)
        for k in self.E:
            if k != "sp":
                self._wait(k, [("sp", self.cnt["sp"])])
        for k in self.E:
            for k2 in self.E:
                self.seen[k][k2] = self.cnt[k2]
            for i, c in enumerate(self.dcnt):
                self.seen[k][i] = c


class Ctx:
    pass


def _col_ap(dram_ap_1d, n):
    return dram_ap_1d.rearrange("(j p) -> p j", p=128)


class Phase:
    def __init__(self, C, name):
        self.C = C
        self.name = name
        self.es = ExitStack()
        self.n = 0

    def __enter__(self):
        self.es.__enter__()
        return self

    def sb(self, shape, dt=F32, name=None):
        self.n += 1
        nm = "%s_%s_%d" % (self.name, name or "t", self.C.uid())
        t = self.es.enter_context(self.C.nc.sbuf_tensor(nm, list(shape), dt))
        return Tile(t, nm)

    def __exit__(self, *a):
        self.C.S.barrier()
        return self.es.__exit__(*a)


def build(cfg):
    nlayers = cfg.get("nlayers", DEPTH)
    dbg = cfg.get("dbg", ())
    mixers = cfg.get("mixers", ("a", "r", "c"))
    nc = bass.Bass("TRN2", target_bir_lowering=False)
    C = Ctx()
    C.nc = nc
    C._uid = 0

    def uid():
        C._uid += 1
        return C._uid
    C.uid = uid
    top = ExitStack()
    S = Sched(nc, top)
    C.S = S

    def din(name, shape, dt=F32):
        return nc.dram_tensor(name, list(shape), dt, kind="ExternalInput").ap()

    def dout(name, shape, dt=F32):
        return nc.dram_tensor(name, list(shape), dt, kind="ExternalOutput").ap()

    def dscr(name, shape, dt=F32):
        a = nc.dram_tensor(name, list(shape), dt, kind="Internal").ap()
        return a, Res(name)

    I = {}
    I["xp"] = din("xp", [NP_TOK, D])
    I["xs"] = din("xs", [NS_TOK, D])
    I["ck"] = din("ck", [DEPTH, 256, 1024])
    I["cv"] = din("cv", [DEPTH, 256, 1024])
    I["s0"] = din("s0", [DEPTH, 128, 1024])
    I["cvec"] = din("cvec", [2, D])
    wshapes = {
        "ln1_g": [DEPTH, D], "ln2_g": [DEPTH, D], "w_mod": [DEPTH, D, 6 * D], "b_mod": [DEPTH, 6 * D],
        "w_in": [DEPTH, D, N_IN], "rpb": [DEPTH, 16, 15, 31],
        "wkv_conv_w": [DEPTH, 3, 3072], "wkv_conv_b": [DEPTH, 3072], "wkv_w0": [DEPTH, 2, 1024],
        "wkv_w1": [DEPTH, 2, D, 64], "wkv_w2": [DEPTH, 2, 64, 1024], "wkv_a0": [DEPTH, 2, 1024],
        "wkv_a1": [DEPTH, 2, D, 64], "wkv_a2": [DEPTH, 2, 64, 1024], "wkv_g1": [DEPTH, D, 128],
        "wkv_g2": [DEPTH, 128, 1024], "wkv_k_k": [DEPTH, 1024], "wkv_k_a": [DEPTH, 1024],
        "wkv_r_k": [DEPTH, 1024], "wkv_gn_g": [DEPTH, 1024], "wkv_gn_b": [DEPTH, 1024],
        "hy_conv_w": [DEPTH, 3, 3072], "hy_conv_b": [DEPTH, 3072], "hy_f1": [DEPTH, 33, 64],
        "hy_fb1": [DEPTH, 64], "hy_f2": [DEPTH, 64, 64], "hy_fb2": [DEPTH, 64], "hy_freq": [DEPTH, 64],
        "hy_f3": [DEPTH, 64, 1024], "hy_d": [DEPTH, 1024],
        "w_pa": [DEPTH, 1024, D], "w_pr": [DEPTH, 1024, D], "w_pc": [DEPTH, 1024, D], "w_out": [DEPTH, D, D],
        "w_ff1": [DEPTH, D, D_FF], "b_ff1": [DEPTH, D_FF], "w_ff2": [DEPTH, D_FF, D], "b_ff2": [DEPTH, D],
        "final_g": [D],
    }
    W = {k: din(k, s) for k, s in wshapes.items()}
    O = {}
    O["yp"] = dout("yp", [NP_TOK, D])
    O["ys"] = dout("ys", [NS_TOK, D])
    O["nk"] = dout("nk", [4 * DEPTH * 256, 1024])
    O["nv"] = dout("nv", [4 * DEPTH * 256, 1024])
    O["ns"] = dout("ns", [4 * DEPTH * 32, 4096])
    ORES = {k: Res("o_" + k) for k in O}
    DBG = {}

    def dbg_out(name, shape, dt=F32):
        DBG[name] = dout("dbg_" + name, shape, dt)
        ORES["dbg_" + name] = Res("dbg_" + name)
        return DBG[name], ORES["dbg_" + name]

    G = []
    for gi, ntok in enumerate((NP_TOK, NS_TOK)):
        g = Ctx()
        g.i = gi
        g.ntok = ntok
        g.tag = "ps"[gi]
        g.xT, g.xT_r = dscr("xT%d" % gi, [16, 128, ntok])
        g.qkT, g.qkT_r = dscr("qkT%d" % gi, [16, 128, ntok], BF16)
        g.vtm, g.vtm_r = dscr("vtm%d" % gi, [ntok, 1024], BF16)
        g.rh, g.rh_r = dscr("rh%d" % gi, [ntok, 6144])
        g.gT, g.gT_r = dscr("gT%d" % gi, [48, 128, ntok])
        g.oT = []
        for b in range(3):
            g.oT.append(dscr("oT%d_%d" % (gi, b), [8, 128, ntok], BF16))
        g.f1T, g.f1T_r = dscr("f1T%d" % gi, [64, 128, ntok], BF16)
        G.append(g)
    mrow, mrow_r = dscr("mrow", [2, 6 * D])
    zero_r = Res("zeros")

    cst = Phase(C, "cst")
    cst.es.__enter__()
    ident = cst.sb([128, 128], F32, "ident")
    identb = cst.sb([128, 128], BF16, "identb")
    ones = cst.sb([128, 128], F32, "ones")
    S.op("pool", lambda e: e.memset(ident[:], 1.0), writes=[ident.res])
    S.op("pool", lambda e: e.affine_select(out=ident[:], in_=ident[:], pattern=[[-1, 128]], compare_op=ALU.is_equal,
                                            fill=0.0, base=0, channel_multiplier=1), reads=[ident.res], writes=[ident.res])
    S.op("dve", lambda e: e.tensor_copy(out=identb[:], in_=ident[:]), reads=[ident.res], writes=[identb.res])
    S.op("dve", lambda e: e.memset(ones[:], 1.0), writes=[ones.res])
    psum = []
    for i in range(8):
        t = top.enter_context(nc.psum_tensor("ps%d" % i, [128, 512], F32))
        psum.append(Tile(t, "ps%d" % i))
    C.ps_rr = 0

    def next_ps(lo=0, hi=8):
        C.ps_rr = (C.ps_rr + 1) % (hi - lo)
        return psum[lo + C.ps_rr]

    C.eng_rr = 0

    def alt_eng():
        C.eng_rr ^= 1
        return "act" if C.eng_rr else "dve"

    def gemm(ph, Wd, KC, col0, ncols, slabw, xT, ntok, form, epi, wbufs, Mrows=128, tok0=0, ps_lo=0, ps_hi=8):
        nslab = (ncols + slabw - 1) // slabw
        Wv = Wd.rearrange("(k p) n -> p k n", p=128)

        def load(s):
            wb = wbufs[s % len(wbufs)]
            c0 = col0 + s * slabw
            cw = min(slabw, col0 + ncols - c0)
            S.dma("pool", wb[:, :, 0:cw], Wv[:, :, c0:c0 + cw], writes=[wb.res])
        load(0)
        for s in range(nslab):
            if s + 1 < nslab:
                load(s + 1)
            wb = wbufs[s % len(wbufs)]
            c0 = col0 + s * slabw
            cw = min(slabw, col0 + ncols - c0)
            if form == "fm":
                for nb in range((cw + 127) // 128):
                    mw = min(128, cw - nb * 128)
                    for tt in range(ntok // 512):
                        pt = next_ps(ps_lo, ps_hi)
                        t0 = tok0 + tt * 512
                        fns = []
                        for kc in range(KC):
                            fns.append(lambda pe, kc=kc, pt=pt, wb=wb, nb=nb, mw=mw, t0=t0: pe.matmul(
                                pt[0:mw, :], lhsT=wb[:, kc, nb * 128:nb * 128 + mw], rhs=xT[:, kc, t0:t0 + 512],
                                start=(kc == 0), stop=(kc == KC - 1)))
                        S.mm(fns, reads=[wb.res, xT.res], writes=[pt.res])
                        epi(c0 + nb * 128, t0, pt, mw, 512)
            else:
                for tt in range(ntok // Mrows if Mrows == 128 else 1):
                    t0 = tok0 + tt * 128
                    for nh in range((cw + 511) // 512):
                        nw = min(512, cw - nh * 512)
                        pt = next_ps(ps_lo, ps_hi)
                        fns = []
                        for kc in range(KC):
                            fns.append(lambda pe, kc=kc, pt=pt, wb=wb, nh=nh, nw=nw, t0=t0: pe.matmul(
                                pt[0:Mrows, 0:nw], lhsT=xT[:, kc, t0:t0 + Mrows], rhs=wb[:, kc, nh * 512:nh * 512 + nw],
                                start=(kc == 0), stop=(kc == KC - 1)))
                        S.mm(fns, reads=[wb.res, xT.res], writes=[pt.res])
                        epi(c0 + nh * 512, t0, pt, Mrows, nw)

    def epi_store(ph, obufs, dst_fn, dst_res, func=None, bias_fn=None, scale=1.0):
        st = {"i": 0}

        def epi(c0, t0, pt, nr, ncv):
            ob = obufs[st["i"] % len(obufs)]
            st["i"] += 1
            if func is not None or bias_fn is not None:
                b = bias_fn(c0) if bias_fn is not None else None
                rd = [pt.res] + ([b[1]] if b is not None else [])
                S.op("act", lambda e: e.activation(out=ob[0:nr, 0:ncv], in_=pt[0:nr, 0:ncv], func=func or AF.Identity,
                                                   bias=(b[0] if b is not None else 0.0), scale=scale),
                     reads=rd, writes=[ob.res])
            else:
                eng = alt_eng()
                if eng == "act":
                    S.op("act", lambda e: e.copy(out=ob[0:nr, 0:ncv], in_=pt[0:nr, 0:ncv]), reads=[pt.res], writes=[ob.res])
                else:
                    S.op("dve", lambda e: e.tensor_copy(out=ob[0:nr, 0:ncv], in_=pt[0:nr, 0:ncv]), reads=[pt.res], writes=[ob.res])
            S.dma("sp", dst_fn(c0, t0, nr, ncv), ob[0:nr, 0:ncv], reads=[ob.res], writes=[dst_res], acc=True)
        return epi

    def load_x_T(g, src):
        with Phase(C, "ldx") as ph:
            xin = [ph.sb([128, D], F32, "xin") for _ in range(2)]
            xo = [ph.sb([128, 16, 128], F32, "xo") for _ in range(2)]
            for tt in range(g.ntok // 128):
                xi = xin[tt % 2]
                xq = xo[tt % 2]
                S.dma("sp", xi[:], src[tt * 128:(tt + 1) * 128, :], writes=[xi.res])
                for kq in range(4):
                    pt = next_ps()
                    fns = [lambda pe, j=j, pt=pt, xi=xi, kq=kq: pe.transpose(pt[:, j * 128:(j + 1) * 128],
                                                                                xi[:, (kq * 4 + j) * 128:(kq * 4 + j + 1) * 128], ident[:])
                           for j in range(4)]
                    S.mm(fns, reads=[xi.res, ident.res], writes=[pt.res])
                    eng = alt_eng()
                    dst = xq[:, kq * 4:(kq + 1) * 4, :]
                    srcp = pt[:, :].rearrange("p (a b) -> p a b", b=128)
                    if eng == "act":
                        S.op("act", lambda e: e.copy(out=dst, in_=srcp), reads=[pt.res], writes=[xq.res])
                    else:
                        S.op("dve", lambda e: e.tensor_copy(out=dst, in_=srcp), reads=[pt.res], writes=[xq.res])
                S.dma("sp", g.xT.rearrange("k p t -> p k t")[:, :, tt * 128:(tt + 1) * 128], xq[:], reads=[xq.res],
                      writes=[g.xT_r], acc=True)

    def modulation(l):
        with Phase(C, "mod") as ph:
            cT = ph.sb([128, 16, 2], F32, "cT")
            cTb = ph.sb([128, 16, 2], BF16, "cTb")
            for gi in range(2):
                S.dma("sp", cT[:, :, gi], I["cvec"][gi].rearrange("(k p) -> p k", p=128), writes=[cT.res], slow=True, acc=(gi > 0))
            S.op("act", lambda e: e.activation(out=cTb[:], in_=cT[:], func=AF.Silu), reads=[cT.res], writes=[cTb.res])
            wb = [ph.sb([128, 16, 512], BF16, "wb") for _ in range(2)]
            bm = [ph.sb([2, 512], F32, "bm") for _ in range(2)]
            ob = [ph.sb([2, 512], F32, "ob") for _ in range(2)]
            st = {"i": 0}

            def epi(c0, t0, pt, nr, ncv):
                i = st["i"] % 2
                st["i"] += 1
                S.dma("sp", bm[i][:], AP(W["b_mod"].tensor, l * 6 * D + c0, [[0, 2], [1, 512]]), writes=[bm[i].res])
                S.op("dve", lambda e: e.tensor_tensor(out=ob[i][:], in0=pt[0:2, :], in1=bm[i][:], op=ALU.add),
                     reads=[pt.res, bm[i].res], writes=[ob[i].res])
                S.dma("sp", mrow[:, c0:c0 + 512], ob[i][:], reads=[ob[i].res], writes=[mrow_r], acc=True)
            gemm(ph, W["w_mod"][l], 16, 0, 6 * D, 512, cTb, 2, "tm", epi, wb, Mrows=2)

    def load_cols(ph, l, g):
        cols = Ctx()
        m = ph.sb([128, 96], F32, "mcol")
        S.dma("sp", m[:], mrow[g.i].rearrange("(j p) -> p j", p=128), reads=[mrow_r], writes=[m.res], slow=True)
        ln = ph.sb([128, 32], F32, "lncol")
        S.dma("sp", ln[:, 0:16], W["ln1_g"][l].rearrange("(j p) -> p j", p=128), writes=[ln.res], slow=True)
        S.dma("sp", ln[:, 16:32], W["ln2_g"][l].rearrange("(j p) -> p j", p=128), writes=[ln.res], slow=True, acc=True)
        bf = ph.sb([128, 80], F32, "bfcol")
        S.dma("sp", bf[:, 0:64], W["b_ff1"][l].rearrange("(j p) -> p j", p=128), writes=[bf.res], slow=True)
        S.dma("sp", bf[:, 64:80], W["b_ff2"][l].rearrange("(j p) -> p j", p=128), writes=[bf.res], slow=True, acc=True)
        d = ph.sb([128, 48], F32, "dcol")
        S.op("dve", lambda e: e.scalar_tensor_tensor(out=d[:, 0:16], in0=m[:, 16:32], scalar=1.0, in1=ln[:, 0:16],
                                                     op0=ALU.add, op1=ALU.mult), reads=[m.res, ln.res], writes=[d.res])
        S.op("dve", lambda e: e.scalar_tensor_tensor(out=d[:, 16:32], in0=m[:, 64:80], scalar=1.0, in1=ln[:, 16:32],
                                                     op0=ALU.add, op1=ALU.mult), reads=[m.res, ln.res, d.res], writes=[d.res])
        S.op("dve", lambda e: e.tensor_tensor(out=d[:, 32:48], in0=m[:, 80:96], in1=bf[:, 64:80], op=ALU.mult),
             reads=[m.res, bf.res, d.res], writes=[d.res])
        cols.m, cols.d, cols.bf = m, d, bf
        cols.sh1 = lambda kc: m[:, kc:kc + 1]
        cols.ga1 = lambda kc: m[:, 32 + kc:33 + kc]
        cols.sh2 = lambda kc: m[:, 48 + kc:49 + kc]
        cols.ga2 = lambda kc: m[:, 80 + kc:81 + kc]
        cols.a1 = lambda kc: d[:, kc:kc + 1]
        cols.a2 = lambda kc: d[:, 16 + kc:17 + kc]
        cols.gb2 = lambda kc: d[:, 32 + kc:33 + kc]
        cols.b1 = lambda j: bf[:, j:j + 1]
        cols.res = [m.res, d.res, bf.res]
        return cols

    def norm_mod(ph, g, a_fn, sh_fn, cres, hT):
        TW = 256
        xt = [ph.sb([128, 16, TW], F32, "nx") for _ in range(2)]
        tm = [ph.sb([128, 16, TW], F32, "nt") for _ in range(2)]
        rs = [ph.sb([128, TW], F32, "nr") for _ in range(2)]
        xv = g.xT.rearrange("k p t -> p k t")
        for tt in range(g.ntok // TW):
            x_, t_, r_ = xt[tt % 2], tm[tt % 2], rs[tt % 2]
            S.dma("sp", x_[:], xv[:, :, tt * TW:(tt + 1) * TW], reads=[g.xT_r], writes=[x_.res])
            S.op("act", lambda e: e.activation(out=t_[:], in_=x_[:], func=AF.Square), reads=[x_.res], writes=[t_.res])
            pt = next_ps()
            fns = [lambda pe, kc=kc: pe.matmul(pt[:, 0:TW], lhsT=ones[:], rhs=t_[:, kc, :], start=(kc == 0), stop=(kc == 15))
                   for kc in range(16)]
            S.mm(fns, reads=[ones.res, t_.res], writes=[pt.res])
            S.op("dve", lambda e: e.tensor_scalar(out=r_[:], in0=pt[:, 0:TW], scalar1=1.0 / D, scalar2=1e-6, op0=ALU.mult,
                                                  op1=ALU.add), reads=[pt.res], writes=[r_.res])
            S.op("act", lambda e: e.sqrt(out=r_[:], in_=r_[:]), reads=[r_.res], writes=[r_.res])
            S.op("dve", lambda e: e.reciprocal(out=r_[:], in_=r_[:]), reads=[r_.res], writes=[r_.res])
            S.op("dve", lambda e: e.tensor_tensor(out=t_[:], in0=x_[:], in1=r_[:].unsqueeze(1).broadcast_to([128, 16, TW]),
                                                  op=ALU.mult), reads=[x_.res, r_.res], writes=[t_.res])
            for kc in range(16):
                S.op("act", lambda e, kc=kc: e.activation(out=hT[:, kc, tt * TW:(tt + 1) * TW], in_=t_[:, kc, :],
                                                          func=AF.Identity, scale=a_fn(kc), bias=sh_fn(kc)),
                     reads=[t_.res] + cres, writes=[hT.res])

    def in_proj(l, g, cols_holder):
        with Phase(C, "inp") as ph:
            cols = load_cols(ph, l, g)
            hT = ph.sb([128, 16, g.ntok], BF16, "hT")
            with Phase(C, "nrm") as ph2:
                norm_mod(ph2, g, cols.a1, cols.sh1, cols.res, hT)
            if "h" in dbg:
                S.dma("sp", DBG["h" + g.tag].rearrange("k p t -> p k t"), hT[:], reads=[hT.res], writes=[ORES["dbg_h" + g.tag]])
            wb = [ph.sb([128, 16, 512], BF16, "wb") for _ in range(2)]
            Wl = W["w_in"][l]
            ob = [ph.sb([128, 512], F32, "ob") for _ in range(4)]
            gemm(ph, Wl, 16, 9216, 6144, 512, hT, g.ntok, "fm",
                 epi_store(ph, ob, lambda c0, t0, nr, ncv: g.gT[(c0 - 9216) // 128, :, t0:t0 + ncv], g.gT_r, func=AF.Sigmoid), wb)
            if "a" in mixers:
                obb = [ph.sb([128, 512], BF16, "obb") for _ in range(4)]
                gemm(ph, Wl, 16, 0, 2048, 512, hT, g.ntok, "fm",
                     epi_store(ph, obb, lambda c0, t0, nr, ncv: g.qkT[c0 // 128, :, t0:t0 + ncv], g.qkT_r), wb)
            obf = [ph.sb([128, 512], F32, "obf") for _ in range(3)]
            obv = [ph.sb([128, 512], BF16, "obv") for _ in range(3)]
            st = {"i": 0}

            def epi_kv(c0, t0, pt, nr, ncv):
                i = st["i"] % 3
                st["i"] += 1
                isv = c0 >= 2048
                cc = c0 - (2048 if isv else 1024)
                if g.i == 0:
                    S.op("dve", lambda e: e.tensor_copy(out=obf[i][:], in_=pt[:, :]), reads=[pt.res], writes=[obf[i].res])
                    key = "nv" if isv else "nk"
                    r0_ = ((t0 // 256) * DEPTH + l) * 256 + (t0 % 256)
                    S.dma("sp", O[key][r0_:r0_ + 128, cc:cc + 512], obf[i][:], reads=[obf[i].res],
                          writes=[ORES[key]], acc=True)
                    if isv:
                        S.op("pool", lambda e: e.tensor_copy(out=obv[i][:], in_=obf[i][:]), reads=[obf[i].res], writes=[obv[i].res])
                elif isv:
                    S.op("dve", lambda e: e.tensor_copy(out=obv[i][:], in_=pt[:, :]), reads=[pt.res], writes=[obv[i].res])
                if isv:
                    S.dma("sp", g.vtm[t0:t0 + 128, cc:cc + 512], obv[i][:], reads=[obv[i].res], writes=[g.vtm_r], acc=True)
            if ("a" in mixers or g.i == 0) and not cfg.get("nokv"):
                if g.i == 0:
                    gemm(ph, Wl, 16, 1024, 2048, 512, hT, g.ntok, "tm", epi_kv, wb)
                else:
                    gemm(ph, Wl, 16, 2048, 1024, 512, hT, g.ntok, "tm", epi_kv, wb)
            if "r" in mixers or "c" in mixers:
                gemm(ph, Wl, 16, 3072, 6144, 512, hT, g.ntok, "tm",
                     epi_store(ph, ob, lambda c0, t0, nr, ncv: g.rh[t0:t0 + nr, c0 - 3072:c0 - 3072 + ncv], g.rh_r), wb)
            if "r" in mixers:
                rwkv_lora(ph, l, g, hT)
            return None

    SCALE = 0.125

    def softmax_rows(nr, ncol, sc_ap, sc_res, scale, small, pn, tag_reads=()):
        mx, nmx, rsum, rinv = small[0:nr, 0:1], small[0:nr, 1:2], small[0:nr, 2:3], small[0:nr, 3:4]
        S.op("dve", lambda e: e.tensor_reduce(out=mx, in_=sc_ap, axis=AX.X, op=ALU.max), reads=[sc_res], writes=[small.res])
        S.op("dve", lambda e: e.tensor_scalar(out=nmx, in0=mx, scalar1=-scale, scalar2=None, op0=ALU.mult),
             reads=[small.res], writes=[small.res])
        S.op("dve", lambda e: e.memset(rsum, 0.0), reads=[small.res], writes=[small.res])
        return mx, nmx, rsum, rinv

    def attn_prompt(l, g):
        with Phase(C, "attp") as ph:
            qk = ph.sb([128, 16, NP_TOK], BF16, "qk")
            V = ph.sb([128, 8, 1024], BF16, "V")
            oa = ph.sb([128, 8, NP_TOK], BF16, "oa")
            S.dma("sp", qk[:], g.qkT.rearrange("k p t -> p k t"), reads=[g.qkT_r], writes=[qk.res])
            S.dma("sp", V[:], g.vtm.rearrange("(j p) c -> p j c", p=128), reads=[g.vtm_r], writes=[V.res])
            pb = [ph.sb([128, 256], F32, "pb") for _ in range(2)]
            pn = [ph.sb([128, 256], BF16, "pn") for _ in range(2)]
            PT = [ph.sb([128, 2, 128], BF16, "PT") for _ in range(2)]
            sm = [ph.sb([128, 8], F32, "sm") for _ in range(2)]
            def stage_a(s_, h, qt, i):
                c, p0 = h // 2, (h % 2) * 64
                q0 = s_ * 256 + qt * 128
                ps = next_ps()
                S.mm([lambda pe: pe.matmul(ps[:, 0:256], lhsT=qk[p0:p0 + 64, c, q0:q0 + 128],
                                           rhs=qk[p0:p0 + 64, 8 + c, s_ * 256:(s_ + 1) * 256], start=True, stop=True)],
                     reads=[qk.res], writes=[ps.res])
                mx, nmx, rsum, rinv = softmax_rows(128, 256, ps[:, 0:256], ps.res, SCALE, sm[i], None)
                S.op("act", lambda e: e.activation(out=pb[i][:], in_=ps[:, 0:256], func=AF.Exp, bias=nmx, scale=SCALE,
                                                   accum_out=rsum), reads=[ps.res, sm[i].res], writes=[pb[i].res, sm[i].res])
                S.op("dve", lambda e: e.reciprocal(out=rinv, in_=rsum), reads=[sm[i].res], writes=[sm[i].res])
                S.op("dve", lambda e: e.tensor_scalar(out=pn[i][:], in0=pb[i][:], scalar1=rinv, scalar2=None, op0=ALU.mult),
                     reads=[pb[i].res, sm[i].res], writes=[pn[i].res])

            def stage_b(s_, h, qt, i):
                c, p0 = h // 2, (h % 2) * 64
                q0 = s_ * 256 + qt * 128
                pt2 = next_ps()
                ptb = pt2[:, :].bitcast(BF16)
                S.mm([lambda pe, kt=kt: pe.transpose(ptb[:, kt * 128:(kt + 1) * 128], pn[i][:, kt * 128:(kt + 1) * 128], identb[:])
                      for kt in range(2)], reads=[pn[i].res, identb.res], writes=[pt2.res])
                S.op("act", lambda e: e.copy(out=PT[i][:], in_=ptb[:, 0:256].rearrange("p (a b) -> p a b", b=128)),
                     reads=[pt2.res], writes=[PT[i].res])
                po = next_ps()
                S.mm([lambda pe, kt=kt: pe.matmul(po[:, 0:128], lhsT=V[:, s_ * 2 + kt, c * 128:(c + 1) * 128], rhs=PT[i][:, kt, :],
                                                  start=(kt == 0), stop=(kt == 1)) for kt in range(2)],
                     reads=[V.res, PT[i].res], writes=[po.res])
                S.op("dve", lambda e: e.tensor_copy(out=oa[p0:p0 + 64, c, q0:q0 + 128], in_=po[p0:p0 + 64, 0:128]),
                     reads=[po.res], writes=[oa.res])
            units = [(s_, h, qt) for s_ in range(4) for h in range(16) for qt in range(2)]
            for u in range(len(units) + 1):
                if u < len(units):
                    stage_a(*units[u], u % 2)
                if u >= 1:
                    stage_b(*units[u - 1], (u - 1) % 2)
            S.dma("sp", g.oT[0][0].rearrange("k p t -> p k t"), oa[:], reads=[oa.res], writes=[g.oT[0][1]])

    rpbp, rpbp_r = dscr("rpbp", [240, 157])
    rrep, rrep_r = dscr("rrep", [240, 64, 157])
    I["natmask"] = din("natmask", [64, 64])

    def rcls(r):
        return 7 - r if r <= 3 else (3 if r <= 28 else 31 - r)

    def attn_sample(l, g):
        with Phase(C, "atts") as ph:
            z = ph.sb([128, 157], F32, "z")
            S.op("pool", lambda e: e.memset(z[:], 0.0), writes=[z.res])
            S.dma("sp", rpbp[0:128, :], z[:], reads=[z.res], writes=[rpbp_r])
            S.dma("sp", rpbp[128:240, :], z[0:112, :], reads=[z.res], writes=[rpbp_r], acc=True)
            S.dma("sp", rpbp[:, 63:94], W["rpb"][l].rearrange("h r c -> (h r) c"), writes=[rpbp_r], slow=True)
            for q4 in range(4):
                S.dma("sp", rrep[q4 * 60:(q4 + 1) * 60], AP(rpbp.tensor, q4 * 60 * 157, [[157, 60], [0, 64], [1, 157]]),
                      reads=[rpbp_r], writes=[rrep_r], acc=(q4 > 0))
            mk = ph.sb([64, 64], F32, "mk")
            S.dma("sp", mk[:], I["natmask"], writes=[mk.res])
            qk = [ph.sb([128, 2, NS_TOK], BF16, "qk") for _ in range(2)]
            Ve = [ph.sb([128, 16, 128], BF16, "Ve") for _ in range(2)]
            Vo = [ph.sb([128, 15, 128], BF16, "Vo") for _ in range(2)]
            Vc = [ph.sb([128, 2, 128], BF16, "Vc") for _ in range(2)]
            ckt = [ph.sb([128, 2, 128], F32, "ckt") for _ in range(2)]
            kcT = [ph.sb([128, 256], BF16, "kcT") for _ in range(2)]
            ob = [ph.sb([128, NS_TOK], BF16, "ob") for _ in range(2)]
            bm = [ph.sb([64, 8, 512], F32, "bm") for _ in range(2)]
            sc = [ph.sb([64, 768], F32, "sc") for _ in range(2)]
            pb = [ph.sb([64, 768], F32, "pb") for _ in range(2)]
            pn = [ph.sb([64, 768], BF16, "pn") for _ in range(2)]
            PT = [ph.sb([128, 6, 64], BF16, "PT") for _ in range(2)]
            sm = [ph.sb([128, 8], F32, "sm") for _ in range(2)]
            qv = g.qkT.rearrange("k p t -> p k t")
            u = 0
            for c in range(8):
                b_ = c % 2
                S.dma("sp", qk[b_][:, 0, :], qv[:, c, :], reads=[g.qkT_r], writes=[qk[b_].res])
                S.dma("sp", qk[b_][:, 1, :], qv[:, 8 + c, :], reads=[g.qkT_r], writes=[qk[b_].res], acc=True)
                S.dma("sp", Ve[b_][:], g.vtm[:, c * 128:(c + 1) * 128].rearrange("(j p) c -> p j c", p=128), reads=[g.vtm_r],
                      writes=[Ve[b_].res])
                S.dma("sp", Vo[b_][:], g.vtm[64:64 + 15 * 128, c * 128:(c + 1) * 128].rearrange("(j p) c -> p j c", p=128),
                      reads=[g.vtm_r], writes=[Vo[b_].res])
                S.dma("pool", Vc[b_][:], I["cv"][l][:, c * 128:(c + 1) * 128].rearrange("(j p) c -> p j c", p=128), writes=[Vc[b_].res])
                S.dma("sp", ckt[b_][:], I["ck"][l][:, c * 128:(c + 1) * 128].rearrange("(j p) c -> p j c", p=128), writes=[ckt[b_].res])
                pk = next_ps()
                S.mm([lambda pe, t=t: pe.transpose(pk[:, t * 128:(t + 1) * 128], ckt[b_][:, t, :], ident[:]) for t in range(2)],
                     reads=[ckt[b_].res, ident.res], writes=[pk.res])
                S.op("act", lambda e: e.copy(out=kcT[b_][:], in_=pk[:, 0:256]), reads=[pk.res], writes=[kcT[b_].res])
                for hh in range(2):
                    h = 2 * c + hh
                    p0 = hh * 64
                    bmh = bm[hh]
                    for o in range(8):
                        S.dma("sp", bmh[:, o, :].rearrange("p (j k) -> p j k", k=64),
                              AP(rrep.tensor, ((h * 15 + o) * 64) * 157 + 78, [[156, 64], [64 * 157, 8], [1, 64]]),
                              reads=[rrep_r], writes=[bmh.res], acc=(o > 0))
                    S.op("pool", lambda e: e.tensor_tensor(out=bmh[:].rearrange("p o (j k) -> p (o j) k", k=64),
                                                           in0=bmh[:].rearrange("p o (j k) -> p (o j) k", k=64),
                                                           in1=mk[:].unsqueeze(1).broadcast_to([64, 64, 64]), op=ALU.add),
                         reads=[bmh.res, mk.res], writes=[bmh.res])
                    def stage_a(r, i):
                        r0 = min(max(r - 4, 0), 24)
                        o = rcls(r)
                        psA, psB = next_ps(), next_ps()
                        qa = qk[b_][p0:p0 + 64, 0, r * 64:(r + 1) * 64]
                        S.mm([lambda pe: pe.matmul(psA[0:64, 0:512], lhsT=qa, rhs=qk[b_][p0:p0 + 64, 1, r0 * 64:r0 * 64 + 512],
                                                   start=True, stop=True)], reads=[qk[b_].res], writes=[psA.res])
                        S.mm([lambda pe: pe.matmul(psB[0:64, 0:256], lhsT=qa, rhs=kcT[b_][p0:p0 + 64, :], start=True, stop=True)],
                             reads=[qk[b_].res, kcT[b_].res], writes=[psB.res])
                        S.op("dve", lambda e: e.scalar_tensor_tensor(out=sc[i][:, 0:512], in0=psA[0:64, 0:512], scalar=SCALE,
                                                                     in1=bmh[:, o, :], op0=ALU.mult, op1=ALU.add),
                             reads=[psA.res, bmh.res], writes=[sc[i].res])
                        S.op("act", lambda e: e.mul(out=sc[i][:, 512:768], in_=psB[0:64, 0:256], mul=SCALE), reads=[psB.res, sc[i].res],
                             writes=[sc[i].res])
                        mx, nmx, rsum, rinv = softmax_rows(64, 768, sc[i][:], sc[i].res, 1.0, sm[i], None)
                        S.op("act", lambda e: e.activation(out=pb[i][:], in_=sc[i][:], func=AF.Exp, bias=nmx, scale=1.0,
                                                           accum_out=rsum), reads=[sc[i].res, sm[i].res], writes=[pb[i].res, sm[i].res])
                        S.op("dve", lambda e: e.reciprocal(out=rinv, in_=rsum), reads=[sm[i].res], writes=[sm[i].res])
                        S.op("dve", lambda e: e.tensor_scalar(out=pn[i][:], in0=pb[i][:], scalar1=rinv, scalar2=None, op0=ALU.mult),
                             reads=[pb[i].res, sm[i].res], writes=[pn[i].res])

                    def stage_b(r, i):
                        r0 = min(max(r - 4, 0), 24)
                        pt2 = next_ps()
                        S.mm([lambda pe, j=j: pe.matmul(pt2[:, j * 64:(j + 1) * 64], lhsT=pn[i][:, j * 128:(j + 1) * 128], rhs=identb[0:64, 0:64],
                                                        start=True, stop=True)
                              for j in range(6)], reads=[pn[i].res, identb.res], writes=[pt2.res])
                        S.op("act", lambda e: e.copy(out=PT[i][:], in_=pt2[:, 0:384].rearrange("p (a b) -> p a b", b=64)),
                             reads=[pt2.res], writes=[PT[i].res])
                        po = next_ps()
                        fns = []
                        for j in range(6):
                            if j < 4:
                                vt = Ve[b_][:, r0 // 2 + j, :] if r0 % 2 == 0 else Vo[b_][:, (r0 - 1) // 2 + j, :]
                            else:
                                vt = Vc[b_][:, j - 4, :]
                            fns.append(lambda pe, j=j, vt=vt: pe.matmul(po[:, 0:64], lhsT=vt, rhs=PT[i][:, j, :], start=(j == 0), stop=(j == 5)))
                        S.mm(fns, reads=[Ve[b_].res, Vo[b_].res, Vc[b_].res, PT[i].res], writes=[po.res])
                        S.op("dve", lambda e: e.tensor_copy(out=ob[b_][p0:p0 + 64, r * 64:(r + 1) * 64], in_=po[p0:p0 + 64, 0:64]),
                             reads=[po.res], writes=[ob[b_].res])
                    for r in range(33):
                        if r < 32:
                            stage_a(r, r % 2)
                        if r >= 1:
                            stage_b(r - 1, (r - 1) % 2)
                S.dma("sp", g.oT[0][0][c], ob[b_][:], reads=[ob[b_].res], writes=[g.oT[0][1]], acc=True)

    HY = {}
    for L_ in (256, 2048):
        HY[L_] = dict(zposT=din("zposT%d" % L_, [33, L_]), win=din("win%d" % L_, [L_, 1024]),
                      Ff=din("Ff%d" % L_, [L_, 2 * L_]), Fi=din("Fi%d" % L_, [2 * L_, L_]))
        HY[L_]["Hs"], HY[L_]["Hs_r"] = dscr("Hs%d" % L_, [2 * L_ // 128, 128, 1024])
    for g in G:
        g.zbs, g.zbs_r = dscr("zbs%d" % g.i, [g.ntok, 1024], BF16)
        g.zd, g.zd_r = dscr("zd%d" % g.i, [g.ntok, 1024])
        g.x0s, g.x0s_r = dscr("x0s%d" % g.i, [g.ntok, 1024])
    TWO_PI = 2.0 * math.pi

    def bcast_row(dram_ap_tensor, offset, n):
        return AP(dram_ap_tensor, offset, [[0, 128], [1, n]])

    def hyena_filter(l, L):
        hy_ = HY[L]
        LT = L // 128
        with Phase(C, "hyf") as ph:
            f1 = ph.sb([33, 64], F32, "f1")
            f2 = ph.sb([64, 64], F32, "f2")
            f3 = ph.sb([64, 1024], F32, "f3")
            cc = ph.sb([64, 4], F32, "cc")
            zp = ph.sb([33, L], F32, "zp")
            S.dma("sp", f1[:], W["hy_f1"][l], writes=[f1.res])
            S.dma("sp", f2[:], W["hy_f2"][l], writes=[f2.res])
            S.dma("sp", f3[:], W["hy_f3"][l], writes=[f3.res])
            for j, nm in enumerate(("hy_fb1", "hy_fb2", "hy_freq")):
                S.dma("sp", cc[:, j:j + 1], W[nm][l].rearrange("(p o) -> p o", o=1), writes=[cc.res], slow=True, acc=(j > 0))
            S.dma("sp", zp[:], hy_["zposT"], writes=[zp.res])
            t1 = ph.sb([64, L], F32, "t1")
            t2 = ph.sb([64, L], F32, "t2")
            a = ph.sb([64, 512], F32, "a")
            ki = ph.sb([64, 512], I32, "ki")
            kf = ph.sb([64, 512], F32, "kf")
            cw = min(512, L)

            def sin_layer(lt, K, src, dst, bcol):
                for cb in range(L // cw):
                    pt = next_ps(0, 6)
                    S.mm([lambda pe: pe.matmul(pt[0:64, 0:cw], lhsT=lt[0:K, :], rhs=src[0:K, cb * cw:(cb + 1) * cw], start=True, stop=True)],
                         reads=[lt.res, src.res], writes=[pt.res])
                    S.op("dve", lambda e: e.tensor_scalar(out=a[:, 0:cw], in0=pt[0:64, 0:cw], scalar1=cc[:, bcol:bcol + 1],
                                                          scalar2=cc[:, 2:3], op0=ALU.add, op1=ALU.mult), reads=[pt.res, cc.res], writes=[a.res])
                    S.op("dve", lambda e: e.tensor_scalar(out=ki[:, 0:cw], in0=a[:, 0:cw], scalar1=1.0 / TWO_PI, scalar2=None, op0=ALU.mult),
                         reads=[a.res], writes=[ki.res])
                    S.op("dve", lambda e: e.tensor_copy(out=kf[:, 0:cw], in_=ki[:, 0:cw]), reads=[ki.res], writes=[kf.res])
                    S.op("dve", lambda e: e.scalar_tensor_tensor(out=a[:, 0:cw], in0=kf[:, 0:cw], scalar=-TWO_PI, in1=a[:, 0:cw],
                                                                 op0=ALU.mult, op1=ALU.add), reads=[kf.res, a.res], writes=[a.res])
                    S.op("dve", lambda e: e.tensor_scalar(out=a[:, 0:cw], in0=a[:, 0:cw], scalar1=-math.pi, scalar2=math.pi, op0=ALU.max,
                                                          op1=ALU.min), reads=[a.res], writes=[a.res])
                    S.op("act", lambda e: e.activation(out=dst[:, cb * cw:(cb + 1) * cw], in_=a[:, 0:cw], func=AF.Sin),
                         reads=[a.res], writes=[dst.res])
            sin_layer(f1, 33, zp, t1, 0)
            sin_layer(f2, 64, t1, t2, 1)
            filtb = ph.sb([128, LT, 1024], BF16, "filtb")
            winb = [ph.sb([128, 1024], F32, "winb") for _ in range(2)]
            ft = [ph.sb([128, 1024], F32, "ft") for _ in range(2)]
            fa = [ph.sb([128, 1024], F32, "fa") for _ in range(2)]
            for tt in range(LT):
                i = tt % 2
                S.dma("sp", winb[i][:], hy_["win"][tt * 128:(tt + 1) * 128, :], writes=[winb[i].res])
                for hf in range(2):
                    pt = next_ps(0, 6)
                    S.mm([lambda pe: pe.matmul(pt[:, :], lhsT=t2[0:64, tt * 128:(tt + 1) * 128], rhs=f3[0:64, hf * 512:(hf + 1) * 512],
                                               start=True, stop=True)], reads=[t2.res, f3.res], writes=[pt.res])
                    S.op("dve", lambda e: e.tensor_tensor(out=ft[i][:, hf * 512:(hf + 1) * 512], in0=pt[:, :],
                                                          in1=winb[i][:, hf * 512:(hf + 1) * 512], op=ALU.mult),
                         reads=[pt.res, winb[i].res], writes=[ft[i].res])
                S.op("act", lambda e: e.activation(out=fa[i][:], in_=ft[i][:], func=AF.Abs), reads=[ft[i].res], writes=[fa[i].res])
                S.op("pool", lambda e: e.tensor_copy(out=filtb[:, tt, :], in_=ft[i][:]), reads=[ft[i].res], writes=[filtb.res])
                for hf in range(2):
                    pacc = psum[6 + hf]
                    S.mm([lambda pe: pe.matmul(pacc[:, :], lhsT=ones[:], rhs=fa[i][:, hf * 512:(hf + 1) * 512], start=(tt == 0),
                                               stop=(tt == LT - 1))], reads=[ones.res, fa[i].res], writes=[pacc.res])
            inv = ph.sb([128, 1024], F32, "inv")
            for hf in range(2):
                S.op("dve", lambda e: e.tensor_scalar(out=inv[:, hf * 512:(hf + 1) * 512], in0=psum[6 + hf][:, :], scalar1=1e-6,
                                                      scalar2=None, op0=ALU.add), reads=[psum[6 + hf].res, inv.res], writes=[inv.res])
            S.op("dve", lambda e: e.reciprocal(out=inv[:], in_=inv[:]), reads=[inv.res], writes=[inv.res])
            wb = [ph.sb([128, LT, 512], BF16, "wbF") for _ in range(2)]
            ob = [ph.sb([128, 512], F32, "ob") for _ in range(3)]
            st = {"i": 0}

            def epi(c0, t0, pt, nr, ncv):
                i = st["i"] % 3
                st["i"] += 1
                S.op("dve", lambda e: e.tensor_tensor(out=ob[i][:], in0=pt[:, :], in1=inv[:, t0:t0 + 512], op=ALU.mult),
                     reads=[pt.res, inv.res], writes=[ob[i].res])
                S.dma("sp", hy_["Hs"][c0 // 128, :, t0:t0 + 512], ob[i][:], reads=[ob[i].res], writes=[hy_["Hs_r"]], acc=True)
            gemm(ph, hy_["Ff"], LT, 0, 2 * L, 512, filtb, 1024, "fm", epi, wb, ps_lo=0, ps_hi=6)

    def hyena_conv(l, g, L):
        with Phase(C, "hyc") as ph:
            cwt = ph.sb([128, 3, 3072], F32, "cw")
            cbt = ph.sb([128, 3072], F32, "cb")
            drt = ph.sb([128, 1024], F32, "dr")
            for j in range(3):
                S.dma("sp", cwt[:, j, :], bcast_row(W["hy_conv_w"].tensor, (l * 3 + j) * 3072, 3072), writes=[cwt.res], acc=(j > 0))
            S.dma("sp", cbt[:], bcast_row(W["hy_conv_b"].tensor, l * 3072, 3072), writes=[cbt.res])
            S.dma("sp", drt[:], bcast_row(W["hy_d"].tensor, l * 1024, 1024), writes=[drt.res])
            xm = [ph.sb([128, 1024], F32, "xm") for _ in range(2)]
            xc = [ph.sb([128, 1024], F32, "xc") for _ in range(2)]
            xp = [ph.sb([128, 1024], F32, "xp") for _ in range(2)]
            uu = [[ph.sb([128, 1024], F32, "u%d" % cg) for cg in range(3)] for _ in range(2)]
            tq = [ph.sb([128, 1024], F32, "tq") for _ in range(2)]
            zf = [ph.sb([128, 1024], F32, "zf") for _ in range(2)]
            zbt = [ph.sb([128, 1024], BF16, "zb") for _ in range(2)]
            zdt = [ph.sb([128, 1024], F32, "zd") for _ in range(2)]
            k = 0
            for tt in range(g.ntok // 128):
                t0 = tt * 128
                sp_ = t0 % L
                bi = tt % 2
                for cg in range(3):
                    ki_ = k % 2
                    k += 1
                    c0 = 3072 + cg * 1024
                    xm_, xc_, xp_, u, t = xm[ki_], xc[ki_], xp[ki_], uu[bi][cg], tq[ki_]
                    if sp_ == 0:
                        S.op("dve", lambda e: e.memset(xm_[0:1, :], 0.0), writes=[xm_.res])
                        S.dma("sp", xm_[1:128, :], g.rh[t0:t0 + 127, c0:c0 + 1024], reads=[g.rh_r], writes=[xm_.res], acc=True)
                    else:
                        S.dma("sp", xm_[:], g.rh[t0 - 1:t0 + 127, c0:c0 + 1024], reads=[g.rh_r], writes=[xm_.res])
                    S.dma("sp", xc_[:], g.rh[t0:t0 + 128, c0:c0 + 1024], reads=[g.rh_r], writes=[xc_.res])
                    if sp_ + 128 == L:
                        S.op("dve", lambda e: e.memset(xp_[:], 0.0), writes=[xp_.res])
                        S.dma("sp", xp_[0:127, :], g.rh[t0 + 1:t0 + 128, c0:c0 + 1024], reads=[g.rh_r], writes=[xp_.res], acc=True)
                    else:
                        S.dma("sp", xp_[:], g.rh[t0 + 1:t0 + 129, c0:c0 + 1024], reads=[g.rh_r], writes=[xp_.res])
                    wv = lambda j: cwt[:, j, cg * 1024:(cg + 1) * 1024]
                    S.op("dve", lambda e: e.tensor_tensor(out=u[:], in0=xm_[:], in1=wv(0), op=ALU.mult), reads=[xm_.res, cwt.res], writes=[u.res])
                    S.op("dve", lambda e: e.tensor_tensor(out=t[:], in0=xc_[:], in1=wv(1), op=ALU.mult), reads=[xc_.res, cwt.res], writes=[t.res])
                    S.op("dve", lambda e: e.tensor_tensor(out=u[:], in0=u[:], in1=t[:], op=ALU.add), reads=[u.res, t.res], writes=[u.res])
                    S.op("dve", lambda e: e.tensor_tensor(out=t[:], in0=xp_[:], in1=wv(2), op=ALU.mult), reads=[xp_.res, cwt.res], writes=[t.res])
                    S.op("dve", lambda e: e.tensor_tensor(out=u[:], in0=u[:], in1=t[:], op=ALU.add), reads=[u.res, t.res], writes=[u.res])
                    S.op("dve", lambda e: e.tensor_tensor(out=u[:], in0=u[:], in1=cbt[:, cg * 1024:(cg + 1) * 1024], op=ALU.add),
                         reads=[u.res, cbt.res], writes=[u.res])
                u0, u1, u2 = uu[bi]
                S.dma("sp", g.x0s[t0:t0 + 128, :], u0[:], reads=[u0.res], writes=[g.x0s_r], acc=True)
                S.op("dve", lambda e: e.tensor_tensor(out=zf[bi][:], in0=u2[:], in1=u1[:], op=ALU.mult), reads=[u1.res, u2.res], writes=[zf[bi].res])
                S.op("act", lambda e: e.copy(out=zbt[bi][:], in_=zf[bi][:]), reads=[zf[bi].res], writes=[zbt[bi].res])
                S.op("dve", lambda e: e.tensor_tensor(out=zdt[bi][:], in0=zf[bi][:], in1=drt[:], op=ALU.mult), reads=[zf[bi].res, drt.res],
                     writes=[zdt[bi].res])
                S.dma("sp", g.zbs[t0:t0 + 128, :], zbt[bi][:], reads=[zbt[bi].res], writes=[g.zbs_r], acc=True)
                S.dma("sp", g.zd[t0:t0 + 128, :], zdt[bi][:], reads=[zdt[bi].res], writes=[g.zd_r], acc=True)

    def hyena_dft(l, g, L):
        hy_ = HY[L]
        LT = L // 128
        with Phase(C, "hyd") as ph:
            zT = ph.sb([128, LT, 512], BF16, "zT")
            YT = ph.sb([128, 2 * LT, 512], BF16, "YT")
            wbF = [ph.sb([128, LT, 512], BF16, "wbF") for _ in range(2)]
            wbI = [ph.sb([128, 2 * LT, 256], BF16, "wbI") for _ in range(2)]
            hre = [ph.sb([128, 512], F32, "hre") for _ in range(2)]
            him = [ph.sb([128, 512], F32, "him") for _ in range(2)]
            zre = [ph.sb([128, 512], F32, "zre") for _ in range(2)]
            zim = [ph.sb([128, 512], F32, "zim") for _ in range(2)]
            ta = [ph.sb([128, 512], F32, "ta") for _ in range(2)]
            tb = [ph.sb([128, 512], F32, "tb") for _ in range(2)]
            tc_ = [ph.sb([128, 512], F32, "tc") for _ in range(2)]
            td = [ph.sb([128, 512], F32, "td") for _ in range(2)]
            zdt = [ph.sb([128, 512], F32, "zdt") for _ in range(2)]
            x0t = [ph.sb([128, 512], F32, "x0t") for _ in range(2)]
            ot = [ph.sb([128, 512], F32, "ot") for _ in range(2)]
            otb = [ph.sb([128, 4, 128], BF16, "otb") for _ in range(2)]
            for sq in range(g.ntok // L):
                s0 = sq * L
                for half in range(2):
                    h0 = half * 512
                    S.dma("sp", zT[:], g.zbs[s0:s0 + L, h0:h0 + 512].rearrange("(j p) c -> p j c", p=128), reads=[g.zbs_r], writes=[zT.res])
                    st = {"i": 0, "re": None}

                    def epi_f(c0, t0, pt, nr, ncv):
                        blk = c0 // 128
                        if blk % 2 == 0:
                            i = st["i"] % 2
                            S.op("act", lambda e: e.copy(out=zre[i][:], in_=pt[:, :]), reads=[pt.res], writes=[zre[i].res])
                            S.dma("sp", hre[i][:], hy_["Hs"][blk, :, h0:h0 + 512], reads=[hy_["Hs_r"]], writes=[hre[i].res])
                            S.dma("sp", him[i][:], hy_["Hs"][blk + 1, :, h0:h0 + 512], reads=[hy_["Hs_r"]], writes=[him[i].res])
                            return
                        i = st["i"] % 2
                        st["i"] += 1
                        S.op("act", lambda e: e.copy(out=zim[i][:], in_=pt[:, :]), reads=[pt.res], writes=[zim[i].res])
                        S.op("dve", lambda e: e.tensor_tensor(out=ta[i][:], in0=zre[i][:], in1=hre[i][:], op=ALU.mult),
                             reads=[zre[i].res, hre[i].res], writes=[ta[i].res])
                        S.op("dve", lambda e: e.tensor_tensor(out=tb[i][:], in0=zim[i][:], in1=him[i][:], op=ALU.mult),
                             reads=[zim[i].res, him[i].res], writes=[tb[i].res])
                        S.op("dve", lambda e: e.tensor_tensor(out=YT[:, blk - 1, :], in0=ta[i][:], in1=tb[i][:], op=ALU.subtract),
                             reads=[ta[i].res, tb[i].res], writes=[YT.res])
                        S.op("dve", lambda e: e.tensor_tensor(out=tc_[i][:], in0=zre[i][:], in1=him[i][:], op=ALU.mult),
                             reads=[zre[i].res, him[i].res], writes=[tc_[i].res])
                        S.op("dve", lambda e: e.tensor_tensor(out=td[i][:], in0=zim[i][:], in1=hre[i][:], op=ALU.mult),
                             reads=[zim[i].res, hre[i].res], writes=[td[i].res])
                        S.op("dve", lambda e: e.tensor_tensor(out=YT[:, blk, :], in0=tc_[i][:], in1=td[i][:], op=ALU.add),
                             reads=[tc_[i].res, td[i].res], writes=[YT.res])
                    gemm(ph, hy_["Ff"], LT, 0, 2 * L, 512, zT, 512, "fm", epi_f, wbF)
                    st2 = {"i": 0}

                    def epi_i(c0, t0, pt, nr, ncv):
                        i = st2["i"] % 2
                        st2["i"] += 1
                        r0 = s0 + c0
                        S.dma("sp", zdt[i][:], g.zd[r0:r0 + 128, h0:h0 + 512], reads=[g.zd_r], writes=[zdt[i].res])
                        S.dma("sp", x0t[i][:], g.x0s[r0:r0 + 128, h0:h0 + 512], reads=[g.x0s_r], writes=[x0t[i].res])
                        S.op("dve", lambda e: e.tensor_tensor(out=ot[i][:], in0=pt[:, :], in1=zdt[i][:], op=ALU.add),
                             reads=[pt.res, zdt[i].res], writes=[ot[i].res])
                        S.op("dve", lambda e: e.tensor_tensor(out=ot[i][:], in0=ot[i][:], in1=x0t[i][:], op=ALU.mult),
                             reads=[ot[i].res, x0t[i].res], writes=[ot[i].res])
                        pt2 = next_ps()
                        S.mm([lambda pe, q=q: pe.transpose(pt2[:, q * 128:(q + 1) * 128], ot[i][:, q * 128:(q + 1) * 128], ident[:])
                              for q in range(4)], reads=[ot[i].res, ident.res], writes=[pt2.res])
                        S.op("act", lambda e: e.copy(out=otb[i][:], in_=pt2[:, :].rearrange("p (a b) -> p a b", b=128)),
                             reads=[pt2.res], writes=[otb[i].res])
                        S.dma("sp", g.oT[2][0][half * 4:(half + 1) * 4, :, r0:r0 + 128].rearrange("k p t -> p k t"), otb[i][:],
                              reads=[otb[i].res], writes=[g.oT[2][1]], acc=True)
                    gemm(ph, hy_["Fi"], 2 * LT, 0, L, 256, YT, 512, "fm", epi_i, wbI)

    for g in G:
        g.lora, g.lora_r = dscr("lora%d" % g.i, [4, 64, g.ntok], BF16)
        g.sg, g.sg_r = dscr("sg%d" % g.i, [128, g.ntok], BF16)
        g.SH, g.SH_r = dscr("SH%d" % g.i, [g.ntok, 3, 1024])
        g.DE = [dscr("DE%d_%d" % (g.i, e), [g.ntok, 3, 1024]) for e in range(2)]
        g.ysc = [dscr("ysc%d_%d" % (g.i, e), [g.ntok, 1024]) for e in range(2)]
        g.gsc, g.gsc_r = dscr("gsc%d" % g.i, [g.ntok, 1024])
        g.bon, g.bon_r = dscr("bon%d" % g.i, [g.ntok, 1024])

    def rwkv_lora(ph, l, g, hT):
        wbs = [ph.sb([128, 16, 128], BF16, "wbl") for _ in range(2)]
        obl = [ph.sb([128, 512], BF16, "obl") for _ in range(3)]
        for e in range(2):
            gemm(ph, W["wkv_w1"][l][e], 16, 0, 64, 64, hT, g.ntok, "fm",
                 epi_store(ph, obl, lambda c0, t0, nr, ncv, e=e: g.lora[e, :, t0:t0 + ncv], g.lora_r, func=AF.Tanh), wbs)
            gemm(ph, W["wkv_a1"][l][e], 16, 0, 64, 64, hT, g.ntok, "fm",
                 epi_store(ph, obl, lambda c0, t0, nr, ncv, e=e: g.lora[2 + e, :, t0:t0 + ncv], g.lora_r), wbs)
        gemm(ph, W["wkv_g1"][l], 16, 0, 128, 128, hT, g.ntok, "fm",
             epi_store(ph, obl, lambda c0, t0, nr, ncv: g.sg[:, t0:t0 + ncv], g.sg_r, func=AF.Sigmoid), wbs)

    def rwkv_pre(l, g, L):
        with Phase(C, "rwp") as ph:
            cwt = ph.sb([128, 3, 3072], F32, "cw")
            cbt = ph.sb([128, 3072], F32, "cb")
            for j in range(3):
                S.dma("sp", cwt[:, j, :], bcast_row(W["wkv_conv_w"].tensor, (l * 3 + j) * 3072, 3072), writes=[cwt.res], acc=(j > 0))
            S.dma("sp", cbt[:], bcast_row(W["wkv_conv_b"].tensor, l * 3072, 3072), writes=[cbt.res])
            rows = ph.sb([128, 8, 1024], F32, "rows")
            for e in range(2):
                S.dma("sp", rows[:, e, :], bcast_row(W["wkv_w0"].tensor, (l * 2 + e) * 1024, 1024), writes=[rows.res], acc=True)
                S.dma("sp", rows[:, 2 + e, :], bcast_row(W["wkv_a0"].tensor, (l * 2 + e) * 1024, 1024), writes=[rows.res], acc=True)
            S.dma("sp", rows[:, 4, :], bcast_row(W["wkv_k_k"].tensor, l * 1024, 1024), writes=[rows.res], acc=True)
            S.dma("sp", rows[:, 5, :], bcast_row(W["wkv_k_a"].tensor, l * 1024, 1024), writes=[rows.res], acc=True)
            S.dma("sp", rows[:, 7, :], bcast_row(W["wkv_r_k"].tensor, l * 1024, 1024), writes=[rows.res], acc=True)
            S.op("dve", lambda e_: e_.tensor_scalar(out=rows[:, 6, :], in0=rows[:, 5, :], scalar1=-1.0, scalar2=1.0, op0=ALU.mult, op1=ALU.add),
                 reads=[rows.res], writes=[rows.res])
            w2b = ph.sb([64, 4, 1024], BF16, "w2b")
            for e in range(2):
                S.dma("pool", w2b[:, e, :], W["wkv_w2"][l][e], writes=[w2b.res], acc=True)
                S.dma("pool", w2b[:, 2 + e, :], W["wkv_a2"][l][e], writes=[w2b.res], acc=True)
            g2b = ph.sb([128, 1024], BF16, "g2b")
            S.dma("pool", g2b[:], W["wkv_g2"][l], writes=[g2b.res])
            xms = [ph.sb([128, 1024], F32, "xm") for _ in range(2)]
            xcs = [ph.sb([128, 1024], F32, "xc") for _ in range(2)]
            xps = [ph.sb([128, 1024], F32, "xp") for _ in range(2)]
            kx = [0]
            rkv = [ph.sb([128, 1024], F32, "rkv%d" % i) for i in range(3)]
            t1 = ph.sb([128, 1024], F32, "t1")
            t2 = ph.sb([128, 1024], F32, "t2")
            kk = ph.sb([128, 1024], F32, "kk")
            at = ph.sb([128, 1024], F32, "at")
            wt = ph.sb([128, 1024], F32, "wt")
            o1 = ph.sb([128, 1024], F32, "o1")
            o2 = ph.sb([128, 1024], F32, "o2")
            sm = ph.sb([128, 64], F32, "sm")
            lt = ph.sb([64, 4, 128], BF16, "lt")
            sgt = ph.sb([128, 128], BF16, "sgt")
            v3 = lambda t_: t_[:].rearrange("p (h k) -> p h k", k=64)
            for tt in range(g.ntok // 128):
                t0 = tt * 128
                sp_ = t0 % L
                for cg in range(3):
                    c0 = cg * 1024
                    u = rkv[cg]
                    xm, xc, xp = xms[kx[0] % 2], xcs[kx[0] % 2], xps[kx[0] % 2]
                    kx[0] += 1
                    if sp_ == 0:
                        S.op("dve", lambda e: e.memset(xm[0:1, :], 0.0), writes=[xm.res])
                        S.dma("sp", xm[1:128, :], g.rh[t0:t0 + 127, c0:c0 + 1024], reads=[g.rh_r], writes=[xm.res], acc=True)
                    else:
                        S.dma("sp", xm[:], g.rh[t0 - 1:t0 + 127, c0:c0 + 1024], reads=[g.rh_r], writes=[xm.res])
                    S.dma("sp", xc[:], g.rh[t0:t0 + 128, c0:c0 + 1024], reads=[g.rh_r], writes=[xc.res])
                    if sp_ + 128 == L:
                        S.op("dve", lambda e: e.memset(xp[:], 0.0), writes=[xp.res])
                        S.dma("sp", xp[0:127, :], g.rh[t0 + 1:t0 + 128, c0:c0 + 1024], reads=[g.rh_r], writes=[xp.res], acc=True)
                    else:
                        S.dma("sp", xp[:], g.rh[t0 + 1:t0 + 129, c0:c0 + 1024], reads=[g.rh_r], writes=[xp.res])
                    wv = lambda j: cwt[:, j, cg * 1024:(cg + 1) * 1024]
                    S.op("dve", lambda e: e.tensor_tensor(out=u[:], in0=xm[:], in1=wv(0), op=ALU.mult), reads=[xm.res, cwt.res], writes=[u.res])
                    S.op("dve", lambda e: e.tensor_tensor(out=t1[:], in0=xc[:], in1=wv(1), op=ALU.mult), reads=[xc.res, cwt.res], writes=[t1.res])
                    S.op("dve", lambda e: e.tensor_tensor(out=u[:], in0=u[:], in1=t1[:], op=ALU.add), reads=[u.res, t1.res], writes=[u.res])
                    S.op("dve", lambda e: e.tensor_tensor(out=t1[:], in0=xp[:], in1=wv(2), op=ALU.mult), reads=[xp.res, cwt.res], writes=[t1.res])
                    S.op("dve", lambda e: e.tensor_tensor(out=u[:], in0=u[:], in1=t1[:], op=ALU.add), reads=[u.res, t1.res], writes=[u.res])
                    S.op("dve", lambda e: e.tensor_tensor(out=u[:], in0=u[:], in1=cbt[:, cg * 1024:(cg + 1) * 1024], op=ALU.add),
                         reads=[u.res, cbt.res], writes=[u.res])
                r_, k_, v_ = rkv
                S.dma("sp", g.SH[t0:t0 + 128, 1, :], r_[:], reads=[r_.res], writes=[g.SH_r], acc=True)
                S.dma("sp", g.SH[t0:t0 + 128, 2, :], v_[:], reads=[v_.res], writes=[g.SH_r], acc=True)
                S.op("dve", lambda e: e.tensor_tensor(out=kk[:], in0=k_[:], in1=rows[:, 4, :], op=ALU.mult), reads=[k_.res, rows.res], writes=[kk.res])
                S.op("dve", lambda e: e.tensor_tensor(out=t1[:], in0=kk[:], in1=kk[:], op=ALU.mult), reads=[kk.res], writes=[t1.res])
                S.op("dve", lambda e: e.tensor_reduce(out=sm[:, 0:16], in_=v3(t1), axis=AX.X, op=ALU.add), reads=[t1.res], writes=[sm.res])
                S.op("dve", lambda e: e.tensor_scalar(out=sm[:, 0:16], in0=sm[:, 0:16], scalar1=1e-12, scalar2=None, op0=ALU.add),
                     reads=[sm.res], writes=[sm.res])
                S.op("act", lambda e: e.sqrt(out=sm[:, 0:16], in_=sm[:, 0:16]), reads=[sm.res], writes=[sm.res])
                S.op("dve", lambda e: e.reciprocal(out=sm[:, 0:16], in_=sm[:, 0:16]), reads=[sm.res], writes=[sm.res])
                S.op("dve", lambda e: e.tensor_tensor(out=v3(kk), in0=v3(kk), in1=sm[:, 0:16].unsqueeze(2).broadcast_to([128, 16, 64]), op=ALU.mult),
                     reads=[kk.res, sm.res], writes=[kk.res])
                S.dma("sp", g.SH[t0:t0 + 128, 0, :], kk[:], reads=[kk.res], writes=[g.SH_r], acc=True)
                S.op("dve", lambda e: e.tensor_tensor(out=t1[:], in0=r_[:], in1=k_[:], op=ALU.mult), reads=[r_.res, k_.res], writes=[t1.res])
                S.op("dve", lambda e: e.tensor_tensor(out=t1[:], in0=t1[:], in1=rows[:, 7, :], op=ALU.mult), reads=[t1.res, rows.res], writes=[t1.res])
                S.op("dve", lambda e: e.tensor_reduce(out=sm[:, 16:32], in_=v3(t1), axis=AX.X, op=ALU.add), reads=[t1.res, sm.res], writes=[sm.res])
                S.op("dve", lambda e: e.tensor_tensor(out=v3(o1), in0=v3(v_), in1=sm[:, 16:32].unsqueeze(2).broadcast_to([128, 16, 64]), op=ALU.mult),
                     reads=[v_.res, sm.res], writes=[o1.res])
                S.dma("sp", g.bon[t0:t0 + 128, :], o1[:], reads=[o1.res], writes=[g.bon_r], acc=True)
                S.dma("sp", lt[:], g.lora[:, :, t0:t0 + 128].rearrange("a p t -> p a t"), reads=[g.lora_r], writes=[lt.res])
                S.dma("sp", sgt[:], g.sg[:, t0:t0 + 128], reads=[g.sg_r], writes=[sgt.res])
                for hf in range(2):
                    pt = next_ps()
                    S.mm([lambda pe: pe.matmul(pt[:, :], lhsT=sgt[:], rhs=g2b[:, hf * 512:(hf + 1) * 512], start=True, stop=True)],
                         reads=[sgt.res, g2b.res], writes=[pt.res])
                    S.op("act", lambda e: e.copy(out=o2[:, hf * 512:(hf + 1) * 512], in_=pt[:, :]), reads=[pt.res, o2.res], writes=[o2.res])
                S.dma("sp", g.gsc[t0:t0 + 128, :], o2[:], reads=[o2.res], writes=[g.gsc_r], acc=True)
                for e in range(2):
                    for hf in range(2):
                        pt = next_ps()
                        S.mm([lambda pe: pe.matmul(pt[:, :], lhsT=lt[:, e, :], rhs=w2b[:, e, hf * 512:(hf + 1) * 512], start=True, stop=True)],
                             reads=[lt.res, w2b.res], writes=[pt.res])
                        S.op("dve", lambda e_: e_.tensor_tensor(out=wt[:, hf * 512:(hf + 1) * 512], in0=pt[:, :],
                                                               in1=rows[:, e, hf * 512:(hf + 1) * 512], op=ALU.add),
                             reads=[pt.res, rows.res, wt.res], writes=[wt.res])
                    S.op("act", lambda e_: e_.activation(out=wt[:], in_=wt[:], func=AF.Sigmoid), reads=[wt.res], writes=[wt.res])
                    S.op("act", lambda e_: e_.activation(out=wt[:], in_=wt[:], func=AF.Exp, scale=-math.exp(-0.5)), reads=[wt.res], writes=[wt.res])
                    S.dma("sp", g.DE[e][0][t0:t0 + 128, 0, :], wt[:], reads=[wt.res], writes=[g.DE[e][1]], acc=True)
                    for hf in range(2):
                        pt = next_ps()
                        S.mm([lambda pe: pe.matmul(pt[:, :], lhsT=lt[:, 2 + e, :], rhs=w2b[:, 2 + e, hf * 512:(hf + 1) * 512], start=True, stop=True)],
                             reads=[lt.res, w2b.res], writes=[pt.res])
                        S.op("dve", lambda e_: e_.tensor_tensor(out=at[:, hf * 512:(hf + 1) * 512], in0=pt[:, :],
                                                               in1=rows[:, 2 + e, hf * 512:(hf + 1) * 512], op=ALU.add),
                             reads=[pt.res, rows.res, at.res], writes=[at.res])
                    S.op("act", lambda e_: e_.activation(out=at[:], in_=at[:], func=AF.Sigmoid), reads=[at.res], writes=[at.res])
                    S.op("dve", lambda e_: e_.tensor_tensor(out=o1[:], in0=kk[:], in1=at[:], op=ALU.mult), reads=[kk.res, at.res], writes=[o1.res])
                    S.dma("sp", g.DE[e][0][t0:t0 + 128, 1, :], o1[:], reads=[o1.res], writes=[g.DE[e][1]], acc=True)
                    S.op("dve", lambda e_: e_.tensor_tensor(out=t2[:], in0=at[:], in1=rows[:, 5, :], op=ALU.mult), reads=[at.res, rows.res], writes=[t2.res])
                    S.op("dve", lambda e_: e_.tensor_tensor(out=t2[:], in0=t2[:], in1=rows[:, 6, :], op=ALU.add), reads=[t2.res, rows.res], writes=[t2.res])
                    S.op("dve", lambda e_: e_.tensor_tensor(out=o2[:], in0=k_[:], in1=t2[:], op=ALU.mult), reads=[k_.res, t2.res], writes=[o2.res])
                    S.dma("sp", g.DE[e][0][t0:t0 + 128, 2, :], o2[:], reads=[o2.res], writes=[g.DE[e][1]], acc=True)

    def rwkv_scan(l, g, L):
        sample = (g.i == 1)
        NV = 16 if sample else 64
        TC = 32 if sample else 16
        with Phase(C, "rws") as ph:
            St = ph.sb([128, NV, 64], F32, "S")
            tmp = ph.sb([128, NV, 64], F32, "tmp")
            At = ph.sb([128, NV, 64], F32, "A")
            Bt = ph.sb([128, NV, 64], F32, "B")
            sa = ph.sb([128, NV], F32, "sa")
            tmpY = ph.sb([128, NV, 64], F32, "tmpY")
            Dq = [[ph.sb([128, TC, 64], F32, "D%d" % q) for q in range(5)] for _ in range(2)]
            Vt = [ph.sb([128, TC, NV], F32, "V") for _ in range(2)]
            Yt = [ph.sb([128, TC, NV], F32, "Y") for _ in range(2)]
            if sample:
                S.dma("sp", St[:].rearrange("p a b -> p (a b)"), I["s0"][l], writes=[St.res])
            else:
                S.op("pool", lambda e: e.memset(St[:], 0.0), writes=[St.res])
            def srcs(e):
                return [(g.SH, g.SH_r, 0), (g.DE[e][0], g.DE[e][1], 0), (g.DE[e][0], g.DE[e][1], 1), (g.DE[e][0], g.DE[e][1], 2), (g.SH, g.SH_r, 1)]
            nchunk = L // TC
            for c in range(nchunk):
                bi = c % 2
                i0 = c * TC
                for e in range(2):
                    tstart = i0 if e == 0 else (L - 1 - i0)
                    sgn = 1 if e == 0 else -1
                    if sample:
                        for vq in range(4):
                            dstp = slice(e * 64 + vq, (e + 1) * 64, 4)
                            first = (e == 0 and vq == 0)
                            for q, (arr, arr_r, slot) in enumerate(srcs(e)):
                                S.dma("sp", Dq[bi][q][dstp, :, :],
                                      AP(arr.tensor, tstart * 3072 + slot * 1024, [[64, 16], [sgn * 3072, TC], [1, 64]]),
                                      reads=[arr_r], writes=[Dq[bi][q].res], acc=(not first))
                            S.dma("sp", Vt[bi][dstp, :, :],
                                  AP(g.SH.tensor, tstart * 3072 + 2 * 1024 + vq * 16, [[64, 16], [sgn * 3072, TC], [1, 16]]),
                                  reads=[g.SH_r], writes=[Vt[bi].res], acc=(not first))
                    else:
                        for sq in range(4):
                            p0 = sq * 32 + e * 16
                            first = (e == 0 and sq == 0)
                            for q, (arr, arr_r, slot) in enumerate(srcs(e)):
                                S.dma("sp", Dq[bi][q][p0:p0 + 16, :, :],
                                      AP(arr.tensor, (sq * L + tstart) * 3072 + slot * 1024, [[64, 16], [sgn * 3072, TC], [1, 64]]),
                                      reads=[arr_r], writes=[Dq[bi][q].res], acc=(not first))
                            S.dma("sp", Vt[bi][p0:p0 + 16, :, :],
                                  AP(g.SH.tensor, (sq * L + tstart) * 3072 + 2 * 1024, [[64, 16], [sgn * 3072, TC], [1, 64]]),
                                  reads=[g.SH_r], writes=[Vt[bi].res], acc=(not first))
                D = Dq[bi]
                pend = None
                for i in range(TC):
                    bc = lambda q: D[q][:, i, :].unsqueeze(1).broadcast_to([128, NV, 64])
                    S.op("dve", lambda e_: e_.tensor_tensor(out=tmp[:], in0=St[:], in1=bc(0), op=ALU.mult), reads=[St.res, D[0].res], writes=[tmp.res])
                    if pend is not None:
                        S.op("dve", lambda e_: e_.tensor_reduce(out=Yt[bi][:, pend, :], in_=tmpY[:], axis=AX.X, op=ALU.add), reads=[tmpY.res], writes=[Yt[bi].res])
                    S.op("dve", lambda e_: e_.tensor_tensor(out=At[:], in0=Vt[bi][:, i, :].unsqueeze(2).broadcast_to([128, NV, 64]), in1=bc(3), op=ALU.mult),
                         reads=[Vt[bi].res, D[3].res], writes=[At.res])
                    S.op("dve", lambda e_: e_.tensor_reduce(out=sa[:], in_=tmp[:], axis=AX.X, op=ALU.add), reads=[tmp.res], writes=[sa.res])
                    S.op("dve", lambda e_: e_.tensor_tensor(out=Bt[:], in0=St[:], in1=bc(1), op=ALU.mult), reads=[St.res, D[1].res], writes=[Bt.res])
                    S.op("dve", lambda e_: e_.tensor_tensor(out=tmp[:], in0=sa[:].unsqueeze(2).broadcast_to([128, NV, 64]), in1=bc(2), op=ALU.mult),
                         reads=[sa.res, D[2].res], writes=[tmp.res])
                    S.op("dve", lambda e_: e_.tensor_tensor(out=Bt[:], in0=Bt[:], in1=At[:], op=ALU.add), reads=[Bt.res, At.res], writes=[Bt.res])
                    S.op("dve", lambda e_: e_.tensor_tensor(out=St[:], in0=Bt[:], in1=tmp[:], op=ALU.subtract), reads=[Bt.res, tmp.res], writes=[St.res])
                    S.op("dve", lambda e_: e_.tensor_tensor(out=tmpY[:], in0=St[:], in1=bc(4), op=ALU.mult), reads=[St.res, D[4].res], writes=[tmpY.res])
                    pend = i
                S.op("dve", lambda e_: e_.tensor_reduce(out=Yt[bi][:, pend, :], in_=tmpY[:], axis=AX.X, op=ALU.add), reads=[tmpY.res], writes=[Yt[bi].res])
                for e in range(2):
                    tstart = i0 if e == 0 else (L - 1 - i0)
                    sgn = 1 if e == 0 else -1
                    ya, ya_r = g.ysc[e]
                    if sample:
                        for vq in range(4):
                            S.dma("sp", AP(ya.tensor, tstart * 1024 + vq * 16, [[64, 16], [sgn * 1024, TC], [1, 16]]),
                                  Yt[bi][slice(e * 64 + vq, (e + 1) * 64, 4), :, :], reads=[Yt[bi].res], writes=[ya_r], acc=True)
                    else:
                        for sq in range(4):
                            p0 = sq * 32 + e * 16
                            S.dma("sp", AP(ya.tensor, (sq * L + tstart) * 1024, [[64, 16], [sgn * 1024, TC], [1, 64]]), Yt[bi][p0:p0 + 16, :, :],
                                  reads=[Yt[bi].res], writes=[ya_r], acc=True)
            if not sample:
                for sq in range(4):
                    r0 = (sq * DEPTH + l) * 32
                    S.dma("sp", O["ns"][r0:r0 + 32, :], St[sq * 32:(sq + 1) * 32, :, :].rearrange("p a b -> p (a b)"), reads=[St.res],
                          writes=[ORES["ns"]], acc=True)

    def rwkv_post(l, g):
        with Phase(C, "rwo") as ph:
            rows = ph.sb([128, 2, 1024], F32, "rows")
            S.dma("sp", rows[:, 0, :], bcast_row(W["wkv_gn_g"].tensor, l * 1024, 1024), writes=[rows.res], acc=True)
            S.dma("sp", rows[:, 1, :], bcast_row(W["wkv_gn_b"].tensor, l * 1024, 1024), writes=[rows.res], acc=True)
            ya = [ph.sb([128, 1024], F32, "ya") for _ in range(2)]
            yb = [ph.sb([128, 1024], F32, "yb") for _ in range(2)]
            bo = [ph.sb([128, 1024], F32, "bo") for _ in range(2)]
            gg = [ph.sb([128, 1024], F32, "gg") for _ in range(2)]
            tq = [ph.sb([128, 1024], F32, "tq") for _ in range(2)]
            sm = [ph.sb([128, 64], F32, "sm") for _ in range(2)]
            otb = [ph.sb([128, 8, 128], BF16, "otb") for _ in range(2)]
            v3 = lambda t_: t_[:].rearrange("p (h k) -> p h k", k=64)
            for tt in range(g.ntok // 128):
                i = tt % 2
                t0 = tt * 128
                y, y2, b_, g_, t_, s_ = ya[i], yb[i], bo[i], gg[i], tq[i], sm[i]
                S.dma("sp", y[:], g.ysc[0][0][t0:t0 + 128, :], reads=[g.ysc[0][1]], writes=[y.res])
                S.dma("sp", y2[:], g.ysc[1][0][t0:t0 + 128, :], reads=[g.ysc[1][1]], writes=[y2.res])
                S.dma("sp", b_[:], g.bon[t0:t0 + 128, :], reads=[g.bon_r], writes=[b_.res])
                S.dma("sp", g_[:], g.gsc[t0:t0 + 128, :], reads=[g.gsc_r], writes=[g_.res])
                S.op("dve", lambda e: e.tensor_tensor(out=y[:], in0=y[:], in1=y2[:], op=ALU.add), reads=[y.res, y2.res], writes=[y.res])
                S.op("dve", lambda e: e.tensor_reduce(out=s_[:, 0:16], in_=v3(y), axis=AX.X, op=ALU.add), reads=[y.res], writes=[s_.res])
                S.op("dve", lambda e: e.tensor_scalar(out=s_[:, 0:16], in0=s_[:, 0:16], scalar1=-1.0 / 64, scalar2=None, op0=ALU.mult),
                     reads=[s_.res], writes=[s_.res])
                S.op("dve", lambda e: e.tensor_tensor(out=v3(y), in0=v3(y), in1=s_[:, 0:16].unsqueeze(2).broadcast_to([128, 16, 64]), op=ALU.add),
                     reads=[y.res, s_.res], writes=[y.res])
                S.op("dve", lambda e: e.tensor_tensor(out=t_[:], in0=y[:], in1=y[:], op=ALU.mult), reads=[y.res], writes=[t_.res])
                S.op("dve", lambda e: e.tensor_reduce(out=s_[:, 16:32], in_=v3(t_), axis=AX.X, op=ALU.add), reads=[t_.res, s_.res], writes=[s_.res])
                S.op("dve", lambda e: e.tensor_scalar(out=s_[:, 16:32], in0=s_[:, 16:32], scalar1=1.0 / 64, scalar2=64e-5, op0=ALU.mult, op1=ALU.add),
                     reads=[s_.res], writes=[s_.res])
                S.op("act", lambda e: e.sqrt(out=s_[:, 16:32], in_=s_[:, 16:32]), reads=[s_.res], writes=[s_.res])
                S.op("dve", lambda e: e.reciprocal(out=s_[:, 16:32], in_=s_[:, 16:32]), reads=[s_.res], writes=[s_.res])
                S.op("dve", lambda e: e.tensor_tensor(out=v3(y), in0=v3(y), in1=s_[:, 16:32].unsqueeze(2).broadcast_to([128, 16, 64]), op=ALU.mult),
                     reads=[y.res, s_.res], writes=[y.res])
                S.op("dve", lambda e: e.tensor_tensor(out=y[:], in0=y[:], in1=rows[:, 0, :], op=ALU.mult), reads=[y.res, rows.res], writes=[y.res])
                S.op("dve", lambda e: e.tensor_tensor(out=y[:], in0=y[:], in1=rows[:, 1, :], op=ALU.add), reads=[y.res, rows.res], writes=[y.res])
                S.op("dve", lambda e: e.tensor_tensor(out=y[:], in0=y[:], in1=b_[:], op=ALU.add), reads=[y.res, b_.res], writes=[y.res])
                S.op("dve", lambda e: e.tensor_tensor(out=y[:], in0=y[:], in1=g_[:], op=ALU.mult), reads=[y.res, g_.res], writes=[y.res])
                for hf in range(2):
                    pt2 = next_ps()
                    S.mm([lambda pe, q=q: pe.transpose(pt2[:, q * 128:(q + 1) * 128], y[:, (hf * 4 + q) * 128:(hf * 4 + q + 1) * 128], ident[:])
                          for q in range(4)], reads=[y.res, ident.res], writes=[pt2.res])
                    S.op("act", lambda e: e.copy(out=otb[i][:, hf * 4:(hf + 1) * 4, :], in_=pt2[:, :].rearrange("p (a b) -> p a b", b=128)),
                         reads=[pt2.res, otb[i].res], writes=[otb[i].res])
                S.dma("sp", g.oT[1][0][:, :, t0:t0 + 128].rearrange("k p t -> p k t"), otb[i][:], reads=[otb[i].res], writes=[g.oT[1][1]], acc=True)

    def merge_out(l, g):
        with Phase(C, "mrg") as ph:
            cols = load_cols(ph, l, g)
            mT = ph.sb([128, 16, g.ntok], BF16, "mT")
            if not mixers or cfg.get("nomerge"):
                S.op("pool", lambda e: e.memset(mT[:], 0.0), writes=[mT.res])
            else:
                with Phase(C, "mrg1") as ph1:
                    branches = [b for b in range(3) if "arc"[b] in mixers]
                    wnames = ["w_pa", "w_pr", "w_pc"]
                    HT = 1024
                    oTt = {b: ph1.sb([128, 8, HT], BF16, "oT%d" % b) for b in branches}
                    wbs = {b: [ph1.sb([128, 8, 512], BF16, "wp%d" % b) for _ in range(2)] for b in branches}
                    gts = {b: [ph1.sb([128, 512], F32, "g%d" % b) for _ in range(2)] for b in branches}
                    tmp = [ph1.sb([128, 512], F32, "mt") for _ in range(2)]
                    tmp2 = [ph1.sb([128, 512], F32, "mt2") for _ in range(2)]
                    cnt = 0
                    for half in range(g.ntok // HT):
                        for b in branches:
                            S.dma("sp", oTt[b][:], g.oT[b][0].rearrange("k p t -> p k t")[:, :, half * HT:(half + 1) * HT],
                                  reads=[g.oT[b][1]], writes=[oTt[b].res])

                        def loadw(s):
                            for b in branches:
                                wbuf = wbs[b][s % 2]
                                S.dma("pool", wbuf[:], W[wnames[b]][l].rearrange("(k p) n -> p k n", p=128)[:, :, s * 512:(s + 1) * 512],
                                      writes=[wbuf.res])
                        loadw(0)
                        for s in range(4):
                            if s + 1 < 4:
                                loadw(s + 1)
                            for nb in range(4):
                                nbg = s * 4 + nb
                                for tt in range(HT // 512):
                                    t0 = half * HT + tt * 512
                                    pts = {}
                                    for b in branches:
                                        pt = next_ps()
                                        pts[b] = pt
                                        wbuf = wbs[b][s % 2]
                                        fns = [lambda pe, kc=kc, pt=pt, wbuf=wbuf, b=b: pe.matmul(
                                            pt[:, :], lhsT=wbuf[:, kc, nb * 128:(nb + 1) * 128],
                                            rhs=oTt[b][:, kc, tt * 512:(tt + 1) * 512], start=(kc == 0), stop=(kc == 7))
                                            for kc in range(8)]
                                        S.mm(fns, reads=[wbuf.res, oTt[b].res], writes=[pt.res])
                                        gt = gts[b][cnt % 2]
                                        S.dma("sp", gt[:], g.gT[b * 16 + nbg, :, t0:t0 + 512], reads=[g.gT_r], writes=[gt.res])
                                    t1, t2 = tmp[cnt % 2], tmp2[cnt % 2]
                                    acc = None
                                    for bi, b in enumerate(branches):
                                        gt = gts[b][cnt % 2]
                                        last = (bi == len(branches) - 1)
                                        if acc is None:
                                            dst = mT[:, nbg, t0:t0 + 512] if last else t1[:]
                                            S.op("dve", lambda e, dst=dst, pt=pts[b], gt=gt: e.tensor_tensor(out=dst, in0=pt[:, :], in1=gt[:], op=ALU.mult),
                                                 reads=[pts[b].res, gt.res], writes=[mT.res if last else t1.res])
                                            acc = t1
                                        else:
                                            S.op("dve", lambda e, pt=pts[b], gt=gt: e.tensor_tensor(out=t2[:], in0=pt[:, :], in1=gt[:], op=ALU.mult),
                                                 reads=[pts[b].res, gt.res], writes=[t2.res])
                                            dst = mT[:, nbg, t0:t0 + 512] if last else t1[:]
                                            S.op("dve", lambda e, dst=dst: e.tensor_tensor(out=dst, in0=t1[:], in1=t2[:], op=ALU.add),
                                                 reads=[t1.res, t2.res], writes=[mT.res if last else t1.res])
                                    cnt += 1
            if "merged" in dbg:
                S.dma("sp", DBG["merged" + g.tag].rearrange("k p t -> p k t"), mT[:], reads=[mT.res], writes=[ORES["dbg_merged" + g.tag]])
            wb = [ph.sb([128, 16, 512], BF16, "wb") for _ in range(2)]
            resid_gemm(ph, l, g, W["w_out"][l], 16, 512, mT, g.ntok, 0, cols.ga1, None, cols.res, wb)

    def resid_gemm(ph, l, g, Wd, KC, slabw, xin, ntok, tok0_x, ga_fn, gb_fn, cres, wb, tok_base=0):
        xr = [ph.sb([128, 512], F32, "xr") for _ in range(3)]
        tb = [ph.sb([128, 512], F32, "tb") for _ in range(3)]
        st = {"i": 0}

        def epi(c0, t0, pt, nr, ncv):
            i = st["i"] % 3
            st["i"] += 1
            nb = c0 // 128
            tg = tok_base + (t0 - tok0_x)
            S.dma("sp", xr[i][:], g.xT[nb, :, tg:tg + 512], reads=[g.xT_r], writes=[xr[i].res])
            if gb_fn is None:
                S.op("dve", lambda e: e.scalar_tensor_tensor(out=xr[i][:], in0=pt[:, :], scalar=ga_fn(nb), in1=xr[i][:],
                                                             op0=ALU.mult, op1=ALU.add), reads=[pt.res, xr[i].res] + cres, writes=[xr[i].res])
            else:
                S.op("act", lambda e: e.activation(out=tb[i][:], in_=pt[:, :], func=AF.Identity, scale=ga_fn(nb), bias=gb_fn(nb)),
                     reads=[pt.res] + cres, writes=[tb[i].res])
                S.op("dve", lambda e: e.tensor_tensor(out=xr[i][:], in0=xr[i][:], in1=tb[i][:], op=ALU.add),
                     reads=[xr[i].res, tb[i].res], writes=[xr[i].res])
            S.dma("sp", g.xT[nb, :, tg:tg + 512], xr[i][:], reads=[xr[i].res], writes=[g.xT_r], acc=True)
        gemm(ph, Wd, KC, 0, D, slabw, xin, ntok, "fm", epi, wb, tok0=tok0_x)

    def mlp(l, g):
        with Phase(C, "ff1") as ph:
            cols = load_cols(ph, l, g)
            hT = ph.sb([128, 16, g.ntok], BF16, "h2T")
            with Phase(C, "nrm2") as ph2:
                norm_mod(ph2, g, cols.a2, cols.sh2, cols.res, hT)
            wb = [ph.sb([128, 16, 512], BF16, "wb") for _ in range(2)]
            rt = [ph.sb([128, 512], F32, "rt") for _ in range(3)]
            ob = [ph.sb([128, 512], BF16, "ob") for _ in range(3)]
            st = {"i": 0}

            def epi(c0, t0, pt, nr, ncv):
                i = st["i"] % 3
                st["i"] += 1
                nb = c0 // 128
                S.op("act", lambda e: e.activation(out=rt[i][:], in_=pt[:, :], func=AF.Relu, bias=cols.b1(nb), scale=1.0),
                     reads=[pt.res] + cols.res, writes=[rt[i].res])
                S.op("pool", lambda e: e.tensor_tensor(out=ob[i][:], in0=rt[i][:], in1=rt[i][:], op=ALU.mult),
                     reads=[rt[i].res], writes=[ob[i].res])
                S.dma("sp", g.f1T[nb, :, t0:t0 + 512], ob[i][:], reads=[ob[i].res], writes=[g.f1T_r], acc=True)
            gemm(ph, W["w_ff1"][l], 16, 0, D_FF, 512, hT, g.ntok, "fm", epi, wb)
        with Phase(C, "ff2") as ph:
            cols = load_cols(ph, l, g)
            f1 = [ph.sb([128, 64, 512], BF16, "f1") for _ in range(1)]
            wb = [ph.sb([128, 64, 128], BF16, "wb2") for _ in range(2)]
            fv = g.f1T.rearrange("k p t -> p k t")
            for tt in range(g.ntok // 512):
                ft = f1[tt % len(f1)]
                for q in range(4):
                    S.dma("sp", ft[:, q * 16:(q + 1) * 16, :], fv[:, q * 16:(q + 1) * 16, tt * 512:(tt + 1) * 512],
                          reads=[g.f1T_r], writes=[ft.res], acc=(q > 0))
                resid_gemm(ph, l, g, W["w_ff2"][l], 64, 128, ft, 512, 0, cols.ga2, cols.gb2, cols.res, wb, tok_base=tt * 512)

    def final_norm(g, dst, dst_res):
        with Phase(C, "fin") as ph:
            fg = ph.sb([128, 16], F32, "fg")
            S.dma("sp", fg[:], W["final_g"].rearrange("(j p) -> p j", p=128), writes=[fg.res], slow=True)
            TW = 256
            xt = [ph.sb([128, 16, TW], F32, "nx") for _ in range(2)]
            tm = [ph.sb([128, 16, TW], F32, "nt") for _ in range(2)]
            rs = [ph.sb([128, TW], F32, "nr") for _ in range(2)]
            yo = [ph.sb([128, D], F32, "yo") for _ in range(2)]
            xv = g.xT.rearrange("k p t -> p k t")
            for tt in range(g.ntok // TW):
                x_, t_, r_ = xt[tt % 2], tm[tt % 2], rs[tt % 2]
                S.dma("sp", x_[:], xv[:, :, tt * TW:(tt + 1) * TW], reads=[g.xT_r], writes=[x_.res])
                S.op("act", lambda e: e.activation(out=t_[:], in_=x_[:], func=AF.Square), reads=[x_.res], writes=[t_.res])
                pt = next_ps()
                fns = [lambda pe, kc=kc: pe.matmul(pt[:, 0:TW], lhsT=ones[:], rhs=t_[:, kc, :], start=(kc == 0), stop=(kc == 15))
                       for kc in range(16)]
                S.mm(fns, reads=[ones.res, t_.res], writes=[pt.res])
                S.op("dve", lambda e: e.tensor_scalar(out=r_[:], in0=pt[:, 0:TW], scalar1=1.0 / D, scalar2=1e-6, op0=ALU.mult,
                                                      op1=ALU.add), reads=[pt.res], writes=[r_.res])
                S.op("act", lambda e: e.sqrt(out=r_[:], in_=r_[:]), reads=[r_.res], writes=[r_.res])
                S.op("dve", lambda e: e.reciprocal(out=r_[:], in_=r_[:]), reads=[r_.res], writes=[r_.res])
                S.op("dve", lambda e: e.tensor_tensor(out=t_[:], in0=x_[:], in1=r_[:].unsqueeze(1).broadcast_to([128, 16, TW]),
                                                      op=ALU.mult), reads=[x_.res, r_.res], writes=[t_.res])
                for kc in range(16):
                    S.op("act", lambda e, kc=kc: e.activation(out=t_[:, kc, :], in_=t_[:, kc, :], func=AF.Identity,
                                                              scale=fg[:, kc:kc + 1]), reads=[t_.res, fg.res], writes=[t_.res])
                for sub in range(TW // 128):
                    y_ = yo[sub % 2]
                    for kq in range(4):
                        pt2 = next_ps()
                        fns = [lambda pe, j=j, pt2=pt2, kq=kq: pe.transpose(pt2[:, j * 128:(j + 1) * 128],
                                                                             t_[:, kq * 4 + j, sub * 128:(sub + 1) * 128], ident[:])
                               for j in range(4)]
                        S.mm(fns, reads=[t_.res, ident.res], writes=[pt2.res])
                        eng = alt_eng()
                        if eng == "act":
                            S.op("act", lambda e, pt2=pt2, kq=kq: e.copy(out=y_[:, kq * 512:(kq + 1) * 512], in_=pt2[:, :]),
                                 reads=[pt2.res], writes=[y_.res])
                        else:
                            S.op("dve", lambda e, pt2=pt2, kq=kq: e.tensor_copy(out=y_[:, kq * 512:(kq + 1) * 512], in_=pt2[:, :]),
                                 reads=[pt2.res], writes=[y_.res])
                    r0 = tt * TW + sub * 128
                    S.dma("sp", dst[r0:r0 + 128, :], y_[:], reads=[y_.res], writes=[dst_res], acc=True)

    for name in dbg:
        for g in G:
            if name in ("h", "merged"):
                dbg_out(name + g.tag, [16, 128, g.ntok], BF16)
            if name == "x":
                dbg_out(name + g.tag, [16, 128, g.ntok], F32)
            if name == "oT":
                for b in range(3):
                    dbg_out("oT%d%s" % (b, g.tag), [8, 128, g.ntok], BF16)
    load_x_T(G[0], I["xp"])
    load_x_T(G[1], I["xs"])
    for l in range(nlayers):
        modulation(l)
        for g in G:
            if g.i not in cfg.get("groups", (0, 1)):
                continue
            in_proj(l, g, None)
            if "a" in mixers and not cfg.get("noattn"):
                (attn_prompt if g.i == 0 else attn_sample)(l, g)
            if "r" in mixers:
                L_ = 256 if g.i == 0 else 2048
                rwkv_pre(l, g, L_)
                rwkv_scan(l, g, L_)
                rwkv_post(l, g)
            if "c" in mixers:
                L_ = 256 if g.i == 0 else 2048
                hyena_filter(l, L_)
                hyena_conv(l, g, L_)
                hyena_dft(l, g, L_)
            if "oT" in dbg:
                for b in range(3):
                    if "arc"[b] in mixers:
                        with Phase(C, "dbgo") as phd:
                            S.dma("sp", DBG["oT%d%s" % (b, g.tag)], g.oT[b][0], reads=[g.oT[b][1]], writes=[ORES["dbg_oT%d%s" % (b, g.tag)]])
            merge_out(l, g)
            mlp(l, g)
    if "x" in dbg:
        for g in G:
            with Phase(C, "dbgx") as ph:
                S.dma("sp", DBG["x" + g.tag], g.xT, reads=[g.xT_r], writes=[ORES["dbg_x" + g.tag]])
    final_norm(G[0], O["yp"], ORES["yp"])
    final_norm(G[1], O["ys"], ORES["ys"])
    S.barrier()
    cst.es.__exit__(None, None, None)
    top.close()
    C.ninst = S.ninst
    return nc, C


def make_in_maps(inputs):
    f = lambda a: np.ascontiguousarray(np.asarray(a, dtype=np.float32))
    maps = []
    wnames = ["ln1_g", "ln2_g", "w_mod", "b_mod", "w_in", "rpb", "wkv_conv_w", "wkv_conv_b", "wkv_w0", "wkv_w1", "wkv_w2",
              "wkv_a0", "wkv_a1", "wkv_a2", "wkv_g1", "wkv_g2", "wkv_k_k", "wkv_k_a", "wkv_r_k", "wkv_gn_g", "wkv_gn_b",
              "hy_conv_w", "hy_conv_b", "hy_f1", "hy_fb1", "hy_f2", "hy_fb2", "hy_freq", "hy_f3", "hy_d", "w_pa", "w_pr",
              "w_pc", "w_out", "w_ff1", "b_ff1", "w_ff2", "b_ff2", "final_g"]
    wd = {k: f(inputs[k]) for k in wnames}
    wd["wkv_r_k"] = wd["wkv_r_k"].reshape(DEPTH, 1024)
    for i in range(8):
        b = i // 2
        m = dict(wd)
        m["xp"] = f(inputs["x_prompt"][4 * i:4 * i + 4]).reshape(NP_TOK, D)
        m["xs"] = f(inputs["x_sample"][b])
        m["ck"] = f(inputs["cache_k"][b]).reshape(DEPTH, 256, 1024)
        m["cv"] = f(inputs["cache_v"][b]).reshape(DEPTH, 256, 1024)
        m["s0"] = f(inputs["state_wkv"][b]).reshape(DEPTH, 128, 1024)
        m["cvec"] = np.stack([f(inputs["c_ctx"]), f(inputs["c"][b])])
        m.update(CONSTS)
        maps.append(m)
    return maps


def _make_consts():
    cst = {}
    cq = np.arange(64)
    c0 = np.clip(cq - 8, 0, 48)
    ck = np.arange(64)
    ok = (ck[None, :] >= c0[:, None]) & (ck[None, :] < c0[:, None] + 16)
    cst["natmask"] = np.where(ok, 0.0, -1e30).astype(np.float32)
    for L in (256, 2048):
        t = np.linspace(0.0, 1.0, L, dtype=np.float32)[:, None]
        w = 2.0 * np.pi * np.arange(L, dtype=np.float32)[:, None] / L
        f = np.linspace(1e-4, 15, 16, dtype=np.float32)[None, :]
        z = np.concatenate([t, np.cos(f * w), -np.sin(f * w)], -1).astype(np.float32)
        cst["zposT%d" % L] = np.ascontiguousarray(z.T)
        dist = (np.abs(np.arange(L) - L // 2).astype(np.float32) / L)[:, None]
        deltas = np.abs(np.linspace(math.log(1e-2) / 1.5, math.log(1e-2) / 0.3, 1024, dtype=np.float32))[None, :]
        cst["win%d" % L] = np.exp(-dist * deltas).astype(np.float32)
        n = 2 * L
        k = np.arange(L, dtype=np.float64)
        om = 2.0 * np.pi * (k + 0.5) / n
        tt = np.arange(L, dtype=np.float64)
        ang = tt[:, None] * om[None, :]
        Ff = np.zeros((L, 2 * L), np.float32)
        Ffv = Ff.reshape(L, L // 128, 2, 128)
        Ffv[:, :, 0, :] = np.cos(ang).reshape(L, L // 128, 128)
        Ffv[:, :, 1, :] = (-np.sin(ang)).reshape(L, L // 128, 128)
        cst["Ff%d" % L] = Ff
        angi = om[:, None] * (tt[None, :] + L // 2)
        Fi = np.zeros((2 * L, L), np.float32)
        Fiv = Fi.reshape(L // 128, 2, 128, L)
        Fiv[:, 0] = ((2.0 / n) * np.cos(angi)).reshape(L // 128, 128, L)
        Fiv[:, 1] = (-(2.0 / n) * np.sin(angi)).reshape(L // 128, 128, L)
        cst["Fi%d" % L] = Fi
    return cst


CONSTS = _make_consts()
_CACHE = {}


def kernel(**inputs):
    if "nc" not in _CACHE:
        _CACHE["nc"] = build({})[0]
    nc = _CACHE["nc"]
    maps = make_in_maps(inputs)
    res = run_bass_kernel_spmd(nc, maps, core_ids=list(range(8)))
    R = res.results
    yp = np.concatenate([R[i]["yp"].reshape(4, 256, D) for i in range(8)], 0)
    ys = np.stack([R[2 * b]["ys"] for b in range(4)], 0)
    nk = np.concatenate([R[i]["nk"].reshape(4, DEPTH, 256, 16, 64) for i in range(8)], 0)
    nv = np.concatenate([R[i]["nv"].reshape(4, DEPTH, 256, 16, 64) for i in range(8)], 0)
    ns = np.concatenate([R[i]["ns"].reshape(4, DEPTH, 2, 16, 64, 64) for i in range(8)], 0)
    return (yp.astype(np.float32), ys.astype(np.float32), nk.astype(np.float32), nv.astype(np.float32), ns.astype(np.float32))
```
